# Optimizing a Trainium2 kernel written in Bass

```python
import math
import jax, jax.numpy as jnp
from jax import lax
import numpy as np

D_MODEL = 1024
BATCH = 4
SEQ = 8192
DEPTH = 1

D_MIX = D_MODEL
MLA_HEADS = 8
MLA_NOPE = 64
MLA_ROPE = 32
MLA_V = 64
MLA_WIDTH = MLA_HEADS * MLA_V
Q_LORA = 256
KV_LORA = 128
ROPE_THETA = 10000.0
Q_BLOCK = 128
CHUNK = 128
G_HEADS = 8
G_WIDTH = D_MIX - MLA_WIDTH
G_HEAD_DIM = G_WIDTH // G_HEADS
D_IN = Q_LORA + KV_LORA + MLA_ROPE + MLA_WIDTH + 3 * G_WIDTH
DN_ALPHA = (2.0 * DEPTH) ** 0.25
DN_BETA = (8.0 * DEPTH) ** -0.25
EPS = 1e-5

kernel_name = "hybrid_mla_gmlp_parallel_deepnorm"


def _rmsnorm(x, g):
    xf = x.astype(jnp.float32)
    y = xf * lax.rsqrt(jnp.mean(xf * xf, axis=-1, keepdims=True) + EPS)
    return (y * g.astype(jnp.float32)).astype(x.dtype)


def _layernorm(x, g, b):
    xf = x.astype(jnp.float32)
    mu = jnp.mean(xf, axis=-1, keepdims=True)
    var = jnp.mean(jnp.square(xf - mu), axis=-1, keepdims=True)
    y = (xf - mu) * lax.rsqrt(var + EPS)
    return (y * g.astype(jnp.float32) + b.astype(jnp.float32)).astype(x.dtype)


def _rope(t, positions):
    half = MLA_ROPE // 2
    inv_freq = 1.0 / (ROPE_THETA ** (jnp.arange(half, dtype=jnp.float32) / half))
    ang = positions.astype(jnp.float32)[..., None] * inv_freq
    cos = jnp.cos(ang)[:, :, None, :].astype(t.dtype)
    sin = jnp.sin(ang)[:, :, None, :].astype(t.dtype)
    t1, t2 = t[..., :half], t[..., half:]
    return jnp.concatenate([t1 * cos - t2 * sin, t1 * sin + t2 * cos], axis=-1)


def _causal_attention(q, k, v):
    b, s, h, dqk = q.shape
    dv = v.shape[-1]
    nb = s // Q_BLOCK
    scale = 1.0 / math.sqrt(dqk)
    qb = q.reshape(b, nb, Q_BLOCK, h, dqk).transpose(1, 0, 2, 3, 4)
    kpos = jnp.arange(s)

    def one_block(args):
        qi, i = args
        sc = jnp.einsum('bqhd,bkhd->bhqk', qi, k).astype(jnp.float32) * scale
        qpos = i * Q_BLOCK + jnp.arange(Q_BLOCK)
        mask = kpos[None, :] <= qpos[:, None]
        sc = jnp.where(mask[None, None], sc, -jnp.inf)
        p = jax.nn.softmax(sc, axis=-1).astype(v.dtype)
        return jnp.einsum('bhqk,bkhd->bqhd', p, v)

    o = lax.map(one_block, (qb, jnp.arange(nb)))
    return o.transpose(1, 0, 2, 3, 4).reshape(b, s, h, dv)


def setup_inputs(seed: int = 0) -> dict:
    key = jax.random.key(seed)
    ks = jax.random.split(key, 16)
    f32 = jnp.float32
    x = jax.random.normal(ks[0], (BATCH, SEQ, D_MODEL), f32)
    positions = jnp.broadcast_to(jnp.arange(SEQ, dtype=jnp.int32)[None, :], (BATCH, SEQ))
    w_in = jax.random.normal(ks[1], (D_MODEL, D_IN), f32) * D_MODEL ** -0.5
    q_norm_g = 1.0 + 0.02 * jax.random.normal(ks[2], (Q_LORA,), f32)
    w_uq = jax.random.normal(ks[3], (Q_LORA, MLA_HEADS * (MLA_NOPE + MLA_ROPE)), f32) * Q_LORA ** -0.5
    kv_norm_g = 1.0 + 0.02 * jax.random.normal(ks[4], (KV_LORA,), f32)
    w_ukv = jax.random.normal(ks[5], (KV_LORA, MLA_HEADS * (MLA_NOPE + MLA_V)), f32) * KV_LORA ** -0.5
    sgu_norm_g = 1.0 + 0.02 * jax.random.normal(ks[6], (G_WIDTH,), f32)
    sgu_norm_b = 0.02 * jax.random.normal(ks[7], (G_WIDTH,), f32)
    w_spatial = jax.random.normal(ks[8], (G_HEADS, CHUNK, CHUNK), f32) * CHUNK ** -0.5
    b_spatial = 1.0 + 0.02 * jax.random.normal(ks[9], (G_HEADS, CHUNK), f32)
    w_out = jax.random.normal(ks[10], (D_MIX, D_MODEL), f32) * (D_MIX ** -0.5) * DN_BETA
    ln_g = 1.0 + 0.02 * jax.random.normal(ks[11], (D_MODEL,), f32)
    ln_b = 0.02 * jax.random.normal(ks[12], (D_MODEL,), f32)
    return {"x": x, "positions": positions, "w_in": w_in, "q_norm_g": q_norm_g,
            "w_uq": w_uq, "kv_norm_g": kv_norm_g, "w_ukv": w_ukv,
            "sgu_norm_g": sgu_norm_g, "sgu_norm_b": sgu_norm_b,
            "w_spatial": w_spatial, "b_spatial": b_spatial, "w_out": w_out,
            "ln_g": ln_g, "ln_b": ln_b}


def _hybrid_mixer(h, positions, w_in, q_norm_g, w_uq, kv_norm_g, w_ukv,
                  sgu_norm_g, sgu_norm_b, w_spatial, b_spatial, w_out):
    b, s, _ = h.shape
    proj = jnp.einsum('bsd,de->bse', h, w_in)
    splits = np.cumsum([Q_LORA, KV_LORA, MLA_ROPE, MLA_WIDTH, G_WIDTH, G_WIDTH]).tolist()
    c_q, c_kv, k_rope, z_a, u, v = jnp.split(proj, splits, axis=-1)[:6]
    z_b = proj[..., splits[-1]:]

    q = jnp.einsum('bsr,re->bse', _rmsnorm(c_q, q_norm_g), w_uq)
    q = q.reshape(b, s, MLA_HEADS, MLA_NOPE + MLA_ROPE)
    q_nope, q_rope = q[..., :MLA_NOPE], _rope(q[..., MLA_NOPE:], positions)
    kv = jnp.einsum('bsr,re->bse', _rmsnorm(c_kv, kv_norm_g), w_ukv)
    kv = kv.reshape(b, s, MLA_HEADS, MLA_NOPE + MLA_V)
    k_nope, val = kv[..., :MLA_NOPE], kv[..., MLA_NOPE:]
    k_r = jnp.broadcast_to(_rope(k_rope[:, :, None, :], positions), (b, s, MLA_HEADS, MLA_ROPE))
    qf = jnp.concatenate([q_nope, q_rope], axis=-1)
    kf = jnp.concatenate([k_nope, k_r], axis=-1)
    attn = _causal_attention(qf, kf, val).reshape(b, s, MLA_WIDTH)
    out_a = attn * jax.nn.silu(z_a)

    u = jax.nn.gelu(u, approximate=False)
    v = _layernorm(jax.nn.gelu(v, approximate=False), sgu_norm_g, sgu_norm_b)
    nc = s // CHUNK
    vc = v.reshape(b, nc, CHUNK, G_HEADS, G_HEAD_DIM)
    causal = jnp.tril(jnp.ones((CHUNK, CHUNK), dtype=bool))
    w_s = jnp.where(causal[None], w_spatial, 0.0).astype(v.dtype)
    sv = jnp.einsum('hts,bcshd->bcthd', w_s, vc) + b_spatial.T[None, None, :, :, None]
    sgu = u * sv.reshape(b, s, G_WIDTH)
    out_b = sgu * jax.nn.silu(z_b)

    merged = jnp.concatenate([out_a, out_b], axis=-1)
    return jnp.einsum('bse,ed->bsd', merged, w_out)


def reference(x, positions, w_in, q_norm_g, w_uq, kv_norm_g, w_ukv,
              sgu_norm_g, sgu_norm_b, w_spatial, b_spatial, w_out, ln_g, ln_b):
    h = x
    for _ in range(DEPTH):
        y = _hybrid_mixer(h, positions, w_in, q_norm_g, w_uq, kv_norm_g, w_ukv,
                          sgu_norm_g, sgu_norm_b, w_spatial, b_spatial, w_out)
        h = _layernorm(DN_ALPHA * h + y, ln_g, ln_b)
    return h
```

```python
import math
from contextlib import ExitStack

import numpy as np
import concourse.bass as bass
import concourse.mybir as mybir
from concourse.bass_utils import run_bass_kernel_spmd

F32 = mybir.dt.float32
BF16 = mybir.dt.bfloat16
I32 = mybir.dt.int32
AF = mybir.ActivationFunctionType
ALU = mybir.AluOpType

D = 1024
SEQ = 8192
NSLOT = 64
DIN = 2464
NA = 928
NEG = -30000.0
EPS = 1e-5
ALPHA = 2.0 ** 0.25
SCALE = 1.0 / math.sqrt(96.0)
TWO_PI = 2.0 * math.pi

ENGS = ("pe", "act", "dve", "pool", "sp")
import os
STOP = int(os.environ.get("KSTOP", "0"))
NGROUPS = int(os.environ.get("KGROUPS", "8"))
KB_STEPS = int(os.environ.get("KB_STEPS", "-1"))
WC_EARLY = int(os.environ.get("WC_EARLY", "1"))
TAB_ENG = os.environ.get("TAB_ENG", "dve")
KB_NOEPI = int(os.environ.get("KB_NOEPI", "0"))
KB_NOPV = int(os.environ.get("KB_NOPV", "0"))
KB_NOMASK = int(os.environ.get("KB_NOMASK", "0"))


class DmaSem:
    def __init__(self, sem):
        self.sem = sem
        self.count = 0


class Op:
    __slots__ = ("eng", "fn", "deps", "dma", "dsem", "dval", "signal", "sigval", "barrier")

    def __init__(self, eng, fn, deps, dma=False, dsem=None, dval=0):
        self.eng = eng
        self.fn = fn
        self.deps = deps
        self.dma = dma
        self.dsem = dsem
        self.dval = dval
        self.signal = False
        self.sigval = 0
        self.barrier = False


class Sched:
    def __init__(self):
        self.ops = []
        self.state = {}
        self.last_on = {e: None for e in ENGS}
        self.open_dma = []

    def _add(self, op):
        self.ops.append(op)
        idx = len(self.ops) - 1
        self.last_on[op.eng] = idx
        return idx

    def _mk(self, idxs):
        d = {}
        for i in idxs:
            p = self.ops[i]
            d[i] = p.dsem.count if p.dma else None
        return d

    def _deps(self, reads, writes):
        deps = set()
        for k in reads:
            st = self.state.get(k)
            if st is not None and st[0] is not None:
                deps.add(st[0])
        for k in writes:
            st = self.state.get(k)
            if st is not None:
                if st[0] is not None:
                    deps.add(st[0])
                deps.update(st[1])
        return self._mk(deps)

    def _commit(self, idx, reads, writes):
        for k in reads:
            st = self.state.setdefault(k, [None, []])
            st[1].append(idx)
        for k in writes:
            self.state[k] = [idx, []]

    def op(self, eng, fn, r=(), w=()):
        deps = self._deps(r, w)
        idx = self._add(Op(eng, fn, deps))
        self._commit(idx, r, w)
        return idx

    def dma(self, queue, fn, dsem, r=(), w=()):
        deps = self._deps(r, w)
        dsem.count += 16
        idx = self._add(Op(queue, fn, deps, dma=True, dsem=dsem, dval=dsem.count))
        self._commit(idx, r, w)
        self.open_dma.append(idx)
        return idx

    def barrier(self):
        targets = set(i for i in self.last_on.values() if i is not None)
        targets.update(self.open_dma)
        for e in ENGS:
            o = Op(e, None, self._mk(targets))
            o.barrier = True
            self.ops.append(o)
        self.open_dma = []
        self.state = {}

    def emit(self, nc, block, esems, final_waits):
        ops = self.ops
        for i, o in enumerate(ops):
            for d in o.deps:
                p = ops[d]
                if p.dma:
                    continue
                if p.eng == o.eng and not o.barrier:
                    if p.eng == "pe" or p.eng == "sp":
                        continue
                p.signal = True
        cnt = {e: 0 for e in ENGS}
        for o in ops:
            if o.signal:
                cnt[o.eng] += 1
                o.sigval = cnt[o.eng]

        def stream(name):
            def body(e):
                seen = {}
                for o in ops:
                    if o.eng != name:
                        continue
                    for d in sorted(o.deps):
                        p = ops[d]
                        if p.dma:
                            sem, val = p.dsem.sem, o.deps[d]
                        else:
                            if p.eng == name and (name == "pe" or name == "sp") :
                                continue
                            if p.eng == name and o.barrier and name == "pe":
                                continue
                            sem, val = esems[p.eng], p.sigval
                        key = id(sem)
                        if seen.get(key, 0) >= val:
                            continue
                        seen[key] = val
                        e.wait_ge(sem, val)
                    if o.fn is None:
                        continue
                    ins = o.fn(e)
                    if o.dma:
                        ins.then_inc(o.dsem.sem, 16)
                    elif o.signal:
                        ins.then_inc(esems[name], 1)
                if name == "sp":
                    for ds in final_waits:
                        e.wait_ge(ds.sem, ds.count)
            return body

        block.sync(stream("sp"))
        block.tensor(stream("pe"))
        block.scalar(stream("act"))
        block.vector(stream("dve"))
        block.gpsimd(stream("pool"))


def build_program(passes=(0, 1), phases="ABC", debug=False):
    nc = bass.Bass("TRN2", target_bir_lowering=False)

    def din(name, shape, dt=F32):
        return nc.dram_tensor(name, shape, dt, kind="ExternalInput").ap()

    xs = din("xs", [SEQ, D])
    pos = din("pos", [1, SEQ], I32)
    w_in = din("w_in", [D, DIN])
    w_krs = din("w_krs", [D, 32])
    w_uq = din("w_uq", [256, 768])
    w_uqs = din("w_uqs", [256, 768])
    qg = din("qg", [128, 2])
    w_ukv = din("w_ukv", [128, 1024])
    kvg = din("kvg", [128, 1])
    wspT = din("wspT", [128, 8, 128])
    tril = din("tril", [128, 128])
    bspT = din("bspT", [128, 8])
    bsp_ht = din("bsp_ht", [8, 128])
    ind_h = din("ind_h", [8, 512])
    sgu_g = din("sgu_g", [1, 512])
    sgu_b = din("sgu_b", [1, 512])
    ln_g = din("ln_g", [1, D])
    ln_b = din("ln_b", [1, D])
    w_out = din("w_out", [D, D])
    ident = din("ident", [128, 128])
    ropec = din("ropec", [128, 6])
    masks = din("masks", [128, 768])
    y = nc.dram_tensor("y", [SEQ // 2, D], F32, kind="ExternalOutput").ap()
    w_in_bf = nc.dram_tensor("w_in_bf", [D, DIN], BF16, kind="Internal").ap()
    w_krs_bf = nc.dram_tensor("w_krs_bf", [D, 32], BF16, kind="Internal").ap()
    w_out_bf = nc.dram_tensor("w_out_bf", [D, D], BF16, kind="Internal").ap()
    dbg = {}
    if debug:
        def dout(name, shape):
            dbg[name] = nc.dram_tensor(name, shape, F32, kind="ExternalOutput").ap()
        dout("d_ckvn", [128, 512])
        dout("d_krope", [32, 512])
        dout("d_q0", [96, 512])
        dout("d_g", [128, 512])
        dout("d_kb", [96, 512])
        dout("d_vb", [128, 512])
        dout("d_merged", [128, 512])

    S = Sched()
    with ExitStack() as es:
        def sb(name, shape, dt):
            return es.enter_context(nc.sbuf_tensor(name, shape, dt))

        def ps(name, shape, dt):
            return es.enter_context(nc.psum_tensor(name, shape, dt))

        def newsem(name):
            return es.enter_context(nc.semaphore(name))

        dsem_n = [0]

        def dsem():
            dsem_n[0] += 1
            return DmaSem(newsem("dq%d" % dsem_n[0]))

        identb = sb("identb", [128, 128], BF16)
        onesf = sb("onesf", [128, 128], F32)
        maskb = sb("maskb", [128, 768], BF16)
        ropecs = sb("ropecs", [128, 6], F32)
        epst = sb("epst", [128, 1], F32)
        neghalf = sb("neghalf", [128, 1], F32)
        identf = sb("identf", [128, 128], F32)
        rt = sb("rt", [128, 16], F32)
        wuq_g = sb("wuq_g", [128, 2, 768], BF16)
        wuqs_g = sb("wuqs_g", [128, 2, 768], BF16)
        wukv_g = sb("wukv_g", [128, 1024], BF16)
        wsp = sb("wsp", [128, 8, 128], BF16)
        bsp = sb("bsp", [128, 8], F32)
        bspb = sb("bspb", [8, 128], BF16)
        indb = sb("indb", [8, 512], BF16)
        sgug = sb("sgug", [128, 512], F32)
        sgub = sb("sgub", [128, 512], F32)
        lng = sb("lng", [128, D], F32)
        lnb = sb("lnb", [128, D], F32)
        ckvn = sb("ckvn", [128, SEQ], BF16)
        KB = [sb("kb%d" % i, [96, SEQ], BF16) for i in range(2)]
        QT = sb("qt", [96, 8, 2048], BF16)
        MG = sb("mg", [128, 4, 2048], BF16)
        ARENA = 86528
        arena = sb("arena", [128, ARENA // 2], BF16)

        class Carver:
            def __init__(self):
                self.off = 0

            def take(self, cols, dt, parts=128):
                size = 4 if dt in (F32, I32) else 2
                self.off = (self.off + 3) // 4 * 4
                nbytes = cols * size
                assert self.off + nbytes <= ARENA, ("arena overflow", self.off, nbytes)
                v = arena[0:parts, self.off // 2:(self.off + nbytes) // 2]
                self.off += nbytes
                if dt != BF16:
                    v = v.bitcast(dt)
                return v

        PT4 = [ps("pt%d" % i, [128, 2, 512], F32) for i in range(4)]
        PSW = [PT4[0], PT4[1], PT4[2]]
        PSO1 = PT4[3][:, 0, :]
        PSG = PT4[3][:, 1, :]
        pst_banks = [PT4[3][:, 0, :].bitcast(BF16)[:, 0:512],
                     PT4[3][:, 1, :].bitcast(BF16)[:, 0:512]]

        def bank(i):
            return PT4[i // 2][:, i % 2, :]

        bank_rr = [0]

        def next_bank():
            b = bank_rr[0] % 6
            bank_rr[0] += 1
            return b, bank(b), ("ps", b)

        esems = {e: newsem("e_" + e) for e in ENGS}
        block = es.enter_context(nc.Block())

        cv = Carver()
        st_a = cv.take(2 * 768, F32)
        st_b = cv.take(2 * 768, F32)
        st_c = cv.take(1024, F32)
        st_d = cv.take(8 * 128, F32)
        st_e = cv.take(128, F32)
        qgs = cv.take(2, F32)
        kvgs = cv.take(1, F32)
        ds0 = dsem()
        ds0p = dsem()
        S.dma("pool", lambda e: e.dma_start(out=identb[:], in_=ident), ds0p, w=["identb"])
        S.dma("pool", lambda e: e.dma_start(out=maskb[:], in_=masks), ds0p, w=["maskb"])
        S.dma("pool", lambda e: e.dma_start(out=bspb[:], in_=bsp_ht), ds0p, w=["bspb"])
        S.dma("pool", lambda e: e.dma_start(out=indb[:], in_=ind_h), ds0p, w=["indb"])
        S.dma("sp", lambda e: e.dma_start(out=ropecs[:], in_=ropec), ds0, w=["ropecs"])
        S.dma("sp", lambda e: e.dma_start(out=identf[:], in_=ident), ds0, w=["identf"])
        S.dma("sp", lambda e: e.dma_start(out=st_a.rearrange("p (c n) -> p c n", c=2),
                                          in_=w_uq.rearrange("(c p) n -> p c n", p=128)), ds0, w=["st_a"])
        S.dma("sp", lambda e: e.dma_start(out=st_b.rearrange("p (c n) -> p c n", c=2),
                                          in_=w_uqs.rearrange("(c p) n -> p c n", p=128)), ds0, w=["st_b"])
        S.dma("sp", lambda e: e.dma_start(out=st_c, in_=w_ukv), ds0, w=["st_c"])
        S.dma("sp", lambda e: e.dma_start(out=st_d.rearrange("p (h t) -> p h t", h=8), in_=wspT), ds0, w=["st_d"])
        S.dma("sp", lambda e: e.dma_start(out=st_e, in_=tril), ds0, w=["st_e"])
        S.dma("sp", lambda e: e.dma_start(out=qgs, in_=qg), ds0, w=["qgs"])
        S.dma("sp", lambda e: e.dma_start(out=kvgs, in_=kvg), ds0, w=["kvgs"])
        S.dma("sp", lambda e: e.dma_start(out=bsp[:], in_=bspT), ds0, w=["bsp"])
        S.dma("sp", lambda e: e.dma_start(out=sgug[:], in_=sgu_g.partition_broadcast(128)), ds0, w=["sgug"])
        S.dma("sp", lambda e: e.dma_start(out=sgub[:], in_=sgu_b.partition_broadcast(128)), ds0, w=["sgub"])
        S.dma("sp", lambda e: e.dma_start(out=lng[:], in_=ln_g.partition_broadcast(128)), ds0, w=["lng"])
        S.dma("sp", lambda e: e.dma_start(out=lnb[:], in_=ln_b.partition_broadcast(128)), ds0, w=["lnb"])
        dscv = dsem()
        S.op("dve", lambda e: e.memset(onesf[:], 1.0), w=["onesf"])
        S.op("dve", lambda e: e.memset(epst[:], EPS), w=["epst"])
        S.op("dve", lambda e: e.memset(neghalf[:], -0.5), w=["neghalf"])
        for c in range(2):
            S.op("dve", lambda e, c=c: e.tensor_scalar(wuq_g[:, c, :], st_a[:, c * 768:(c + 1) * 768],
                                                       qgs[:, c:c + 1], None, op0=ALU.mult),
                 r=["st_a", "qgs"], w=[("wuq", c)])
            S.op("dve", lambda e, c=c: e.tensor_scalar(wuqs_g[:, c, :], st_b[:, c * 768:(c + 1) * 768],
                                                       qgs[:, c:c + 1], None, op0=ALU.mult),
                 r=["st_b", "qgs"], w=[("wuqs", c)])
        S.op("dve", lambda e: e.tensor_scalar(wukv_g[:], st_c, kvgs[:, 0:1], None, op0=ALU.mult),
             r=["st_c", "kvgs"], w=["wukv"])
        for h in range(8):
            S.op("dve", lambda e, h=h: e.tensor_tensor(wsp[:, h, :], st_d[:, h * 128:(h + 1) * 128], st_e,
                                                       op=ALU.mult), r=["st_d", "st_e"], w=[("wsp", h)])
        S.barrier()

        def phase_A(p):
            cv = Carver()
            WA = cv.take(8 * NA, BF16).rearrange("p (c n) -> p c n", c=8)
            WK = cv.take(8 * 32, BF16).rearrange("p (c n) -> p c n", c=8)
            XS = [cv.take(4 * D, BF16).rearrange("p (s d) -> p s d", s=4) for _ in range(2)]
            XT = cv.take(8 * 512, BF16).rearrange("p (c t) -> p c t", c=8)
            POSI = [cv.take(512, I32) for _ in range(2)]
            posf = cv.take(512, F32)
            tk = posf.bitcast(I32)
            TABS = [[cv.take(512, F32) for _ in range(2)] for _ in range(2)]
            tt = cv.take(512, F32)
            tkf = cv.take(512, F32)
            sq = [cv.take(512, F32) for _ in range(2)]
            rstd = cv.take(512, F32)
            dg = cv.take(512, F32)
            dg2 = cv.take(256, F32)
            rstdq = cv.take(256, F32)
            KR = [[cv.take(512, F32) for _ in range(2)] for _ in range(2)]
            CQB = [cv.take(512, BF16) for _ in range(2)]
            crsr = cv.take(512, F32)
            qtmp = [cv.take(512, F32) for _ in range(2)]
            dsW = dsem()
            dsX = [dsem(), dsem()]
            dsP = [dsem(), dsem()]
            dsD = dsem()
            groups = list(range(8 * p, 8 * p + NGROUPS))

            def load_group(G):
                i = G % 2
                S.dma("pool", lambda e: e.dma_start(
                    out=XS[i], in_=xs[G * 512:(G + 1) * 512, :].rearrange("(s p) d -> p s d", p=128)),
                    dsX[i], w=[("XS", i)])
                S.dma("sp", lambda e: e.dma_start(
                    out=POSI[i], in_=pos[0:1, G * 512:(G + 1) * 512].partition_broadcast(128)),
                    dsP[i], w=[("POSI", i)])

            load_group(groups[0])
            if p == 0:
                dsW0 = dsem()
                S.dma("pool", lambda e: e.dma_start(out=WA, in_=w_in.rearrange("(c p) n -> p c n", p=128)[:, :, 0:NA]),
                      dsW0, w=["WA"])
                S.dma("pool", lambda e: e.dma_start(out=WK, in_=w_krs.rearrange("(c p) n -> p c n", p=128)),
                      dsW0, w=["WK"])
            else:
                S.dma("sp", lambda e: e.dma_start(out=WA, in_=w_in_bf.rearrange("(c p) n -> p c n", p=128)[:, :, 0:NA]),
                      dsW, w=["WA"])
                S.dma("sp", lambda e: e.dma_start(out=WK, in_=w_krs_bf.rearrange("(c p) n -> p c n", p=128)),
                      dsW, w=["WK"])
            if len(groups) > 1:
                load_group(groups[1])
            if p == 0:
                S.dma("pool", lambda e: e.dma_start(out=w_in_bf, in_=w_in), dscv, w=["w_in_bf"])
                S.dma("pool", lambda e: e.dma_start(out=w_krs_bf, in_=w_krs), dscv, w=["w_krs_bf"])
                S.dma("pool", lambda e: e.dma_start(out=w_out_bf, in_=w_out), dscv, w=["w_out_bf"])

            def make_tables(G):
                i = G % 2
                S.op("dve", lambda e: e.tensor_copy(posf, POSI[i]), r=[("POSI", i)], w=["posf"])
                S.op("dve", lambda e: e.tensor_scalar(tt, posf, ropecs[:, 0:1], None, op0=ALU.mult),
                     r=["posf", "ropecs"], w=["tt"])
                S.op("dve", lambda e: e.tensor_copy(tk, tt), r=["tt"], w=["posf"])
                S.op("dve", lambda e: e.tensor_copy(tkf, tk), r=["posf"], w=["tkf"])
                S.op("dve", lambda e: e.tensor_tensor(tt, tt, tkf, op=ALU.subtract), r=["tt", "tkf"], w=["tt"])
                S.op("dve", lambda e: e.scalar_tensor_tensor(out=tt, in0=tt, scalar=0.5, in1=tt,
                                                             op0=ALU.is_gt, op1=ALU.subtract),
                     r=["tt"], w=["tt"])
                S.op("act", lambda e: e.activation(TABS[i][1], tt, AF.Sin, scale=ropecs[:, 1:2]),
                     r=["tt", "ropecs"], w=[("tab", i, 1)])
                S.op("act", lambda e: e.activation(tkf, tt, AF.Abs), r=["tt"], w=["tkf"])
                S.op("act", lambda e: e.activation(TABS[i][0], tkf, AF.Sin, bias=ropecs[:, 3:4], scale=ropecs[:, 2:3]),
                     r=["tkf", "ropecs"], w=[("tab", i, 0)])

            def stage_x1(G):
                i = G % 2
                xt = XT
                for k in range(8):
                    half = k % 2
                    pt = pst_banks[half]
                    for s_ in range(4):
                        S.op("pe", lambda e, pt=pt, s_=s_, k=k: e.transpose(
                            pt[:, s_ * 128:(s_ + 1) * 128], XS[i][:, s_, k * 128:(k + 1) * 128], identb[:]),
                            r=[("XS", i), "identb"], w=[("pst", half)])
                    if k % 2 == 0:
                        S.op("act", lambda e, pt=pt, k=k: e.copy(xt[:, k, :], pt),
                             r=[("pst", half)], w=[("XT", k)])
                    else:
                        S.op("dve", lambda e, pt=pt, k=k: e.tensor_copy(xt[:, k, :], pt),
                             r=[("pst", half)], w=[("XT", k)])

            def stage_x2(G):
                i = G % 2
                tok = slice(G * 512, (G + 1) * 512)
                lt = slice((G - 8 * p) * 256, (G - 8 * p) * 256 + 256)
                xt = XT
                bc, pc, kc = next_bank()
                for k in range(8):
                    S.op("pe", lambda e, pc=pc, k=k: e.matmul(pc, WA[:, k, 256:384], xt[:, k, :],
                                                              start=(k == 0), stop=(k == 7)),
                         r=["WA", ("XT", k)], w=[kc])
                br, pr, kr = next_bank()
                for k in range(8):
                    S.op("pe", lambda e, pr=pr, k=k: e.matmul(pr[64:96, :], WA[:, k, 384:416], xt[:, k, :],
                                                              start=(k == 0), stop=(k == 7), tile_position=(0, 64)),
                         r=["WA", ("XT", k)], w=[kr])
                bs, psw, ks = next_bank()
                for k in range(8):
                    S.op("pe", lambda e, psw=psw, k=k: e.matmul(psw[64:96, :], WK[:, k, :], xt[:, k, :],
                                                                start=(k == 0), stop=(k == 7), tile_position=(0, 64)),
                         r=["WK", ("XT", k)], w=[ks])
                S.op("act", lambda e, pc=pc: e.copy(ckvn[:, tok], pc), r=[kc], w=[("ckvn", G)])
                S.op("act", lambda e, pc=pc: e.activation(sq[0], pc, AF.Square), r=[kc], w=[("sq", 0)])
                S.op("act", lambda e, pr=pr: e.copy(KR[i][0][64:96, :], pr[64:96, :]), r=[kr], w=[("KR", i, 0)])
                S.op("act", lambda e, psw=psw: e.copy(KR[i][1][64:96, :], psw[64:96, :]), r=[ks], w=[("KR", i, 1)])
                bcq, pcq, kcq = next_bank()
                for c in range(2):
                    for k in range(8):
                        S.op("pe", lambda e, pcq=pcq, c=c, k=k: e.matmul(
                            pcq[:, c * 256:(c + 1) * 256], WA[:, k, c * 128:(c + 1) * 128], xt[:, k, 0:256],
                            start=(k == 0), stop=(k == 7)),
                            r=["WA", ("XT", k)], w=[kcq])
                S.op("act", lambda e, pcq=pcq: e.copy(CQB[i], pcq), r=[kcq], w=[("cqb", i)])
                S.op("act", lambda e, pcq=pcq: e.activation(sq[1], pcq, AF.Square), r=[kcq], w=[("sq", 1)])
                for cc in range(2):
                    bz, pz, kz = next_bank()
                    for c2 in range(2):
                        c = cc * 2 + c2
                        for k in range(8):
                            S.op("pe", lambda e, pz=pz, c=c, c2=c2, k=k: e.matmul(
                                pz[:, c2 * 256:(c2 + 1) * 256], WA[:, k, 416 + c * 128:416 + (c + 1) * 128],
                                xt[:, k, 0:256], start=(k == 0), stop=(k == 7)),
                                r=["WA", ("XT", k)], w=[kz])
                    S.op("act", lambda e, pz=pz, cc=cc: e.activation(
                        MG[:, 2 * cc:2 * cc + 2, lt], pz.rearrange("p (c t) -> p c t", c=2), AF.Silu),
                        r=[kz], w=[("MG", cc, G)])
                bq, pq, kq = next_bank()
                for j in range(4):
                    S.op("pe", lambda e, pq=pq, j=j: e.matmul(pq[:, j:j + 1], sq[0][:, j * 128:(j + 1) * 128], onesf[:, 0:1],
                                                              start=True, stop=True, skip_group_check=True),
                         r=[("sq", 0), "onesf"], w=[kq])
                for j in range(2):
                    for c in range(2):
                        S.op("pe", lambda e, pq=pq, c=c, j=j: e.matmul(
                            pq[:, 4 + j:5 + j], sq[1][:, c * 256 + j * 128:c * 256 + (j + 1) * 128], onesf[:, 0:1],
                            start=(c == 0), stop=(c == 1), skip_group_check=True),
                            r=[("sq", 1), "onesf"], w=[kq])
                ro = 8 * i
                S.op("act", lambda e, pq=pq: e.activation(rt[:, ro:ro + 4], pq[:, 0:4], AF.Identity, bias=epst[:, 0:1],
                                                          scale=1.0 / 128.0), r=[kq, "epst"], w=[("rt", i)])
                S.op("act", lambda e, pq=pq: e.activation(rt[:, ro + 4:ro + 6], pq[:, 4:6], AF.Identity, bias=epst[:, 0:1],
                                                          scale=1.0 / 256.0), r=[kq, "epst"], w=[("rt", i)])
                S.op("pool", lambda e: e.tensor_tensor(rt[:, ro:ro + 6], rt[:, ro:ro + 6],
                                                       neghalf[:, 0:1].to_broadcast([128, 6]), op=ALU.pow),
                     r=[("rt", i), "neghalf"], w=[("rt", i)])

            def stage_yn(G):
                i = G % 2
                tok = slice(G * 512, (G + 1) * 512)
                lt = slice((G - 8 * p) * 256, (G - 8 * p) * 256 + 256)
                tabs = TABS[i]
                tabk = [("tab", i, 0), ("tab", i, 1)]
                ro = 8 * i
                cqb = CQB[i]
                for j in range(4):
                    S.op("dve", lambda e, j=j: e.tensor_scalar(dg[:, j * 128:(j + 1) * 128], identf[:], rt[:, ro + j:ro + j + 1],
                                                               None, op0=ALU.mult), r=[("rt", i), "identf"], w=["dg"])
                for j in range(2):
                    S.op("dve", lambda e, j=j: e.tensor_scalar(dg2[:, j * 128:(j + 1) * 128], identf[:],
                                                               rt[:, ro + 4 + j:ro + 5 + j], None, op0=ALU.mult),
                         r=[("rt", i), "identf"], w=["dg2"])
                bb, pbc, kbc = next_bank()
                for j in range(4):
                    S.op("pe", lambda e, pbc=pbc, j=j: e.matmul(pbc[:, j * 128:(j + 1) * 128], onesf[:],
                                                                dg[:, j * 128:(j + 1) * 128], start=True, stop=True,
                                                                skip_group_check=True),
                         r=["dg", "onesf"], w=[kbc])
                bb2, pbc2, kbc2 = next_bank()
                for j in range(2):
                    S.op("pe", lambda e, pbc2=pbc2, j=j: e.matmul(pbc2[:, j * 128:(j + 1) * 128], onesf[:],
                                                                  dg2[:, j * 128:(j + 1) * 128], start=True, stop=True,
                                                                  skip_group_check=True),
                         r=["dg2", "onesf"], w=[kbc2])
                S.op("act", lambda e, pbc=pbc: e.copy(rstd, pbc), r=[kbc], w=["rstd"])
                S.op("act", lambda e, pbc2=pbc2: e.copy(rstdq[:, 0:256], pbc2[:, 0:256]), r=[kbc2], w=["rstdq"])
                S.op("pool", lambda e: e.tensor_tensor(ckvn[:, tok], ckvn[:, tok], rstd, op=ALU.mult),
                     r=[("ckvn", G), "rstd"], w=[("ckvn", G)])
                kr0, kr1 = KR[i]
                S.op("dve", lambda e: e.tensor_tensor(kr0[64:96, :], kr0[64:96, :], tabs[0][64:96, :], op=ALU.mult),
                     r=[("KR", i, 0), tabk[0]], w=[("KR", i, 0)])
                S.op("dve", lambda e: e.tensor_tensor(kr1[64:96, :], kr1[64:96, :], tabs[1][64:96, :], op=ALU.mult),
                     r=[("KR", i, 1), tabk[1]], w=[("KR", i, 1)])
                S.op("pool", lambda e: e.tensor_tensor(KB[0][64:96, tok], kr0[64:96, :], kr1[64:96, :], op=ALU.add),
                     r=[("KR", i, 0), ("KR", i, 1)], w=[("kbr", 0, G)])
                S.op("pool", lambda e: e.tensor_copy(KB[1][64:96, tok], KB[0][64:96, tok]),
                     r=[("kbr", 0, G)], w=[("kbr", 1, G)])
                for ti in range(2):
                    S.op("dve", lambda e, ti=ti: e.tensor_tensor(crsr[0:96, ti * 256:(ti + 1) * 256],
                                                                 tabs[ti][0:96, 0:256], rstdq[0:96, 0:256], op=ALU.mult),
                         r=[tabk[ti], "rstdq"], w=["crsr"])

            def stage_yq(G):
                i = G % 2
                lt = slice((G - 8 * p) * 256, (G - 8 * p) * 256 + 256)
                tok = slice(G * 512, (G + 1) * 512)
                cqb = CQB[i]
                for h in range(8):
                    bh, ph_, kh = next_bank()
                    for half, wsrc, wk in ((0, wuq_g, "wuq"), (1, wuqs_g, "wuqs")):
                        for c in range(2):
                            S.op("pe", lambda e, ph_=ph_, half=half, wsrc=wsrc, c=c, h=h: e.matmul(
                                ph_[0:96, half * 256:(half + 1) * 256], wsrc[:, c, h * 96:(h + 1) * 96],
                                cqb[:, c * 256:(c + 1) * 256], start=(c == 0), stop=(c == 1)),
                                r=[(wk, c), ("cqb", i)], w=[kh])
                    qt_ = qtmp[h % 2]
                    S.op("dve", lambda e, ph_=ph_, qt_=qt_: e.tensor_tensor(qt_[0:96, :], ph_[0:96, :], crsr[0:96, :],
                                                                            op=ALU.mult),
                         r=[kh, "crsr"], w=[("qtmp", h % 2)])
                    S.op("pool", lambda e, qt_=qt_, h=h: e.tensor_tensor(
                        QT[:, h, lt], qt_[0:96, 0:256], qt_[0:96, 256:512], op=ALU.add),
                        r=[("qtmp", h % 2)], w=[("QT", h, G)])
                if debug and G == 8 * p + 1 and p == 0:
                    dd = qtmp[0]
                    kk = ("qtmp", 0)
                    S.op("dve", lambda e: e.tensor_copy(dd, ckvn[:, tok]), r=[("ckvn", G), kk], w=[kk])
                    S.dma("sp", lambda e: e.dma_start(out=dbg["d_ckvn"], in_=dd), dsD, r=[kk])
                    S.op("dve", lambda e: e.tensor_copy(dd[64:96, :], KB[1][64:96, tok]),
                         r=[("kbr", 1, G), kk], w=[kk])
                    S.dma("sp", lambda e: e.dma_start(out=dbg["d_krope"], in_=dd[64:96, :]), dsD, r=[kk])
                    S.op("dve", lambda e: e.tensor_copy(dd[0:96, 0:256], QT[:, 0, lt]), r=[("QT", 0, G), kk], w=[kk])
                    S.op("dve", lambda e: e.tensor_copy(dd[0:96, 256:512], QT[:, 5, lt]), r=[("QT", 5, G), kk], w=[kk])
                    S.dma("sp", lambda e: e.dma_start(out=dbg["d_q0"], in_=dd[0:96, :]), dsD, r=[kk])
                    S.op("dve", lambda e: e.tensor_copy(dd[:, 0:256], MG[:, 0, lt]), r=[("MG", 0, G), kk], w=[kk])
                    S.op("dve", lambda e: e.tensor_copy(dd[:, 256:512], MG[:, 3, lt]), r=[("MG", 1, G), kk], w=[kk])
                    S.dma("sp", lambda e: e.dma_start(out=dbg["d_g"], in_=dd), dsD, r=[kk])

            ng = len(groups)
            stage_x1(groups[0])
            make_tables(groups[0])
            stage_x2(groups[0])
            if ng > 2:
                load_group(groups[2])
            for gi, G in enumerate(groups):
                nxt = groups[gi + 1] if gi + 1 < ng else None
                if nxt is not None:
                    stage_x1(nxt)
                stage_yn(G)
                if nxt is not None:
                    make_tables(nxt)
                    stage_x2(nxt)
                stage_yq(G)
                if gi + 3 < ng:
                    load_group(groups[gi + 3])

        def phase_B(p):
            cv = Carver()
            nslots = 32 * (p + 1)
            VB = [cv.take(64 * 65, BF16).rearrange("p (s d) -> p s d", s=64),
                  cv.take(64 * 128, BF16).rearrange("p (s d) -> p s d", s=64)]
            PT = [cv.take(1024, BF16).rearrange("p (j n) -> p j n", j=2) for _ in range(3)]
            osb = [cv.take(512, F32) for _ in range(2)]
            rec = [cv.take(512, F32) for _ in range(2)]
            otmp = [cv.take(512, F32) for _ in range(2)]
            dsD = dsem()
            S.op("pool", lambda e: e.memset(VB[0][:, :, 64:65], 1.0), w=[("vbc", 0)])
            S.op("pool", lambda e: e.memset(VB[1][:, :, 0:64], 0.0), w=[("vbc", 1)])
            S.op("pool", lambda e: e.memset(VB[1][:, :, 0:1], 1.0), w=[("vbc", 1)])

            def vaug(b, h, slot):
                return VB[b][:, slot, :]

            def vdst(b, c):
                if b == 0:
                    return VB[0][:, c * 8:(c + 1) * 8, 0:64]
                return VB[1][:, c * 8:(c + 1) * 8, 64:128]

            def gen_chunks(h):
                b = h % 2
                out = []
                for c in range(nslots // 4):
                    def kchunk(c=c):
                        tok = slice(c * 512, (c + 1) * 512)
                        S.op("pe", lambda e: e.matmul(PSG[0:64, :], wukv_g[:, h * 128:h * 128 + 64], ckvn[:, tok],
                                                      start=True, stop=True), r=["wukv"], w=["psg"])
                        S.op("dve", lambda e: e.tensor_copy(KB[b][0:64, tok], PSG[0:64, :]),
                             r=["psg"], w=[("kb", b, c)])
                    out.append(kchunk)
                for c in range(nslots // 8):
                    def vchunk(c=c):
                        for s in range(8):
                            slot = c * 8 + s
                            S.op("pe", lambda e, s=s, slot=slot: e.matmul(
                                PSG[:, s * 64:(s + 1) * 64], ckvn[:, slot * 128:(slot + 1) * 128],
                                wukv_g[:, h * 128 + 64:h * 128 + 128], start=True, stop=True),
                                r=["wukv"], w=["psg"])
                        S.op("dve", lambda e: e.tensor_copy(vdst(b, c), PSG.rearrange("p (s d) -> p s d", s=8)),
                             r=["psg"], w=[("vb", b, c)])
                    out.append(vchunk)
                return out

            steps = []
            for h in range(8):
                for g in range(4):
                    gg = 4 * p + g
                    nfull = 4 * gg
                    for i in range(nfull):
                        steps.append(dict(h=h, g=g, slots=(2 * i, 2 * i + 1), c0=0, masks=None))
                    base = 8 * gg
                    tails = [
                        ((base + 0, base + 1), 0, ((0, 128, 0), (128, 256, 0))),
                        ((base + 2, base + 3), 0, ((384, 128, 0), (512, 256, 0))),
                        ((base + 4, base + 5), 256, ((0, 128, 256), (128, 256, 256))),
                        ((base + 6, base + 7), 256, ((384, 128, 256), (512, 256, 256))),
                    ]
                    for slots, c0, mk in tails:
                        steps.append(dict(h=h, g=g, slots=slots, c0=c0, masks=mk))
                    steps[-1]["last"] = True
                    steps[-(nfull + 4)]["first"] = True
            for fn in gen_chunks(0):
                fn()
            pending_gen = []
            pending_epi = []
            n_steps = len(steps)
            if KB_STEPS >= 0:
                n_steps = KB_STEPS
            ocount = [0]

            def emit_qk(si):
                stp = steps[si]
                h, g, c0 = stp["h"], stp["g"], stp["c0"]
                b = h % 2
                wtile = PSW[si % 3]
                for j, slot in enumerate(stp["slots"]):
                    has_mask = stp["masks"] is not None
                    S.op("pe", lambda e, j=j, slot=slot, has_mask=has_mask: e.matmul(
                        wtile[:, j, c0:512], KB[b][:, slot * 128:(slot + 1) * 128],
                        QT[:, h, g * 512 + c0:(g + 1) * 512], start=True, stop=(not has_mask or bool(KB_NOMASK)),
                        skip_group_check=True),
                        r=[("kb", b, slot // 4)], w=[("psw", si % 3)])
                    if has_mask and not KB_NOMASK:
                        mo, mw, mc = stp["masks"][j]
                        S.op("pe", lambda e, j=j, mo=mo, mw=mw, mc=mc: e.matmul(
                            wtile[:, j, mc:mc + mw], identb[:], maskb[:, mo:mo + mw], start=False, stop=True,
                            skip_group_check=True),
                            r=[], w=[("psw", si % 3)])

            def emit_exp(si):
                stp = steps[si]
                c0 = stp["c0"]
                S.op("act", lambda e: e.activation(PT[si % 3][:, :, c0:512], PSW[si % 3][:, :, c0:512], AF.Exp,
                                                   scale=SCALE),
                     r=[("psw", si % 3)], w=[("pt", si % 3)])

            def emit_pv(si):
                stp = steps[si]
                h, g, c0 = stp["h"], stp["g"], stp["c0"]
                b = h % 2
                if stp.get("first"):
                    ocount[0] += 1
                o = ocount[0] % 2
                for j, slot in enumerate(stp["slots"]):
                    first = bool(stp.get("first")) and j == 0
                    last = bool(stp.get("last")) and j == 1
                    mrows = 65 if b == 0 else 128
                    S.op("pe", lambda e, j=j, slot=slot, first=first, last=last, mrows=mrows: e.matmul(
                        PSO1[0:mrows, c0:512], vaug(b, h, slot), PT[si % 3][:, j, c0:512], start=first, stop=last,
                        skip_group_check=True),
                        r=[("pt", si % 3), ("vb", b, slot // 8), ("vbc", b)], w=["pso"])
                if stp.get("last"):
                    mrows = 65 if b == 0 else 128
                    S.op("dve", lambda e, mrows=mrows: e.tensor_copy(osb[o][0:mrows, :], PSO1[0:mrows, :]),
                         r=["pso"], w=[("osb", o)])
                    pending_epi.append((si + 2, h, g, o))

            def emit_epilogue(h, g, o):
                if KB_NOEPI:
                    return
                dr = 64 if h % 2 == 0 else 0
                rows = slice(0, 64) if h % 2 == 0 else slice(64, 128)
                r_ = rec[o]
                ot = otmp[o]
                ob_ = osb[o]
                S.op("dve", lambda e: e.reciprocal(r_[dr:dr + 1, :], ob_[dr:dr + 1, :]), r=[("osb", o)], w=[("rec", o)])
                S.op("pe", lambda e: e.matmul(PSG, onesf[dr:dr + 1, :], r_[dr:dr + 1, :], start=True, stop=True),
                     r=[("rec", o)], w=["psg"])
                S.op("dve", lambda e: e.tensor_tensor(ot[rows, :], ob_[rows, :], PSG[rows, :], op=ALU.mult),
                     r=[("osb", o), "psg"], w=[("otmp", o)])
                mg = MG[rows, h // 2, g * 512:(g + 1) * 512]
                S.op("pool", lambda e: e.tensor_tensor(mg, ot[rows, :], mg, op=ALU.mult),
                     r=[("otmp", o)], w=[("mgo", h, g)])

            if n_steps > 0:
                emit_qk(0)
            if n_steps > 1:
                emit_qk(1)
            cur_h = 0
            gen_every = 3 if p == 0 else 4
            since_gen = 0
            for si in range(n_steps):
                emit_exp(si)
                if si + 2 < n_steps:
                    emit_qk(si + 2)
                if not KB_NOPV:
                    emit_pv(si)
                while pending_epi and pending_epi[0][0] <= si:
                    _, eh, eg, eo = pending_epi.pop(0)
                    emit_epilogue(eh, eg, eo)
                h = steps[si]["h"]
                if steps[si].get("first") and steps[si]["g"] == 0 and h + 1 < 8:
                    pending_gen = gen_chunks(h + 1)
                    since_gen = 0
                since_gen += 1
                if pending_gen and since_gen >= gen_every:
                    pending_gen.pop(0)()
                    since_gen = 0
                if steps[si].get("last") and steps[si]["g"] == 3:
                    while pending_gen:
                        pending_gen.pop(0)()
            while pending_epi:
                _, eh, eg, eo = pending_epi.pop(0)
                emit_epilogue(eh, eg, eo)
            if debug and p == 0:
                dd = cv.take(512, F32)
                S.op("dve", lambda e: e.tensor_copy(dd[0:96, :], KB[1][:, 512:1024]), r=[("kb", 1, 1)], w=["dd"])
                S.dma("sp", lambda e: e.dma_start(out=dbg["d_kb"], in_=dd[0:96, :]), dsD, r=["dd"])
                S.op("dve", lambda e: e.tensor_copy(dd.rearrange("p (s d) -> p s d", s=8), VB[1][:, 8:16, 64:128]),
                     r=[("vb", 1, 1), "dd"], w=["dd"])
                S.dma("sp", lambda e: e.dma_start(out=dbg["d_vb"], in_=dd), dsD, r=["dd"])
                S.op("dve", lambda e: e.tensor_copy(dd[:, 0:256], MG[:, 0, 256:512]),
                     r=[("mgo", 0, 0), ("mgo", 1, 0), "dd"], w=["dd"])
                S.op("dve", lambda e: e.tensor_copy(dd[:, 256:512], MG[:, 3, 1024 + 256:1024 + 512]),
                     r=[("mgo", 6, 2), ("mgo", 7, 2), "dd"], w=["dd"])
                S.dma("sp", lambda e: e.dma_start(out=dbg["d_merged"], in_=dd), dsD, r=["dd"])

        WC_bytes = 8 * 1536 * 2 + 8 * 1024 * 2

        def wc_views():
            base = (ARENA - WC_bytes) // 2
            wci = arena[:, base:base + 8 * 1536].rearrange("p (c n) -> p c n", c=8)
            wco = arena[:, base + 8 * 1536:base + 8 * 1536 + 8 * 1024].rearrange("p (c n) -> p c n", c=8)
            return wci, wco

        dsWC = dsem()

        def load_WC():
            wci, wco = wc_views()
            S.dma("sp", lambda e: e.dma_start(out=wci, in_=w_in_bf.rearrange("(c p) n -> p c n", p=128)[:, :, NA:DIN]),
                  dsWC, w=["WCI"])
            S.dma("sp", lambda e: e.dma_start(out=wco, in_=w_out_bf.rearrange("(c p) n -> p c n", p=128)),
                  dsWC, w=["WCO"])

        dsY = [dsem(), dsem()]

        def phase_C(p):
            cv = Carver()
            wci, wco = wc_views()
            XR = [cv.take(D, F32) for _ in range(2)]
            XB = [cv.take(D, BF16) for _ in range(2)]
            XTC = [cv.take(8 * 128, BF16).rearrange("p (c t) -> p c t", c=8) for _ in range(2)]
            UG = [cv.take(512, F32) for _ in range(2)]
            VG = [cv.take(512, F32) for _ in range(2)]
            ZB = [cv.take(512, F32) for _ in range(2)]
            VN = [cv.take(512, BF16) for _ in range(2)]
            ob = cv.take(512, BF16)
            obT = cv.take(512, BF16)
            hh = cv.take(D, F32)
            YO = [cv.take(D, F32) for _ in range(2)]
            st1 = [cv.take(16, F32) for _ in range(2)]
            st2 = cv.take(24, F32)
            assert cv.off <= ARENA - WC_bytes, ("phase C arena", cv.off)
            dsXR = [dsem(), dsem()]
            dsXB = [dsem(), dsem()]

            def load_block(t):
                i = t % 2
                gslot = 4 * (t // 2) + (t % 2)
                rows = slice(gslot * 128, (gslot + 1) * 128)
                S.dma("sp", lambda e: e.dma_start(out=XR[i], in_=xs[rows, :]), dsXR[i], w=[("XR", i)])
                S.dma("pool", lambda e: e.dma_start(out=XB[i], in_=xs[rows, :]), dsXB[i], w=[("XB", i)])

            def s1a(t):
                i = t % 2
                xtc = XTC[i]
                for hf in range(2):
                    pt = pst_banks[hf]
                    for kk in range(4):
                        k = hf * 4 + kk
                        S.op("pe", lambda e, pt=pt, kk=kk, k=k: e.transpose(
                            pt[:, kk * 128:(kk + 1) * 128], XB[i][:, k * 128:(k + 1) * 128], identb[:]),
                            r=[("XB", i), "identb"], w=[("pst", hf)])
                    dst = xtc[:, hf * 4:hf * 4 + 4, :]
                    if hf == 0:
                        S.op("act", lambda e, pt=pt, dst=dst: e.copy(dst, pt.rearrange("p (c t) -> p c t", c=4)),
                             r=[("pst", hf)], w=[("XTC", i, hf)])
                    else:
                        S.op("dve", lambda e, pt=pt, dst=dst: e.tensor_copy(dst, pt.rearrange("p (c t) -> p c t", c=4)),
                             r=[("pst", hf)], w=[("XTC", i, hf)])

            def s1b_mm(t):
                i = t % 2
                xtc = XTC[i]
                xk = [("XTC", i, 0), ("XTC", i, 1)]
                ug, vg, zb = UG[i], VG[i], ZB[i]
                th = hh[:, 0:512]
                pss = {}
                for j in (1, 0, 2):
                    bb, pb, kb_ = next_bank()
                    pss[j] = (pb, kb_)
                    for k in range(8):
                        S.op("pe", lambda e, pb=pb, j=j, k=k: e.matmul(
                            pb, xtc[:, k, :], wci[:, k, j * 512:(j + 1) * 512], start=(k == 0), stop=(k == 7)),
                            r=xk + ["WCI"], w=[kb_])
                S.op("act", lambda e: e.activation(vg, pss[1][0], AF.Gelu), r=[pss[1][1]], w=[("vg", i)])
                S.op("act", lambda e: e.activation(ug, pss[0][0], AF.Gelu), r=[pss[0][1]], w=[("ug", i)])
                S.op("act", lambda e: e.activation(th, pss[2][0], AF.Tanh, scale=0.5), r=[pss[2][1]], w=[("hh", 0)])
                S.op("act", lambda e: e.copy(zb, pss[2][0]), r=[pss[2][1]], w=[("zb", i)])
                S.op("pool", lambda e: e.tensor_tensor(th, th, zb, op=ALU.mult), r=[("hh", 0), ("zb", i)], w=[("hh", 0)])
                S.op("pool", lambda e: e.tensor_tensor(zb, th, zb, op=ALU.add), r=[("hh", 0), ("zb", i)], w=[("zb", i)])
                S.op("pool", lambda e: e.tensor_tensor(ug, ug, zb, op=ALU.mult), r=[("ug", i), ("zb", i)], w=[("ug", i)])

            def s1b_ln(t):
                i = t % 2
                vg, vn, st = VG[i], VN[i], st1[i]
                S.op("dve", lambda e: e.bn_stats(st[:, 0:6], vg), r=[("vg", i)], w=[("st1a", i)])
                S.op("dve", lambda e: e.bn_aggr(st[:, 6:8], st[:, 0:6]), r=[("st1a", i)], w=[("st1b", i)])
                S.op("pool", lambda e: e.tensor_scalar(st[:, 8:9], st[:, 7:8], EPS, None, op0=ALU.add),
                     r=[("st1b", i)], w=[("st1c", i)])
                S.op("pool", lambda e: e.tensor_tensor(st[:, 8:9], st[:, 8:9], neghalf[:, 0:1], op=ALU.pow),
                     r=[("st1c", i), "neghalf"], w=[("st1c", i)])
                S.op("dve", lambda e: e.scalar_tensor_tensor(out=vg, in0=vg, scalar=st[:, 6:7], in1=sgug[:],
                                                             op0=ALU.subtract, op1=ALU.mult),
                     r=[("vg", i), ("st1b", i), "sgug"], w=[("vg", i)])
                S.op("dve", lambda e: e.scalar_tensor_tensor(out=vn, in0=vg, scalar=st[:, 8:9], in1=sgub[:],
                                                             op0=ALU.mult, op1=ALU.add),
                     r=[("vg", i), ("st1c", i), "sgub"], w=[("vn", i)])

            def s2a(t):
                i = t % 2
                ug, vn = UG[i], VN[i]
                bsv, psv, ksv = next_bank()
                S.op("pe", lambda e: e.matmul(psv, bspb[:], indb[:], start=True, stop=False, skip_group_check=True),
                     r=["bspb", "indb"], w=[ksv])
                for h in range(8):
                    S.op("pe", lambda e, h=h: e.matmul(psv[:, h * 64:(h + 1) * 64], wsp[:, h, :],
                                                      vn[:, h * 64:(h + 1) * 64], start=False, stop=(h == 7),
                                                      skip_group_check=True),
                         r=[("wsp", h), ("vn", i)], w=[ksv])
                S.op("dve", lambda e: e.scalar_tensor_tensor(out=ob, in0=psv, scalar=0.5, in1=ug, op0=ALU.mult, op1=ALU.mult),
                     r=[ksv, ("ug", i)], w=["ob"])

            def s2b(t):
                pt = pst_banks[0]
                for c in range(4):
                    S.op("pe", lambda e, c=c: e.transpose(pt[:, c * 128:(c + 1) * 128],
                                                         ob[:, c * 128:(c + 1) * 128], identb[:]),
                         r=["ob", "identb"], w=[("pst", 0)])
                S.op("act", lambda e: e.copy(obT, pt), r=[("pst", 0)], w=["obT"])

            def s2c(t):
                i = t % 2
                lt = slice((t - 16 * p) * 128, (t - 16 * p + 1) * 128)
                outs = []
                for nh in range(2):
                    bo, po, ko = next_bank()
                    outs.append((po, ko))
                    for c in range(8):
                        if c < 4:
                            lhsT = MG[:, c, lt]
                            rr = []
                        else:
                            lhsT = obT[:, (c - 4) * 128:(c - 3) * 128]
                            rr = ["obT"]
                        S.op("pe", lambda e, po=po, lhsT=lhsT, c=c, nh=nh: e.matmul(
                            po, lhsT, wco[:, c, nh * 512:(nh + 1) * 512], start=(c == 0), stop=(c == 7)),
                            r=rr + ["WCO"], w=[ko])
                for nh in range(2):
                    po, ko = outs[nh]
                    S.op("dve", lambda e, po=po, nh=nh: e.scalar_tensor_tensor(
                        out=hh[:, nh * 512:(nh + 1) * 512], in0=XR[i][:, nh * 512:(nh + 1) * 512], scalar=ALPHA,
                        in1=po, op0=ALU.mult, op1=ALU.add),
                        r=[ko, ("XR", i)], w=[("hh", nh)])
                    S.op("dve", lambda e, nh=nh: e.bn_stats(st2[:, 6 * nh:6 * nh + 6], hh[:, nh * 512:(nh + 1) * 512]),
                         r=[("hh", nh)], w=[("st2a", nh)])
                S.op("dve", lambda e: e.bn_aggr(st2[:, 12:14], st2[:, 0:12]), r=[("st2a", 0), ("st2a", 1)], w=["st2b"])
                S.op("pool", lambda e: e.tensor_scalar(st2[:, 14:15], st2[:, 13:14], EPS, None, op0=ALU.add),
                     r=["st2b"], w=["st2c"])
                S.op("pool", lambda e: e.tensor_tensor(st2[:, 14:15], st2[:, 14:15], neghalf[:, 0:1], op=ALU.pow),
                     r=["st2c", "neghalf"], w=["st2c"])
                S.op("dve", lambda e: e.scalar_tensor_tensor(out=hh, in0=hh, scalar=st2[:, 12:13], in1=lng[:],
                                                             op0=ALU.subtract, op1=ALU.mult),
                     r=[("hh", 0), ("hh", 1), "st2b", "lng"], w=[("hh", 0), ("hh", 1)])
                yo = YO[i]
                S.op("dve", lambda e: e.scalar_tensor_tensor(out=yo, in0=hh, scalar=st2[:, 14:15], in1=lnb[:],
                                                             op0=ALU.mult, op1=ALU.add),
                     r=[("hh", 0), ("hh", 1), "st2c", "lnb"], w=[("YO", i)])
                S.dma("sp", lambda e: e.dma_start(out=y[t * 128:(t + 1) * 128, :], in_=yo),
                      dsY[i], r=[("YO", i)])

            def load_xb(t):
                i = t % 2
                gslot = 4 * (t // 2) + (t % 2)
                rows = slice(gslot * 128, (gslot + 1) * 128)
                S.dma("pool", lambda e: e.dma_start(out=XB[i], in_=xs[rows, :]), dsXB[i], w=[("XB", i)])

            def load_xr(t):
                i = t % 2
                gslot = 4 * (t // 2) + (t % 2)
                rows = slice(gslot * 128, (gslot + 1) * 128)
                S.dma("sp", lambda e: e.dma_start(out=XR[i], in_=xs[rows, :]), dsXR[i], w=[("XR", i)])

            blocks = list(range(16 * p, 16 * p + 16))
            nb = len(blocks)
            load_xb(blocks[0])
            load_xb(blocks[1])
            load_xr(blocks[0])
            load_xr(blocks[1])
            s1a(blocks[0])
            s1b_mm(blocks[0])
            s1b_ln(blocks[0])
            s1a(blocks[1])
            load_xb(blocks[2])
            for bi, t in enumerate(blocks):
                s2a(t)
                if bi + 1 < nb:
                    s1b_mm(blocks[bi + 1])
                if bi + 2 < nb:
                    s1a(blocks[bi + 2])
                if bi + 3 < nb:
                    load_xb(blocks[bi + 3])
                s2b(t)
                if bi + 1 < nb:
                    s1b_ln(blocks[bi + 1])
                s2c(t)
                if bi + 2 < nb:
                    load_xr(blocks[bi + 2])

        for p in passes:
            if "A" in phases:
                phase_A(p)
                S.barrier()
            if "B" in phases:
                if "C" in phases and WC_EARLY:
                    load_WC()
                phase_B(p)
                S.barrier()
            if "C" in phases:
                if "B" not in phases or not WC_EARLY:
                    load_WC()
                phase_C(p)
                S.barrier()
        S.emit(nc, block, esems, dsY)
    return nc


def _slot_perm(par):
    order = []
    for m in range(16):
        if par == 0:
            order += [4 * m, 4 * m + 3, 4 * m + 1, 4 * m + 2]
        else:
            order += [4 * m + 1, 4 * m + 2, 4 * m, 4 * m + 3]
    return order


def make_in_maps(x, positions, w_in, q_norm_g, w_uq, kv_norm_g, w_ukv, sgu_norm_g, sgu_norm_b,
                 w_spatial, b_spatial, w_out, ln_g, ln_b):
    f = np.float32
    w_in = np.ascontiguousarray(w_in, dtype=f)
    swap = np.concatenate([np.arange(16, 32), np.arange(0, 16)])
    w_krs = np.ascontiguousarray(w_in[:, 384:416][:, swap])
    w_uq = np.ascontiguousarray(w_uq, dtype=f)
    w_uqs = np.zeros_like(w_uq)
    for h in range(8):
        w_uqs[:, h * 96 + 64:h * 96 + 96] = w_uq[:, h * 96 + 64:h * 96 + 96][:, swap]
    qg = np.ascontiguousarray(np.asarray(q_norm_g, dtype=f).reshape(2, 128).T)
    kvg = np.ascontiguousarray(np.asarray(kv_norm_g, dtype=f).reshape(128, 1))
    wspT = np.ascontiguousarray(np.transpose(np.asarray(w_spatial, dtype=f), (2, 0, 1)))
    si = np.arange(128)
    tril = (si[:, None] <= si[None, :]).astype(f)
    bspT = np.ascontiguousarray(np.asarray(b_spatial, dtype=f).T)
    ident = np.eye(128, dtype=f)
    invf = (1.0 / (10000.0 ** (np.arange(16, dtype=np.float64) / 16.0)))
    ropec = np.zeros((128, 6), f)
    ropec[:, 1] = -TWO_PI
    ropec[:, 2] = -TWO_PI
    ropec[:, 3] = 0.5 * math.pi
    for pp in range(64, 96):
        i = (pp - 64) % 16
        ropec[pp, 0] = invf[i] / TWO_PI
        ropec[pp, 1] = TWO_PI if pp < 80 else -TWO_PI
    tri = np.where(si[:, None] > si[None, :], NEG, 0.0).astype(f)
    allm = np.full((128, 128), NEG, f)
    zero = np.zeros((128, 128), f)
    ind_h = np.zeros((8, 512), f)
    for h in range(8):
        ind_h[h, h * 64:(h + 1) * 64] = 1.0
    common = dict(bsp_ht=np.ascontiguousarray(np.asarray(b_spatial, dtype=f)), ind_h=ind_h, w_in=w_in, w_krs=w_krs, w_uq=w_uq, w_uqs=w_uqs, qg=qg, w_ukv=np.ascontiguousarray(w_ukv, dtype=f),
                  kvg=kvg, wspT=wspT, tril=tril, bspT=bspT,
                  sgu_g=np.asarray(sgu_norm_g, dtype=f).reshape(1, 512), sgu_b=np.asarray(sgu_norm_b, dtype=f).reshape(1, 512),
                  ln_g=np.asarray(ln_g, dtype=f).reshape(1, D), ln_b=np.asarray(ln_b, dtype=f).reshape(1, D),
                  w_out=np.ascontiguousarray(w_out, dtype=f), ident=ident, ropec=ropec)
    in_maps = []
    perms = []
    for c in range(8):
        b, par = c // 2, c % 2
        order = _slot_perm(par)
        perms.append(order)
        xb = np.asarray(x[b], dtype=f).reshape(64, 128, D)[order].reshape(SEQ, D)
        pb = np.asarray(positions[b], dtype=np.int32).reshape(64, 128)[order].reshape(1, SEQ)
        mA = allm if par == 0 else zero
        mB = zero if par == 0 else allm
        mk = np.concatenate([tri, allm, tri, mA, allm, mB], axis=1)
        d = dict(common)
        d.update(xs=np.ascontiguousarray(xb), pos=np.ascontiguousarray(pb), masks=np.ascontiguousarray(mk))
        in_maps.append(d)
    return in_maps, perms


_NC_CACHE = {}


def kernel(x, positions, w_in, q_norm_g, w_uq, kv_norm_g, w_ukv, sgu_norm_g, sgu_norm_b,
           w_spatial, b_spatial, w_out, ln_g, ln_b):
    in_maps, perms = make_in_maps(x, positions, w_in, q_norm_g, w_uq, kv_norm_g, w_ukv, sgu_norm_g,
                                  sgu_norm_b, w_spatial, b_spatial, w_out, ln_g, ln_b)
    if "nc" not in _NC_CACHE:
        _NC_CACHE["nc"] = build_program()
    nc = _NC_CACHE["nc"]
    res = run_bass_kernel_spmd(nc, in_maps, core_ids=list(range(8)))
    out = np.empty((4, SEQ, D), np.float32)
    for c in range(8):
        b = c // 2
        order = perms[c]
        yc = np.asarray(res.results[c]["y"], dtype=np.float32).reshape(32, 128, D)
        ov = out[b].reshape(64, 128, D)
        for t in range(32):
            slot = 4 * (t // 2) + (t % 2)
            ov[order[slot]] = yc[t]
    return out
```

```python
import math
from contextlib import ExitStack

import numpy as np
import concourse.bass as bass
import concourse.mybir as mybir
from concourse.bass_utils import run_bass_kernel_spmd

F32 = mybir.dt.float32
BF16 = mybir.dt.bfloat16
I32 = mybir.dt.int32
AF = mybir.ActivationFunctionType
ALU = mybir.AluOpType

D = 1024
SEQ = 8192
NSLOT = 64
DIN = 2464
NA = 928
NEG = -30000.0
EPS = 1e-5
ALPHA = 2.0 ** 0.25
SCALE = 1.0 / math.sqrt(96.0)
TWO_PI = 2.0 * math.pi

ENGS = ("pe", "act", "dve", "pool", "sp")
import os
STOP = int(os.environ.get("KSTOP", "0"))
NGROUPS = int(os.environ.get("KGROUPS", "8"))
KB_STEPS = int(os.environ.get("KB_STEPS", "-1"))
WC_EARLY = int(os.environ.get("WC_EARLY", "1"))
TAB_ENG = os.environ.get("TAB_ENG", "dve")
KB_NOEPI = int(os.environ.get("KB_NOEPI", "0"))
KB_NOPV = int(os.environ.get("KB_NOPV", "0"))
KB_NOMASK = int(os.environ.get("KB_NOMASK", "0"))


class DmaSem:
    def __init__(self, sem):
        self.sem = sem
        self.count = 0


class Op:
    __slots__ = ("eng", "fn", "deps", "dma", "dsem", "dval", "signal", "sigval", "barrier")

    def __init__(self, eng, fn, deps, dma=False, dsem=None, dval=0):
        self.eng = eng
        self.fn = fn
        self.deps = deps
        self.dma = dma
        self.dsem = dsem
        self.dval = dval
        self.signal = False
        self.sigval = 0
        self.barrier = False


class Sched:
    def __init__(self):
        self.ops = []
        self.state = {}
        self.last_on = {e: None for e in ENGS}
        self.open_dma = []

    def _add(self, op):
        self.ops.append(op)
        idx = len(self.ops) - 1
        self.last_on[op.eng] = idx
        return idx

    def _mk(self, idxs):
        d = {}
        for i in idxs:
            p = self.ops[i]
            d[i] = p.dsem.count if p.dma else None
        return d

    def _deps(self, reads, writes):
        deps = set()
        for k in reads:
            st = self.state.get(k)
            if st is not None and st[0] is not None:
                deps.add(st[0])
        for k in writes:
            st = self.state.get(k)
            if st is not None:
                if st[0] is not None:
                    deps.add(st[0])
                deps.update(st[1])
        return self._mk(deps)

    def _commit(self, idx, reads, writes):
        for k in reads:
            st = self.state.setdefault(k, [None, []])
            st[1].append(idx)
        for k in writes:
            self.state[k] = [idx, []]

    def op(self, eng, fn, r=(), w=()):
        deps = self._deps(r, w)
        idx = self._add(Op(eng, fn, deps))
        self._commit(idx, r, w)
        return idx

    def dma(self, queue, fn, dsem, r=(), w=()):
        deps = self._deps(r, w)
        dsem.count += 16
        idx = self._add(Op(queue, fn, deps, dma=True, dsem=dsem, dval=dsem.count))
        self._commit(idx, r, w)
        self.open_dma.append(idx)
        return idx

    def barrier(self):
        targets = set(i for i in self.last_on.values() if i is not None)
        targets.update(self.open_dma)
        for e in ENGS:
            o = Op(e, None, self._mk(targets))
            o.barrier = True
            self.ops.append(o)
        self.open_dma = []
        self.state = {}

    def emit(self, nc, block, esems, final_waits):
        ops = self.ops
        for i, o in enumerate(ops):
            for d in o.deps:
                p = ops[d]
                if p.dma:
                    continue
                if p.eng == o.eng and not o.barrier:
                    if p.eng == "pe" or p.eng == "sp":
                        continue
                p.signal = True
        cnt = {e: 0 for e in ENGS}
        for o in ops:
            if o.signal:
                cnt[o.eng] += 1
                o.sigval = cnt[o.eng]

        def stream(name):
            def body(e):
                seen = {}
                for o in ops:
                    if o.eng != name:
                        continue
                    for d in sorted(o.deps):
                        p = ops[d]
                        if p.dma:
                            sem, val = p.dsem.sem, o.deps[d]
                        else:
                            if p.eng == name and (name == "pe" or name == "sp") :
                                continue
                            if p.eng == name and o.barrier and name == "pe":
                                continue
                            sem, val = esems[p.eng], p.sigval
                        key = id(sem)
                        if seen.get(key, 0) >= val:
                            continue
                        seen[key] = val
                        e.wait_ge(sem, val)
                    if o.fn is None:
                        continue
                    ins = o.fn(e)
                    if o.dma:
                        ins.then_inc(o.dsem.sem, 16)
                    elif o.signal:
                        ins.then_inc(esems[name], 1)
                if name == "sp":
                    for ds in final_waits:
                        e.wait_ge(ds.sem, ds.count)
            return body

        block.sync(stream("sp"))
        block.tensor(stream("pe"))
        block.scalar(stream("act"))
        block.vector(stream("dve"))
        block.gpsimd(stream("pool"))


def build_program(passes=(0, 1), phases="ABC", debug=False):
    nc = bass.Bass("TRN2", target_bir_lowering=False)

    def din(name, shape, dt=F32):
        return nc.dram_tensor(name, shape, dt, kind="ExternalInput").ap()

    xs = din("xs", [SEQ, D])
    pos = din("pos", [1, SEQ], I32)
    w_in = din("w_in", [D, DIN])
    w_krs = din("w_krs", [D, 32])
    w_uq = din("w_uq", [256, 768])
    w_uqs = din("w_uqs", [256, 768])
    qg = din("qg", [128, 2])
    w_ukv = din("w_ukv", [128, 1024])
    kvg = din("kvg", [128, 1])
    wspT = din("wspT", [128, 8, 128])
    tril = din("tril", [128, 128])
    bspT = din("bspT", [128, 8])
    bsp_ht = din("bsp_ht", [8, 128])
    ind_h = din("ind_h", [8, 512])
    sgu_g = din("sgu_g", [1, 512])
    sgu_b = din("sgu_b", [1, 512])
    ln_g = din("ln_g", [1, D])
    ln_b = din("ln_b", [1, D])
    w_out = din("w_out", [D, D])
    ident = din("ident", [128, 128])
    ropec = din("ropec", [128, 6])
    masks = din("masks", [128, 768])
    y = nc.dram_tensor("y", [SEQ // 2, D], F32, kind="ExternalOutput").ap()
    w_in_bf = nc.dram_tensor("w_in_bf", [D, DIN], BF16, kind="Internal").ap()
    w_krs_bf = nc.dram_tensor("w_krs_bf", [D, 32], BF16, kind="Internal").ap()
    w_out_bf = nc.dram_tensor("w_out_bf", [D, D], BF16, kind="Internal").ap()
    dbg = {}
    if debug:
        def dout(name, shape):
            dbg[name] = nc.dram_tensor(name, shape, F32, kind="ExternalOutput").ap()
        dout("d_ckvn", [128, 512])
        dout("d_krope", [32, 512])
        dout("d_q0", [96, 512])
        dout("d_g", [128, 512])
        dout("d_kb", [96, 512])
        dout("d_vb", [128, 512])
        dout("d_merged", [128, 512])

    S = Sched()
    with ExitStack() as es:
        def sb(name, shape, dt):
            return es.enter_context(nc.sbuf_tensor(name, shape, dt))

        def ps(name, shape, dt):
            return es.enter_context(nc.psum_tensor(name, shape, dt))

        def newsem(name):
            return es.enter_context(nc.semaphore(name))

        dsem_n = [0]

        def dsem():
            dsem_n[0] += 1
            return DmaSem(newsem("dq%d" % dsem_n[0]))

        identb = sb("identb", [128, 128], BF16)
        onesf = sb("onesf", [128, 128], F32)
        maskb = sb("maskb", [128, 768], BF16)
        ropecs = sb("ropecs", [128, 6], F32)
        epst = sb("epst", [128, 1], F32)
        neghalf = sb("neghalf", [128, 1], F32)
        identf = sb("identf", [128, 128], F32)
        rt = sb("rt", [128, 16], F32)
        wuq_g = sb("wuq_g", [128, 2, 768], BF16)
        wuqs_g = sb("wuqs_g", [128, 2, 768], BF16)
        wukv_g = sb("wukv_g", [128, 1024], BF16)
        wsp = sb("wsp", [128, 8, 128], BF16)
        bsp = sb("bsp", [128, 8], F32)
        bspb = sb("bspb", [8, 128], BF16)
        indb = sb("indb", [8, 512], BF16)
        sgug = sb("sgug", [128, 512], F32)
        sgub = sb("sgub", [128, 512], F32)
        lng = sb("lng", [128, D], F32)
        lnb = sb("lnb", [128, D], F32)
        ckvn = sb("ckvn", [128, SEQ], BF16)
        KB = [sb("kb%d" % i, [96, SEQ], BF16) for i in range(2)]
        QT = sb("qt", [96, 8, 2048], BF16)
        MG = sb("mg", [128, 4, 2048], BF16)
        ARENA = 86528
        arena = sb("arena", [128, ARENA // 2], BF16)

        class Carver:
            def __init__(self):
                self.off = 0

            def take(self, cols, dt, parts=128):
                size = 4 if dt in (F32, I32) else 2
                self.off = (self.off + 3) // 4 * 4
                nbytes = cols * size
                assert self.off + nbytes <= ARENA, ("arena overflow", self.off, nbytes)
                v = arena[0:parts, self.off // 2:(self.off + nbytes) // 2]
                self.off += nbytes
                if dt != BF16:
                    v = v.bitcast(dt)
                return v

        PT4 = [ps("pt%d" % i, [128, 2, 512], F32) for i in range(4)]
        PSW = [PT4[0], PT4[1], PT4[2]]
        PSO1 = PT4[3][:, 0, :]
        PSG = PT4[3][:, 1, :]
        pst_banks = [PT4[3][:, 0, :].bitcast(BF16)[:, 0:512],
                     PT4[3][:, 1, :].bitcast(BF16)[:, 0:512]]

        def bank(i):
            return PT4[i // 2][:, i % 2, :]

        bank_rr = [0]

        def next_bank():
            b = bank_rr[0] % 6
            bank_rr[0] += 1
            return b, bank(b), ("ps", b)

        esems = {e: newsem("e_" + e) for e in ENGS}
        block = es.enter_context(nc.Block())

        cv = Carver()
        st_a = cv.take(2 * 768, F32)
        st_b = cv.take(2 * 768, F32)
        st_c = cv.take(1024, F32)
        st_d = cv.take(8 * 128, F32)
        st_e = cv.take(128, F32)
        qgs = cv.take(2, F32)
        kvgs = cv.take(1, F32)
        ds0 = dsem()
        ds0p = dsem()
        S.dma("pool", lambda e: e.dma_start(out=identb[:], in_=ident), ds0p, w=["identb"])
        S.dma("pool", lambda e: e.dma_start(out=maskb[:], in_=masks), ds0p, w=["maskb"])
        S.dma("pool", lambda e: e.dma_start(out=bspb[:], in_=bsp_ht), ds0p, w=["bspb"])
        S.dma("pool", lambda e: e.dma_start(out=indb[:], in_=ind_h), ds0p, w=["indb"])
        S.dma("sp", lambda e: e.dma_start(out=ropecs[:], in_=ropec), ds0, w=["ropecs"])
        S.dma("sp", lambda e: e.dma_start(out=identf[:], in_=ident), ds0, w=["identf"])
        S.dma("sp", lambda e: e.dma_start(out=st_a.rearrange("p (c n) -> p c n", c=2),
                                          in_=w_uq.rearrange("(c p) n -> p c n", p=128)), ds0, w=["st_a"])
        S.dma("sp", lambda e: e.dma_start(out=st_b.rearrange("p (c n) -> p c n", c=2),
                                          in_=w_uqs.rearrange("(c p) n -> p c n", p=128)), ds0, w=["st_b"])
        S.dma("sp", lambda e: e.dma_start(out=st_c, in_=w_ukv), ds0, w=["st_c"])
        S.dma("sp", lambda e: e.dma_start(out=st_d.rearrange("p (h t) -> p h t", h=8), in_=wspT), ds0, w=["st_d"])
        S.dma("sp", lambda e: e.dma_start(out=st_e, in_=tril), ds0, w=["st_e"])
        S.dma("sp", lambda e: e.dma_start(out=qgs, in_=qg), ds0, w=["qgs"])
        S.dma("sp", lambda e: e.dma_start(out=kvgs, in_=kvg), ds0, w=["kvgs"])
        S.dma("sp", lambda e: e.dma_start(out=bsp[:], in_=bspT), ds0, w=["bsp"])
        S.dma("sp", lambda e: e.dma_start(out=sgug[:], in_=sgu_g.partition_broadcast(128)), ds0, w=["sgug"])
        S.dma("sp", lambda e: e.dma_start(out=sgub[:], in_=sgu_b.partition_broadcast(128)), ds0, w=["sgub"])
        S.dma("sp", lambda e: e.dma_start(out=lng[:], in_=ln_g.partition_broadcast(128)), ds0, w=["lng"])
        S.dma("sp", lambda e: e.dma_start(out=lnb[:], in_=ln_b.partition_broadcast(128)), ds0, w=["lnb"])
        dscv = dsem()
        S.op("dve", lambda e: e.memset(onesf[:], 1.0), w=["onesf"])
        S.op("dve", lambda e: e.memset(epst[:], EPS), w=["epst"])
        S.op("dve", lambda e: e.memset(neghalf[:], -0.5), w=["neghalf"])
        for c in range(2):
            S.op("dve", lambda e, c=c: e.tensor_scalar(wuq_g[:, c, :], st_a[:, c * 768:(c + 1) * 768],
                                                       qgs[:, c:c + 1], None, op0=ALU.mult),
                 r=["st_a", "qgs"], w=[("wuq", c)])
            S.op("dve", lambda e, c=c: e.tensor_scalar(wuqs_g[:, c, :], st_b[:, c * 768:(c + 1) * 768],
                                                       qgs[:, c:c + 1], None, op0=ALU.mult),
                 r=["st_b", "qgs"], w=[("wuqs", c)])
        S.op("dve", lambda e: e.tensor_scalar(wukv_g[:], st_c, kvgs[:, 0:1], None, op0=ALU.mult),
             r=["st_c", "kvgs"], w=["wukv"])
        for h in range(8):
            S.op("dve", lambda e, h=h: e.tensor_tensor(wsp[:, h, :], st_d[:, h * 128:(h + 1) * 128], st_e,
                                                       op=ALU.mult), r=["st_d", "st_e"], w=[("wsp", h)])
        S.barrier()

        def phase_A(p):
            cv = Carver()
            WA = cv.take(8 * NA, BF16).rearrange("p (c n) -> p c n", c=8)
            WK = cv.take(8 * 32, BF16).rearrange("p (c n) -> p c n", c=8)
            XS = [cv.take(4 * D, BF16).rearrange("p (s d) -> p s d", s=4) for _ in range(2)]
            XT = cv.take(8 * 512, BF16).rearrange("p (c t) -> p c t", c=8)
            POSI = [cv.take(512, I32) for _ in range(2)]
            posf = cv.take(512, F32)
            tk = posf.bitcast(I32)
            TABS = [[cv.take(512, F32) for _ in range(2)] for _ in range(2)]
            tt = cv.take(512, F32)
            tkf = cv.take(512, F32)
            sq = [cv.take(512, F32) for _ in range(2)]
            rstd = cv.take(512, F32)
            dg = cv.take(512, F32)
            dg2 = cv.take(256, F32)
            rstdq = cv.take(256, F32)
            KR = [[cv.take(512, F32) for _ in range(2)] for _ in range(2)]
            CQB = [cv.take(512, BF16) for _ in range(2)]
            crsr = cv.take(512, F32)
            qtmp = [cv.take(512, F32) for _ in range(2)]
            dsW = dsem()
            dsX = [dsem(), dsem()]
            dsP = [dsem(), dsem()]
            dsD = dsem()
            groups = list(range(8 * p, 8 * p + NGROUPS))

            def load_group(G):
                i = G % 2
                S.dma("pool", lambda e: e.dma_start(
                    out=XS[i], in_=xs[G * 512:(G + 1) * 512, :].rearrange("(s p) d -> p s d", p=128)),
                    dsX[i], w=[("XS", i)])
                S.dma("sp", lambda e: e.dma_start(
                    out=POSI[i], in_=pos[0:1, G * 512:(G + 1) * 512].partition_broadcast(128)),
                    dsP[i], w=[("POSI", i)])

            load_group(groups[0])
            if p == 0:
                dsW0 = dsem()
                S.dma("pool", lambda e: e.dma_start(out=WA, in_=w_in.rearrange("(c p) n -> p c n", p=128)[:, :, 0:NA]),
                      dsW0, w=["WA"])
                S.dma("pool", lambda e: e.dma_start(out=WK, in_=w_krs.rearrange("(c p) n -> p c n", p=128)),
                      dsW0, w=["WK"])
            else:
                S.dma("sp", lambda e: e.dma_start(out=WA, in_=w_in_bf.rearrange("(c p) n -> p c n", p=128)[:, :, 0:NA]),
                      dsW, w=["WA"])
                S.dma("sp", lambda e: e.dma_start(out=WK, in_=w_krs_bf.rearrange("(c p) n -> p c n", p=128)),
                      dsW, w=["WK"])
            if len(groups) > 1:
                load_group(groups[1])
            if p == 0:
                S.dma("pool", lambda e: e.dma_start(out=w_in_bf, in_=w_in), dscv, w=["w_in_bf"])
                S.dma("pool", lambda e: e.dma_start(out=w_krs_bf, in_=w_krs), dscv, w=["w_krs_bf"])
                S.dma("pool", lambda e: e.dma_start(out=w_out_bf, in_=w_out), dscv, w=["w_out_bf"])

            def make_tables(G):
                i = G % 2
                S.op("dve", lambda e: e.tensor_copy(posf, POSI[i]), r=[("POSI", i)], w=["posf"])
                S.op("dve", lambda e: e.tensor_scalar(tt, posf, ropecs[:, 0:1], None, op0=ALU.mult),
                     r=["posf", "ropecs"], w=["tt"])
                S.op("dve", lambda e: e.tensor_copy(tk, tt), r=["tt"], w=["posf"])
                S.op("dve", lambda e: e.tensor_copy(tkf, tk), r=["posf"], w=["tkf"])
                S.op("dve", lambda e: e.tensor_tensor(tt, tt, tkf, op=ALU.subtract), r=["tt", "tkf"], w=["tt"])
                S.op("dve", lambda e: e.scalar_tensor_tensor(out=tt, in0=tt, scalar=0.5, in1=tt,
                                                             op0=ALU.is_gt, op1=ALU.subtract),
                     r=["tt"], w=["tt"])
                S.op("act", lambda e: e.activation(TABS[i][1], tt, AF.Sin, scale=ropecs[:, 1:2]),
                     r=["tt", "ropecs"], w=[("tab", i, 1)])
                S.op("act", lambda e: e.activation(tkf, tt, AF.Abs), r=["tt"], w=["tkf"])
                S.op("act", lambda e: e.activation(TABS[i][0], tkf, AF.Sin, bias=ropecs[:, 3:4], scale=ropecs[:, 2:3]),
                     r=["tkf", "ropecs"], w=[("tab", i, 0)])

            def stage_x1(G):
                i = G % 2
                xt = XT
                for k in range(8):
                    half = k % 2
                    pt = pst_banks[half]
                    for s_ in range(4):
                        S.op("pe", lambda e, pt=pt, s_=s_, k=k: e.transpose(
                            pt[:, s_ * 128:(s_ + 1) * 128], XS[i][:, s_, k * 128:(k + 1) * 128], identb[:]),
                            r=[("XS", i), "identb"], w=[("pst", half)])
                    if k % 2 == 0:
                        S.op("act", lambda e, pt=pt, k=k: e.copy(xt[:, k, :], pt),
                             r=[("pst", half)], w=[("XT", k)])
                    else:
                        S.op("dve", lambda e, pt=pt, k=k: e.tensor_copy(xt[:, k, :], pt),
                             r=[("pst", half)], w=[("XT", k)])

            def stage_x2(G):
                i = G % 2
                tok = slice(G * 512, (G + 1) * 512)
                lt = slice((G - 8 * p) * 256, (G - 8 * p) * 256 + 256)
                xt = XT
                bc, pc, kc = next_bank()
                for k in range(8):
                    S.op("pe", lambda e, pc=pc, k=k: e.matmul(pc, WA[:, k, 256:384], xt[:, k, :],
                                                              start=(k == 0), stop=(k == 7)),
                         r=["WA", ("XT", k)], w=[kc])
                br, pr, kr = next_bank()
                for k in range(8):
                    S.op("pe", lambda e, pr=pr, k=k: e.matmul(pr[64:96, :], WA[:, k, 384:416], xt[:, k, :],
                                                              start=(k == 0), stop=(k == 7), tile_position=(0, 64)),
                         r=["WA", ("XT", k)], w=[kr])
                bs, psw, ks = next_bank()
                for k in range(8):
                    S.op("pe", lambda e, psw=psw, k=k: e.matmul(psw[64:96, :], WK[:, k, :], xt[:, k, :],
                                                                start=(k == 0), stop=(k == 7), tile_position=(0, 64)),
                         r=["WK", ("XT", k)], w=[ks])
                S.op("act", lambda e, pc=pc: e.copy(ckvn[:, tok], pc), r=[kc], w=[("ckvn", G)])
                S.op("act", lambda e, pc=pc: e.activation(sq[0], pc, AF.Square), r=[kc], w=[("sq", 0)])
                S.op("act", lambda e, pr=pr: e.copy(KR[i][0][64:96, :], pr[64:96, :]), r=[kr], w=[("KR", i, 0)])
                S.op("act", lambda e, psw=psw: e.copy(KR[i][1][64:96, :], psw[64:96, :]), r=[ks], w=[("KR", i, 1)])
                bcq, pcq, kcq = next_bank()
                for c in range(2):
                    for k in range(8):
                        S.op("pe", lambda e, pcq=pcq, c=c, k=k: e.matmul(
                            pcq[:, c * 256:(c + 1) * 256], WA[:, k, c * 128:(c + 1) * 128], xt[:, k, 0:256],
                            start=(k == 0), stop=(k == 7)),
                            r=["WA", ("XT", k)], w=[kcq])
                S.op("act", lambda e, pcq=pcq: e.copy(CQB[i], pcq), r=[kcq], w=[("cqb", i)])
                S.op("act", lambda e, pcq=pcq: e.activation(sq[1], pcq, AF.Square), r=[kcq], w=[("sq", 1)])
                for cc in range(2):
                    bz, pz, kz = next_bank()
                    for c2 in range(2):
                        c = cc * 2 + c2
                        for k in range(8):
                            S.op("pe", lambda e, pz=pz, c=c, c2=c2, k=k: e.matmul(
                                pz[:, c2 * 256:(c2 + 1) * 256], WA[:, k, 416 + c * 128:416 + (c + 1) * 128],
                                xt[:, k, 0:256], start=(k == 0), stop=(k == 7)),
                                r=["WA", ("XT", k)], w=[kz])
                    S.op("act", lambda e, pz=pz, cc=cc: e.activation(
                        MG[:, 2 * cc:2 * cc + 2, lt], pz.rearrange("p (c t) -> p c t", c=2), AF.Silu),
                        r=[kz], w=[("MG", cc, G)])
                bq, pq, kq = next_bank()
                for j in range(4):
                    S.op("pe", lambda e, pq=pq, j=j: e.matmul(pq[:, j:j + 1], sq[0][:, j * 128:(j + 1) * 128], onesf[:, 0:1],
                                                              start=True, stop=True, skip_group_check=True),
                         r=[("sq", 0), "onesf"], w=[kq])
                for j in range(2):
                    for c in range(2):
                        S.op("pe", lambda e, pq=pq, c=c, j=j: e.matmul(
                            pq[:, 4 + j:5 + j], sq[1][:, c * 256 + j * 128:c * 256 + (j + 1) * 128], onesf[:, 0:1],
                            start=(c == 0), stop=(c == 1), skip_group_check=True),
                            r=[("sq", 1), "onesf"], w=[kq])
                ro = 8 * i
                S.op("act", lambda e, pq=pq: e.activation(rt[:, ro:ro + 4], pq[:, 0:4], AF.Identity, bias=epst[:, 0:1],
                                                          scale=1.0 / 128.0), r=[kq, "epst"], w=[("rt", i)])
                S.op("act", lambda e, pq=pq: e.activation(rt[:, ro + 4:ro + 6], pq[:, 4:6], AF.Identity, bias=epst[:, 0:1],
                                                          scale=1.0 / 256.0), r=[kq, "epst"], w=[("rt", i)])
                S.op("pool", lambda e: e.tensor_tensor(rt[:, ro:ro + 6], rt[:, ro:ro + 6],
                                                       neghalf[:, 0:1].to_broadcast([128, 6]), op=ALU.pow),
                     r=[("rt", i), "neghalf"], w=[("rt", i)])

            def stage_yn(G):
                i = G % 2
                tok = slice(G * 512, (G + 1) * 512)
                lt = slice((G - 8 * p) * 256, (G - 8 * p) * 256 + 256)
                tabs = TABS[i]
                tabk = [("tab", i, 0), ("tab", i, 1)]
                ro = 8 * i
                cqb = CQB[i]
                for j in range(4):
                    S.op("dve", lambda e, j=j: e.tensor_scalar(dg[:, j * 128:(j + 1) * 128], identf[:], rt[:, ro + j:ro + j + 1],
                                                               None, op0=ALU.mult), r=[("rt", i), "identf"], w=["dg"])
                for j in range(2):
                    S.op("dve", lambda e, j=j: e.tensor_scalar(dg2[:, j * 128:(j + 1) * 128], identf[:],
                                                               rt[:, ro + 4 + j:ro + 5 + j], None, op0=ALU.mult),
                         r=[("rt", i), "identf"], w=["dg2"])
                bb, pbc, kbc = next_bank()
                for j in range(4):
                    S.op("pe", lambda e, pbc=pbc, j=j: e.matmul(pbc[:, j * 128:(j + 1) * 128], onesf[:],
                                                                dg[:, j * 128:(j + 1) * 128], start=True, stop=True,
                                                                skip_group_check=True),
                         r=["dg", "onesf"], w=[kbc])
                bb2, pbc2, kbc2 = next_bank()
                for j in range(2):
                    S.op("pe", lambda e, pbc2=pbc2, j=j: e.matmul(pbc2[:, j * 128:(j + 1) * 128], onesf[:],
                                                                  dg2[:, j * 128:(j + 1) * 128], start=True, stop=True,
                                                                  skip_group_check=True),
                         r=["dg2", "onesf"], w=[kbc2])
                S.op("act", lambda e, pbc=pbc: e.copy(rstd, pbc), r=[kbc], w=["rstd"])
                S.op("act", lambda e, pbc2=pbc2: e.copy(rstdq[:, 0:256], pbc2[:, 0:256]), r=[kbc2], w=["rstdq"])
                S.op("pool", lambda e: e.tensor_tensor(ckvn[:, tok], ckvn[:, tok], rstd, op=ALU.mult),
                     r=[("ckvn", G), "rstd"], w=[("ckvn", G)])
                kr0, kr1 = KR[i]
                S.op("dve", lambda e: e.tensor_tensor(kr0[64:96, :], kr0[64:96, :], tabs[0][64:96, :], op=ALU.mult),
                     r=[("KR", i, 0), tabk[0]], w=[("KR", i, 0)])
                S.op("dve", lambda e: e.tensor_tensor(kr1[64:96, :], kr1[64:96, :], tabs[1][64:96, :], op=ALU.mult),
                     r=[("KR", i, 1), tabk[1]], w=[("KR", i, 1)])
                S.op("pool", lambda e: e.tensor_tensor(KB[0][64:96, tok], kr0[64:96, :], kr1[64:96, :], op=ALU.add),
                     r=[("KR", i, 0), ("KR", i, 1)], w=[("kbr", 0, G)])
                S.op("pool", lambda e: e.tensor_copy(KB[1][64:96, tok], KB[0][64:96, tok]),
                     r=[("kbr", 0, G)], w=[("kbr", 1, G)])
                for ti in range(2):
                    S.op("dve", lambda e, ti=ti: e.tensor_tensor(crsr[0:96, ti * 256:(ti + 1) * 256],
                                                                 tabs[ti][0:96, 0:256], rstdq[0:96, 0:256], op=ALU.mult),
                         r=[tabk[ti], "rstdq"], w=["crsr"])

            def stage_yq(G):
                i = G % 2
                lt = slice((G - 8 * p) * 256, (G - 8 * p) * 256 + 256)
                tok = slice(G * 512, (G + 1) * 512)
                cqb = CQB[i]
                for h in range(8):
                    bh, ph_, kh = next_bank()
                    for half, wsrc, wk in ((0, wuq_g, "wuq"), (1, wuqs_g, "wuqs")):
                        for c in range(2):
                            S.op("pe", lambda e, ph_=ph_, half=half, wsrc=wsrc, c=c, h=h: e.matmul(
                                ph_[0:96, half * 256:(half + 1) * 256], wsrc[:, c, h * 96:(h + 1) * 96],
                                cqb[:, c * 256:(c + 1) * 256], start=(c == 0), stop=(c == 1)),
                                r=[(wk, c), ("cqb", i)], w=[kh])
                    qt_ = qtmp[h % 2]
                    S.op("dve", lambda e, ph_=ph_, qt_=qt_: e.tensor_tensor(qt_[0:96, :], ph_[0:96, :], crsr[0:96, :],
                                                                            op=ALU.mult),
                         r=[kh, "crsr"], w=[("qtmp", h % 2)])
                    S.op("pool", lambda e, qt_=qt_, h=h: e.tensor_tensor(
                        QT[:, h, lt], qt_[0:96, 0:256], qt_[0:96, 256:512], op=ALU.add),
                        r=[("qtmp", h % 2)], w=[("QT", h, G)])
                if debug and G == 8 * p + 1 and p == 0:
                    dd = qtmp[0]
                    kk = ("qtmp", 0)
                    S.op("dve", lambda e: e.tensor_copy(dd, ckvn[:, tok]), r=[("ckvn", G), kk], w=[kk])
                    S.dma("sp", lambda e: e.dma_start(out=dbg["d_ckvn"], in_=dd), dsD, r=[kk])
                    S.op("dve", lambda e: e.tensor_copy(dd[64:96, :], KB[1][64:96, tok]),
                         r=[("kbr", 1, G), kk], w=[kk])
                    S.dma("sp", lambda e: e.dma_start(out=dbg["d_krope"], in_=dd[64:96, :]), dsD, r=[kk])
                    S.op("dve", lambda e: e.tensor_copy(dd[0:96, 0:256], QT[:, 0, lt]), r=[("QT", 0, G), kk], w=[kk])
                    S.op("dve", lambda e: e.tensor_copy(dd[0:96, 256:512], QT[:, 5, lt]), r=[("QT", 5, G), kk], w=[kk])
                    S.dma("sp", lambda e: e.dma_start(out=dbg["d_q0"], in_=dd[0:96, :]), dsD, r=[kk])
                    S.op("dve", lambda e: e.tensor_copy(dd[:, 0:256], MG[:, 0, lt]), r=[("MG", 0, G), kk], w=[kk])
                    S.op("dve", lambda e: e.tensor_copy(dd[:, 256:512], MG[:, 3, lt]), r=[("MG", 1, G), kk], w=[kk])
                    S.dma("sp", lambda e: e.dma_start(out=dbg["d_g"], in_=dd), dsD, r=[kk])

            ng = len(groups)
            stage_x1(groups[0])
            make_tables(groups[0])
            stage_x2(groups[0])
            if ng > 2:
                load_group(groups[2])
            for gi, G in enumerate(groups):
                nxt = groups[gi + 1] if gi + 1 < ng else None
                if nxt is not None:
                    stage_x1(nxt)
                stage_yn(G)
                if nxt is not None:
                    make_tables(nxt)
                    stage_x2(nxt)
                stage_yq(G)
                if gi + 3 < ng:
                    load_group(groups[gi + 3])

        def phase_B(p):
            cv = Carver()
            nslots = 32 * (p + 1)
            VB = [cv.take(64 * 65, BF16).rearrange("p (s d) -> p s d", s=64),
                  cv.take(64 * 128, BF16).rearrange("p (s d) -> p s d", s=64)]
            PT = [cv.take(1024, BF16).rearrange("p (j n) -> p j n", j=2) for _ in range(3)]
            osb = [cv.take(512, F32) for _ in range(2)]
            rec = [cv.take(512, F32) for _ in range(2)]
            otmp = [cv.take(512, F32) for _ in range(2)]
            dsD = dsem()
            S.op("pool", lambda e: e.memset(VB[0][:, :, 64:65], 1.0), w=[("vbc", 0)])
            S.op("pool", lambda e: e.memset(VB[1][:, :, 0:64], 0.0), w=[("vbc", 1)])
            S.op("pool", lambda e: e.memset(VB[1][:, :, 0:1], 1.0), w=[("vbc", 1)])

            def vaug(b, h, slot):
                return VB[b][:, slot, :]

            def vdst(b, c):
                if b == 0:
                    return VB[0][:, c * 8:(c + 1) * 8, 0:64]
                return VB[1][:, c * 8:(c + 1) * 8, 64:128]

            def gen_chunks(h):
                b = h % 2
                out = []
                for c in range(nslots // 4):
                    def kchunk(c=c):
                        tok = slice(c * 512, (c + 1) * 512)
                        S.op("pe", lambda e: e.matmul(PSG[0:64, :], wukv_g[:, h * 128:h * 128 + 64], ckvn[:, tok],
                                                      start=True, stop=True), r=["wukv"], w=["psg"])
                        S.op("dve", lambda e: e.tensor_copy(KB[b][0:64, tok], PSG[0:64, :]),
                             r=["psg"], w=[("kb", b, c)])
                    out.append(kchunk)
                for c in range(nslots // 8):
                    def vchunk(c=c):
                        for s in range(8):
                            slot = c * 8 + s
                            S.op("pe", lambda e, s=s, slot=slot: e.matmul(
                                PSG[:, s * 64:(s + 1) * 64], ckvn[:, slot * 128:(slot + 1) * 128],
                                wukv_g[:, h * 128 + 64:h * 128 + 128], start=True, stop=True),
                                r=["wukv"], w=["psg"])
                        S.op("dve", lambda e: e.tensor_copy(vdst(b, c), PSG.rearrange("p (s d) -> p s d", s=8)),
                             r=["psg"], w=[("vb", b, c)])
                    out.append(vchunk)
                return out

            steps = []
            for h in range(8):
                for g in range(4):
                    gg = 4 * p + g
                    nfull = 4 * gg
                    for i in range(nfull):
                        steps.append(dict(h=h, g=g, slots=(2 * i, 2 * i + 1), c0=0, masks=None))
                    base = 8 * gg
                    tails = [
                        ((base + 0, base + 1), 0, ((0, 128, 0), (128, 256, 0))),
                        ((base + 2, base + 3), 0, ((384, 128, 0), (512, 256, 0))),
                        ((base + 4, base + 5), 256, ((0, 128, 256), (128, 256, 256))),
                        ((base + 6, base + 7), 256, ((384, 128, 256), (512, 256, 256))),
                    ]
                    for slots, c0, mk in tails:
                        steps.append(dict(h=h, g=g, slots=slots, c0=c0, masks=mk))
                    steps[-1]["last"] = True
                    steps[-(nfull + 4)]["first"] = True
            for fn in gen_chunks(0):
                fn()
            pending_gen = []
            pending_epi = []
            n_steps = len(steps)
            if KB_STEPS >= 0:
                n_steps = KB_STEPS
            ocount = [0]

            def emit_qk(si):
                stp = steps[si]
                h, g, c0 = stp["h"], stp["g"], stp["c0"]
                b = h % 2
                wtile = PSW[si % 3]
                for j, slot in enumerate(stp["slots"]):
                    has_mask = stp["masks"] is not None
                    S.op("pe", lambda e, j=j, slot=slot, has_mask=has_mask: e.matmul(
                        wtile[:, j, c0:512], KB[b][:, slot * 128:(slot + 1) * 128],
                        QT[:, h, g * 512 + c0:(g + 1) * 512], start=True, stop=(not has_mask or bool(KB_NOMASK)),
                        skip_group_check=True),
                        r=[("kb", b, slot // 4)], w=[("psw", si % 3)])
                    if has_mask and not KB_NOMASK:
                        mo, mw, mc = stp["masks"][j]
                        S.op("pe", lambda e, j=j, mo=mo, mw=mw, mc=mc: e.matmul(
                            wtile[:, j, mc:mc + mw], identb[:], maskb[:, mo:mo + mw], start=False, stop=True,
                            skip_group_check=True),
                            r=[], w=[("psw", si % 3)])

            def emit_exp(si):
                stp = steps[si]
                c0 = stp["c0"]
                S.op("act", lambda e: e.activation(PT[si % 3][:, :, c0:512], PSW[si % 3][:, :, c0:512], AF.Exp,
                                                   scale=SCALE),
                     r=[("psw", si % 3)], w=[("pt", si % 3)])

            def emit_pv(si):
                stp = steps[si]
                h, g, c0 = stp["h"], stp["g"], stp["c0"]
                b = h % 2
                if stp.get("first"):
                    ocount[0] += 1
                o = ocount[0] % 2
                for j, slot in enumerate(stp["slots"]):
                    first = bool(stp.get("first")) and j == 0
                    last = bool(stp.get("last")) and j == 1
                    mrows = 65 if b == 0 else 128
                    S.op("pe", lambda e, j=j, slot=slot, first=first, last=last, mrows=mrows: e.matmul(
                        PSO1[0:mrows, c0:512], vaug(b, h, slot), PT[si % 3][:, j, c0:512], start=first, stop=last,
                        skip_group_check=True),
                        r=[("pt", si % 3), ("vb", b, slot // 8), ("vbc", b)], w=["pso"])
                if stp.get("last"):
                    mrows = 65 if b == 0 else 128
                    S.op("dve", lambda e, mrows=mrows: e.tensor_copy(osb[o][0:mrows, :], PSO1[0:mrows, :]),
                         r=["pso"], w=[("osb", o)])
                    dr = 64 if h % 2 == 0 else 0
                    S.op("dve", lambda e, dr=dr: e.reciprocal(rec[o][dr:dr + 1, :], osb[o][dr:dr + 1, :]),
                         r=[("osb", o)], w=[("rec", o)])
                    pending_epi.append((si + 5, h, g, o))

            def emit_epilogue(h, g, o):
                if KB_NOEPI:
                    return
                dr = 64 if h % 2 == 0 else 0
                rows = slice(0, 64) if h % 2 == 0 else slice(64, 128)
                r_ = rec[o]
                ot = otmp[o]
                ob_ = osb[o]
                S.op("pe", lambda e: e.matmul(PSG, onesf[dr:dr + 1, :], r_[dr:dr + 1, :], start=True, stop=True),
                     r=[("rec", o)], w=["psg"])
                S.op("dve", lambda e: e.tensor_tensor(ot[rows, :], ob_[rows, :], PSG[rows, :], op=ALU.mult),
                     r=[("osb", o), "psg"], w=[("otmp", o)])
                mg = MG[rows, h // 2, g * 512:(g + 1) * 512]
                S.op("pool", lambda e: e.tensor_tensor(mg, ot[rows, :], mg, op=ALU.mult),
                     r=[("otmp", o)], w=[("mgo", h, g)])

            if n_steps > 0:
                emit_qk(0)
            if n_steps > 1:
                emit_qk(1)
            cur_h = 0
            gen_every = 3 if p == 0 else 4
            since_gen = 0
            for si in range(n_steps):
                emit_exp(si)
                if si + 2 < n_steps:
                    emit_qk(si + 2)
                if not KB_NOPV:
                    emit_pv(si)
                while pending_epi and pending_epi[0][0] <= si:
                    _, eh, eg, eo = pending_epi.pop(0)
                    emit_epilogue(eh, eg, eo)
                h = steps[si]["h"]
                if steps[si].get("first") and steps[si]["g"] == 0 and h + 1 < 8:
                    pending_gen = gen_chunks(h + 1)
                    since_gen = 0
                since_gen += 1
                if pending_gen and since_gen >= gen_every:
                    pending_gen.pop(0)()
                    since_gen = 0
                if steps[si].get("last") and steps[si]["g"] == 3:
                    while pending_gen:
                        pending_gen.pop(0)()
            while pending_epi:
                _, eh, eg, eo = pending_epi.pop(0)
                emit_epilogue(eh, eg, eo)
            if debug and p == 0:
                dd = cv.take(512, F32)
                S.op("dve", lambda e: e.tensor_copy(dd[0:96, :], KB[1][:, 512:1024]), r=[("kb", 1, 1)], w=["dd"])
                S.dma("sp", lambda e: e.dma_start(out=dbg["d_kb"], in_=dd[0:96, :]), dsD, r=["dd"])
                S.op("dve", lambda e: e.tensor_copy(dd.rearrange("p (s d) -> p s d", s=8), VB[1][:, 8:16, 64:128]),
                     r=[("vb", 1, 1), "dd"], w=["dd"])
                S.dma("sp", lambda e: e.dma_start(out=dbg["d_vb"], in_=dd), dsD, r=["dd"])
                S.op("dve", lambda e: e.tensor_copy(dd[:, 0:256], MG[:, 0, 256:512]),
                     r=[("mgo", 0, 0), ("mgo", 1, 0), "dd"], w=["dd"])
                S.op("dve", lambda e: e.tensor_copy(dd[:, 256:512], MG[:, 3, 1024 + 256:1024 + 512]),
                     r=[("mgo", 6, 2), ("mgo", 7, 2), "dd"], w=["dd"])
                S.dma("sp", lambda e: e.dma_start(out=dbg["d_merged"], in_=dd), dsD, r=["dd"])

        WC_bytes = 8 * 1536 * 2 + 8 * 1024 * 2

        def wc_views():
            base = (ARENA - WC_bytes) // 2
            wci = arena[:, base:base + 8 * 1536].rearrange("p (c n) -> p c n", c=8)
            wco = arena[:, base + 8 * 1536:base + 8 * 1536 + 8 * 1024].rearrange("p (c n) -> p c n", c=8)
            return wci, wco

        dsWC = dsem()

        def load_WC():
            wci, wco = wc_views()
            S.dma("sp", lambda e: e.dma_start(out=wci, in_=w_in_bf.rearrange("(c p) n -> p c n", p=128)[:, :, NA:DIN]),
                  dsWC, w=["WCI"])
            S.dma("sp", lambda e: e.dma_start(out=wco, in_=w_out_bf.rearrange("(c p) n -> p c n", p=128)),
                  dsWC, w=["WCO"])

        dsY = [dsem(), dsem()]

        def phase_C(p):
            cv = Carver()
            wci, wco = wc_views()
            XR = [cv.take(D, F32) for _ in range(2)]
            XB = [cv.take(D, BF16) for _ in range(2)]
            XTC = [cv.take(8 * 128, BF16).rearrange("p (c t) -> p c t", c=8) for _ in range(2)]
            UG = [cv.take(512, F32) for _ in range(2)]
            VG = [cv.take(512, F32) for _ in range(2)]
            ZB = [cv.take(512, F32) for _ in range(2)]
            VN = [cv.take(512, BF16) for _ in range(2)]
            ob = cv.take(512, BF16)
            obT = cv.take(512, BF16)
            hh = cv.take(D, F32)
            YO = [cv.take(D, F32) for _ in range(2)]
            st1 = [cv.take(16, F32) for _ in range(2)]
            st2 = cv.take(24, F32)
            assert cv.off <= ARENA - WC_bytes, ("phase C arena", cv.off)
            dsXR = [dsem(), dsem()]
            dsXB = [dsem(), dsem()]

            def load_block(t):
                i = t % 2
                gslot = 4 * (t // 2) + (t % 2)
                rows = slice(gslot * 128, (gslot + 1) * 128)
                S.dma("sp", lambda e: e.dma_start(out=XR[i], in_=xs[rows, :]), dsXR[i], w=[("XR", i)])
                S.dma("pool", lambda e: e.dma_start(out=XB[i], in_=xs[rows, :]), dsXB[i], w=[("XB", i)])

            def s1a(t):
                i = t % 2
                xtc = XTC[i]
                for hf in range(2):
                    pt = pst_banks[hf]
                    for kk in range(4):
                        k = hf * 4 + kk
                        S.op("pe", lambda e, pt=pt, kk=kk, k=k: e.transpose(
                            pt[:, kk * 128:(kk + 1) * 128], XB[i][:, k * 128:(k + 1) * 128], identb[:]),
                            r=[("XB", i), "identb"], w=[("pst", hf)])
                    dst = xtc[:, hf * 4:hf * 4 + 4, :]
                    if hf == 0:
                        S.op("act", lambda e, pt=pt, dst=dst: e.copy(dst, pt.rearrange("p (c t) -> p c t", c=4)),
                             r=[("pst", hf)], w=[("XTC", i, hf)])
                    else:
                        S.op("dve", lambda e, pt=pt, dst=dst: e.tensor_copy(dst, pt.rearrange("p (c t) -> p c t", c=4)),
                             r=[("pst", hf)], w=[("XTC", i, hf)])

            def s1b_mm(t):
                i = t % 2
                xtc = XTC[i]
                xk = [("XTC", i, 0), ("XTC", i, 1)]
                ug, vg, zb = UG[i], VG[i], ZB[i]
                th = hh[:, 0:512]
                pss = {}
                for j in (1, 0, 2):
                    bb, pb, kb_ = next_bank()
                    pss[j] = (pb, kb_)
                    for k in range(8):
                        S.op("pe", lambda e, pb=pb, j=j, k=k: e.matmul(
                            pb, xtc[:, k, :], wci[:, k, j * 512:(j + 1) * 512], start=(k == 0), stop=(k == 7)),
                            r=xk + ["WCI"], w=[kb_])
                S.op("act", lambda e: e.activation(vg, pss[1][0], AF.Gelu), r=[pss[1][1]], w=[("vg", i)])
                S.op("act", lambda e: e.activation(ug, pss[0][0], AF.Gelu), r=[pss[0][1]], w=[("ug", i)])
                S.op("act", lambda e: e.activation(th, pss[2][0], AF.Tanh, scale=0.5), r=[pss[2][1]], w=[("hh", 0)])
                S.op("act", lambda e: e.copy(zb, pss[2][0]), r=[pss[2][1]], w=[("zb", i)])
                S.op("pool", lambda e: e.tensor_tensor(th, th, zb, op=ALU.mult), r=[("hh", 0), ("zb", i)], w=[("hh", 0)])
                S.op("pool", lambda e: e.tensor_tensor(zb, th, zb, op=ALU.add), r=[("hh", 0), ("zb", i)], w=[("zb", i)])
                S.op("pool", lambda e: e.tensor_tensor(ug, ug, zb, op=ALU.mult), r=[("ug", i), ("zb", i)], w=[("ug", i)])

            def s1b_ln(t):
                i = t % 2
                vg, vn, st = VG[i], VN[i], st1[i]
                S.op("dve", lambda e: e.bn_stats(st[:, 0:6], vg), r=[("vg", i)], w=[("st1a", i)])
                S.op("dve", lambda e: e.bn_aggr(st[:, 6:8], st[:, 0:6]), r=[("st1a", i)], w=[("st1b", i)])
                S.op("pool", lambda e: e.tensor_scalar(st[:, 8:9], st[:, 7:8], EPS, None, op0=ALU.add),
                     r=[("st1b", i)], w=[("st1c", i)])
                S.op("pool", lambda e: e.tensor_tensor(st[:, 8:9], st[:, 8:9], neghalf[:, 0:1], op=ALU.pow),
                     r=[("st1c", i), "neghalf"], w=[("st1c", i)])
                S.op("dve", lambda e: e.scalar_tensor_tensor(out=vg, in0=vg, scalar=st[:, 6:7], in1=sgug[:],
                                                             op0=ALU.subtract, op1=ALU.mult),
                     r=[("vg", i), ("st1b", i), "sgug"], w=[("vg", i)])
                S.op("dve", lambda e: e.scalar_tensor_tensor(out=vn, in0=vg, scalar=st[:, 8:9], in1=sgub[:],
                                                             op0=ALU.mult, op1=ALU.add),
                     r=[("vg", i), ("st1c", i), "sgub"], w=[("vn", i)])

            def s2a(t):
                i = t % 2
                ug, vn = UG[i], VN[i]
                bsv, psv, ksv = next_bank()
                S.op("pe", lambda e: e.matmul(psv, bspb[:], indb[:], start=True, stop=False, skip_group_check=True),
                     r=["bspb", "indb"], w=[ksv])
                for h in range(8):
                    S.op("pe", lambda e, h=h: e.matmul(psv[:, h * 64:(h + 1) * 64], wsp[:, h, :],
                                                      vn[:, h * 64:(h + 1) * 64], start=False, stop=(h == 7),
                                                      skip_group_check=True),
                         r=[("wsp", h), ("vn", i)], w=[ksv])
                S.op("dve", lambda e: e.scalar_tensor_tensor(out=ob, in0=psv, scalar=0.5, in1=ug, op0=ALU.mult, op1=ALU.mult),
                     r=[ksv, ("ug", i)], w=["ob"])

            def s2b(t):
                pt = pst_banks[0]
                for c in range(4):
                    S.op("pe", lambda e, c=c: e.transpose(pt[:, c * 128:(c + 1) * 128],
                                                         ob[:, c * 128:(c + 1) * 128], identb[:]),
                         r=["ob", "identb"], w=[("pst", 0)])
                S.op("act", lambda e: e.copy(obT, pt), r=[("pst", 0)], w=["obT"])

            def s2c(t):
                i = t % 2
                lt = slice((t - 16 * p) * 128, (t - 16 * p + 1) * 128)
                outs = []
                for nh in range(2):
                    bo, po, ko = next_bank()
                    outs.append((po, ko))
                    for c in range(8):
                        if c < 4:
                            lhsT = MG[:, c, lt]
                            rr = []
                        else:
                            lhsT = obT[:, (c - 4) * 128:(c - 3) * 128]
                            rr = ["obT"]
                        S.op("pe", lambda e, po=po, lhsT=lhsT, c=c, nh=nh: e.matmul(
                            po, lhsT, wco[:, c, nh * 512:(nh + 1) * 512], start=(c == 0), stop=(c == 7)),
                            r=rr + ["WCO"], w=[ko])
                for nh in range(2):
                    po, ko = outs[nh]
                    S.op("dve", lambda e, po=po, nh=nh: e.scalar_tensor_tensor(
                        out=hh[:, nh * 512:(nh + 1) * 512], in0=XR[i][:, nh * 512:(nh + 1) * 512], scalar=ALPHA,
                        in1=po, op0=ALU.mult, op1=ALU.add),
                        r=[ko, ("XR", i)], w=[("hh", nh)])
                    S.op("dve", lambda e, nh=nh: e.bn_stats(st2[:, 6 * nh:6 * nh + 6], hh[:, nh * 512:(nh + 1) * 512]),
                         r=[("hh", nh)], w=[("st2a", nh)])
                S.op("dve", lambda e: e.bn_aggr(st2[:, 12:14], st2[:, 0:12]), r=[("st2a", 0), ("st2a", 1)], w=["st2b"])
                S.op("pool", lambda e: e.tensor_scalar(st2[:, 14:15], st2[:, 13:14], EPS, None, op0=ALU.add),
                     r=["st2b"], w=["st2c"])
                S.op("pool", lambda e: e.tensor_tensor(st2[:, 14:15], st2[:, 14:15], neghalf[:, 0:1], op=ALU.pow),
                     r=["st2c", "neghalf"], w=["st2c"])
                S.op("dve", lambda e: e.scalar_tensor_tensor(out=hh, in0=hh, scalar=st2[:, 12:13], in1=lng[:],
                                                             op0=ALU.subtract, op1=ALU.mult),
                     r=[("hh", 0), ("hh", 1), "st2b", "lng"], w=[("hh", 0), ("hh", 1)])
                yo = YO[i]
                S.op("dve", lambda e: e.scalar_tensor_tensor(out=yo, in0=hh, scalar=st2[:, 14:15], in1=lnb[:],
                                                             op0=ALU.mult, op1=ALU.add),
                     r=[("hh", 0), ("hh", 1), "st2c", "lnb"], w=[("YO", i)])
                S.dma("sp", lambda e: e.dma_start(out=y[t * 128:(t + 1) * 128, :], in_=yo),
                      dsY[i], r=[("YO", i)])

            def load_xb(t):
                i = t % 2
                gslot = 4 * (t // 2) + (t % 2)
                rows = slice(gslot * 128, (gslot + 1) * 128)
                S.dma("pool", lambda e: e.dma_start(out=XB[i], in_=xs[rows, :]), dsXB[i], w=[("XB", i)])

            def load_xr(t):
                i = t % 2
                gslot = 4 * (t // 2) + (t % 2)
                rows = slice(gslot * 128, (gslot + 1) * 128)
                S.dma("sp", lambda e: e.dma_start(out=XR[i], in_=xs[rows, :]), dsXR[i], w=[("XR", i)])

            blocks = list(range(16 * p, 16 * p + 16))
            nb = len(blocks)
            load_xb(blocks[0])
            load_xb(blocks[1])
            load_xr(blocks[0])
            load_xr(blocks[1])
            s1a(blocks[0])
            s1b_mm(blocks[0])
            s1b_ln(blocks[0])
            s1a(blocks[1])
            load_xb(blocks[2])
            for bi, t in enumerate(blocks):
                s2a(t)
                if bi + 1 < nb:
                    s1b_mm(blocks[bi + 1])
                if bi + 2 < nb:
                    s1a(blocks[bi + 2])
                if bi + 3 < nb:
                    load_xb(blocks[bi + 3])
                s2b(t)
                if bi + 1 < nb:
                    s1b_ln(blocks[bi + 1])
                s2c(t)
                if bi + 2 < nb:
                    load_xr(blocks[bi + 2])

        for p in passes:
            if "A" in phases:
                phase_A(p)
                S.barrier()
            if "B" in phases:
                if "C" in phases and WC_EARLY:
                    load_WC()
                phase_B(p)
                S.barrier()
            if "C" in phases:
                if "B" not in phases or not WC_EARLY:
                    load_WC()
                phase_C(p)
                S.barrier()
        S.emit(nc, block, esems, dsY)
    return nc


def _slot_perm(par):
    order = []
    for m in range(16):
        if par == 0:
            order += [4 * m, 4 * m + 3, 4 * m + 1, 4 * m + 2]
        else:
            order += [4 * m + 1, 4 * m + 2, 4 * m, 4 * m + 3]
    return order


def make_in_maps(x, positions, w_in, q_norm_g, w_uq, kv_norm_g, w_ukv, sgu_norm_g, sgu_norm_b,
                 w_spatial, b_spatial, w_out, ln_g, ln_b):
    f = np.float32
    w_in = np.ascontiguousarray(w_in, dtype=f)
    swap = np.concatenate([np.arange(16, 32), np.arange(0, 16)])
    w_krs = np.ascontiguousarray(w_in[:, 384:416][:, swap])
    w_uq = np.ascontiguousarray(w_uq, dtype=f)
    w_uqs = np.zeros_like(w_uq)
    for h in range(8):
        w_uqs[:, h * 96 + 64:h * 96 + 96] = w_uq[:, h * 96 + 64:h * 96 + 96][:, swap]
    qg = np.ascontiguousarray(np.asarray(q_norm_g, dtype=f).reshape(2, 128).T)
    kvg = np.ascontiguousarray(np.asarray(kv_norm_g, dtype=f).reshape(128, 1))
    wspT = np.ascontiguousarray(np.transpose(np.asarray(w_spatial, dtype=f), (2, 0, 1)))
    si = np.arange(128)
    tril = (si[:, None] <= si[None, :]).astype(f)
    bspT = np.ascontiguousarray(np.asarray(b_spatial, dtype=f).T)
    ident = np.eye(128, dtype=f)
    invf = (1.0 / (10000.0 ** (np.arange(16, dtype=np.float64) / 16.0)))
    ropec = np.zeros((128, 6), f)
    ropec[:, 1] = -TWO_PI
    ropec[:, 2] = -TWO_PI
    ropec[:, 3] = 0.5 * math.pi
    for pp in range(64, 96):
        i = (pp - 64) % 16
        ropec[pp, 0] = invf[i] / TWO_PI
        ropec[pp, 1] = TWO_PI if pp < 80 else -TWO_PI
    tri = np.where(si[:, None] > si[None, :], NEG, 0.0).astype(f)
    allm = np.full((128, 128), NEG, f)
    zero = np.zeros((128, 128), f)
    ind_h = np.zeros((8, 512), f)
    for h in range(8):
        ind_h[h, h * 64:(h + 1) * 64] = 1.0
    common = dict(bsp_ht=np.ascontiguousarray(np.asarray(b_spatial, dtype=f)), ind_h=ind_h, w_in=w_in, w_krs=w_krs, w_uq=w_uq, w_uqs=w_uqs, qg=qg, w_ukv=np.ascontiguousarray(w_ukv, dtype=f),
                  kvg=kvg, wspT=wspT, tril=tril, bspT=bspT,
                  sgu_g=np.asarray(sgu_norm_g, dtype=f).reshape(1, 512), sgu_b=np.asarray(sgu_norm_b, dtype=f).reshape(1, 512),
                  ln_g=np.asarray(ln_g, dtype=f).reshape(1, D), ln_b=np.asarray(ln_b, dtype=f).reshape(1, D),
                  w_out=np.ascontiguousarray(w_out, dtype=f), ident=ident, ropec=ropec)
    in_maps = []
    perms = []
    for c in range(8):
        b, par = c // 2, c % 2
        order = _slot_perm(par)
        perms.append(order)
        xb = np.asarray(x[b], dtype=f).reshape(64, 128, D)[order].reshape(SEQ, D)
        pb = np.asarray(positions[b], dtype=np.int32).reshape(64, 128)[order].reshape(1, SEQ)
        mA = allm if par == 0 else zero
        mB = zero if par == 0 else allm
        mk = np.concatenate([tri, allm, tri, mA, allm, mB], axis=1)
        d = dict(common)
        d.update(xs=np.ascontiguousarray(xb), pos=np.ascontiguousarray(pb), masks=np.ascontiguousarray(mk))
        in_maps.append(d)
    return in_maps, perms


_NC_CACHE = {}


def kernel(x, positions, w_in, q_norm_g, w_uq, kv_norm_g, w_ukv, sgu_norm_g, sgu_norm_b,
           w_spatial, b_spatial, w_out, ln_g, ln_b):
    in_maps, perms = make_in_maps(x, positions, w_in, q_norm_g, w_uq, kv_norm_g, w_ukv, sgu_norm_g,
                                  sgu_norm_b, w_spatial, b_spatial, w_out, ln_g, ln_b)
    if "nc" not in _NC_CACHE:
        _NC_CACHE["nc"] = build_program()
    nc = _NC_CACHE["nc"]
    res = run_bass_kernel_spmd(nc, in_maps, core_ids=list(range(8)))
    out = np.empty((4, SEQ, D), np.float32)
    for c in range(8):
        b = c // 2
        order = perms[c]
        yc = np.asarray(res.results[c]["y"], dtype=np.float32).reshape(32, 128, D)
        ov = out[b].reshape(64, 128, D)
        for t in range(32):
            slot = 4 * (t // 2) + (t % 2)
            ov[order[slot]] = yc[t]
    return out
```

```python
import math
from contextlib import ExitStack

import numpy as np
import concourse.bass as bass
import concourse.mybir as mybir
from concourse.bass_utils import run_bass_kernel_spmd

F32 = mybir.dt.float32
BF16 = mybir.dt.bfloat16
I32 = mybir.dt.int32
AF = mybir.ActivationFunctionType
ALU = mybir.AluOpType

D = 1024
SEQ = 8192
NSLOT = 64
DIN = 2464
NA = 928
NEG = -30000.0
EPS = 1e-5
ALPHA = 2.0 ** 0.25
SCALE = 1.0 / math.sqrt(96.0)
TWO_PI = 2.0 * math.pi

ENGS = ("pe", "act", "dve", "pool", "sp")
import os
STOP = int(os.environ.get("KSTOP", "0"))
NGROUPS = int(os.environ.get("KGROUPS", "8"))
KB_STEPS = int(os.environ.get("KB_STEPS", "-1"))
WC_EARLY = int(os.environ.get("WC_EARLY", "1"))
TAB_ENG = os.environ.get("TAB_ENG", "dve")
KB_NOEPI = int(os.environ.get("KB_NOEPI", "0"))
KB_NOPV = int(os.environ.get("KB_NOPV", "0"))
KB_NOMASK = int(os.environ.get("KB_NOMASK", "0"))


class DmaSem:
    def __init__(self, sem):
        self.sem = sem
        self.count = 0


class Op:
    __slots__ = ("eng", "fn", "deps", "dma", "dsem", "dval", "signal", "sigval", "barrier")

    def __init__(self, eng, fn, deps, dma=False, dsem=None, dval=0):
        self.eng = eng
        self.fn = fn
        self.deps = deps
        self.dma = dma
        self.dsem = dsem
        self.dval = dval
        self.signal = False
        self.sigval = 0
        self.barrier = False


class Sched:
    def __init__(self):
        self.ops = []
        self.state = {}
        self.last_on = {e: None for e in ENGS}
        self.open_dma = []

    def _add(self, op):
        self.ops.append(op)
        idx = len(self.ops) - 1
        self.last_on[op.eng] = idx
        return idx

    def _mk(self, idxs):
        d = {}
        for i in idxs:
            p = self.ops[i]
            d[i] = p.dsem.count if p.dma else None
        return d

    def _deps(self, reads, writes):
        deps = set()
        for k in reads:
            st = self.state.get(k)
            if st is not None and st[0] is not None:
                deps.add(st[0])
        for k in writes:
            st = self.state.get(k)
            if st is not None:
                if st[0] is not None:
                    deps.add(st[0])
                deps.update(st[1])
        return self._mk(deps)

    def _commit(self, idx, reads, writes):
        for k in reads:
            st = self.state.setdefault(k, [None, []])
            st[1].append(idx)
        for k in writes:
            self.state[k] = [idx, []]

    def op(self, eng, fn, r=(), w=()):
        deps = self._deps(r, w)
        idx = self._add(Op(eng, fn, deps))
        self._commit(idx, r, w)
        return idx

    def dma(self, queue, fn, dsem, r=(), w=()):
        deps = self._deps(r, w)
        dsem.count += 16
        idx = self._add(Op(queue, fn, deps, dma=True, dsem=dsem, dval=dsem.count))
        self._commit(idx, r, w)
        self.open_dma.append(idx)
        return idx

    def barrier(self):
        targets = set(i for i in self.last_on.values() if i is not None)
        targets.update(self.open_dma)
        for e in ENGS:
            o = Op(e, None, self._mk(targets))
            o.barrier = True
            self.ops.append(o)
        self.open_dma = []
        self.state = {}

    def emit(self, nc, block, esems, final_waits):
        ops = self.ops
        for i, o in enumerate(ops):
            for d in o.deps:
                p = ops[d]
                if p.dma:
                    continue
                if p.eng == o.eng and not o.barrier:
                    if p.eng == "pe" or p.eng == "sp":
                        continue
                p.signal = True
        cnt = {e: 0 for e in ENGS}
        for o in ops:
            if o.signal:
                cnt[o.eng] += 1
                o.sigval = cnt[o.eng]

        def stream(name):
            def body(e):
                seen = {}
                for o in ops:
                    if o.eng != name:
                        continue
                    for d in sorted(o.deps):
                        p = ops[d]
                        if p.dma:
                            sem, val = p.dsem.sem, o.deps[d]
                        else:
                            if p.eng == name and (name == "pe" or name == "sp") :
                                continue
                            if p.eng == name and o.barrier and name == "pe":
                                continue
                            sem, val = esems[p.eng], p.sigval
                        key = id(sem)
                        if seen.get(key, 0) >= val:
                            continue
                        seen[key] = val
                        e.wait_ge(sem, val)
                    if o.fn is None:
                        continue
                    ins = o.fn(e)
                    if o.dma:
                        ins.then_inc(o.dsem.sem, 16)
                    elif o.signal:
                        ins.then_inc(esems[name], 1)
                if name == "sp":
                    for ds in final_waits:
                        e.wait_ge(ds.sem, ds.count)
            return body

        block.sync(stream("sp"))
        block.tensor(stream("pe"))
        block.scalar(stream("act"))
        block.vector(stream("dve"))
        block.gpsimd(stream("pool"))


def build_program(passes=(0, 1), phases="ABC", debug=False):
    nc = bass.Bass("TRN2", target_bir_lowering=False)

    def din(name, shape, dt=F32):
        return nc.dram_tensor(name, shape, dt, kind="ExternalInput").ap()

    xs = din("xs", [SEQ, D])
    pos = din("pos", [1, SEQ], I32)
    w_in = din("w_in", [D, DIN])
    w_krs = din("w_krs", [D, 32])
    w_uq = din("w_uq", [256, 768])
    w_uqs = din("w_uqs", [256, 768])
    qg = din("qg", [128, 2])
    w_ukv = din("w_ukv", [128, 1024])
    kvg = din("kvg", [128, 1])
    wspT = din("wspT", [128, 8, 128])
    tril = din("tril", [128, 128])
    bspT = din("bspT", [128, 8])
    bsp_ht = din("bsp_ht", [8, 128])
    ind_h = din("ind_h", [8, 512])
    sgu_g = din("sgu_g", [1, 512])
    sgu_b = din("sgu_b", [1, 512])
    ln_g = din("ln_g", [1, D])
    ln_b = din("ln_b", [1, D])
    w_out = din("w_out", [D, D])
    ident = din("ident", [128, 128])
    ropec = din("ropec", [128, 6])
    masks = din("masks", [128, 768])
    y = nc.dram_tensor("y", [SEQ // 2, D], F32, kind="ExternalOutput").ap()
    w_in_bf = nc.dram_tensor("w_in_bf", [D, DIN], BF16, kind="Internal").ap()
    w_krs_bf = nc.dram_tensor("w_krs_bf", [D, 32], BF16, kind="Internal").ap()
    w_out_bf = nc.dram_tensor("w_out_bf", [D, D], BF16, kind="Internal").ap()
    dbg = {}
    if debug:
        def dout(name, shape):
            dbg[name] = nc.dram_tensor(name, shape, F32, kind="ExternalOutput").ap()
        dout("d_ckvn", [128, 512])
        dout("d_krope", [32, 512])
        dout("d_q0", [96, 512])
        dout("d_g", [128, 512])
        dout("d_kb", [96, 512])
        dout("d_vb", [128, 512])
        dout("d_merged", [128, 512])

    S = Sched()
    with ExitStack() as es:
        def sb(name, shape, dt):
            return es.enter_context(nc.sbuf_tensor(name, shape, dt))

        def ps(name, shape, dt):
            return es.enter_context(nc.psum_tensor(name, shape, dt))

        def newsem(name):
            return es.enter_context(nc.semaphore(name))

        dsem_n = [0]

        def dsem():
            dsem_n[0] += 1
            return DmaSem(newsem("dq%d" % dsem_n[0]))

        identb = sb("identb", [128, 128], BF16)
        onesf = sb("onesf", [128, 128], F32)
        maskb = sb("maskb", [128, 768], BF16)
        ropecs = sb("ropecs", [128, 6], F32)
        epst = sb("epst", [128, 1], F32)
        neghalf = sb("neghalf", [128, 1], F32)
        identf = sb("identf", [128, 128], F32)
        rt = sb("rt", [128, 16], F32)
        wuq_g = sb("wuq_g", [128, 2, 768], BF16)
        wuqs_g = sb("wuqs_g", [128, 2, 768], BF16)
        wukv_g = sb("wukv_g", [128, 1024], BF16)
        wsp = sb("wsp", [128, 8, 128], BF16)
        bsp = sb("bsp", [128, 8], F32)
        bspb = sb("bspb", [8, 128], BF16)
        indb = sb("indb", [8, 512], BF16)
        sgug = sb("sgug", [128, 512], F32)
        sgub = sb("sgub", [128, 512], F32)
        lng = sb("lng", [128, D], F32)
        lnb = sb("lnb", [128, D], F32)
        ckvn = sb("ckvn", [128, SEQ], BF16)
        KB = [sb("kb%d" % i, [96, SEQ], BF16) for i in range(2)]
        QT = sb("qt", [96, 8, 2048], BF16)
        MG = sb("mg", [128, 4, 2048], BF16)
        ARENA = 86528
        arena = sb("arena", [128, ARENA // 2], BF16)

        class Carver:
            def __init__(self):
                self.off = 0

            def take(self, cols, dt, parts=128):
                size = 4 if dt in (F32, I32) else 2
                self.off = (self.off + 3) // 4 * 4
                nbytes = cols * size
                assert self.off + nbytes <= ARENA, ("arena overflow", self.off, nbytes)
                v = arena[0:parts, self.off // 2:(self.off + nbytes) // 2]
                self.off += nbytes
                if dt != BF16:
                    v = v.bitcast(dt)
                return v

        PT4 = [ps("pt%d" % i, [128, 2, 512], F32) for i in range(4)]
        PSW = [PT4[0], PT4[1], PT4[2]]
        PSO1 = PT4[3][:, 0, :]
        PSG = PT4[3][:, 1, :]
        pst_banks = [PT4[3][:, 0, :].bitcast(BF16)[:, 0:512],
                     PT4[3][:, 1, :].bitcast(BF16)[:, 0:512]]

        def bank(i):
            return PT4[i // 2][:, i % 2, :]

        bank_rr = [0]

        def next_bank():
            b = bank_rr[0] % 6
            bank_rr[0] += 1
            return b, bank(b), ("ps", b)

        esems = {e: newsem("e_" + e) for e in ENGS}
        block = es.enter_context(nc.Block())

        cv = Carver()
        st_a = cv.take(2 * 768, F32)
        st_b = cv.take(2 * 768, F32)
        st_c = cv.take(1024, F32)
        st_d = cv.take(8 * 128, F32)
        st_e = cv.take(128, F32)
        qgs = cv.take(2, F32)
        kvgs = cv.take(1, F32)
        ds0 = dsem()
        ds0p = dsem()
        S.dma("pool", lambda e: e.dma_start(out=identb[:], in_=ident), ds0p, w=["identb"])
        S.dma("pool", lambda e: e.dma_start(out=maskb[:], in_=masks), ds0p, w=["maskb"])
        S.dma("pool", lambda e: e.dma_start(out=bspb[:], in_=bsp_ht), ds0p, w=["bspb"])
        S.dma("pool", lambda e: e.dma_start(out=indb[:], in_=ind_h), ds0p, w=["indb"])
        S.dma("sp", lambda e: e.dma_start(out=ropecs[:], in_=ropec), ds0, w=["ropecs"])
        S.dma("sp", lambda e: e.dma_start(out=identf[:], in_=ident), ds0, w=["identf"])
        S.dma("sp", lambda e: e.dma_start(out=st_a.rearrange("p (c n) -> p c n", c=2),
                                          in_=w_uq.rearrange("(c p) n -> p c n", p=128)), ds0, w=["st_a"])
        S.dma("sp", lambda e: e.dma_start(out=st_b.rearrange("p (c n) -> p c n", c=2),
                                          in_=w_uqs.rearrange("(c p) n -> p c n", p=128)), ds0, w=["st_b"])
        S.dma("sp", lambda e: e.dma_start(out=st_c, in_=w_ukv), ds0, w=["st_c"])
        S.dma("sp", lambda e: e.dma_start(out=st_d.rearrange("p (h t) -> p h t", h=8), in_=wspT), ds0, w=["st_d"])
        S.dma("sp", lambda e: e.dma_start(out=st_e, in_=tril), ds0, w=["st_e"])
        S.dma("sp", lambda e: e.dma_start(out=qgs, in_=qg), ds0, w=["qgs"])
        S.dma("sp", lambda e: e.dma_start(out=kvgs, in_=kvg), ds0, w=["kvgs"])
        S.dma("sp", lambda e: e.dma_start(out=bsp[:], in_=bspT), ds0, w=["bsp"])
        S.dma("sp", lambda e: e.dma_start(out=sgug[:], in_=sgu_g.partition_broadcast(128)), ds0, w=["sgug"])
        S.dma("sp", lambda e: e.dma_start(out=sgub[:], in_=sgu_b.partition_broadcast(128)), ds0, w=["sgub"])
        S.dma("sp", lambda e: e.dma_start(out=lng[:], in_=ln_g.partition_broadcast(128)), ds0, w=["lng"])
        S.dma("sp", lambda e: e.dma_start(out=lnb[:], in_=ln_b.partition_broadcast(128)), ds0, w=["lnb"])
        dscv = dsem()
        S.op("dve", lambda e: e.memset(onesf[:], 1.0), w=["onesf"])
        S.op("dve", lambda e: e.memset(epst[:], EPS), w=["epst"])
        S.op("dve", lambda e: e.memset(neghalf[:], -0.5), w=["neghalf"])
        for c in range(2):
            S.op("dve", lambda e, c=c: e.tensor_scalar(wuq_g[:, c, :], st_a[:, c * 768:(c + 1) * 768],
                                                       qgs[:, c:c + 1], None, op0=ALU.mult),
                 r=["st_a", "qgs"], w=[("wuq", c)])
            S.op("dve", lambda e, c=c: e.tensor_scalar(wuqs_g[:, c, :], st_b[:, c * 768:(c + 1) * 768],
                                                       qgs[:, c:c + 1], None, op0=ALU.mult),
                 r=["st_b", "qgs"], w=[("wuqs", c)])
        S.op("dve", lambda e: e.tensor_scalar(wukv_g[:], st_c, kvgs[:, 0:1], None, op0=ALU.mult),
             r=["st_c", "kvgs"], w=["wukv"])
        for h in range(8):
            S.op("dve", lambda e, h=h: e.tensor_tensor(wsp[:, h, :], st_d[:, h * 128:(h + 1) * 128], st_e,
                                                       op=ALU.mult), r=["st_d", "st_e"], w=[("wsp", h)])
        S.barrier()

        def phase_A(p):
            cv = Carver()
            WA = cv.take(8 * NA, BF16).rearrange("p (c n) -> p c n", c=8)
            WK = cv.take(8 * 32, BF16).rearrange("p (c n) -> p c n", c=8)
            XS = [cv.take(4 * D, BF16).rearrange("p (s d) -> p s d", s=4) for _ in range(2)]
            XT = cv.take(8 * 512, BF16).rearrange("p (c t) -> p c t", c=8)
            POSI = [cv.take(512, I32) for _ in range(2)]
            posf = cv.take(512, F32)
            tk = posf.bitcast(I32)
            TABS = [[cv.take(512, F32) for _ in range(2)] for _ in range(2)]
            tt = cv.take(512, F32)
            tkf = cv.take(512, F32)
            sq = [cv.take(512, F32) for _ in range(2)]
            rstd = cv.take(512, F32)
            dg = cv.take(512, F32)
            dg2 = cv.take(256, F32)
            rstdq = cv.take(256, F32)
            KR = [[cv.take(512, F32) for _ in range(2)] for _ in range(2)]
            CQB = [cv.take(512, BF16) for _ in range(2)]
            crsr = cv.take(512, F32)
            qtmp = [cv.take(512, F32) for _ in range(2)]
            dsW = dsem()
            dsX = [dsem(), dsem()]
            dsP = [dsem(), dsem()]
            dsD = dsem()
            groups = list(range(8 * p, 8 * p + NGROUPS))

            def load_group(G):
                i = G % 2
                S.dma("pool", lambda e: e.dma_start(
                    out=XS[i], in_=xs[G * 512:(G + 1) * 512, :].rearrange("(s p) d -> p s d", p=128)),
                    dsX[i], w=[("XS", i)])
                S.dma("sp", lambda e: e.dma_start(
                    out=POSI[i], in_=pos[0:1, G * 512:(G + 1) * 512].partition_broadcast(128)),
                    dsP[i], w=[("POSI", i)])

            load_group(groups[0])
            if p == 0:
                dsW0 = dsem()
                S.dma("pool", lambda e: e.dma_start(out=WA, in_=w_in.rearrange("(c p) n -> p c n", p=128)[:, :, 0:NA]),
                      dsW0, w=["WA"])
                S.dma("pool", lambda e: e.dma_start(out=WK, in_=w_krs.rearrange("(c p) n -> p c n", p=128)),
                      dsW0, w=["WK"])
            else:
                S.dma("sp", lambda e: e.dma_start(out=WA, in_=w_in_bf.rearrange("(c p) n -> p c n", p=128)[:, :, 0:NA]),
                      dsW, w=["WA"])
                S.dma("sp", lambda e: e.dma_start(out=WK, in_=w_krs_bf.rearrange("(c p) n -> p c n", p=128)),
                      dsW, w=["WK"])
            if len(groups) > 1:
                load_group(groups[1])
            if p == 0:
                S.dma("pool", lambda e: e.dma_start(out=w_in_bf, in_=w_in), dscv, w=["w_in_bf"])
                S.dma("pool", lambda e: e.dma_start(out=w_krs_bf, in_=w_krs), dscv, w=["w_krs_bf"])
                S.dma("pool", lambda e: e.dma_start(out=w_out_bf, in_=w_out), dscv, w=["w_out_bf"])

            def make_tables(G):
                i = G % 2
                S.op("dve", lambda e: e.tensor_copy(posf, POSI[i]), r=[("POSI", i)], w=["posf"])
                S.op("dve", lambda e: e.tensor_scalar(tt, posf, ropecs[:, 0:1], None, op0=ALU.mult),
                     r=["posf", "ropecs"], w=["tt"])
                S.op("dve", lambda e: e.tensor_copy(tk, tt), r=["tt"], w=["posf"])
                S.op("dve", lambda e: e.tensor_copy(tkf, tk), r=["posf"], w=["tkf"])
                S.op("dve", lambda e: e.tensor_tensor(tt, tt, tkf, op=ALU.subtract), r=["tt", "tkf"], w=["tt"])
                S.op("dve", lambda e: e.scalar_tensor_tensor(out=tt, in0=tt, scalar=0.5, in1=tt,
                                                             op0=ALU.is_gt, op1=ALU.subtract),
                     r=["tt"], w=["tt"])
                S.op("act", lambda e: e.activation(TABS[i][1], tt, AF.Sin, scale=ropecs[:, 1:2]),
                     r=["tt", "ropecs"], w=[("tab", i, 1)])
                S.op("act", lambda e: e.activation(tkf, tt, AF.Abs), r=["tt"], w=["tkf"])
                S.op("act", lambda e: e.activation(TABS[i][0], tkf, AF.Sin, bias=ropecs[:, 3:4], scale=ropecs[:, 2:3]),
                     r=["tkf", "ropecs"], w=[("tab", i, 0)])

            def stage_x1(G):
                i = G % 2
                xt = XT
                for k in range(8):
                    half = k % 2
                    pt = pst_banks[half]
                    for s_ in range(4):
                        S.op("pe", lambda e, pt=pt, s_=s_, k=k: e.transpose(
                            pt[:, s_ * 128:(s_ + 1) * 128], XS[i][:, s_, k * 128:(k + 1) * 128], identb[:]),
                            r=[("XS", i), "identb"], w=[("pst", half)])
                    if k % 2 == 0:
                        S.op("act", lambda e, pt=pt, k=k: e.copy(xt[:, k, :], pt),
                             r=[("pst", half)], w=[("XT", k)])
                    else:
                        S.op("dve", lambda e, pt=pt, k=k: e.tensor_copy(xt[:, k, :], pt),
                             r=[("pst", half)], w=[("XT", k)])

            def stage_x2(G):
                i = G % 2
                tok = slice(G * 512, (G + 1) * 512)
                lt = slice((G - 8 * p) * 256, (G - 8 * p) * 256 + 256)
                xt = XT
                bc, pc, kc = next_bank()
                for k in range(8):
                    S.op("pe", lambda e, pc=pc, k=k: e.matmul(pc, WA[:, k, 256:384], xt[:, k, :],
                                                              start=(k == 0), stop=(k == 7)),
                         r=["WA", ("XT", k)], w=[kc])
                br, pr, kr = next_bank()
                for k in range(8):
                    S.op("pe", lambda e, pr=pr, k=k: e.matmul(pr[64:96, :], WA[:, k, 384:416], xt[:, k, :],
                                                              start=(k == 0), stop=(k == 7), tile_position=(0, 64)),
                         r=["WA", ("XT", k)], w=[kr])
                bs, psw, ks = next_bank()
                for k in range(8):
                    S.op("pe", lambda e, psw=psw, k=k: e.matmul(psw[64:96, :], WK[:, k, :], xt[:, k, :],
                                                                start=(k == 0), stop=(k == 7), tile_position=(0, 64)),
                         r=["WK", ("XT", k)], w=[ks])
                S.op("act", lambda e, pc=pc: e.copy(ckvn[:, tok], pc), r=[kc], w=[("ckvn", G)])
                S.op("act", lambda e, pc=pc: e.activation(sq[0], pc, AF.Square), r=[kc], w=[("sq", 0)])
                S.op("act", lambda e, pr=pr: e.copy(KR[i][0][64:96, :], pr[64:96, :]), r=[kr], w=[("KR", i, 0)])
                S.op("act", lambda e, psw=psw: e.copy(KR[i][1][64:96, :], psw[64:96, :]), r=[ks], w=[("KR", i, 1)])
                bcq, pcq, kcq = next_bank()
                for c in range(2):
                    for k in range(8):
                        S.op("pe", lambda e, pcq=pcq, c=c, k=k: e.matmul(
                            pcq[:, c * 256:(c + 1) * 256], WA[:, k, c * 128:(c + 1) * 128], xt[:, k, 0:256],
                            start=(k == 0), stop=(k == 7)),
                            r=["WA", ("XT", k)], w=[kcq])
                S.op("act", lambda e, pcq=pcq: e.copy(CQB[i], pcq), r=[kcq], w=[("cqb", i)])
                S.op("act", lambda e, pcq=pcq: e.activation(sq[1], pcq, AF.Square), r=[kcq], w=[("sq", 1)])
                for cc in range(2):
                    bz, pz, kz = next_bank()
                    for c2 in range(2):
                        c = cc * 2 + c2
                        for k in range(8):
                            S.op("pe", lambda e, pz=pz, c=c, c2=c2, k=k: e.matmul(
                                pz[:, c2 * 256:(c2 + 1) * 256], WA[:, k, 416 + c * 128:416 + (c + 1) * 128],
                                xt[:, k, 0:256], start=(k == 0), stop=(k == 7)),
                                r=["WA", ("XT", k)], w=[kz])
                    S.op("act", lambda e, pz=pz, cc=cc: e.activation(
                        MG[:, 2 * cc:2 * cc + 2, lt], pz.rearrange("p (c t) -> p c t", c=2), AF.Silu),
                        r=[kz], w=[("MG", cc, G)])
                bq, pq, kq = next_bank()
                for j in range(4):
                    S.op("pe", lambda e, pq=pq, j=j: e.matmul(pq[:, j:j + 1], sq[0][:, j * 128:(j + 1) * 128], onesf[:, 0:1],
                                                              start=True, stop=True, skip_group_check=True),
                         r=[("sq", 0), "onesf"], w=[kq])
                for j in range(2):
                    for c in range(2):
                        S.op("pe", lambda e, pq=pq, c=c, j=j: e.matmul(
                            pq[:, 4 + j:5 + j], sq[1][:, c * 256 + j * 128:c * 256 + (j + 1) * 128], onesf[:, 0:1],
                            start=(c == 0), stop=(c == 1), skip_group_check=True),
                            r=[("sq", 1), "onesf"], w=[kq])
                ro = 8 * i
                S.op("act", lambda e, pq=pq: e.activation(rt[:, ro:ro + 4], pq[:, 0:4], AF.Identity, bias=epst[:, 0:1],
                                                          scale=1.0 / 128.0), r=[kq, "epst"], w=[("rt", i)])
                S.op("act", lambda e, pq=pq: e.activation(rt[:, ro + 4:ro + 6], pq[:, 4:6], AF.Identity, bias=epst[:, 0:1],
                                                          scale=1.0 / 256.0), r=[kq, "epst"], w=[("rt", i)])
                S.op("pool", lambda e: e.tensor_tensor(rt[:, ro:ro + 6], rt[:, ro:ro + 6],
                                                       neghalf[:, 0:1].to_broadcast([128, 6]), op=ALU.pow),
                     r=[("rt", i), "neghalf"], w=[("rt", i)])

            def stage_yn(G):
                i = G % 2
                tok = slice(G * 512, (G + 1) * 512)
                lt = slice((G - 8 * p) * 256, (G - 8 * p) * 256 + 256)
                tabs = TABS[i]
                tabk = [("tab", i, 0), ("tab", i, 1)]
                ro = 8 * i
                cqb = CQB[i]
                for j in range(4):
                    S.op("dve", lambda e, j=j: e.tensor_scalar(dg[:, j * 128:(j + 1) * 128], identf[:], rt[:, ro + j:ro + j + 1],
                                                               None, op0=ALU.mult), r=[("rt", i), "identf"], w=["dg"])
                for j in range(2):
                    S.op("dve", lambda e, j=j: e.tensor_scalar(dg2[:, j * 128:(j + 1) * 128], identf[:],
                                                               rt[:, ro + 4 + j:ro + 5 + j], None, op0=ALU.mult),
                         r=[("rt", i), "identf"], w=["dg2"])
                bb, pbc, kbc = next_bank()
                for j in range(4):
                    S.op("pe", lambda e, pbc=pbc, j=j: e.matmul(pbc[:, j * 128:(j + 1) * 128], onesf[:],
                                                                dg[:, j * 128:(j + 1) * 128], start=True, stop=True,
                                                                skip_group_check=True),
                         r=["dg", "onesf"], w=[kbc])
                bb2, pbc2, kbc2 = next_bank()
                for j in range(2):
                    S.op("pe", lambda e, pbc2=pbc2, j=j: e.matmul(pbc2[:, j * 128:(j + 1) * 128], onesf[:],
                                                                  dg2[:, j * 128:(j + 1) * 128], start=True, stop=True,
                                                                  skip_group_check=True),
                         r=["dg2", "onesf"], w=[kbc2])
                S.op("act", lambda e, pbc=pbc: e.copy(rstd, pbc), r=[kbc], w=["rstd"])
                S.op("act", lambda e, pbc2=pbc2: e.copy(rstdq[:, 0:256], pbc2[:, 0:256]), r=[kbc2], w=["rstdq"])
                S.op("pool", lambda e: e.tensor_tensor(ckvn[:, tok], ckvn[:, tok], rstd, op=ALU.mult),
                     r=[("ckvn", G), "rstd"], w=[("ckvn", G)])
                kr0, kr1 = KR[i]
                S.op("dve", lambda e: e.tensor_tensor(kr0[64:96, :], kr0[64:96, :], tabs[0][64:96, :], op=ALU.mult),
                     r=[("KR", i, 0), tabk[0]], w=[("KR", i, 0)])
                S.op("dve", lambda e: e.tensor_tensor(kr1[64:96, :], kr1[64:96, :], tabs[1][64:96, :], op=ALU.mult),
                     r=[("KR", i, 1), tabk[1]], w=[("KR", i, 1)])
                S.op("pool", lambda e: e.tensor_tensor(KB[0][64:96, tok], kr0[64:96, :], kr1[64:96, :], op=ALU.add),
                     r=[("KR", i, 0), ("KR", i, 1)], w=[("kbr", 0, G)])
                S.op("pool", lambda e: e.tensor_copy(KB[1][64:96, tok], KB[0][64:96, tok]),
                     r=[("kbr", 0, G)], w=[("kbr", 1, G)])
                for ti in range(2):
                    S.op("dve", lambda e, ti=ti: e.tensor_tensor(crsr[0:96, ti * 256:(ti + 1) * 256],
                                                                 tabs[ti][0:96, 0:256], rstdq[0:96, 0:256], op=ALU.mult),
                         r=[tabk[ti], "rstdq"], w=["crsr"])

            def stage_yq(G):
                i = G % 2
                lt = slice((G - 8 * p) * 256, (G - 8 * p) * 256 + 256)
                tok = slice(G * 512, (G + 1) * 512)
                cqb = CQB[i]
                for h in range(8):
                    bh, ph_, kh = next_bank()
                    for half, wsrc, wk in ((0, wuq_g, "wuq"), (1, wuqs_g, "wuqs")):
                        for c in range(2):
                            S.op("pe", lambda e, ph_=ph_, half=half, wsrc=wsrc, c=c, h=h: e.matmul(
                                ph_[0:96, half * 256:(half + 1) * 256], wsrc[:, c, h * 96:(h + 1) * 96],
                                cqb[:, c * 256:(c + 1) * 256], start=(c == 0), stop=(c == 1)),
                                r=[(wk, c), ("cqb", i)], w=[kh])
                    qt_ = qtmp[h % 2]
                    S.op("dve", lambda e, ph_=ph_, qt_=qt_: e.tensor_tensor(qt_[0:96, :], ph_[0:96, :], crsr[0:96, :],
                                                                            op=ALU.mult),
                         r=[kh, "crsr"], w=[("qtmp", h % 2)])
                    S.op("pool", lambda e, qt_=qt_, h=h: e.tensor_tensor(
                        QT[:, h, lt], qt_[0:96, 0:256], qt_[0:96, 256:512], op=ALU.add),
                        r=[("qtmp", h % 2)], w=[("QT", h, G)])
                if debug and G == 8 * p + 1 and p == 0:
                    dd = qtmp[0]
                    kk = ("qtmp", 0)
                    S.op("dve", lambda e: e.tensor_copy(dd, ckvn[:, tok]), r=[("ckvn", G), kk], w=[kk])
                    S.dma("sp", lambda e: e.dma_start(out=dbg["d_ckvn"], in_=dd), dsD, r=[kk])
                    S.op("dve", lambda e: e.tensor_copy(dd[64:96, :], KB[1][64:96, tok]),
                         r=[("kbr", 1, G), kk], w=[kk])
                    S.dma("sp", lambda e: e.dma_start(out=dbg["d_krope"], in_=dd[64:96, :]), dsD, r=[kk])
                    S.op("dve", lambda e: e.tensor_copy(dd[0:96, 0:256], QT[:, 0, lt]), r=[("QT", 0, G), kk], w=[kk])
                    S.op("dve", lambda e: e.tensor_copy(dd[0:96, 256:512], QT[:, 5, lt]), r=[("QT", 5, G), kk], w=[kk])
                    S.dma("sp", lambda e: e.dma_start(out=dbg["d_q0"], in_=dd[0:96, :]), dsD, r=[kk])
                    S.op("dve", lambda e: e.tensor_copy(dd[:, 0:256], MG[:, 0, lt]), r=[("MG", 0, G), kk], w=[kk])
                    S.op("dve", lambda e: e.tensor_copy(dd[:, 256:512], MG[:, 3, lt]), r=[("MG", 1, G), kk], w=[kk])
                    S.dma("sp", lambda e: e.dma_start(out=dbg["d_g"], in_=dd), dsD, r=[kk])

            ng = len(groups)
            stage_x1(groups[0])
            make_tables(groups[0])
            stage_x2(groups[0])
            if ng > 2:
                load_group(groups[2])
            for gi, G in enumerate(groups):
                nxt = groups[gi + 1] if gi + 1 < ng else None
                if nxt is not None:
                    stage_x1(nxt)
                stage_yn(G)
                if nxt is not None:
                    make_tables(nxt)
                    stage_x2(nxt)
                stage_yq(G)
                if gi + 3 < ng:
                    load_group(groups[gi + 3])

        def phase_B(p):
            cv = Carver()
            nslots = 32 * (p + 1)
            VB = [cv.take(64 * 65, BF16).rearrange("p (s d) -> p s d", s=64),
                  cv.take(64 * 128, BF16).rearrange("p (s d) -> p s d", s=64)]
            PT = [cv.take(1024, BF16).rearrange("p (j n) -> p j n", j=2) for _ in range(3)]
            osb = [cv.take(512, F32) for _ in range(2)]
            rec = [cv.take(512, F32) for _ in range(2)]
            otmp = [cv.take(512, F32) for _ in range(2)]
            dsD = dsem()
            S.op("pool", lambda e: e.memset(VB[0][:, :, 64:65], 1.0), w=[("vbc", 0)])
            S.op("pool", lambda e: e.memset(VB[1][:, :, 0:64], 0.0), w=[("vbc", 1)])
            S.op("pool", lambda e: e.memset(VB[1][:, :, 0:1], 1.0), w=[("vbc", 1)])

            def vaug(b, h, slot):
                return VB[b][:, slot, :]

            def vdst(b, c):
                if b == 0:
                    return VB[0][:, c * 8:(c + 1) * 8, 0:64]
                return VB[1][:, c * 8:(c + 1) * 8, 64:128]

            def gen_chunks(h):
                b = h % 2
                out = []
                for c in range(nslots // 4):
                    def kchunk(c=c):
                        tok = slice(c * 512, (c + 1) * 512)
                        S.op("pe", lambda e: e.matmul(PSG[0:64, :], wukv_g[:, h * 128:h * 128 + 64], ckvn[:, tok],
                                                      start=True, stop=True), r=["wukv"], w=["psg"])
                        S.op("dve", lambda e: e.tensor_copy(KB[b][0:64, tok], PSG[0:64, :]),
                             r=["psg"], w=[("kb", b, c)])
                    out.append(kchunk)
                for c in range(nslots // 8):
                    def vchunk(c=c):
                        for s in range(8):
                            slot = c * 8 + s
                            S.op("pe", lambda e, s=s, slot=slot: e.matmul(
                                PSG[:, s * 64:(s + 1) * 64], ckvn[:, slot * 128:(slot + 1) * 128],
                                wukv_g[:, h * 128 + 64:h * 128 + 128], start=True, stop=True),
                                r=["wukv"], w=["psg"])
                        S.op("dve", lambda e: e.tensor_copy(vdst(b, c), PSG.rearrange("p (s d) -> p s d", s=8)),
                             r=["psg"], w=[("vb", b, c)])
                    out.append(vchunk)
                return out

            steps = []
            for h in range(8):
                for g in range(4):
                    gg = 4 * p + g
                    nfull = 4 * gg
                    for i in range(nfull):
                        steps.append(dict(h=h, g=g, slots=(2 * i, 2 * i + 1), c0=0, masks=None))
                    base = 8 * gg
                    tails = [
                        ((base + 0, base + 1), 0, ((0, 128, 0), (128, 256, 0))),
                        ((base + 2, base + 3), 0, ((384, 128, 0), (512, 256, 0))),
                        ((base + 4, base + 5), 256, ((0, 128, 256), (128, 256, 256))),
                        ((base + 6, base + 7), 256, ((384, 128, 256), (512, 256, 256))),
                    ]
                    for slots, c0, mk in tails:
                        steps.append(dict(h=h, g=g, slots=slots, c0=c0, masks=mk))
                    steps[-1]["last"] = True
                    steps[-(nfull + 4)]["first"] = True
            for fn in gen_chunks(0):
                fn()
            pending_gen = []
            pending_epi = []
            pending_rcp = []
            n_steps = len(steps)
            if KB_STEPS >= 0:
                n_steps = KB_STEPS
            ocount = [0]

            def emit_qk(si):
                stp = steps[si]
                h, g, c0 = stp["h"], stp["g"], stp["c0"]
                b = h % 2
                wtile = PSW[si % 3]
                for j, slot in enumerate(stp["slots"]):
                    has_mask = stp["masks"] is not None
                    S.op("pe", lambda e, j=j, slot=slot, has_mask=has_mask: e.matmul(
                        wtile[:, j, c0:512], KB[b][:, slot * 128:(slot + 1) * 128],
                        QT[:, h, g * 512 + c0:(g + 1) * 512], start=True, stop=(not has_mask or bool(KB_NOMASK)),
                        skip_group_check=True),
                        r=[("kb", b, slot // 4)], w=[("psw", si % 3)])
                    if has_mask and not KB_NOMASK:
                        mo, mw, mc = stp["masks"][j]
                        S.op("pe", lambda e, j=j, mo=mo, mw=mw, mc=mc: e.matmul(
                            wtile[:, j, mc:mc + mw], identb[:], maskb[:, mo:mo + mw], start=False, stop=True,
                            skip_group_check=True),
                            r=[], w=[("psw", si % 3)])

            def emit_exp(si):
                stp = steps[si]
                c0 = stp["c0"]
                S.op("act", lambda e: e.activation(PT[si % 3][:, :, c0:512], PSW[si % 3][:, :, c0:512], AF.Exp,
                                                   scale=SCALE),
                     r=[("psw", si % 3)], w=[("pt", si % 3)])

            def emit_pv(si):
                stp = steps[si]
                h, g, c0 = stp["h"], stp["g"], stp["c0"]
                b = h % 2
                if stp.get("first"):
                    ocount[0] += 1
                o = ocount[0] % 2
                for j, slot in enumerate(stp["slots"]):
                    first = bool(stp.get("first")) and j == 0
                    last = bool(stp.get("last")) and j == 1
                    mrows = 65 if b == 0 else 128
                    S.op("pe", lambda e, j=j, slot=slot, first=first, last=last, mrows=mrows: e.matmul(
                        PSO1[0:mrows, c0:512], vaug(b, h, slot), PT[si % 3][:, j, c0:512], start=first, stop=last,
                        skip_group_check=True),
                        r=[("pt", si % 3), ("vb", b, slot // 8), ("vbc", b)], w=["pso"])
                if stp.get("last"):
                    mrows = 65 if b == 0 else 128
                    S.op("dve", lambda e, mrows=mrows: e.tensor_copy(osb[o][0:mrows, :], PSO1[0:mrows, :]),
                         r=["pso"], w=[("osb", o)])
                    dr = 64 if h % 2 == 0 else 0
                    for q4 in range(4):
                        pending_rcp.append((si + q4, o, dr, q4))
                    pending_epi.append((si + 6, h, g, o))

            def emit_epilogue(h, g, o):
                if KB_NOEPI:
                    return
                dr = 64 if h % 2 == 0 else 0
                rows = slice(0, 64) if h % 2 == 0 else slice(64, 128)
                r_ = rec[o]
                ot = otmp[o]
                ob_ = osb[o]
                S.op("pe", lambda e: e.matmul(PSG, onesf[dr:dr + 1, :], r_[dr:dr + 1, :], start=True, stop=True),
                     r=[("rec", o)], w=["psg"])
                S.op("dve", lambda e: e.tensor_tensor(ot[rows, :], ob_[rows, :], PSG[rows, :], op=ALU.mult),
                     r=[("osb", o), "psg"], w=[("otmp", o)])
                mg = MG[rows, h // 2, g * 512:(g + 1) * 512]
                S.op("pool", lambda e: e.tensor_tensor(mg, ot[rows, :], mg, op=ALU.mult),
                     r=[("otmp", o)], w=[("mgo", h, g)])

            if n_steps > 0:
                emit_qk(0)
            if n_steps > 1:
                emit_qk(1)
            cur_h = 0
            gen_every = 3 if p == 0 else 4
            since_gen = 0
            for si in range(n_steps):
                emit_exp(si)
                if si + 2 < n_steps:
                    emit_qk(si + 2)
                if not KB_NOPV:
                    emit_pv(si)
                while pending_rcp and pending_rcp[0][0] <= si:
                    _, ro_, rdr, q4 = pending_rcp.pop(0)
                    cs = slice(q4 * 128, (q4 + 1) * 128)
                    S.op("dve", lambda e, ro_=ro_, rdr=rdr, cs=cs: e.reciprocal(rec[ro_][rdr:rdr + 1, cs],
                                                                               osb[ro_][rdr:rdr + 1, cs]),
                         r=[("osb", ro_)], w=[("rec", ro_)])
                while pending_epi and pending_epi[0][0] <= si:
                    _, eh, eg, eo = pending_epi.pop(0)
                    emit_epilogue(eh, eg, eo)
                h = steps[si]["h"]
                if steps[si].get("first") and steps[si]["g"] == 0 and h + 1 < 8:
                    pending_gen = gen_chunks(h + 1)
                    since_gen = 0
                since_gen += 1
                if pending_gen and since_gen >= gen_every:
                    pending_gen.pop(0)()
                    since_gen = 0
                if steps[si].get("last") and steps[si]["g"] == 3:
                    while pending_gen:
                        pending_gen.pop(0)()
            while pending_rcp:
                _, ro_, rdr, q4 = pending_rcp.pop(0)
                cs = slice(q4 * 128, (q4 + 1) * 128)
                S.op("dve", lambda e, ro_=ro_, rdr=rdr, cs=cs: e.reciprocal(rec[ro_][rdr:rdr + 1, cs],
                                                                           osb[ro_][rdr:rdr + 1, cs]),
                     r=[("osb", ro_)], w=[("rec", ro_)])
            while pending_epi:
                _, eh, eg, eo = pending_epi.pop(0)
                emit_epilogue(eh, eg, eo)
            if debug and p == 0:
                dd = cv.take(512, F32)
                S.op("dve", lambda e: e.tensor_copy(dd[0:96, :], KB[1][:, 512:1024]), r=[("kb", 1, 1)], w=["dd"])
                S.dma("sp", lambda e: e.dma_start(out=dbg["d_kb"], in_=dd[0:96, :]), dsD, r=["dd"])
                S.op("dve", lambda e: e.tensor_copy(dd.rearrange("p (s d) -> p s d", s=8), VB[1][:, 8:16, 64:128]),
                     r=[("vb", 1, 1), "dd"], w=["dd"])
                S.dma("sp", lambda e: e.dma_start(out=dbg["d_vb"], in_=dd), dsD, r=["dd"])
                S.op("dve", lambda e: e.tensor_copy(dd[:, 0:256], MG[:, 0, 256:512]),
                     r=[("mgo", 0, 0), ("mgo", 1, 0), "dd"], w=["dd"])
                S.op("dve", lambda e: e.tensor_copy(dd[:, 256:512], MG[:, 3, 1024 + 256:1024 + 512]),
                     r=[("mgo", 6, 2), ("mgo", 7, 2), "dd"], w=["dd"])
                S.dma("sp", lambda e: e.dma_start(out=dbg["d_merged"], in_=dd), dsD, r=["dd"])

        WC_bytes = 8 * 1536 * 2 + 8 * 1024 * 2

        def wc_views():
            base = (ARENA - WC_bytes) // 2
            wci = arena[:, base:base + 8 * 1536].rearrange("p (c n) -> p c n", c=8)
            wco = arena[:, base + 8 * 1536:base + 8 * 1536 + 8 * 1024].rearrange("p (c n) -> p c n", c=8)
            return wci, wco

        dsWC = dsem()

        def load_WC():
            wci, wco = wc_views()
            S.dma("sp", lambda e: e.dma_start(out=wci, in_=w_in_bf.rearrange("(c p) n -> p c n", p=128)[:, :, NA:DIN]),
                  dsWC, w=["WCI"])
            S.dma("sp", lambda e: e.dma_start(out=wco, in_=w_out_bf.rearrange("(c p) n -> p c n", p=128)),
                  dsWC, w=["WCO"])

        dsY = [dsem(), dsem()]

        def phase_C(p):
            cv = Carver()
            wci, wco = wc_views()
            XR = [cv.take(D, F32) for _ in range(2)]
            XB = [cv.take(D, BF16) for _ in range(2)]
            XTC = [cv.take(8 * 128, BF16).rearrange("p (c t) -> p c t", c=8) for _ in range(2)]
            UG = [cv.take(512, F32) for _ in range(2)]
            VG = [cv.take(512, F32) for _ in range(2)]
            ZB = [cv.take(512, F32) for _ in range(2)]
            VN = [cv.take(512, BF16) for _ in range(2)]
            ob = cv.take(512, BF16)
            obT = cv.take(512, BF16)
            hh = cv.take(D, F32)
            YO = [cv.take(D, F32) for _ in range(2)]
            st1 = [cv.take(16, F32) for _ in range(2)]
            st2 = cv.take(24, F32)
            assert cv.off <= ARENA - WC_bytes, ("phase C arena", cv.off)
            dsXR = [dsem(), dsem()]
            dsXB = [dsem(), dsem()]

            def load_block(t):
                i = t % 2
                gslot = 4 * (t // 2) + (t % 2)
                rows = slice(gslot * 128, (gslot + 1) * 128)
                S.dma("sp", lambda e: e.dma_start(out=XR[i], in_=xs[rows, :]), dsXR[i], w=[("XR", i)])
                S.dma("pool", lambda e: e.dma_start(out=XB[i], in_=xs[rows, :]), dsXB[i], w=[("XB", i)])

            def s1a(t):
                i = t % 2
                xtc = XTC[i]
                for hf in range(2):
                    pt = pst_banks[hf]
                    for kk in range(4):
                        k = hf * 4 + kk
                        S.op("pe", lambda e, pt=pt, kk=kk, k=k: e.transpose(
                            pt[:, kk * 128:(kk + 1) * 128], XB[i][:, k * 128:(k + 1) * 128], identb[:]),
                            r=[("XB", i), "identb"], w=[("pst", hf)])
                    dst = xtc[:, hf * 4:hf * 4 + 4, :]
                    if hf == 0:
                        S.op("act", lambda e, pt=pt, dst=dst: e.copy(dst, pt.rearrange("p (c t) -> p c t", c=4)),
                             r=[("pst", hf)], w=[("XTC", i, hf)])
                    else:
                        S.op("dve", lambda e, pt=pt, dst=dst: e.tensor_copy(dst, pt.rearrange("p (c t) -> p c t", c=4)),
                             r=[("pst", hf)], w=[("XTC", i, hf)])

            def s1b_mm(t):
                i = t % 2
                xtc = XTC[i]
                xk = [("XTC", i, 0), ("XTC", i, 1)]
                ug, vg, zb = UG[i], VG[i], ZB[i]
                th = hh[:, 0:512]
                pss = {}
                for j in (1, 0, 2):
                    bb, pb, kb_ = next_bank()
                    pss[j] = (pb, kb_)
                    for k in range(8):
                        S.op("pe", lambda e, pb=pb, j=j, k=k: e.matmul(
                            pb, xtc[:, k, :], wci[:, k, j * 512:(j + 1) * 512], start=(k == 0), stop=(k == 7)),
                            r=xk + ["WCI"], w=[kb_])
                S.op("act", lambda e: e.activation(vg, pss[1][0], AF.Gelu), r=[pss[1][1]], w=[("vg", i)])
                S.op("act", lambda e: e.activation(ug, pss[0][0], AF.Gelu), r=[pss[0][1]], w=[("ug", i)])
                S.op("act", lambda e: e.activation(th, pss[2][0], AF.Tanh, scale=0.5), r=[pss[2][1]], w=[("hh", 0)])
                S.op("act", lambda e: e.copy(zb, pss[2][0]), r=[pss[2][1]], w=[("zb", i)])
                S.op("pool", lambda e: e.tensor_tensor(th, th, zb, op=ALU.mult), r=[("hh", 0), ("zb", i)], w=[("hh", 0)])
                S.op("pool", lambda e: e.tensor_tensor(zb, th, zb, op=ALU.add), r=[("hh", 0), ("zb", i)], w=[("zb", i)])
                S.op("pool", lambda e: e.tensor_tensor(ug, ug, zb, op=ALU.mult), r=[("ug", i), ("zb", i)], w=[("ug", i)])

            def s1b_ln(t):
                i = t % 2
                vg, vn, st = VG[i], VN[i], st1[i]
                S.op("dve", lambda e: e.bn_stats(st[:, 0:6], vg), r=[("vg", i)], w=[("st1a", i)])
                S.op("dve", lambda e: e.bn_aggr(st[:, 6:8], st[:, 0:6]), r=[("st1a", i)], w=[("st1b", i)])
                S.op("pool", lambda e: e.tensor_scalar(st[:, 8:9], st[:, 7:8], EPS, None, op0=ALU.add),
                     r=[("st1b", i)], w=[("st1c", i)])
                S.op("pool", lambda e: e.tensor_tensor(st[:, 8:9], st[:, 8:9], neghalf[:, 0:1], op=ALU.pow),
                     r=[("st1c", i), "neghalf"], w=[("st1c", i)])
                S.op("dve", lambda e: e.scalar_tensor_tensor(out=vg, in0=vg, scalar=st[:, 6:7], in1=sgug[:],
                                                             op0=ALU.subtract, op1=ALU.mult),
                     r=[("vg", i), ("st1b", i), "sgug"], w=[("vg", i)])
                S.op("dve", lambda e: e.scalar_tensor_tensor(out=vn, in0=vg, scalar=st[:, 8:9], in1=sgub[:],
                                                             op0=ALU.mult, op1=ALU.add),
                     r=[("vg", i), ("st1c", i), "sgub"], w=[("vn", i)])

            def s2a(t):
                i = t % 2
                ug, vn = UG[i], VN[i]
                bsv, psv, ksv = next_bank()
                S.op("pe", lambda e: e.matmul(psv, bspb[:], indb[:], start=True, stop=False, skip_group_check=True),
                     r=["bspb", "indb"], w=[ksv])
                for h in range(8):
                    S.op("pe", lambda e, h=h: e.matmul(psv[:, h * 64:(h + 1) * 64], wsp[:, h, :],
                                                      vn[:, h * 64:(h + 1) * 64], start=False, stop=(h == 7),
                                                      skip_group_check=True),
                         r=[("wsp", h), ("vn", i)], w=[ksv])
                S.op("dve", lambda e: e.scalar_tensor_tensor(out=ob, in0=psv, scalar=0.5, in1=ug, op0=ALU.mult, op1=ALU.mult),
                     r=[ksv, ("ug", i)], w=["ob"])

            def s2b(t):
                pt = pst_banks[0]
                for c in range(4):
                    S.op("pe", lambda e, c=c: e.transpose(pt[:, c * 128:(c + 1) * 128],
                                                         ob[:, c * 128:(c + 1) * 128], identb[:]),
                         r=["ob", "identb"], w=[("pst", 0)])
                S.op("act", lambda e: e.copy(obT, pt), r=[("pst", 0)], w=["obT"])

            def s2c(t):
                i = t % 2
                lt = slice((t - 16 * p) * 128, (t - 16 * p + 1) * 128)
                outs = []
                for nh in range(2):
                    bo, po, ko = next_bank()
                    outs.append((po, ko))
                    for c in range(8):
                        if c < 4:
                            lhsT = MG[:, c, lt]
                            rr = []
                        else:
                            lhsT = obT[:, (c - 4) * 128:(c - 3) * 128]
                            rr = ["obT"]
                        S.op("pe", lambda e, po=po, lhsT=lhsT, c=c, nh=nh: e.matmul(
                            po, lhsT, wco[:, c, nh * 512:(nh + 1) * 512], start=(c == 0), stop=(c == 7)),
                            r=rr + ["WCO"], w=[ko])
                for nh in range(2):
                    po, ko = outs[nh]
                    S.op("dve", lambda e, po=po, nh=nh: e.scalar_tensor_tensor(
                        out=hh[:, nh * 512:(nh + 1) * 512], in0=XR[i][:, nh * 512:(nh + 1) * 512], scalar=ALPHA,
                        in1=po, op0=ALU.mult, op1=ALU.add),
                        r=[ko, ("XR", i)], w=[("hh", nh)])
                    S.op("dve", lambda e, nh=nh: e.bn_stats(st2[:, 6 * nh:6 * nh + 6], hh[:, nh * 512:(nh + 1) * 512]),
                         r=[("hh", nh)], w=[("st2a", nh)])
                S.op("dve", lambda e: e.bn_aggr(st2[:, 12:14], st2[:, 0:12]), r=[("st2a", 0), ("st2a", 1)], w=["st2b"])
                S.op("pool", lambda e: e.tensor_scalar(st2[:, 14:15], st2[:, 13:14], EPS, None, op0=ALU.add),
                     r=["st2b"], w=["st2c"])
                S.op("pool", lambda e: e.tensor_tensor(st2[:, 14:15], st2[:, 14:15], neghalf[:, 0:1], op=ALU.pow),
                     r=["st2c", "neghalf"], w=["st2c"])
                S.op("dve", lambda e: e.scalar_tensor_tensor(out=hh, in0=hh, scalar=st2[:, 12:13], in1=lng[:],
                                                             op0=ALU.subtract, op1=ALU.mult),
                     r=[("hh", 0), ("hh", 1), "st2b", "lng"], w=[("hh", 0), ("hh", 1)])
                yo = YO[i]
                S.op("dve", lambda e: e.scalar_tensor_tensor(out=yo, in0=hh, scalar=st2[:, 14:15], in1=lnb[:],
                                                             op0=ALU.mult, op1=ALU.add),
                     r=[("hh", 0), ("hh", 1), "st2c", "lnb"], w=[("YO", i)])
                S.dma("sp", lambda e: e.dma_start(out=y[t * 128:(t + 1) * 128, :], in_=yo),
                      dsY[i], r=[("YO", i)])

            def load_xb(t):
                i = t % 2
                gslot = 4 * (t // 2) + (t % 2)
                rows = slice(gslot * 128, (gslot + 1) * 128)
                S.dma("pool", lambda e: e.dma_start(out=XB[i], in_=xs[rows, :]), dsXB[i], w=[("XB", i)])

            def load_xr(t):
                i = t % 2
                gslot = 4 * (t // 2) + (t % 2)
                rows = slice(gslot * 128, (gslot + 1) * 128)
                S.dma("sp", lambda e: e.dma_start(out=XR[i], in_=xs[rows, :]), dsXR[i], w=[("XR", i)])

            blocks = list(range(16 * p, 16 * p + 16))
            nb = len(blocks)
            load_xb(blocks[0])
            load_xb(blocks[1])
            load_xr(blocks[0])
            load_xr(blocks[1])
            s1a(blocks[0])
            s1b_mm(blocks[0])
            s1b_ln(blocks[0])
            s1a(blocks[1])
            load_xb(blocks[2])
            for bi, t in enumerate(blocks):
                s2a(t)
                if bi + 1 < nb:
                    s1b_mm(blocks[bi + 1])
                if bi + 2 < nb:
                    s1a(blocks[bi + 2])
                if bi + 3 < nb:
                    load_xb(blocks[bi + 3])
                s2b(t)
                if bi + 1 < nb:
                    s1b_ln(blocks[bi + 1])
                s2c(t)
                if bi + 2 < nb:
                    load_xr(blocks[bi + 2])

        for p in passes:
            if "A" in phases:
                phase_A(p)
                S.barrier()
            if "B" in phases:
                if "C" in phases and WC_EARLY:
                    load_WC()
                phase_B(p)
                S.barrier()
            if "C" in phases:
                if "B" not in phases or not WC_EARLY:
                    load_WC()
                phase_C(p)
                S.barrier()
        S.emit(nc, block, esems, dsY)
    return nc


def _slot_perm(par):
    order = []
    for m in range(16):
        if par == 0:
            order += [4 * m, 4 * m + 3, 4 * m + 1, 4 * m + 2]
        else:
            order += [4 * m + 1, 4 * m + 2, 4 * m, 4 * m + 3]
    return order


def make_in_maps(x, positions, w_in, q_norm_g, w_uq, kv_norm_g, w_ukv, sgu_norm_g, sgu_norm_b,
                 w_spatial, b_spatial, w_out, ln_g, ln_b):
    f = np.float32
    w_in = np.ascontiguousarray(w_in, dtype=f)
    swap = np.concatenate([np.arange(16, 32), np.arange(0, 16)])
    w_krs = np.ascontiguousarray(w_in[:, 384:416][:, swap])
    w_uq = np.ascontiguousarray(w_uq, dtype=f)
    w_uqs = np.zeros_like(w_uq)
    for h in range(8):
        w_uqs[:, h * 96 + 64:h * 96 + 96] = w_uq[:, h * 96 + 64:h * 96 + 96][:, swap]
    qg = np.ascontiguousarray(np.asarray(q_norm_g, dtype=f).reshape(2, 128).T)
    kvg = np.ascontiguousarray(np.asarray(kv_norm_g, dtype=f).reshape(128, 1))
    wspT = np.ascontiguousarray(np.transpose(np.asarray(w_spatial, dtype=f), (2, 0, 1)))
    si = np.arange(128)
    tril = (si[:, None] <= si[None, :]).astype(f)
    bspT = np.ascontiguousarray(np.asarray(b_spatial, dtype=f).T)
    ident = np.eye(128, dtype=f)
    invf = (1.0 / (10000.0 ** (np.arange(16, dtype=np.float64) / 16.0)))
    ropec = np.zeros((128, 6), f)
    ropec[:, 1] = -TWO_PI
    ropec[:, 2] = -TWO_PI
    ropec[:, 3] = 0.5 * math.pi
    for pp in range(64, 96):
        i = (pp - 64) % 16
        ropec[pp, 0] = invf[i] / TWO_PI
        ropec[pp, 1] = TWO_PI if pp < 80 else -TWO_PI
    tri = np.where(si[:, None] > si[None, :], NEG, 0.0).astype(f)
    allm = np.full((128, 128), NEG, f)
    zero = np.zeros((128, 128), f)
    ind_h = np.zeros((8, 512), f)
    for h in range(8):
        ind_h[h, h * 64:(h + 1) * 64] = 1.0
    common = dict(bsp_ht=np.ascontiguousarray(np.asarray(b_spatial, dtype=f)), ind_h=ind_h, w_in=w_in, w_krs=w_krs, w_uq=w_uq, w_uqs=w_uqs, qg=qg, w_ukv=np.ascontiguousarray(w_ukv, dtype=f),
                  kvg=kvg, wspT=wspT, tril=tril, bspT=bspT,
                  sgu_g=np.asarray(sgu_norm_g, dtype=f).reshape(1, 512), sgu_b=np.asarray(sgu_norm_b, dtype=f).reshape(1, 512),
                  ln_g=np.asarray(ln_g, dtype=f).reshape(1, D), ln_b=np.asarray(ln_b, dtype=f).reshape(1, D),
                  w_out=np.ascontiguousarray(w_out, dtype=f), ident=ident, ropec=ropec)
    in_maps = []
    perms = []
    for c in range(8):
        b, par = c // 2, c % 2
        order = _slot_perm(par)
        perms.append(order)
        xb = np.asarray(x[b], dtype=f).reshape(64, 128, D)[order].reshape(SEQ, D)
        pb = np.asarray(positions[b], dtype=np.int32).reshape(64, 128)[order].reshape(1, SEQ)
        mA = allm if par == 0 else zero
        mB = zero if par == 0 else allm
        mk = np.concatenate([tri, allm, tri, mA, allm, mB], axis=1)
        d = dict(common)
        d.update(xs=np.ascontiguousarray(xb), pos=np.ascontiguousarray(pb), masks=np.ascontiguousarray(mk))
        in_maps.append(d)
    return in_maps, perms


_NC_CACHE = {}


def kernel(x, positions, w_in, q_norm_g, w_uq, kv_norm_g, w_ukv, sgu_norm_g, sgu_norm_b,
           w_spatial, b_spatial, w_out, ln_g, ln_b):
    in_maps, perms = make_in_maps(x, positions, w_in, q_norm_g, w_uq, kv_norm_g, w_ukv, sgu_norm_g,
                                  sgu_norm_b, w_spatial, b_spatial, w_out, ln_g, ln_b)
    if "nc" not in _NC_CACHE:
        _NC_CACHE["nc"] = build_program()
    nc = _NC_CACHE["nc"]
    res = run_bass_kernel_spmd(nc, in_maps, core_ids=list(range(8)))
    out = np.empty((4, SEQ, D), np.float32)
    for c in range(8):
        b = c // 2
        order = perms[c]
        yc = np.asarray(res.results[c]["y"], dtype=np.float32).reshape(32, 128, D)
        ov = out[b].reshape(64, 128, D)
        for t in range(32):
            slot = 4 * (t // 2) + (t % 2)
            ov[order[slot]] = yc[t]
    return out
```

```python
import math
from contextlib import ExitStack

import numpy as np
import concourse.bass as bass
import concourse.mybir as mybir
from concourse.bass_utils import run_bass_kernel_spmd

F32 = mybir.dt.float32
BF16 = mybir.dt.bfloat16
I32 = mybir.dt.int32
AF = mybir.ActivationFunctionType
ALU = mybir.AluOpType

D = 1024
SEQ = 8192
NSLOT = 64
DIN = 2464
NA = 928
NEG = -30000.0
EPS = 1e-5
ALPHA = 2.0 ** 0.25
SCALE = 1.0 / math.sqrt(96.0)
TWO_PI = 2.0 * math.pi

ENGS = ("pe", "act", "dve", "pool", "sp")
import os
STOP = int(os.environ.get("KSTOP", "0"))
NGROUPS = int(os.environ.get("KGROUPS", "8"))
KB_STEPS = int(os.environ.get("KB_STEPS", "-1"))
WC_EARLY = int(os.environ.get("WC_EARLY", "1"))
TAB_ENG = os.environ.get("TAB_ENG", "dve")
KB_NOEPI = int(os.environ.get("KB_NOEPI", "0"))
KB_NOPV = int(os.environ.get("KB_NOPV", "0"))
KB_NOMASK = int(os.environ.get("KB_NOMASK", "0"))


class DmaSem:
    def __init__(self, sem):
        self.sem = sem
        self.count = 0


class Op:
    __slots__ = ("eng", "fn", "deps", "dma", "dsem", "dval", "signal", "sigval", "barrier")

    def __init__(self, eng, fn, deps, dma=False, dsem=None, dval=0):
        self.eng = eng
        self.fn = fn
        self.deps = deps
        self.dma = dma
        self.dsem = dsem
        self.dval = dval
        self.signal = False
        self.sigval = 0
        self.barrier = False


class Sched:
    def __init__(self):
        self.ops = []
        self.state = {}
        self.last_on = {e: None for e in ENGS}
        self.open_dma = []

    def _add(self, op):
        self.ops.append(op)
        idx = len(self.ops) - 1
        self.last_on[op.eng] = idx
        return idx

    def _mk(self, idxs):
        d = {}
        for i in idxs:
            p = self.ops[i]
            d[i] = p.dsem.count if p.dma else None
        return d

    def _deps(self, reads, writes):
        deps = set()
        for k in reads:
            st = self.state.get(k)
            if st is not None and st[0] is not None:
                deps.add(st[0])
        for k in writes:
            st = self.state.get(k)
            if st is not None:
                if st[0] is not None:
                    deps.add(st[0])
                deps.update(st[1])
        return self._mk(deps)

    def _commit(self, idx, reads, writes):
        for k in reads:
            st = self.state.setdefault(k, [None, []])
            st[1].append(idx)
        for k in writes:
            self.state[k] = [idx, []]

    def op(self, eng, fn, r=(), w=()):
        deps = self._deps(r, w)
        idx = self._add(Op(eng, fn, deps))
        self._commit(idx, r, w)
        return idx

    def dma(self, queue, fn, dsem, r=(), w=()):
        deps = self._deps(r, w)
        dsem.count += 16
        idx = self._add(Op(queue, fn, deps, dma=True, dsem=dsem, dval=dsem.count))
        self._commit(idx, r, w)
        self.open_dma.append(idx)
        return idx

    def barrier(self):
        targets = set(i for i in self.last_on.values() if i is not None)
        targets.update(self.open_dma)
        for e in ENGS:
            o = Op(e, None, self._mk(targets))
            o.barrier = True
            self.ops.append(o)
        self.open_dma = []
        self.state = {}

    def emit(self, nc, block, esems, final_waits):
        ops = self.ops
        for i, o in enumerate(ops):
            for d in o.deps:
                p = ops[d]
                if p.dma:
                    continue
                if p.eng == o.eng and not o.barrier:
                    if p.eng == "pe" or p.eng == "sp":
                        continue
                p.signal = True
        cnt = {e: 0 for e in ENGS}
        for o in ops:
            if o.signal:
                cnt[o.eng] += 1
                o.sigval = cnt[o.eng]

        def stream(name):
            def body(e):
                seen = {}
                for o in ops:
                    if o.eng != name:
                        continue
                    for d in sorted(o.deps):
                        p = ops[d]
                        if p.dma:
                            sem, val = p.dsem.sem, o.deps[d]
                        else:
                            if p.eng == name and (name == "pe" or name == "sp") :
                                continue
                            if p.eng == name and o.barrier and name == "pe":
                                continue
                            sem, val = esems[p.eng], p.sigval
                        key = id(sem)
                        if seen.get(key, 0) >= val:
                            continue
                        seen[key] = val
                        e.wait_ge(sem, val)
                    if o.fn is None:
                        continue
                    ins = o.fn(e)
                    if o.dma:
                        ins.then_inc(o.dsem.sem, 16)
                    elif o.signal:
                        ins.then_inc(esems[name], 1)
                if name == "sp":
                    for ds in final_waits:
                        e.wait_ge(ds.sem, ds.count)
            return body

        block.sync(stream("sp"))
        block.tensor(stream("pe"))
        block.scalar(stream("act"))
        block.vector(stream("dve"))
        block.gpsimd(stream("pool"))


def build_program(passes=(0, 1), phases="ABC", debug=False):
    nc = bass.Bass("TRN2", target_bir_lowering=False)

    def din(name, shape, dt=F32):
        return nc.dram_tensor(name, shape, dt, kind="ExternalInput").ap()

    xs = din("xs", [SEQ, D])
    pos = din("pos", [1, SEQ], I32)
    w_in = din("w_in", [D, DIN])
    w_krs = din("w_krs", [D, 32])
    w_uq = din("w_uq", [256, 768])
    w_uqs = din("w_uqs", [256, 768])
    qg = din("qg", [128, 2])
    w_ukv = din("w_ukv", [128, 1024])
    kvg = din("kvg", [128, 1])
    wspT = din("wspT", [128, 8, 128])
    tril = din("tril", [128, 128])
    bspT = din("bspT", [128, 8])
    bsp_ht = din("bsp_ht", [8, 128])
    ind_h = din("ind_h", [8, 512])
    sgu_g = din("sgu_g", [1, 512])
    sgu_b = din("sgu_b", [1, 512])
    ln_g = din("ln_g", [1, D])
    ln_b = din("ln_b", [1, D])
    w_out = din("w_out", [D, D])
    ident = din("ident", [128, 128])
    ropec = din("ropec", [128, 6])
    masks = din("masks", [128, 768])
    y = nc.dram_tensor("y", [SEQ // 2, D], F32, kind="ExternalOutput").ap()
    w_in_bf = nc.dram_tensor("w_in_bf", [D, DIN], BF16, kind="Internal").ap()
    w_krs_bf = nc.dram_tensor("w_krs_bf", [D, 32], BF16, kind="Internal").ap()
    w_out_bf = nc.dram_tensor("w_out_bf", [D, D], BF16, kind="Internal").ap()
    dbg = {}
    if debug:
        def dout(name, shape):
            dbg[name] = nc.dram_tensor(name, shape, F32, kind="ExternalOutput").ap()
        dout("d_ckvn", [128, 512])
        dout("d_krope", [32, 512])
        dout("d_q0", [96, 512])
        dout("d_g", [128, 512])
        dout("d_kb", [96, 512])
        dout("d_vb", [128, 512])
        dout("d_merged", [128, 512])

    S = Sched()
    with ExitStack() as es:
        def sb(name, shape, dt):
            return es.enter_context(nc.sbuf_tensor(name, shape, dt))

        def ps(name, shape, dt):
            return es.enter_context(nc.psum_tensor(name, shape, dt))

        def newsem(name):
            return es.enter_context(nc.semaphore(name))

        dsem_n = [0]

        def dsem():
            dsem_n[0] += 1
            return DmaSem(newsem("dq%d" % dsem_n[0]))

        identb = sb("identb", [128, 128], BF16)
        onesf = sb("onesf", [128, 128], F32)
        maskb = sb("maskb", [128, 768], BF16)
        ropecs = sb("ropecs", [128, 6], F32)
        epst = sb("epst", [128, 1], F32)
        neghalf = sb("neghalf", [128, 1], F32)
        identf = sb("identf", [128, 128], F32)
        rt = sb("rt", [128, 16], F32)
        wuq_g = sb("wuq_g", [128, 2, 768], BF16)
        wuqs_g = sb("wuqs_g", [128, 2, 768], BF16)
        wukv_g = sb("wukv_g", [128, 1024], BF16)
        wsp = sb("wsp", [128, 8, 128], BF16)
        bsp = sb("bsp", [128, 8], F32)
        bspb = sb("bspb", [8, 128], BF16)
        indb = sb("indb", [8, 512], BF16)
        sgug = sb("sgug", [128, 512], F32)
        sgub = sb("sgub", [128, 512], F32)
        lng = sb("lng", [128, D], F32)
        lnb = sb("lnb", [128, D], F32)
        ckvn = sb("ckvn", [128, SEQ], BF16)
        KB = [sb("kb%d" % i, [96, SEQ], BF16) for i in range(2)]
        QT = sb("qt", [96, 8, 2048], BF16)
        MG = sb("mg", [128, 4, 2048], BF16)
        ARENA = 86528
        arena = sb("arena", [128, ARENA // 2], BF16)

        class Carver:
            def __init__(self):
                self.off = 0

            def take(self, cols, dt, parts=128):
                size = 4 if dt in (F32, I32) else 2
                self.off = (self.off + 3) // 4 * 4
                nbytes = cols * size
                assert self.off + nbytes <= ARENA, ("arena overflow", self.off, nbytes)
                v = arena[0:parts, self.off // 2:(self.off + nbytes) // 2]
                self.off += nbytes
                if dt != BF16:
                    v = v.bitcast(dt)
                return v

        PT4 = [ps("pt%d" % i, [128, 2, 512], F32) for i in range(4)]
        PSW = [PT4[0], PT4[1], PT4[2]]
        PSO1 = PT4[3][:, 0, :]
        PSG = PT4[3][:, 1, :]
        pst_banks = [PT4[3][:, 0, :].bitcast(BF16)[:, 0:512],
                     PT4[3][:, 1, :].bitcast(BF16)[:, 0:512]]

        def bank(i):
            return PT4[i // 2][:, i % 2, :]

        bank_rr = [0]

        def next_bank():
            b = bank_rr[0] % 6
            bank_rr[0] += 1
            return b, bank(b), ("ps", b)

        esems = {e: newsem("e_" + e) for e in ENGS}
        block = es.enter_context(nc.Block())

        cv = Carver()
        st_a = cv.take(2 * 768, F32)
        st_b = cv.take(2 * 768, F32)
        st_c = cv.take(1024, F32)
        st_d = cv.take(8 * 128, F32)
        st_e = cv.take(128, F32)
        qgs = cv.take(2, F32)
        kvgs = cv.take(1, F32)
        ds0 = dsem()
        ds0p = dsem()
        S.dma("pool", lambda e: e.dma_start(out=identb[:], in_=ident), ds0p, w=["identb"])
        S.dma("pool", lambda e: e.dma_start(out=maskb[:], in_=masks), ds0p, w=["maskb"])
        S.dma("pool", lambda e: e.dma_start(out=bspb[:], in_=bsp_ht), ds0p, w=["bspb"])
        S.dma("pool", lambda e: e.dma_start(out=indb[:], in_=ind_h), ds0p, w=["indb"])
        S.dma("sp", lambda e: e.dma_start(out=ropecs[:], in_=ropec), ds0, w=["ropecs"])
        S.dma("sp", lambda e: e.dma_start(out=identf[:], in_=ident), ds0, w=["identf"])
        S.dma("sp", lambda e: e.dma_start(out=st_a.rearrange("p (c n) -> p c n", c=2),
                                          in_=w_uq.rearrange("(c p) n -> p c n", p=128)), ds0, w=["st_a"])
        S.dma("sp", lambda e: e.dma_start(out=st_b.rearrange("p (c n) -> p c n", c=2),
                                          in_=w_uqs.rearrange("(c p) n -> p c n", p=128)), ds0, w=["st_b"])
        S.dma("sp", lambda e: e.dma_start(out=st_c, in_=w_ukv), ds0, w=["st_c"])
        S.dma("sp", lambda e: e.dma_start(out=st_d.rearrange("p (h t) -> p h t", h=8), in_=wspT), ds0, w=["st_d"])
        S.dma("sp", lambda e: e.dma_start(out=st_e, in_=tril), ds0, w=["st_e"])
        S.dma("sp", lambda e: e.dma_start(out=qgs, in_=qg), ds0, w=["qgs"])
        S.dma("sp", lambda e: e.dma_start(out=kvgs, in_=kvg), ds0, w=["kvgs"])
        S.dma("sp", lambda e: e.dma_start(out=bsp[:], in_=bspT), ds0, w=["bsp"])
        S.dma("sp", lambda e: e.dma_start(out=sgug[:], in_=sgu_g.partition_broadcast(128)), ds0, w=["sgug"])
        S.dma("sp", lambda e: e.dma_start(out=sgub[:], in_=sgu_b.partition_broadcast(128)), ds0, w=["sgub"])
        S.dma("sp", lambda e: e.dma_start(out=lng[:], in_=ln_g.partition_broadcast(128)), ds0, w=["lng"])
        S.dma("sp", lambda e: e.dma_start(out=lnb[:], in_=ln_b.partition_broadcast(128)), ds0, w=["lnb"])
        dscv = dsem()
        S.op("dve", lambda e: e.memset(onesf[:], 1.0), w=["onesf"])
        S.op("dve", lambda e: e.memset(epst[:], EPS), w=["epst"])
        S.op("dve", lambda e: e.memset(neghalf[:], -0.5), w=["neghalf"])
        for c in range(2):
            S.op("dve", lambda e, c=c: e.tensor_scalar(wuq_g[:, c, :], st_a[:, c * 768:(c + 1) * 768],
                                                       qgs[:, c:c + 1], None, op0=ALU.mult),
                 r=["st_a", "qgs"], w=[("wuq", c)])
            S.op("dve", lambda e, c=c: e.tensor_scalar(wuqs_g[:, c, :], st_b[:, c * 768:(c + 1) * 768],
                                                       qgs[:, c:c + 1], None, op0=ALU.mult),
                 r=["st_b", "qgs"], w=[("wuqs", c)])
        S.op("dve", lambda e: e.tensor_scalar(wukv_g[:], st_c, kvgs[:, 0:1], None, op0=ALU.mult),
             r=["st_c", "kvgs"], w=["wukv"])
        for h in range(8):
            S.op("dve", lambda e, h=h: e.tensor_tensor(wsp[:, h, :], st_d[:, h * 128:(h + 1) * 128], st_e,
                                                       op=ALU.mult), r=["st_d", "st_e"], w=[("wsp", h)])
        S.barrier()

        def phase_A(p):
            cv = Carver()
            WA = cv.take(8 * NA, BF16).rearrange("p (c n) -> p c n", c=8)
            WK = cv.take(8 * 32, BF16).rearrange("p (c n) -> p c n", c=8)
            XS = [cv.take(4 * D, BF16).rearrange("p (s d) -> p s d", s=4) for _ in range(2)]
            XT = cv.take(8 * 512, BF16).rearrange("p (c t) -> p c t", c=8)
            POSI = [cv.take(512, I32) for _ in range(2)]
            posf = cv.take(512, F32)
            tk = posf.bitcast(I32)
            TABS = [[cv.take(512, F32) for _ in range(2)] for _ in range(2)]
            tt = cv.take(512, F32)
            tkf = cv.take(512, F32)
            sq = [cv.take(512, F32) for _ in range(2)]
            rstd = cv.take(512, F32)
            dg = cv.take(512, F32)
            dg2 = cv.take(256, F32)
            rstdq = cv.take(256, F32)
            KR = [[cv.take(512, F32) for _ in range(2)] for _ in range(2)]
            CQB = [cv.take(512, BF16) for _ in range(2)]
            crsr = cv.take(512, F32)
            qtmp = [cv.take(512, F32) for _ in range(2)]
            dsW = dsem()
            dsX = [dsem(), dsem()]
            dsP = [dsem(), dsem()]
            dsD = dsem()
            groups = list(range(8 * p, 8 * p + NGROUPS))

            def load_group(G):
                i = G % 2
                S.dma("pool", lambda e: e.dma_start(
                    out=XS[i], in_=xs[G * 512:(G + 1) * 512, :].rearrange("(s p) d -> p s d", p=128)),
                    dsX[i], w=[("XS", i)])
                S.dma("sp", lambda e: e.dma_start(
                    out=POSI[i], in_=pos[0:1, G * 512:(G + 1) * 512].partition_broadcast(128)),
                    dsP[i], w=[("POSI", i)])

            load_group(groups[0])
            if p == 0:
                dsW0 = dsem()
                S.dma("pool", lambda e: e.dma_start(out=WA, in_=w_in.rearrange("(c p) n -> p c n", p=128)[:, :, 0:NA]),
                      dsW0, w=["WA"])
                S.dma("pool", lambda e: e.dma_start(out=WK, in_=w_krs.rearrange("(c p) n -> p c n", p=128)),
                      dsW0, w=["WK"])
            else:
                S.dma("sp", lambda e: e.dma_start(out=WA, in_=w_in_bf.rearrange("(c p) n -> p c n", p=128)[:, :, 0:NA]),
                      dsW, w=["WA"])
                S.dma("sp", lambda e: e.dma_start(out=WK, in_=w_krs_bf.rearrange("(c p) n -> p c n", p=128)),
                      dsW, w=["WK"])
            if len(groups) > 1:
                load_group(groups[1])
            if p == 0:
                S.dma("pool", lambda e: e.dma_start(out=w_in_bf, in_=w_in), dscv, w=["w_in_bf"])
                S.dma("pool", lambda e: e.dma_start(out=w_krs_bf, in_=w_krs), dscv, w=["w_krs_bf"])
                S.dma("pool", lambda e: e.dma_start(out=w_out_bf, in_=w_out), dscv, w=["w_out_bf"])

            def make_tables(G):
                i = G % 2
                S.op("dve", lambda e: e.tensor_copy(posf, POSI[i]), r=[("POSI", i)], w=["posf"])
                S.op("dve", lambda e: e.tensor_scalar(tt, posf, ropecs[:, 0:1], None, op0=ALU.mult),
                     r=["posf", "ropecs"], w=["tt"])
                S.op("dve", lambda e: e.tensor_copy(tk, tt), r=["tt"], w=["posf"])
                S.op("dve", lambda e: e.tensor_copy(tkf, tk), r=["posf"], w=["tkf"])
                S.op("dve", lambda e: e.tensor_tensor(tt, tt, tkf, op=ALU.subtract), r=["tt", "tkf"], w=["tt"])
                S.op("dve", lambda e: e.scalar_tensor_tensor(out=tt, in0=tt, scalar=0.5, in1=tt,
                                                             op0=ALU.is_gt, op1=ALU.subtract),
                     r=["tt"], w=["tt"])
                S.op("act", lambda e: e.activation(TABS[i][1], tt, AF.Sin, scale=ropecs[:, 1:2]),
                     r=["tt", "ropecs"], w=[("tab", i, 1)])
                S.op("act", lambda e: e.activation(tkf, tt, AF.Abs), r=["tt"], w=["tkf"])
                S.op("act", lambda e: e.activation(TABS[i][0], tkf, AF.Sin, bias=ropecs[:, 3:4], scale=ropecs[:, 2:3]),
                     r=["tkf", "ropecs"], w=[("tab", i, 0)])

            def stage_x1(G):
                i = G % 2
                xt = XT
                for k in range(8):
                    half = k % 2
                    pt = pst_banks[half]
                    for s_ in range(4):
                        S.op("pe", lambda e, pt=pt, s_=s_, k=k: e.transpose(
                            pt[:, s_ * 128:(s_ + 1) * 128], XS[i][:, s_, k * 128:(k + 1) * 128], identb[:]),
                            r=[("XS", i), "identb"], w=[("pst", half)])
                    if k % 2 == 0:
                        S.op("act", lambda e, pt=pt, k=k: e.copy(xt[:, k, :], pt),
                             r=[("pst", half)], w=[("XT", k)])
                    else:
                        S.op("dve", lambda e, pt=pt, k=k: e.tensor_copy(xt[:, k, :], pt),
                             r=[("pst", half)], w=[("XT", k)])

            def stage_x2(G):
                i = G % 2
                tok = slice(G * 512, (G + 1) * 512)
                lt = slice((G - 8 * p) * 256, (G - 8 * p) * 256 + 256)
                xt = XT
                bc, pc, kc = next_bank()
                for k in range(8):
                    S.op("pe", lambda e, pc=pc, k=k: e.matmul(pc, WA[:, k, 256:384], xt[:, k, :],
                                                              start=(k == 0), stop=(k == 7)),
                         r=["WA", ("XT", k)], w=[kc])
                br, pr, kr = next_bank()
                for k in range(8):
                    S.op("pe", lambda e, pr=pr, k=k: e.matmul(pr[64:96, :], WA[:, k, 384:416], xt[:, k, :],
                                                              start=(k == 0), stop=(k == 7), tile_position=(0, 64)),
                         r=["WA", ("XT", k)], w=[kr])
                bs, psw, ks = next_bank()
                for k in range(8):
                    S.op("pe", lambda e, psw=psw, k=k: e.matmul(psw[64:96, :], WK[:, k, :], xt[:, k, :],
                                                                start=(k == 0), stop=(k == 7), tile_position=(0, 64)),
                         r=["WK", ("XT", k)], w=[ks])
                S.op("act", lambda e, pc=pc: e.copy(ckvn[:, tok], pc), r=[kc], w=[("ckvn", G)])
                S.op("act", lambda e, pc=pc: e.activation(sq[0], pc, AF.Square), r=[kc], w=[("sq", 0)])
                S.op("act", lambda e, pr=pr: e.copy(KR[i][0][64:96, :], pr[64:96, :]), r=[kr], w=[("KR", i, 0)])
                S.op("act", lambda e, psw=psw: e.copy(KR[i][1][64:96, :], psw[64:96, :]), r=[ks], w=[("KR", i, 1)])
                bcq, pcq, kcq = next_bank()
                for c in range(2):
                    for k in range(8):
                        S.op("pe", lambda e, pcq=pcq, c=c, k=k: e.matmul(
                            pcq[:, c * 256:(c + 1) * 256], WA[:, k, c * 128:(c + 1) * 128], xt[:, k, 0:256],
                            start=(k == 0), stop=(k == 7)),
                            r=["WA", ("XT", k)], w=[kcq])
                S.op("act", lambda e, pcq=pcq: e.copy(CQB[i], pcq), r=[kcq], w=[("cqb", i)])
                S.op("act", lambda e, pcq=pcq: e.activation(sq[1], pcq, AF.Square), r=[kcq], w=[("sq", 1)])
                for cc in range(2):
                    bz, pz, kz = next_bank()
                    for c2 in range(2):
                        c = cc * 2 + c2
                        for k in range(8):
                            S.op("pe", lambda e, pz=pz, c=c, c2=c2, k=k: e.matmul(
                                pz[:, c2 * 256:(c2 + 1) * 256], WA[:, k, 416 + c * 128:416 + (c + 1) * 128],
                                xt[:, k, 0:256], start=(k == 0), stop=(k == 7)),
                                r=["WA", ("XT", k)], w=[kz])
                    S.op("act", lambda e, pz=pz, cc=cc: e.activation(
                        MG[:, 2 * cc:2 * cc + 2, lt], pz.rearrange("p (c t) -> p c t", c=2), AF.Silu),
                        r=[kz], w=[("MG", cc, G)])
                bq, pq, kq = next_bank()
                for j in range(4):
                    S.op("pe", lambda e, pq=pq, j=j: e.matmul(pq[:, j:j + 1], sq[0][:, j * 128:(j + 1) * 128], onesf[:, 0:1],
                                                              start=True, stop=True, skip_group_check=True),
                         r=[("sq", 0), "onesf"], w=[kq])
                for j in range(2):
                    for c in range(2):
                        S.op("pe", lambda e, pq=pq, c=c, j=j: e.matmul(
                            pq[:, 4 + j:5 + j], sq[1][:, c * 256 + j * 128:c * 256 + (j + 1) * 128], onesf[:, 0:1],
                            start=(c == 0), stop=(c == 1), skip_group_check=True),
                            r=[("sq", 1), "onesf"], w=[kq])
                ro = 8 * i
                S.op("act", lambda e, pq=pq: e.activation(rt[:, ro:ro + 4], pq[:, 0:4], AF.Identity, bias=epst[:, 0:1],
                                                          scale=1.0 / 128.0), r=[kq, "epst"], w=[("rt", i)])
                S.op("act", lambda e, pq=pq: e.activation(rt[:, ro + 4:ro + 6], pq[:, 4:6], AF.Identity, bias=epst[:, 0:1],
                                                          scale=1.0 / 256.0), r=[kq, "epst"], w=[("rt", i)])
                S.op("pool", lambda e: e.tensor_tensor(rt[:, ro:ro + 6], rt[:, ro:ro + 6],
                                                       neghalf[:, 0:1].to_broadcast([128, 6]), op=ALU.pow),
                     r=[("rt", i), "neghalf"], w=[("rt", i)])

            def stage_yn(G):
                i = G % 2
                tok = slice(G * 512, (G + 1) * 512)
                lt = slice((G - 8 * p) * 256, (G - 8 * p) * 256 + 256)
                tabs = TABS[i]
                tabk = [("tab", i, 0), ("tab", i, 1)]
                ro = 8 * i
                cqb = CQB[i]
                for j in range(4):
                    S.op("dve", lambda e, j=j: e.tensor_scalar(dg[:, j * 128:(j + 1) * 128], identf[:], rt[:, ro + j:ro + j + 1],
                                                               None, op0=ALU.mult), r=[("rt", i), "identf"], w=["dg"])
                for j in range(2):
                    S.op("dve", lambda e, j=j: e.tensor_scalar(dg2[:, j * 128:(j + 1) * 128], identf[:],
                                                               rt[:, ro + 4 + j:ro + 5 + j], None, op0=ALU.mult),
                         r=[("rt", i), "identf"], w=["dg2"])
                bb, pbc, kbc = next_bank()
                for j in range(4):
                    S.op("pe", lambda e, pbc=pbc, j=j: e.matmul(pbc[:, j * 128:(j + 1) * 128], onesf[:],
                                                                dg[:, j * 128:(j + 1) * 128], start=True, stop=True,
                                                                skip_group_check=True),
                         r=["dg", "onesf"], w=[kbc])
                bb2, pbc2, kbc2 = next_bank()
                for j in range(2):
                    S.op("pe", lambda e, pbc2=pbc2, j=j: e.matmul(pbc2[:, j * 128:(j + 1) * 128], onesf[:],
                                                                  dg2[:, j * 128:(j + 1) * 128], start=True, stop=True,
                                                                  skip_group_check=True),
                         r=["dg2", "onesf"], w=[kbc2])
                S.op("act", lambda e, pbc=pbc: e.copy(rstd, pbc), r=[kbc], w=["rstd"])
                S.op("act", lambda e, pbc2=pbc2: e.copy(rstdq[:, 0:256], pbc2[:, 0:256]), r=[kbc2], w=["rstdq"])
                S.op("pool", lambda e: e.tensor_tensor(ckvn[:, tok], ckvn[:, tok], rstd, op=ALU.mult),
                     r=[("ckvn", G), "rstd"], w=[("ckvn", G)])
                kr0, kr1 = KR[i]
                S.op("dve", lambda e: e.tensor_tensor(kr0[64:96, :], kr0[64:96, :], tabs[0][64:96, :], op=ALU.mult),
                     r=[("KR", i, 0), tabk[0]], w=[("KR", i, 0)])
                S.op("dve", lambda e: e.tensor_tensor(kr1[64:96, :], kr1[64:96, :], tabs[1][64:96, :], op=ALU.mult),
                     r=[("KR", i, 1), tabk[1]], w=[("KR", i, 1)])
                S.op("pool", lambda e: e.tensor_tensor(KB[0][64:96, tok], kr0[64:96, :], kr1[64:96, :], op=ALU.add),
                     r=[("KR", i, 0), ("KR", i, 1)], w=[("kbr", 0, G)])
                S.op("pool", lambda e: e.tensor_copy(KB[1][64:96, tok], KB[0][64:96, tok]),
                     r=[("kbr", 0, G)], w=[("kbr", 1, G)])
                for ti in range(2):
                    S.op("dve", lambda e, ti=ti: e.tensor_tensor(crsr[0:96, ti * 256:(ti + 1) * 256],
                                                                 tabs[ti][0:96, 0:256], rstdq[0:96, 0:256], op=ALU.mult),
                         r=[tabk[ti], "rstdq"], w=["crsr"])

            def stage_yq(G):
                i = G % 2
                lt = slice((G - 8 * p) * 256, (G - 8 * p) * 256 + 256)
                tok = slice(G * 512, (G + 1) * 512)
                cqb = CQB[i]
                for h in range(8):
                    bh, ph_, kh = next_bank()
                    for half, wsrc, wk in ((0, wuq_g, "wuq"), (1, wuqs_g, "wuqs")):
                        for c in range(2):
                            S.op("pe", lambda e, ph_=ph_, half=half, wsrc=wsrc, c=c, h=h: e.matmul(
                                ph_[0:96, half * 256:(half + 1) * 256], wsrc[:, c, h * 96:(h + 1) * 96],
                                cqb[:, c * 256:(c + 1) * 256], start=(c == 0), stop=(c == 1)),
                                r=[(wk, c), ("cqb", i)], w=[kh])
                    qt_ = qtmp[h % 2]
                    S.op("dve", lambda e, ph_=ph_, qt_=qt_: e.tensor_tensor(qt_[0:96, :], ph_[0:96, :], crsr[0:96, :],
                                                                            op=ALU.mult),
                         r=[kh, "crsr"], w=[("qtmp", h % 2)])
                    S.op("pool", lambda e, qt_=qt_, h=h: e.tensor_tensor(
                        QT[:, h, lt], qt_[0:96, 0:256], qt_[0:96, 256:512], op=ALU.add),
                        r=[("qtmp", h % 2)], w=[("QT", h, G)])
                if debug and G == 8 * p + 1 and p == 0:
                    dd = qtmp[0]
                    kk = ("qtmp", 0)
                    S.op("dve", lambda e: e.tensor_copy(dd, ckvn[:, tok]), r=[("ckvn", G), kk], w=[kk])
                    S.dma("sp", lambda e: e.dma_start(out=dbg["d_ckvn"], in_=dd), dsD, r=[kk])
                    S.op("dve", lambda e: e.tensor_copy(dd[64:96, :], KB[1][64:96, tok]),
                         r=[("kbr", 1, G), kk], w=[kk])
                    S.dma("sp", lambda e: e.dma_start(out=dbg["d_krope"], in_=dd[64:96, :]), dsD, r=[kk])
                    S.op("dve", lambda e: e.tensor_copy(dd[0:96, 0:256], QT[:, 0, lt]), r=[("QT", 0, G), kk], w=[kk])
                    S.op("dve", lambda e: e.tensor_copy(dd[0:96, 256:512], QT[:, 5, lt]), r=[("QT", 5, G), kk], w=[kk])
                    S.dma("sp", lambda e: e.dma_start(out=dbg["d_q0"], in_=dd[0:96, :]), dsD, r=[kk])
                    S.op("dve", lambda e: e.tensor_copy(dd[:, 0:256], MG[:, 0, lt]), r=[("MG", 0, G), kk], w=[kk])
                    S.op("dve", lambda e: e.tensor_copy(dd[:, 256:512], MG[:, 3, lt]), r=[("MG", 1, G), kk], w=[kk])
                    S.dma("sp", lambda e: e.dma_start(out=dbg["d_g"], in_=dd), dsD, r=[kk])

            ng = len(groups)
            stage_x1(groups[0])
            make_tables(groups[0])
            stage_x2(groups[0])
            if ng > 2:
                load_group(groups[2])
            for gi, G in enumerate(groups):
                nxt = groups[gi + 1] if gi + 1 < ng else None
                if nxt is not None:
                    stage_x1(nxt)
                stage_yn(G)
                if nxt is not None:
                    make_tables(nxt)
                    stage_x2(nxt)
                stage_yq(G)
                if gi + 3 < ng:
                    load_group(groups[gi + 3])

        def phase_B(p):
            cv = Carver()
            nslots = 32 * (p + 1)
            VB = [cv.take(64 * 65, BF16).rearrange("p (s d) -> p s d", s=64),
                  cv.take(64 * 128, BF16).rearrange("p (s d) -> p s d", s=64)]
            PT = [cv.take(1024, BF16).rearrange("p (j n) -> p j n", j=2) for _ in range(3)]
            osb = [cv.take(512, F32) for _ in range(2)]
            rec = [cv.take(512, F32) for _ in range(2)]
            otmp = [cv.take(512, F32) for _ in range(2)]
            dsD = dsem()
            S.op("pool", lambda e: e.memset(VB[0][:, :, 64:65], 1.0), w=[("vbc", 0)])
            S.op("pool", lambda e: e.memset(VB[1][:, :, 0:64], 0.0), w=[("vbc", 1)])
            S.op("pool", lambda e: e.memset(VB[1][:, :, 0:1], 1.0), w=[("vbc", 1)])

            def vaug(b, h, slot):
                return VB[b][:, slot, :]

            def vdst(b, c):
                if b == 0:
                    return VB[0][:, c * 8:(c + 1) * 8, 0:64]
                return VB[1][:, c * 8:(c + 1) * 8, 64:128]

            def gen_chunks(h):
                b = h % 2
                out = []
                for c in range(nslots // 4):
                    def kchunk(c=c):
                        tok = slice(c * 512, (c + 1) * 512)
                        S.op("pe", lambda e: e.matmul(PSG[0:64, :], wukv_g[:, h * 128:h * 128 + 64], ckvn[:, tok],
                                                      start=True, stop=True), r=["wukv"], w=["psg"])
                        S.op("dve", lambda e: e.tensor_copy(KB[b][0:64, tok], PSG[0:64, :]),
                             r=["psg"], w=[("kb", b, c)])
                    out.append(kchunk)
                for c in range(nslots // 8):
                    def vchunk(c=c):
                        for s in range(8):
                            slot = c * 8 + s
                            S.op("pe", lambda e, s=s, slot=slot: e.matmul(
                                PSG[:, s * 64:(s + 1) * 64], ckvn[:, slot * 128:(slot + 1) * 128],
                                wukv_g[:, h * 128 + 64:h * 128 + 128], start=True, stop=True),
                                r=["wukv"], w=["psg"])
                        S.op("dve", lambda e: e.tensor_copy(vdst(b, c), PSG.rearrange("p (s d) -> p s d", s=8)),
                             r=["psg"], w=[("vb", b, c)])
                    out.append(vchunk)
                return out

            steps = []
            for h in range(8):
                for g in range(4):
                    gg = 4 * p + g
                    nfull = 4 * gg
                    for i in range(nfull):
                        steps.append(dict(h=h, g=g, slots=(2 * i, 2 * i + 1), c0=0, masks=None))
                    base = 8 * gg
                    tails = [
                        ((base + 0, base + 1), 0, ((0, 128, 0), (256, 128, 128))),
                        ((base + 2, base + 3), 0, ((384, 128, 0), (640, 128, 128))),
                        ((base + 4, base + 5), 256, ((0, 128, 256), (256, 128, 384))),
                        ((base + 6, base + 7), 256, ((384, 128, 256), (640, 128, 384))),
                    ]
                    for slots, c0, mk in tails:
                        steps.append(dict(h=h, g=g, slots=slots, c0=c0, masks=mk))
                    steps[-1]["last"] = True
                    steps[-(nfull + 4)]["first"] = True
            for fn in gen_chunks(0):
                fn()
            pending_gen = []
            pending_epi = []
            pending_rcp = []
            n_steps = len(steps)
            if KB_STEPS >= 0:
                n_steps = KB_STEPS
            ocount = [0]

            def emit_qk(si):
                stp = steps[si]
                h, g, c0 = stp["h"], stp["g"], stp["c0"]
                b = h % 2
                wtile = PSW[si % 3]
                for j, slot in enumerate(stp["slots"]):
                    has_mask = stp["masks"] is not None
                    cj = stp["masks"][j][2] if has_mask else c0
                    S.op("pe", lambda e, j=j, slot=slot, has_mask=has_mask, cj=cj: e.matmul(
                        wtile[:, j, cj:512], KB[b][:, slot * 128:(slot + 1) * 128],
                        QT[:, h, g * 512 + cj:(g + 1) * 512], start=True, stop=(not has_mask or bool(KB_NOMASK)),
                        skip_group_check=True),
                        r=[("kb", b, slot // 4)], w=[("psw", si % 3)])
                    if has_mask and not KB_NOMASK:
                        mo, mw, mc = stp["masks"][j]
                        S.op("pe", lambda e, j=j, mo=mo, mw=mw, mc=mc: e.matmul(
                            wtile[:, j, mc:mc + mw], identb[:], maskb[:, mo:mo + mw], start=False, stop=True,
                            skip_group_check=True),
                            r=[], w=[("psw", si % 3)])

            def emit_exp(si):
                stp = steps[si]
                c0 = stp["c0"]
                S.op("act", lambda e: e.activation(PT[si % 3][:, :, c0:512], PSW[si % 3][:, :, c0:512], AF.Exp,
                                                   scale=SCALE),
                     r=[("psw", si % 3)], w=[("pt", si % 3)])

            def emit_pv(si):
                stp = steps[si]
                h, g, c0 = stp["h"], stp["g"], stp["c0"]
                b = h % 2
                if stp.get("first"):
                    ocount[0] += 1
                o = ocount[0] % 2
                for j, slot in enumerate(stp["slots"]):
                    first = bool(stp.get("first")) and j == 0
                    last = bool(stp.get("last")) and j == 1
                    mrows = 65 if b == 0 else 128
                    cj = stp["masks"][j][2] if stp["masks"] is not None else c0
                    S.op("pe", lambda e, j=j, slot=slot, first=first, last=last, mrows=mrows, cj=cj: e.matmul(
                        PSO1[0:mrows, cj:512], vaug(b, h, slot), PT[si % 3][:, j, cj:512], start=first, stop=last,
                        skip_group_check=True),
                        r=[("pt", si % 3), ("vb", b, slot // 8), ("vbc", b)], w=["pso"])
                if stp.get("last"):
                    mrows = 65 if b == 0 else 128
                    S.op("dve", lambda e, mrows=mrows: e.tensor_copy(osb[o][0:mrows, :], PSO1[0:mrows, :]),
                         r=["pso"], w=[("osb", o)])
                    dr = 64 if h % 2 == 0 else 0
                    for q4 in range(4):
                        pending_rcp.append((si + q4, o, dr, q4))
                    pending_epi.append((si + 6, h, g, o))

            def emit_epilogue(h, g, o):
                if KB_NOEPI:
                    return
                dr = 64 if h % 2 == 0 else 0
                rows = slice(0, 64) if h % 2 == 0 else slice(64, 128)
                r_ = rec[o]
                ot = otmp[o]
                ob_ = osb[o]
                S.op("pe", lambda e: e.matmul(PSG, onesf[dr:dr + 1, :], r_[dr:dr + 1, :], start=True, stop=True),
                     r=[("rec", o)], w=["psg"])
                S.op("dve", lambda e: e.tensor_tensor(ot[rows, :], ob_[rows, :], PSG[rows, :], op=ALU.mult),
                     r=[("osb", o), "psg"], w=[("otmp", o)])
                mg = MG[rows, h // 2, g * 512:(g + 1) * 512]
                S.op("pool", lambda e: e.tensor_tensor(mg, ot[rows, :], mg, op=ALU.mult),
                     r=[("otmp", o)], w=[("mgo", h, g)])

            if n_steps > 0:
                emit_qk(0)
            if n_steps > 1:
                emit_qk(1)
            cur_h = 0
            gen_every = 3 if p == 0 else 4
            since_gen = 0
            for si in range(n_steps):
                emit_exp(si)
                if si + 2 < n_steps:
                    emit_qk(si + 2)
                if not KB_NOPV:
                    emit_pv(si)
                while pending_rcp and pending_rcp[0][0] <= si:
                    _, ro_, rdr, q4 = pending_rcp.pop(0)
                    cs = slice(q4 * 128, (q4 + 1) * 128)
                    S.op("dve", lambda e, ro_=ro_, rdr=rdr, cs=cs: e.reciprocal(rec[ro_][rdr:rdr + 1, cs],
                                                                               osb[ro_][rdr:rdr + 1, cs]),
                         r=[("osb", ro_)], w=[("rec", ro_)])
                while pending_epi and pending_epi[0][0] <= si:
                    _, eh, eg, eo = pending_epi.pop(0)
                    emit_epilogue(eh, eg, eo)
                h = steps[si]["h"]
                if steps[si].get("first") and steps[si]["g"] == 0 and h + 1 < 8:
                    pending_gen = gen_chunks(h + 1)
                    since_gen = 0
                since_gen += 1
                if pending_gen and since_gen >= gen_every:
                    pending_gen.pop(0)()
                    since_gen = 0
                if steps[si].get("last") and steps[si]["g"] == 3:
                    while pending_gen:
                        pending_gen.pop(0)()
            while pending_rcp:
                _, ro_, rdr, q4 = pending_rcp.pop(0)
                cs = slice(q4 * 128, (q4 + 1) * 128)
                S.op("dve", lambda e, ro_=ro_, rdr=rdr, cs=cs: e.reciprocal(rec[ro_][rdr:rdr + 1, cs],
                                                                           osb[ro_][rdr:rdr + 1, cs]),
                     r=[("osb", ro_)], w=[("rec", ro_)])
            while pending_epi:
                _, eh, eg, eo = pending_epi.pop(0)
                emit_epilogue(eh, eg, eo)
            if debug and p == 0:
                dd = cv.take(512, F32)
                S.op("dve", lambda e: e.tensor_copy(dd[0:96, :], KB[1][:, 512:1024]), r=[("kb", 1, 1)], w=["dd"])
                S.dma("sp", lambda e: e.dma_start(out=dbg["d_kb"], in_=dd[0:96, :]), dsD, r=["dd"])
                S.op("dve", lambda e: e.tensor_copy(dd.rearrange("p (s d) -> p s d", s=8), VB[1][:, 8:16, 64:128]),
                     r=[("vb", 1, 1), "dd"], w=["dd"])
                S.dma("sp", lambda e: e.dma_start(out=dbg["d_vb"], in_=dd), dsD, r=["dd"])
                S.op("dve", lambda e: e.tensor_copy(dd[:, 0:256], MG[:, 0, 256:512]),
                     r=[("mgo", 0, 0), ("mgo", 1, 0), "dd"], w=["dd"])
                S.op("dve", lambda e: e.tensor_copy(dd[:, 256:512], MG[:, 3, 1024 + 256:1024 + 512]),
                     r=[("mgo", 6, 2), ("mgo", 7, 2), "dd"], w=["dd"])
                S.dma("sp", lambda e: e.dma_start(out=dbg["d_merged"], in_=dd), dsD, r=["dd"])

        WC_bytes = 8 * 1536 * 2 + 8 * 1024 * 2

        def wc_views():
            base = (ARENA - WC_bytes) // 2
            wci = arena[:, base:base + 8 * 1536].rearrange("p (c n) -> p c n", c=8)
            wco = arena[:, base + 8 * 1536:base + 8 * 1536 + 8 * 1024].rearrange("p (c n) -> p c n", c=8)
            return wci, wco

        dsWC = dsem()

        def load_WC():
            wci, wco = wc_views()
            S.dma("sp", lambda e: e.dma_start(out=wci, in_=w_in_bf.rearrange("(c p) n -> p c n", p=128)[:, :, NA:DIN]),
                  dsWC, w=["WCI"])
            S.dma("sp", lambda e: e.dma_start(out=wco, in_=w_out_bf.rearrange("(c p) n -> p c n", p=128)),
                  dsWC, w=["WCO"])

        dsY = [dsem(), dsem()]

        def phase_C(p):
            cv = Carver()
            wci, wco = wc_views()
            XR = [cv.take(D, F32) for _ in range(2)]
            XB = [cv.take(D, BF16) for _ in range(2)]
            XTC = [cv.take(8 * 128, BF16).rearrange("p (c t) -> p c t", c=8) for _ in range(2)]
            UG = [cv.take(512, F32) for _ in range(2)]
            VG = [cv.take(512, F32) for _ in range(2)]
            ZB = [cv.take(512, F32) for _ in range(2)]
            VN = [cv.take(512, BF16) for _ in range(2)]
            ob = cv.take(512, BF16)
            obT = cv.take(512, BF16)
            hh = cv.take(D, F32)
            YO = [cv.take(D, F32) for _ in range(2)]
            st1 = [cv.take(16, F32) for _ in range(2)]
            st2 = cv.take(24, F32)
            assert cv.off <= ARENA - WC_bytes, ("phase C arena", cv.off)
            dsXR = [dsem(), dsem()]
            dsXB = [dsem(), dsem()]

            def load_block(t):
                i = t % 2
                gslot = 4 * (t // 2) + (t % 2)
                rows = slice(gslot * 128, (gslot + 1) * 128)
                S.dma("sp", lambda e: e.dma_start(out=XR[i], in_=xs[rows, :]), dsXR[i], w=[("XR", i)])
                S.dma("pool", lambda e: e.dma_start(out=XB[i], in_=xs[rows, :]), dsXB[i], w=[("XB", i)])

            def s1a(t):
                i = t % 2
                xtc = XTC[i]
                for hf in range(2):
                    pt = pst_banks[hf]
                    for kk in range(4):
                        k = hf * 4 + kk
                        S.op("pe", lambda e, pt=pt, kk=kk, k=k: e.transpose(
                            pt[:, kk * 128:(kk + 1) * 128], XB[i][:, k * 128:(k + 1) * 128], identb[:]),
                            r=[("XB", i), "identb"], w=[("pst", hf)])
                    dst = xtc[:, hf * 4:hf * 4 + 4, :]
                    if hf == 0:
                        S.op("act", lambda e, pt=pt, dst=dst: e.copy(dst, pt.rearrange("p (c t) -> p c t", c=4)),
                             r=[("pst", hf)], w=[("XTC", i, hf)])
                    else:
                        S.op("dve", lambda e, pt=pt, dst=dst: e.tensor_copy(dst, pt.rearrange("p (c t) -> p c t", c=4)),
                             r=[("pst", hf)], w=[("XTC", i, hf)])

            def s1b_mm(t):
                i = t % 2
                xtc = XTC[i]
                xk = [("XTC", i, 0), ("XTC", i, 1)]
                ug, vg, zb = UG[i], VG[i], ZB[i]
                th = hh[:, 0:512]
                pss = {}
                for j in (1, 0, 2):
                    bb, pb, kb_ = next_bank()
                    pss[j] = (pb, kb_)
                    for k in range(8):
                        S.op("pe", lambda e, pb=pb, j=j, k=k: e.matmul(
                            pb, xtc[:, k, :], wci[:, k, j * 512:(j + 1) * 512], start=(k == 0), stop=(k == 7)),
                            r=xk + ["WCI"], w=[kb_])
                S.op("act", lambda e: e.activation(vg, pss[1][0], AF.Gelu), r=[pss[1][1]], w=[("vg", i)])
                S.op("act", lambda e: e.activation(ug, pss[0][0], AF.Gelu), r=[pss[0][1]], w=[("ug", i)])
                S.op("act", lambda e: e.activation(th, pss[2][0], AF.Tanh, scale=0.5), r=[pss[2][1]], w=[("hh", 0)])
                S.op("act", lambda e: e.copy(zb, pss[2][0]), r=[pss[2][1]], w=[("zb", i)])
                S.op("pool", lambda e: e.tensor_tensor(th, th, zb, op=ALU.mult), r=[("hh", 0), ("zb", i)], w=[("hh", 0)])
                S.op("pool", lambda e: e.tensor_tensor(zb, th, zb, op=ALU.add), r=[("hh", 0), ("zb", i)], w=[("zb", i)])
                S.op("pool", lambda e: e.tensor_tensor(ug, ug, zb, op=ALU.mult), r=[("ug", i), ("zb", i)], w=[("ug", i)])

            def s1b_ln(t):
                i = t % 2
                vg, vn, st = VG[i], VN[i], st1[i]
                S.op("dve", lambda e: e.bn_stats(st[:, 0:6], vg), r=[("vg", i)], w=[("st1a", i)])
                S.op("dve", lambda e: e.bn_aggr(st[:, 6:8], st[:, 0:6]), r=[("st1a", i)], w=[("st1b", i)])
                S.op("dve", lambda e: e.tensor_scalar(st[:, 8:9], st[:, 7:8], EPS, None, op0=ALU.add),
                     r=[("st1b", i)], w=[("st1c", i)])
                S.op("pool", lambda e: e.tensor_tensor(st[:, 8:9], st[:, 8:9], neghalf[:, 0:1], op=ALU.pow),
                     r=[("st1c", i), "neghalf"], w=[("st1c", i)])
                S.op("dve", lambda e: e.scalar_tensor_tensor(out=vg, in0=vg, scalar=st[:, 6:7], in1=sgug[:],
                                                             op0=ALU.subtract, op1=ALU.mult),
                     r=[("vg", i), ("st1b", i), "sgug"], w=[("vg", i)])
                S.op("dve", lambda e: e.scalar_tensor_tensor(out=vn, in0=vg, scalar=st[:, 8:9], in1=sgub[:],
                                                             op0=ALU.mult, op1=ALU.add),
                     r=[("vg", i), ("st1c", i), "sgub"], w=[("vn", i)])

            def s2a(t):
                i = t % 2
                ug, vn = UG[i], VN[i]
                bsv, psv, ksv = next_bank()
                S.op("pe", lambda e: e.matmul(psv, bspb[:], indb[:], start=True, stop=False, skip_group_check=True),
                     r=["bspb", "indb"], w=[ksv])
                for h in range(8):
                    S.op("pe", lambda e, h=h: e.matmul(psv[:, h * 64:(h + 1) * 64], wsp[:, h, :],
                                                      vn[:, h * 64:(h + 1) * 64], start=False, stop=(h == 7),
                                                      skip_group_check=True),
                         r=[("wsp", h), ("vn", i)], w=[ksv])
                S.op("dve", lambda e: e.scalar_tensor_tensor(out=ob, in0=psv, scalar=0.5, in1=ug, op0=ALU.mult, op1=ALU.mult),
                     r=[ksv, ("ug", i)], w=["ob"])

            def s2b(t):
                pt = pst_banks[0]
                for c in range(4):
                    S.op("pe", lambda e, c=c: e.transpose(pt[:, c * 128:(c + 1) * 128],
                                                         ob[:, c * 128:(c + 1) * 128], identb[:]),
                         r=["ob", "identb"], w=[("pst", 0)])
                S.op("act", lambda e: e.copy(obT, pt), r=[("pst", 0)], w=["obT"])

            def s2c(t):
                i = t % 2
                lt = slice((t - 16 * p) * 128, (t - 16 * p + 1) * 128)
                outs = []
                for nh in range(2):
                    bo, po, ko = next_bank()
                    outs.append((po, ko))
                    for c in range(8):
                        if c < 4:
                            lhsT = MG[:, c, lt]
                            rr = []
                        else:
                            lhsT = obT[:, (c - 4) * 128:(c - 3) * 128]
                            rr = ["obT"]
                        S.op("pe", lambda e, po=po, lhsT=lhsT, c=c, nh=nh: e.matmul(
                            po, lhsT, wco[:, c, nh * 512:(nh + 1) * 512], start=(c == 0), stop=(c == 7)),
                            r=rr + ["WCO"], w=[ko])
                for nh in range(2):
                    po, ko = outs[nh]
                    S.op("dve", lambda e, po=po, nh=nh: e.scalar_tensor_tensor(
                        out=hh[:, nh * 512:(nh + 1) * 512], in0=XR[i][:, nh * 512:(nh + 1) * 512], scalar=ALPHA,
                        in1=po, op0=ALU.mult, op1=ALU.add),
                        r=[ko, ("XR", i)], w=[("hh", nh)])
                    S.op("dve", lambda e, nh=nh: e.bn_stats(st2[:, 6 * nh:6 * nh + 6], hh[:, nh * 512:(nh + 1) * 512]),
                         r=[("hh", nh)], w=[("st2a", nh)])
                S.op("dve", lambda e: e.bn_aggr(st2[:, 12:14], st2[:, 0:12]), r=[("st2a", 0), ("st2a", 1)], w=["st2b"])
                S.op("dve", lambda e: e.tensor_scalar(st2[:, 14:15], st2[:, 13:14], EPS, None, op0=ALU.add),
                     r=["st2b"], w=["st2c"])
                S.op("pool", lambda e: e.tensor_tensor(st2[:, 14:15], st2[:, 14:15], neghalf[:, 0:1], op=ALU.pow),
                     r=["st2c", "neghalf"], w=["st2c"])
                S.op("dve", lambda e: e.scalar_tensor_tensor(out=hh, in0=hh, scalar=st2[:, 12:13], in1=lng[:],
                                                             op0=ALU.subtract, op1=ALU.mult),
                     r=[("hh", 0), ("hh", 1), "st2b", "lng"], w=[("hh", 0), ("hh", 1)])
                yo = YO[i]
                S.op("dve", lambda e: e.scalar_tensor_tensor(out=yo, in0=hh, scalar=st2[:, 14:15], in1=lnb[:],
                                                             op0=ALU.mult, op1=ALU.add),
                     r=[("hh", 0), ("hh", 1), "st2c", "lnb"], w=[("YO", i)])
                S.dma("sp", lambda e: e.dma_start(out=y[t * 128:(t + 1) * 128, :], in_=yo),
                      dsY[i], r=[("YO", i)])

            def load_xb(t):
                i = t % 2
                gslot = 4 * (t // 2) + (t % 2)
                rows = slice(gslot * 128, (gslot + 1) * 128)
                S.dma("pool", lambda e: e.dma_start(out=XB[i], in_=xs[rows, :]), dsXB[i], w=[("XB", i)])

            def load_xr(t):
                i = t % 2
                gslot = 4 * (t // 2) + (t % 2)
                rows = slice(gslot * 128, (gslot + 1) * 128)
                S.dma("sp", lambda e: e.dma_start(out=XR[i], in_=xs[rows, :]), dsXR[i], w=[("XR", i)])

            blocks = list(range(16 * p, 16 * p + 16))
            nb = len(blocks)
            load_xb(blocks[0])
            load_xb(blocks[1])
            load_xr(blocks[0])
            load_xr(blocks[1])
            s1a(blocks[0])
            s1b_mm(blocks[0])
            s1b_ln(blocks[0])
            s1a(blocks[1])
            load_xb(blocks[2])
            for bi, t in enumerate(blocks):
                s2a(t)
                if bi + 1 < nb:
                    s1b_mm(blocks[bi + 1])
                if bi + 2 < nb:
                    s1a(blocks[bi + 2])
                if bi + 3 < nb:
                    load_xb(blocks[bi + 3])
                s2b(t)
                if bi + 1 < nb:
                    s1b_ln(blocks[bi + 1])
                s2c(t)
                if bi + 2 < nb:
                    load_xr(blocks[bi + 2])

        for p in passes:
            if "A" in phases:
                phase_A(p)
                S.barrier()
            if "B" in phases:
                if "C" in phases and WC_EARLY:
                    load_WC()
                phase_B(p)
                S.barrier()
            if "C" in phases:
                if "B" not in phases or not WC_EARLY:
                    load_WC()
                phase_C(p)
                S.barrier()
        S.emit(nc, block, esems, dsY)
    return nc


def _slot_perm(par):
    order = []
    for m in range(16):
        if par == 0:
            order += [4 * m, 4 * m + 3, 4 * m + 1, 4 * m + 2]
        else:
            order += [4 * m + 1, 4 * m + 2, 4 * m, 4 * m + 3]
    return order


def make_in_maps(x, positions, w_in, q_norm_g, w_uq, kv_norm_g, w_ukv, sgu_norm_g, sgu_norm_b,
                 w_spatial, b_spatial, w_out, ln_g, ln_b):
    f = np.float32
    w_in = np.ascontiguousarray(w_in, dtype=f)
    swap = np.concatenate([np.arange(16, 32), np.arange(0, 16)])
    w_krs = np.ascontiguousarray(w_in[:, 384:416][:, swap])
    w_uq = np.ascontiguousarray(w_uq, dtype=f)
    w_uqs = np.zeros_like(w_uq)
    for h in range(8):
        w_uqs[:, h * 96 + 64:h * 96 + 96] = w_uq[:, h * 96 + 64:h * 96 + 96][:, swap]
    qg = np.ascontiguousarray(np.asarray(q_norm_g, dtype=f).reshape(2, 128).T)
    kvg = np.ascontiguousarray(np.asarray(kv_norm_g, dtype=f).reshape(128, 1))
    wspT = np.ascontiguousarray(np.transpose(np.asarray(w_spatial, dtype=f), (2, 0, 1)))
    si = np.arange(128)
    tril = (si[:, None] <= si[None, :]).astype(f)
    bspT = np.ascontiguousarray(np.asarray(b_spatial, dtype=f).T)
    ident = np.eye(128, dtype=f)
    invf = (1.0 / (10000.0 ** (np.arange(16, dtype=np.float64) / 16.0)))
    ropec = np.zeros((128, 6), f)
    ropec[:, 1] = -TWO_PI
    ropec[:, 2] = -TWO_PI
    ropec[:, 3] = 0.5 * math.pi
    for pp in range(64, 96):
        i = (pp - 64) % 16
        ropec[pp, 0] = invf[i] / TWO_PI
        ropec[pp, 1] = TWO_PI if pp < 80 else -TWO_PI
    tri = np.where(si[:, None] > si[None, :], NEG, 0.0).astype(f)
    allm = np.full((128, 128), NEG, f)
    zero = np.zeros((128, 128), f)
    ind_h = np.zeros((8, 512), f)
    for h in range(8):
        ind_h[h, h * 64:(h + 1) * 64] = 1.0
    common = dict(bsp_ht=np.ascontiguousarray(np.asarray(b_spatial, dtype=f)), ind_h=ind_h, w_in=w_in, w_krs=w_krs, w_uq=w_uq, w_uqs=w_uqs, qg=qg, w_ukv=np.ascontiguousarray(w_ukv, dtype=f),
                  kvg=kvg, wspT=wspT, tril=tril, bspT=bspT,
                  sgu_g=np.asarray(sgu_norm_g, dtype=f).reshape(1, 512), sgu_b=np.asarray(sgu_norm_b, dtype=f).reshape(1, 512),
                  ln_g=np.asarray(ln_g, dtype=f).reshape(1, D), ln_b=np.asarray(ln_b, dtype=f).reshape(1, D),
                  w_out=np.ascontiguousarray(w_out, dtype=f), ident=ident, ropec=ropec)
    in_maps = []
    perms = []
    for c in range(8):
        b, par = c // 2, c % 2
        order = _slot_perm(par)
        perms.append(order)
        xb = np.asarray(x[b], dtype=f).reshape(64, 128, D)[order].reshape(SEQ, D)
        pb = np.asarray(positions[b], dtype=np.int32).reshape(64, 128)[order].reshape(1, SEQ)
        mA = allm if par == 0 else zero
        mB = zero if par == 0 else allm
        mk = np.concatenate([tri, allm, tri, mA, allm, mB], axis=1)
        d = dict(common)
        d.update(xs=np.ascontiguousarray(xb), pos=np.ascontiguousarray(pb), masks=np.ascontiguousarray(mk))
        in_maps.append(d)
    return in_maps, perms


_NC_CACHE = {}


def kernel(x, positions, w_in, q_norm_g, w_uq, kv_norm_g, w_ukv, sgu_norm_g, sgu_norm_b,
           w_spatial, b_spatial, w_out, ln_g, ln_b):
    in_maps, perms = make_in_maps(x, positions, w_in, q_norm_g, w_uq, kv_norm_g, w_ukv, sgu_norm_g,
                                  sgu_norm_b, w_spatial, b_spatial, w_out, ln_g, ln_b)
    if "nc" not in _NC_CACHE:
        _NC_CACHE["nc"] = build_program()
    nc = _NC_CACHE["nc"]
    res = run_bass_kernel_spmd(nc, in_maps, core_ids=list(range(8)))
    out = np.empty((4, SEQ, D), np.float32)
    for c in range(8):
        b = c // 2
        order = perms[c]
        yc = np.asarray(res.results[c]["y"], dtype=np.float32).reshape(32, 128, D)
        ov = out[b].reshape(64, 128, D)
        for t in range(32):
            slot = 4 * (t // 2) + (t % 2)
            ov[order[slot]] = yc[t]
    return out
```

```python
import math
from contextlib import ExitStack

import numpy as np
import concourse.bass as bass
import concourse.mybir as mybir
from concourse.bass_utils import run_bass_kernel_spmd

F32 = mybir.dt.float32
BF16 = mybir.dt.bfloat16
I32 = mybir.dt.int32
AF = mybir.ActivationFunctionType
ALU = mybir.AluOpType

D = 1024
SEQ = 8192
NSLOT = 64
DIN = 2464
NA = 928
NEG = -30000.0
EPS = 1e-5
ALPHA = 2.0 ** 0.25
SCALE = 1.0 / math.sqrt(96.0)
TWO_PI = 2.0 * math.pi

ENGS = ("pe", "act", "dve", "pool", "sp")
import os
STOP = int(os.environ.get("KSTOP", "0"))
NGROUPS = int(os.environ.get("KGROUPS", "8"))
KB_STEPS = int(os.environ.get("KB_STEPS", "-1"))
WC_EARLY = int(os.environ.get("WC_EARLY", "1"))
TAB_ENG = os.environ.get("TAB_ENG", "dve")
KB_NOEPI = int(os.environ.get("KB_NOEPI", "0"))
KB_NOPV = int(os.environ.get("KB_NOPV", "0"))
KB_NOMASK = int(os.environ.get("KB_NOMASK", "0"))


class DmaSem:
    def __init__(self, sem):
        self.sem = sem
        self.count = 0


class Op:
    __slots__ = ("eng", "fn", "deps", "dma", "dsem", "dval", "signal", "sigval", "barrier")

    def __init__(self, eng, fn, deps, dma=False, dsem=None, dval=0):
        self.eng = eng
        self.fn = fn
        self.deps = deps
        self.dma = dma
        self.dsem = dsem
        self.dval = dval
        self.signal = False
        self.sigval = 0
        self.barrier = False


class Sched:
    def __init__(self):
        self.ops = []
        self.state = {}
        self.last_on = {e: None for e in ENGS}
        self.open_dma = []

    def _add(self, op):
        self.ops.append(op)
        idx = len(self.ops) - 1
        self.last_on[op.eng] = idx
        return idx

    def _mk(self, idxs):
        d = {}
        for i in idxs:
            p = self.ops[i]
            d[i] = p.dsem.count if p.dma else None
        return d

    def _deps(self, reads, writes):
        deps = set()
        for k in reads:
            st = self.state.get(k)
            if st is not None and st[0] is not None:
                deps.add(st[0])
        for k in writes:
            st = self.state.get(k)
            if st is not None:
                if st[0] is not None:
                    deps.add(st[0])
                deps.update(st[1])
        return self._mk(deps)

    def _commit(self, idx, reads, writes):
        for k in reads:
            st = self.state.setdefault(k, [None, []])
            st[1].append(idx)
        for k in writes:
            self.state[k] = [idx, []]

    def op(self, eng, fn, r=(), w=()):
        deps = self._deps(r, w)
        idx = self._add(Op(eng, fn, deps))
        self._commit(idx, r, w)
        return idx

    def dma(self, queue, fn, dsem, r=(), w=()):
        deps = self._deps(r, w)
        dsem.count += 16
        idx = self._add(Op(queue, fn, deps, dma=True, dsem=dsem, dval=dsem.count))
        self._commit(idx, r, w)
        self.open_dma.append(idx)
        return idx

    def barrier(self):
        targets = set(i for i in self.last_on.values() if i is not None)
        targets.update(self.open_dma)
        for e in ENGS:
            o = Op(e, None, self._mk(targets))
            o.barrier = True
            self.ops.append(o)
        self.open_dma = []
        self.state = {}

    def emit(self, nc, block, esems, final_waits):
        ops = self.ops
        for i, o in enumerate(ops):
            for d in o.deps:
                p = ops[d]
                if p.dma:
                    continue
                if p.eng == o.eng and not o.barrier:
                    if p.eng == "pe" or p.eng == "sp":
                        continue
                p.signal = True
        cnt = {e: 0 for e in ENGS}
        for o in ops:
            if o.signal:
                cnt[o.eng] += 1
                o.sigval = cnt[o.eng]

        def stream(name):
            def body(e):
                seen = {}
                for o in ops:
                    if o.eng != name:
                        continue
                    for d in sorted(o.deps):
                        p = ops[d]
                        if p.dma:
                            sem, val = p.dsem.sem, o.deps[d]
                        else:
                            if p.eng == name and (name == "pe" or name == "sp") :
                                continue
                            if p.eng == name and o.barrier and name == "pe":
                                continue
                            sem, val = esems[p.eng], p.sigval
                        key = id(sem)
                        if seen.get(key, 0) >= val:
                            continue
                        seen[key] = val
                        e.wait_ge(sem, val)
                    if o.fn is None:
                        continue
                    ins = o.fn(e)
                    if o.dma:
                        ins.then_inc(o.dsem.sem, 16)
                    elif o.signal:
                        ins.then_inc(esems[name], 1)
                if name == "sp":
                    for ds in final_waits:
                        e.wait_ge(ds.sem, ds.count)
            return body

        block.sync(stream("sp"))
        block.tensor(stream("pe"))
        block.scalar(stream("act"))
        block.vector(stream("dve"))
        block.gpsimd(stream("pool"))


def build_program(passes=(0, 1), phases="ABC", debug=False):
    nc = bass.Bass("TRN2", target_bir_lowering=False)

    def din(name, shape, dt=F32):
        return nc.dram_tensor(name, shape, dt, kind="ExternalInput").ap()

    xs = din("xs", [SEQ, D])
    pos = din("pos", [1, SEQ], I32)
    w_in = din("w_in", [D, DIN])
    w_krs = din("w_krs", [D, 32])
    w_uq = din("w_uq", [256, 768])
    w_uqs = din("w_uqs", [256, 768])
    qg = din("qg", [128, 2])
    w_ukv = din("w_ukv", [128, 1024])
    kvg = din("kvg", [128, 1])
    wspT = din("wspT", [128, 8, 128])
    tril = din("tril", [128, 128])
    bspT = din("bspT", [128, 8])
    bsp_ht = din("bsp_ht", [8, 128])
    ind_h = din("ind_h", [8, 512])
    sgu_g = din("sgu_g", [1, 512])
    sgu_b = din("sgu_b", [1, 512])
    ln_g = din("ln_g", [1, D])
    ln_b = din("ln_b", [1, D])
    w_out = din("w_out", [D, D])
    ident = din("ident", [128, 128])
    ropec = din("ropec", [128, 6])
    masks = din("masks", [128, 768])
    y = nc.dram_tensor("y", [SEQ // 2, D], F32, kind="ExternalOutput").ap()
    w_in_bf = nc.dram_tensor("w_in_bf", [D, DIN], BF16, kind="Internal").ap()
    w_krs_bf = nc.dram_tensor("w_krs_bf", [D, 32], BF16, kind="Internal").ap()
    w_out_bf = nc.dram_tensor("w_out_bf", [D, D], BF16, kind="Internal").ap()
    dbg = {}
    if debug:
        def dout(name, shape):
            dbg[name] = nc.dram_tensor(name, shape, F32, kind="ExternalOutput").ap()
        dout("d_ckvn", [128, 512])
        dout("d_krope", [32, 512])
        dout("d_q0", [96, 512])
        dout("d_g", [128, 512])
        dout("d_kb", [96, 512])
        dout("d_vb", [128, 512])
        dout("d_merged", [128, 512])

    S = Sched()
    with ExitStack() as es:
        def sb(name, shape, dt):
            return es.enter_context(nc.sbuf_tensor(name, shape, dt))

        def ps(name, shape, dt):
            return es.enter_context(nc.psum_tensor(name, shape, dt))

        def newsem(name):
            return es.enter_context(nc.semaphore(name))

        dsem_n = [0]

        def dsem():
            dsem_n[0] += 1
            return DmaSem(newsem("dq%d" % dsem_n[0]))

        identb = sb("identb", [128, 128], BF16)
        onesf = sb("onesf", [128, 128], F32)
        maskb = sb("maskb", [128, 768], BF16)
        ropecs = sb("ropecs", [128, 6], F32)
        epst = sb("epst", [128, 1], F32)
        neghalf = sb("neghalf", [128, 1], F32)
        identf = sb("identf", [128, 128], F32)
        rt = sb("rt", [128, 16], F32)
        wuq_g = sb("wuq_g", [128, 2, 768], BF16)
        wuqs_g = sb("wuqs_g", [128, 2, 768], BF16)
        wukv_g = sb("wukv_g", [128, 1024], BF16)
        wsp = sb("wsp", [128, 8, 128], BF16)
        bsp = sb("bsp", [128, 8], F32)
        bspb = sb("bspb", [8, 128], BF16)
        indb = sb("indb", [8, 512], BF16)
        sgug = sb("sgug", [128, 512], F32)
        sgub = sb("sgub", [128, 512], F32)
        lng = sb("lng", [128, D], F32)
        lnb = sb("lnb", [128, D], F32)
        ckvn = sb("ckvn", [128, SEQ], BF16)
        KB = [sb("kb%d" % i, [96, SEQ], BF16) for i in range(2)]
        QT = sb("qt", [96, 8, 2048], BF16)
        MG = sb("mg", [128, 4, 2048], BF16)
        ARENA = 86528
        arena = sb("arena", [128, ARENA // 2], BF16)

        class Carver:
            def __init__(self):
                self.off = 0

            def take(self, cols, dt, parts=128):
                size = 4 if dt in (F32, I32) else 2
                self.off = (self.off + 3) // 4 * 4
                nbytes = cols * size
                assert self.off + nbytes <= ARENA, ("arena overflow", self.off, nbytes)
                v = arena[0:parts, self.off // 2:(self.off + nbytes) // 2]
                self.off += nbytes
                if dt != BF16:
                    v = v.bitcast(dt)
                return v

        PT4 = [ps("pt%d" % i, [128, 2, 512], F32) for i in range(4)]
        PSW = [PT4[0], PT4[1], PT4[2]]
        PSO1 = PT4[3][:, 0, :]
        PSG = PT4[3][:, 1, :]
        pst_banks = [PT4[3][:, 0, :].bitcast(BF16)[:, 0:512],
                     PT4[3][:, 1, :].bitcast(BF16)[:, 0:512]]

        def bank(i):
            return PT4[i // 2][:, i % 2, :]

        bank_rr = [0]

        def next_bank():
            b = bank_rr[0] % 6
            bank_rr[0] += 1
            return b, bank(b), ("ps", b)

        esems = {e: newsem("e_" + e) for e in ENGS}
        block = es.enter_context(nc.Block())

        cv = Carver()
        st_a = cv.take(2 * 768, F32)
        st_b = cv.take(2 * 768, F32)
        st_c = cv.take(1024, F32)
        st_d = cv.take(8 * 128, F32)
        st_e = cv.take(128, F32)
        qgs = cv.take(2, F32)
        kvgs = cv.take(1, F32)
        ds0 = dsem()
        ds0p = dsem()
        S.dma("pool", lambda e: e.dma_start(out=identb[:], in_=ident), ds0p, w=["identb"])
        S.dma("pool", lambda e: e.dma_start(out=maskb[:], in_=masks), ds0p, w=["maskb"])
        S.dma("pool", lambda e: e.dma_start(out=bspb[:], in_=bsp_ht), ds0p, w=["bspb"])
        S.dma("pool", lambda e: e.dma_start(out=indb[:], in_=ind_h), ds0p, w=["indb"])
        S.dma("sp", lambda e: e.dma_start(out=ropecs[:], in_=ropec), ds0, w=["ropecs"])
        S.dma("sp", lambda e: e.dma_start(out=identf[:], in_=ident), ds0, w=["identf"])
        S.dma("sp", lambda e: e.dma_start(out=st_a.rearrange("p (c n) -> p c n", c=2),
                                          in_=w_uq.rearrange("(c p) n -> p c n", p=128)), ds0, w=["st_a"])
        S.dma("sp", lambda e: e.dma_start(out=st_b.rearrange("p (c n) -> p c n", c=2),
                                          in_=w_uqs.rearrange("(c p) n -> p c n", p=128)), ds0, w=["st_b"])
        S.dma("sp", lambda e: e.dma_start(out=st_c, in_=w_ukv), ds0, w=["st_c"])
        S.dma("sp", lambda e: e.dma_start(out=st_d.rearrange("p (h t) -> p h t", h=8), in_=wspT), ds0, w=["st_d"])
        S.dma("sp", lambda e: e.dma_start(out=st_e, in_=tril), ds0, w=["st_e"])
        S.dma("sp", lambda e: e.dma_start(out=qgs, in_=qg), ds0, w=["qgs"])
        S.dma("sp", lambda e: e.dma_start(out=kvgs, in_=kvg), ds0, w=["kvgs"])
        S.dma("sp", lambda e: e.dma_start(out=bsp[:], in_=bspT), ds0, w=["bsp"])
        S.dma("sp", lambda e: e.dma_start(out=sgug[:], in_=sgu_g.partition_broadcast(128)), ds0, w=["sgug"])
        S.dma("sp", lambda e: e.dma_start(out=sgub[:], in_=sgu_b.partition_broadcast(128)), ds0, w=["sgub"])
        S.dma("sp", lambda e: e.dma_start(out=lng[:], in_=ln_g.partition_broadcast(128)), ds0, w=["lng"])
        S.dma("sp", lambda e: e.dma_start(out=lnb[:], in_=ln_b.partition_broadcast(128)), ds0, w=["lnb"])
        dscv = dsem()
        S.op("dve", lambda e: e.memset(onesf[:], 1.0), w=["onesf"])
        S.op("dve", lambda e: e.memset(epst[:], EPS), w=["epst"])
        S.op("dve", lambda e: e.memset(neghalf[:], -0.5), w=["neghalf"])
        for c in range(2):
            S.op("dve", lambda e, c=c: e.tensor_scalar(wuq_g[:, c, :], st_a[:, c * 768:(c + 1) * 768],
                                                       qgs[:, c:c + 1], None, op0=ALU.mult),
                 r=["st_a", "qgs"], w=[("wuq", c)])
            S.op("dve", lambda e, c=c: e.tensor_scalar(wuqs_g[:, c, :], st_b[:, c * 768:(c + 1) * 768],
                                                       qgs[:, c:c + 1], None, op0=ALU.mult),
                 r=["st_b", "qgs"], w=[("wuqs", c)])
        S.op("dve", lambda e: e.tensor_scalar(wukv_g[:], st_c, kvgs[:, 0:1], None, op0=ALU.mult),
             r=["st_c", "kvgs"], w=["wukv"])
        for h in range(8):
            S.op("dve", lambda e, h=h: e.tensor_tensor(wsp[:, h, :], st_d[:, h * 128:(h + 1) * 128], st_e,
                                                       op=ALU.mult), r=["st_d", "st_e"], w=[("wsp", h)])
        S.barrier()

        def phase_A(p):
            cv = Carver()
            WA = cv.take(8 * NA, BF16).rearrange("p (c n) -> p c n", c=8)
            WK = cv.take(8 * 32, BF16).rearrange("p (c n) -> p c n", c=8)
            XS = [cv.take(4 * D, BF16).rearrange("p (s d) -> p s d", s=4) for _ in range(2)]
            XT = cv.take(8 * 512, BF16).rearrange("p (c t) -> p c t", c=8)
            POSI = [cv.take(512, I32) for _ in range(2)]
            posf = cv.take(512, F32)
            tk = posf.bitcast(I32)
            TABS = [[cv.take(512, F32) for _ in range(2)] for _ in range(2)]
            tt = cv.take(512, F32)
            tkf = cv.take(512, F32)
            sq = [cv.take(512, F32) for _ in range(2)]
            rstd = cv.take(512, F32)
            dg = cv.take(512, F32)
            dg2 = cv.take(256, F32)
            rstdq = cv.take(256, F32)
            KR = [[cv.take(512, F32) for _ in range(2)] for _ in range(2)]
            CQB = [cv.take(512, BF16) for _ in range(2)]
            crsr = cv.take(512, F32)
            qtmp = [cv.take(512, F32) for _ in range(2)]
            dsW = dsem()
            dsX = [dsem(), dsem()]
            dsP = [dsem(), dsem()]
            dsD = dsem()
            groups = list(range(8 * p, 8 * p + NGROUPS))

            def load_group(G):
                i = G % 2
                S.dma("pool", lambda e: e.dma_start(
                    out=XS[i], in_=xs[G * 512:(G + 1) * 512, :].rearrange("(s p) d -> p s d", p=128)),
                    dsX[i], w=[("XS", i)])
                S.dma("sp", lambda e: e.dma_start(
                    out=POSI[i], in_=pos[0:1, G * 512:(G + 1) * 512].partition_broadcast(128)),
                    dsP[i], w=[("POSI", i)])

            load_group(groups[0])
            if p == 0:
                dsW0 = dsem()
                S.dma("pool", lambda e: e.dma_start(out=WA, in_=w_in.rearrange("(c p) n -> p c n", p=128)[:, :, 0:NA]),
                      dsW0, w=["WA"])
                S.dma("pool", lambda e: e.dma_start(out=WK, in_=w_krs.rearrange("(c p) n -> p c n", p=128)),
                      dsW0, w=["WK"])
            else:
                S.dma("sp", lambda e: e.dma_start(out=WA, in_=w_in_bf.rearrange("(c p) n -> p c n", p=128)[:, :, 0:NA]),
                      dsW, w=["WA"])
                S.dma("sp", lambda e: e.dma_start(out=WK, in_=w_krs_bf.rearrange("(c p) n -> p c n", p=128)),
                      dsW, w=["WK"])
            if len(groups) > 1:
                load_group(groups[1])
            if p == 0:
                S.dma("pool", lambda e: e.dma_start(out=w_in_bf, in_=w_in), dscv, w=["w_in_bf"])
                S.dma("pool", lambda e: e.dma_start(out=w_krs_bf, in_=w_krs), dscv, w=["w_krs_bf"])
                S.dma("pool", lambda e: e.dma_start(out=w_out_bf, in_=w_out), dscv, w=["w_out_bf"])

            def make_tables(G):
                i = G % 2
                S.op("dve", lambda e: e.tensor_copy(posf, POSI[i]), r=[("POSI", i)], w=["posf"])
                S.op("dve", lambda e: e.tensor_scalar(tt, posf, ropecs[:, 0:1], None, op0=ALU.mult),
                     r=["posf", "ropecs"], w=["tt"])
                S.op("dve", lambda e: e.tensor_copy(tk, tt), r=["tt"], w=["posf"])
                S.op("dve", lambda e: e.tensor_copy(tkf, tk), r=["posf"], w=["tkf"])
                S.op("dve", lambda e: e.tensor_tensor(tt, tt, tkf, op=ALU.subtract), r=["tt", "tkf"], w=["tt"])
                S.op("dve", lambda e: e.scalar_tensor_tensor(out=tt, in0=tt, scalar=0.5, in1=tt,
                                                             op0=ALU.is_gt, op1=ALU.subtract),
                     r=["tt"], w=["tt"])
                S.op("act", lambda e: e.activation(TABS[i][1], tt, AF.Sin, scale=ropecs[:, 1:2]),
                     r=["tt", "ropecs"], w=[("tab", i, 1)])
                S.op("act", lambda e: e.activation(tkf, tt, AF.Abs), r=["tt"], w=["tkf"])
                S.op("act", lambda e: e.activation(TABS[i][0], tkf, AF.Sin, bias=ropecs[:, 3:4], scale=ropecs[:, 2:3]),
                     r=["tkf", "ropecs"], w=[("tab", i, 0)])

            def stage_x1(G):
                i = G % 2
                xt = XT
                for k in range(8):
                    half = k % 2
                    pt = pst_banks[half]
                    for s_ in range(4):
                        S.op("pe", lambda e, pt=pt, s_=s_, k=k: e.transpose(
                            pt[:, s_ * 128:(s_ + 1) * 128], XS[i][:, s_, k * 128:(k + 1) * 128], identb[:]),
                            r=[("XS", i), "identb"], w=[("pst", half)])
                    if k % 2 == 0:
                        S.op("act", lambda e, pt=pt, k=k: e.copy(xt[:, k, :], pt),
                             r=[("pst", half)], w=[("XT", k)])
                    else:
                        S.op("dve", lambda e, pt=pt, k=k: e.tensor_copy(xt[:, k, :], pt),
                             r=[("pst", half)], w=[("XT", k)])

            def stage_x2(G):
                i = G % 2
                tok = slice(G * 512, (G + 1) * 512)
                lt = slice((G - 8 * p) * 256, (G - 8 * p) * 256 + 256)
                xt = XT
                bc, pc, kc = next_bank()
                for k in range(8):
                    S.op("pe", lambda e, pc=pc, k=k: e.matmul(pc, WA[:, k, 256:384], xt[:, k, :],
                                                              start=(k == 0), stop=(k == 7)),
                         r=["WA", ("XT", k)], w=[kc])
                br, pr, kr = next_bank()
                for k in range(8):
                    S.op("pe", lambda e, pr=pr, k=k: e.matmul(pr[64:96, :], WA[:, k, 384:416], xt[:, k, :],
                                                              start=(k == 0), stop=(k == 7), tile_position=(0, 64)),
                         r=["WA", ("XT", k)], w=[kr])
                bs, psw, ks = next_bank()
                for k in range(8):
                    S.op("pe", lambda e, psw=psw, k=k: e.matmul(psw[64:96, :], WK[:, k, :], xt[:, k, :],
                                                                start=(k == 0), stop=(k == 7), tile_position=(0, 64)),
                         r=["WK", ("XT", k)], w=[ks])
                S.op("act", lambda e, pc=pc: e.copy(ckvn[:, tok], pc), r=[kc], w=[("ckvn", G)])
                S.op("act", lambda e, pc=pc: e.activation(sq[0], pc, AF.Square), r=[kc], w=[("sq", 0)])
                S.op("act", lambda e, pr=pr: e.copy(KR[i][0][64:96, :], pr[64:96, :]), r=[kr], w=[("KR", i, 0)])
                S.op("act", lambda e, psw=psw: e.copy(KR[i][1][64:96, :], psw[64:96, :]), r=[ks], w=[("KR", i, 1)])
                bcq, pcq, kcq = next_bank()
                for c in range(2):
                    for k in range(8):
                        S.op("pe", lambda e, pcq=pcq, c=c, k=k: e.matmul(
                            pcq[:, c * 256:(c + 1) * 256], WA[:, k, c * 128:(c + 1) * 128], xt[:, k, 0:256],
                            start=(k == 0), stop=(k == 7)),
                            r=["WA", ("XT", k)], w=[kcq])
                S.op("act", lambda e, pcq=pcq: e.copy(CQB[i], pcq), r=[kcq], w=[("cqb", i)])
                S.op("act", lambda e, pcq=pcq: e.activation(sq[1], pcq, AF.Square), r=[kcq], w=[("sq", 1)])
                for cc in range(2):
                    bz, pz, kz = next_bank()
                    for c2 in range(2):
                        c = cc * 2 + c2
                        for k in range(8):
                            S.op("pe", lambda e, pz=pz, c=c, c2=c2, k=k: e.matmul(
                                pz[:, c2 * 256:(c2 + 1) * 256], WA[:, k, 416 + c * 128:416 + (c + 1) * 128],
                                xt[:, k, 0:256], start=(k == 0), stop=(k == 7)),
                                r=["WA", ("XT", k)], w=[kz])
                    S.op("act", lambda e, pz=pz, cc=cc: e.activation(
                        MG[:, 2 * cc:2 * cc + 2, lt], pz.rearrange("p (c t) -> p c t", c=2), AF.Silu),
                        r=[kz], w=[("MG", cc, G)])
                bq, pq, kq = next_bank()
                for j in range(4):
                    S.op("pe", lambda e, pq=pq, j=j: e.matmul(pq[:, j:j + 1], sq[0][:, j * 128:(j + 1) * 128], onesf[:, 0:1],
                                                              start=True, stop=True, skip_group_check=True),
                         r=[("sq", 0), "onesf"], w=[kq])
                for j in range(2):
                    for c in range(2):
                        S.op("pe", lambda e, pq=pq, c=c, j=j: e.matmul(
                            pq[:, 4 + j:5 + j], sq[1][:, c * 256 + j * 128:c * 256 + (j + 1) * 128], onesf[:, 0:1],
                            start=(c == 0), stop=(c == 1), skip_group_check=True),
                            r=[("sq", 1), "onesf"], w=[kq])
                ro = 8 * i
                S.op("act", lambda e, pq=pq: e.activation(rt[:, ro:ro + 4], pq[:, 0:4], AF.Identity, bias=epst[:, 0:1],
                                                          scale=1.0 / 128.0), r=[kq, "epst"], w=[("rt", i)])
                S.op("act", lambda e, pq=pq: e.activation(rt[:, ro + 4:ro + 6], pq[:, 4:6], AF.Identity, bias=epst[:, 0:1],
                                                          scale=1.0 / 256.0), r=[kq, "epst"], w=[("rt", i)])
                S.op("pool", lambda e: e.tensor_tensor(rt[:, ro:ro + 6], rt[:, ro:ro + 6],
                                                       neghalf[:, 0:1].to_broadcast([128, 6]), op=ALU.pow),
                     r=[("rt", i), "neghalf"], w=[("rt", i)])

            def stage_yn(G):
                i = G % 2
                tok = slice(G * 512, (G + 1) * 512)
                lt = slice((G - 8 * p) * 256, (G - 8 * p) * 256 + 256)
                tabs = TABS[i]
                tabk = [("tab", i, 0), ("tab", i, 1)]
                ro = 8 * i
                cqb = CQB[i]
                for j in range(4):
                    S.op("dve", lambda e, j=j: e.tensor_scalar(dg[:, j * 128:(j + 1) * 128], identf[:], rt[:, ro + j:ro + j + 1],
                                                               None, op0=ALU.mult), r=[("rt", i), "identf"], w=["dg"])
                for j in range(2):
                    S.op("dve", lambda e, j=j: e.tensor_scalar(dg2[:, j * 128:(j + 1) * 128], identf[:],
                                                               rt[:, ro + 4 + j:ro + 5 + j], None, op0=ALU.mult),
                         r=[("rt", i), "identf"], w=["dg2"])
                bb, pbc, kbc = next_bank()
                for j in range(4):
                    S.op("pe", lambda e, pbc=pbc, j=j: e.matmul(pbc[:, j * 128:(j + 1) * 128], onesf[:],
                                                                dg[:, j * 128:(j + 1) * 128], start=True, stop=True,
                                                                skip_group_check=True),
                         r=["dg", "onesf"], w=[kbc])
                bb2, pbc2, kbc2 = next_bank()
                for j in range(2):
                    S.op("pe", lambda e, pbc2=pbc2, j=j: e.matmul(pbc2[:, j * 128:(j + 1) * 128], onesf[:],
                                                                  dg2[:, j * 128:(j + 1) * 128], start=True, stop=True,
                                                                  skip_group_check=True),
                         r=["dg2", "onesf"], w=[kbc2])
                S.op("act", lambda e, pbc=pbc: e.copy(rstd, pbc), r=[kbc], w=["rstd"])
                S.op("act", lambda e, pbc2=pbc2: e.copy(rstdq[:, 0:256], pbc2[:, 0:256]), r=[kbc2], w=["rstdq"])
                S.op("pool", lambda e: e.tensor_tensor(ckvn[:, tok], ckvn[:, tok], rstd, op=ALU.mult),
                     r=[("ckvn", G), "rstd"], w=[("ckvn", G)])
                kr0, kr1 = KR[i]
                S.op("dve", lambda e: e.tensor_tensor(kr0[64:96, :], kr0[64:96, :], tabs[0][64:96, :], op=ALU.mult),
                     r=[("KR", i, 0), tabk[0]], w=[("KR", i, 0)])
                S.op("dve", lambda e: e.tensor_tensor(kr1[64:96, :], kr1[64:96, :], tabs[1][64:96, :], op=ALU.mult),
                     r=[("KR", i, 1), tabk[1]], w=[("KR", i, 1)])
                S.op("pool", lambda e: e.tensor_tensor(KB[0][64:96, tok], kr0[64:96, :], kr1[64:96, :], op=ALU.add),
                     r=[("KR", i, 0), ("KR", i, 1)], w=[("kbr", 0, G)])
                S.op("pool", lambda e: e.tensor_copy(KB[1][64:96, tok], KB[0][64:96, tok]),
                     r=[("kbr", 0, G)], w=[("kbr", 1, G)])
                for ti in range(2):
                    S.op("dve", lambda e, ti=ti: e.tensor_tensor(crsr[0:96, ti * 256:(ti + 1) * 256],
                                                                 tabs[ti][0:96, 0:256], rstdq[0:96, 0:256], op=ALU.mult),
                         r=[tabk[ti], "rstdq"], w=["crsr"])

            def stage_yq(G):
                i = G % 2
                lt = slice((G - 8 * p) * 256, (G - 8 * p) * 256 + 256)
                tok = slice(G * 512, (G + 1) * 512)
                cqb = CQB[i]
                for h in range(8):
                    bh, ph_, kh = next_bank()
                    for half, wsrc, wk in ((0, wuq_g, "wuq"), (1, wuqs_g, "wuqs")):
                        for c in range(2):
                            S.op("pe", lambda e, ph_=ph_, half=half, wsrc=wsrc, c=c, h=h: e.matmul(
                                ph_[0:96, half * 256:(half + 1) * 256], wsrc[:, c, h * 96:(h + 1) * 96],
                                cqb[:, c * 256:(c + 1) * 256], start=(c == 0), stop=(c == 1)),
                                r=[(wk, c), ("cqb", i)], w=[kh])
                    qt_ = qtmp[h % 2]
                    S.op("dve", lambda e, ph_=ph_, qt_=qt_: e.tensor_tensor(qt_[0:96, :], ph_[0:96, :], crsr[0:96, :],
                                                                            op=ALU.mult),
                         r=[kh, "crsr"], w=[("qtmp", h % 2)])
                    S.op("pool", lambda e, qt_=qt_, h=h: e.tensor_tensor(
                        QT[:, h, lt], qt_[0:96, 0:256], qt_[0:96, 256:512], op=ALU.add),
                        r=[("qtmp", h % 2)], w=[("QT", h, G)])
                if debug and G == 8 * p + 1 and p == 0:
                    dd = qtmp[0]
                    kk = ("qtmp", 0)
                    S.op("dve", lambda e: e.tensor_copy(dd, ckvn[:, tok]), r=[("ckvn", G), kk], w=[kk])
                    S.dma("sp", lambda e: e.dma_start(out=dbg["d_ckvn"], in_=dd), dsD, r=[kk])
                    S.op("dve", lambda e: e.tensor_copy(dd[64:96, :], KB[1][64:96, tok]),
                         r=[("kbr", 1, G), kk], w=[kk])
                    S.dma("sp", lambda e: e.dma_start(out=dbg["d_krope"], in_=dd[64:96, :]), dsD, r=[kk])
                    S.op("dve", lambda e: e.tensor_copy(dd[0:96, 0:256], QT[:, 0, lt]), r=[("QT", 0, G), kk], w=[kk])
                    S.op("dve", lambda e: e.tensor_copy(dd[0:96, 256:512], QT[:, 5, lt]), r=[("QT", 5, G), kk], w=[kk])
                    S.dma("sp", lambda e: e.dma_start(out=dbg["d_q0"], in_=dd[0:96, :]), dsD, r=[kk])
                    S.op("dve", lambda e: e.tensor_copy(dd[:, 0:256], MG[:, 0, lt]), r=[("MG", 0, G), kk], w=[kk])
                    S.op("dve", lambda e: e.tensor_copy(dd[:, 256:512], MG[:, 3, lt]), r=[("MG", 1, G), kk], w=[kk])
                    S.dma("sp", lambda e: e.dma_start(out=dbg["d_g"], in_=dd), dsD, r=[kk])

            ng = len(groups)
            stage_x1(groups[0])
            make_tables(groups[0])
            stage_x2(groups[0])
            if ng > 2:
                load_group(groups[2])
            for gi, G in enumerate(groups):
                nxt = groups[gi + 1] if gi + 1 < ng else None
                if nxt is not None:
                    stage_x1(nxt)
                stage_yn(G)
                if nxt is not None:
                    make_tables(nxt)
                    stage_x2(nxt)
                stage_yq(G)
                if gi + 3 < ng:
                    load_group(groups[gi + 3])

        def phase_B(p):
            cv = Carver()
            nslots = 32 * (p + 1)
            VB = [cv.take(64 * 65, BF16).rearrange("p (s d) -> p s d", s=64),
                  cv.take(64 * 128, BF16).rearrange("p (s d) -> p s d", s=64)]
            PT = [cv.take(1024, BF16).rearrange("p (j n) -> p j n", j=2) for _ in range(3)]
            osb = [cv.take(512, F32) for _ in range(2)]
            rec = [cv.take(512, F32) for _ in range(2)]
            otmp = [cv.take(512, F32) for _ in range(2)]
            dsD = dsem()
            S.op("pool", lambda e: e.memset(VB[0][:, :, 64:65], 1.0), w=[("vbc", 0)])
            S.op("pool", lambda e: e.memset(VB[1][:, :, 0:64], 0.0), w=[("vbc", 1)])
            S.op("pool", lambda e: e.memset(VB[1][:, :, 0:1], 1.0), w=[("vbc", 1)])

            def vaug(b, h, slot):
                return VB[b][:, slot, :]

            def vdst(b, c):
                if b == 0:
                    return VB[0][:, c * 8:(c + 1) * 8, 0:64]
                return VB[1][:, c * 8:(c + 1) * 8, 64:128]

            def gen_chunks(h, upfront=False):
                b = h % 2
                out = []
                cnt = [0]

                def target():
                    n = cnt[0]
                    cnt[0] += 1
                    if not upfront:
                        return PSG, "psg", "dve"
                    t = n % 3
                    return PSW[t][:, (n // 3) % 2, :], ("psw", t), ("dve" if n % 2 == 0 else "act")

                for c in range(nslots // 4):
                    def kchunk(c=c):
                        pg, kg, eng = target()
                        tok = slice(c * 512, (c + 1) * 512)
                        S.op("pe", lambda e: e.matmul(pg[0:64, :], wukv_g[:, h * 128:h * 128 + 64], ckvn[:, tok],
                                                      start=True, stop=True), r=["wukv"], w=[kg])
                        if eng == "dve":
                            S.op("dve", lambda e: e.tensor_copy(KB[b][0:64, tok], pg[0:64, :]),
                                 r=[kg], w=[("kb", b, c)])
                        else:
                            S.op("act", lambda e: e.copy(KB[b][0:64, tok], pg[0:64, :]),
                                 r=[kg], w=[("kb", b, c)])
                    out.append(kchunk)
                for c in range(nslots // 8):
                    def vchunk(c=c):
                        pg, kg, eng = target()
                        for s in range(8):
                            slot = c * 8 + s
                            S.op("pe", lambda e, s=s, slot=slot: e.matmul(
                                pg[:, s * 64:(s + 1) * 64], ckvn[:, slot * 128:(slot + 1) * 128],
                                wukv_g[:, h * 128 + 64:h * 128 + 128], start=True, stop=True),
                                r=["wukv"], w=[kg])
                        if eng == "dve":
                            S.op("dve", lambda e: e.tensor_copy(vdst(b, c), pg.rearrange("p (s d) -> p s d", s=8)),
                                 r=[kg], w=[("vb", b, c)])
                        else:
                            S.op("act", lambda e: e.copy(vdst(b, c), pg.rearrange("p (s d) -> p s d", s=8)),
                                 r=[kg], w=[("vb", b, c)])
                    out.append(vchunk)
                return out

            steps = []
            for h in range(8):
                for g in range(4):
                    gg = 4 * p + g
                    nfull = 4 * gg
                    for i in range(nfull):
                        steps.append(dict(h=h, g=g, slots=(2 * i, 2 * i + 1), c0=0, masks=None))
                    base = 8 * gg
                    tails = [
                        ((base + 0, base + 1), 0, ((0, 128, 0), (256, 128, 128))),
                        ((base + 2, base + 3), 0, ((384, 128, 0), (640, 128, 128))),
                        ((base + 4, base + 5), 256, ((0, 128, 256), (256, 128, 384))),
                        ((base + 6, base + 7), 256, ((384, 128, 256), (640, 128, 384))),
                    ]
                    for slots, c0, mk in tails:
                        steps.append(dict(h=h, g=g, slots=slots, c0=c0, masks=mk))
                    steps[-1]["last"] = True
                    steps[-(nfull + 4)]["first"] = True
            for fn in gen_chunks(0, upfront=True):
                fn()
            pending_gen = []
            pending_epi = []
            pending_rcp = []
            n_steps = len(steps)
            if KB_STEPS >= 0:
                n_steps = KB_STEPS
            ocount = [0]

            def emit_qk(si):
                stp = steps[si]
                h, g, c0 = stp["h"], stp["g"], stp["c0"]
                b = h % 2
                wtile = PSW[si % 3]
                for j, slot in enumerate(stp["slots"]):
                    has_mask = stp["masks"] is not None
                    cj = stp["masks"][j][2] if has_mask else c0
                    S.op("pe", lambda e, j=j, slot=slot, has_mask=has_mask, cj=cj: e.matmul(
                        wtile[:, j, cj:512], KB[b][:, slot * 128:(slot + 1) * 128],
                        QT[:, h, g * 512 + cj:(g + 1) * 512], start=True, stop=(not has_mask or bool(KB_NOMASK)),
                        skip_group_check=True),
                        r=[("kb", b, slot // 4)], w=[("psw", si % 3)])
                    if has_mask and not KB_NOMASK:
                        mo, mw, mc = stp["masks"][j]
                        S.op("pe", lambda e, j=j, mo=mo, mw=mw, mc=mc: e.matmul(
                            wtile[:, j, mc:mc + mw], identb[:], maskb[:, mo:mo + mw], start=False, stop=True,
                            skip_group_check=True),
                            r=[], w=[("psw", si % 3)])

            def emit_exp(si):
                stp = steps[si]
                c0 = stp["c0"]
                S.op("act", lambda e: e.activation(PT[si % 3][:, :, c0:512], PSW[si % 3][:, :, c0:512], AF.Exp,
                                                   scale=SCALE),
                     r=[("psw", si % 3)], w=[("pt", si % 3)])

            def emit_pv(si):
                stp = steps[si]
                h, g, c0 = stp["h"], stp["g"], stp["c0"]
                b = h % 2
                if stp.get("first"):
                    ocount[0] += 1
                o = ocount[0] % 2
                for j, slot in enumerate(stp["slots"]):
                    first = bool(stp.get("first")) and j == 0
                    last = bool(stp.get("last")) and j == 1
                    mrows = 65 if b == 0 else 128
                    cj = stp["masks"][j][2] if stp["masks"] is not None else c0
                    S.op("pe", lambda e, j=j, slot=slot, first=first, last=last, mrows=mrows, cj=cj: e.matmul(
                        PSO1[0:mrows, cj:512], vaug(b, h, slot), PT[si % 3][:, j, cj:512], start=first, stop=last,
                        skip_group_check=True),
                        r=[("pt", si % 3), ("vb", b, slot // 8), ("vbc", b)], w=["pso"])
                if stp.get("last"):
                    mrows = 65 if b == 0 else 128
                    S.op("dve", lambda e, mrows=mrows: e.tensor_copy(osb[o][0:mrows, :], PSO1[0:mrows, :]),
                         r=["pso"], w=[("osb", o)])
                    dr = 64 if h % 2 == 0 else 0
                    for q4 in range(4):
                        pending_rcp.append((si + q4, o, dr, q4))
                    pending_epi.append((si + 6, h, g, o))

            def emit_epilogue(h, g, o):
                if KB_NOEPI:
                    return
                dr = 64 if h % 2 == 0 else 0
                rows = slice(0, 64) if h % 2 == 0 else slice(64, 128)
                r_ = rec[o]
                ot = otmp[o]
                ob_ = osb[o]
                S.op("pe", lambda e: e.matmul(PSG, onesf[dr:dr + 1, :], r_[dr:dr + 1, :], start=True, stop=True),
                     r=[("rec", o)], w=["psg"])
                S.op("dve", lambda e: e.tensor_tensor(ot[rows, :], ob_[rows, :], PSG[rows, :], op=ALU.mult),
                     r=[("osb", o), "psg"], w=[("otmp", o)])
                mg = MG[rows, h // 2, g * 512:(g + 1) * 512]
                S.op("pool", lambda e: e.tensor_tensor(mg, ot[rows, :], mg, op=ALU.mult),
                     r=[("otmp", o)], w=[("mgo", h, g)])

            if n_steps > 0:
                emit_qk(0)
            if n_steps > 1:
                emit_qk(1)
            cur_h = 0
            gen_every = 3 if p == 0 else 4
            since_gen = 0
            for si in range(n_steps):
                emit_exp(si)
                if si + 2 < n_steps:
                    emit_qk(si + 2)
                if not KB_NOPV:
                    emit_pv(si)
                while pending_rcp and pending_rcp[0][0] <= si:
                    _, ro_, rdr, q4 = pending_rcp.pop(0)
                    cs = slice(q4 * 128, (q4 + 1) * 128)
                    S.op("dve", lambda e, ro_=ro_, rdr=rdr, cs=cs: e.reciprocal(rec[ro_][rdr:rdr + 1, cs],
                                                                               osb[ro_][rdr:rdr + 1, cs]),
                         r=[("osb", ro_)], w=[("rec", ro_)])
                while pending_epi and pending_epi[0][0] <= si:
                    _, eh, eg, eo = pending_epi.pop(0)
                    emit_epilogue(eh, eg, eo)
                h = steps[si]["h"]
                if steps[si].get("first") and steps[si]["g"] == 0 and h + 1 < 8:
                    pending_gen = gen_chunks(h + 1)
                    since_gen = 0
                since_gen += 1
                if pending_gen and since_gen >= gen_every:
                    pending_gen.pop(0)()
                    since_gen = 0
                if steps[si].get("last") and steps[si]["g"] == 3:
                    while pending_gen:
                        pending_gen.pop(0)()
            while pending_rcp:
                _, ro_, rdr, q4 = pending_rcp.pop(0)
                cs = slice(q4 * 128, (q4 + 1) * 128)
                S.op("dve", lambda e, ro_=ro_, rdr=rdr, cs=cs: e.reciprocal(rec[ro_][rdr:rdr + 1, cs],
                                                                           osb[ro_][rdr:rdr + 1, cs]),
                     r=[("osb", ro_)], w=[("rec", ro_)])
            while pending_epi:
                _, eh, eg, eo = pending_epi.pop(0)
                emit_epilogue(eh, eg, eo)
            if debug and p == 0:
                dd = cv.take(512, F32)
                S.op("dve", lambda e: e.tensor_copy(dd[0:96, :], KB[1][:, 512:1024]), r=[("kb", 1, 1)], w=["dd"])
                S.dma("sp", lambda e: e.dma_start(out=dbg["d_kb"], in_=dd[0:96, :]), dsD, r=["dd"])
                S.op("dve", lambda e: e.tensor_copy(dd.rearrange("p (s d) -> p s d", s=8), VB[1][:, 8:16, 64:128]),
                     r=[("vb", 1, 1), "dd"], w=["dd"])
                S.dma("sp", lambda e: e.dma_start(out=dbg["d_vb"], in_=dd), dsD, r=["dd"])
                S.op("dve", lambda e: e.tensor_copy(dd[:, 0:256], MG[:, 0, 256:512]),
                     r=[("mgo", 0, 0), ("mgo", 1, 0), "dd"], w=["dd"])
                S.op("dve", lambda e: e.tensor_copy(dd[:, 256:512], MG[:, 3, 1024 + 256:1024 + 512]),
                     r=[("mgo", 6, 2), ("mgo", 7, 2), "dd"], w=["dd"])
                S.dma("sp", lambda e: e.dma_start(out=dbg["d_merged"], in_=dd), dsD, r=["dd"])

        WC_bytes = 8 * 1536 * 2 + 8 * 1024 * 2

        def wc_views():
            base = (ARENA - WC_bytes) // 2
            wci = arena[:, base:base + 8 * 1536].rearrange("p (c n) -> p c n", c=8)
            wco = arena[:, base + 8 * 1536:base + 8 * 1536 + 8 * 1024].rearrange("p (c n) -> p c n", c=8)
            return wci, wco

        dsWC = dsem()

        def load_WC():
            wci, wco = wc_views()
            S.dma("sp", lambda e: e.dma_start(out=wci, in_=w_in_bf.rearrange("(c p) n -> p c n", p=128)[:, :, NA:DIN]),
                  dsWC, w=["WCI"])
            S.dma("sp", lambda e: e.dma_start(out=wco, in_=w_out_bf.rearrange("(c p) n -> p c n", p=128)),
                  dsWC, w=["WCO"])

        dsY = [dsem(), dsem()]

        def phase_C(p):
            cv = Carver()
            wci, wco = wc_views()
            XR = [cv.take(D, F32) for _ in range(2)]
            XB = [cv.take(D, BF16) for _ in range(2)]
            XTC = [cv.take(8 * 128, BF16).rearrange("p (c t) -> p c t", c=8) for _ in range(2)]
            UG = [cv.take(512, F32) for _ in range(2)]
            VG = [cv.take(512, F32) for _ in range(2)]
            ZB = [cv.take(512, F32) for _ in range(2)]
            VN = [cv.take(512, BF16) for _ in range(2)]
            ob = cv.take(512, BF16)
            obT = cv.take(512, BF16)
            hh = cv.take(D, F32)
            YO = [cv.take(D, F32) for _ in range(2)]
            st1 = [cv.take(16, F32) for _ in range(2)]
            st2 = cv.take(24, F32)
            assert cv.off <= ARENA - WC_bytes, ("phase C arena", cv.off)
            dsXR = [dsem(), dsem()]
            dsXB = [dsem(), dsem()]

            def load_block(t):
                i = t % 2
                gslot = 4 * (t // 2) + (t % 2)
                rows = slice(gslot * 128, (gslot + 1) * 128)
                S.dma("sp", lambda e: e.dma_start(out=XR[i], in_=xs[rows, :]), dsXR[i], w=[("XR", i)])
                S.dma("pool", lambda e: e.dma_start(out=XB[i], in_=xs[rows, :]), dsXB[i], w=[("XB", i)])

            def s1a(t):
                i = t % 2
                xtc = XTC[i]
                for hf in range(2):
                    pt = pst_banks[hf]
                    for kk in range(4):
                        k = hf * 4 + kk
                        S.op("pe", lambda e, pt=pt, kk=kk, k=k: e.transpose(
                            pt[:, kk * 128:(kk + 1) * 128], XB[i][:, k * 128:(k + 1) * 128], identb[:]),
                            r=[("XB", i), "identb"], w=[("pst", hf)])
                    dst = xtc[:, hf * 4:hf * 4 + 4, :]
                    if hf == 0:
                        S.op("act", lambda e, pt=pt, dst=dst: e.copy(dst, pt.rearrange("p (c t) -> p c t", c=4)),
                             r=[("pst", hf)], w=[("XTC", i, hf)])
                    else:
                        S.op("dve", lambda e, pt=pt, dst=dst: e.tensor_copy(dst, pt.rearrange("p (c t) -> p c t", c=4)),
                             r=[("pst", hf)], w=[("XTC", i, hf)])

            def s1b_mm(t):
                i = t % 2
                xtc = XTC[i]
                xk = [("XTC", i, 0), ("XTC", i, 1)]
                ug, vg, zb = UG[i], VG[i], ZB[i]
                th = hh[:, 0:512]
                pss = {}
                for j in (1, 0, 2):
                    bb, pb, kb_ = next_bank()
                    pss[j] = (pb, kb_)
                    for k in range(8):
                        S.op("pe", lambda e, pb=pb, j=j, k=k: e.matmul(
                            pb, xtc[:, k, :], wci[:, k, j * 512:(j + 1) * 512], start=(k == 0), stop=(k == 7)),
                            r=xk + ["WCI"], w=[kb_])
                S.op("act", lambda e: e.activation(vg, pss[1][0], AF.Gelu), r=[pss[1][1]], w=[("vg", i)])
                S.op("act", lambda e: e.activation(ug, pss[0][0], AF.Gelu), r=[pss[0][1]], w=[("ug", i)])
                S.op("act", lambda e: e.activation(th, pss[2][0], AF.Tanh, scale=0.5), r=[pss[2][1]], w=[("hh", 0)])
                S.op("act", lambda e: e.copy(zb, pss[2][0]), r=[pss[2][1]], w=[("zb", i)])
                S.op("pool", lambda e: e.tensor_tensor(th, th, zb, op=ALU.mult), r=[("hh", 0), ("zb", i)], w=[("hh", 0)])
                S.op("pool", lambda e: e.tensor_tensor(zb, th, zb, op=ALU.add), r=[("hh", 0), ("zb", i)], w=[("zb", i)])
                S.op("pool", lambda e: e.tensor_tensor(ug, ug, zb, op=ALU.mult), r=[("ug", i), ("zb", i)], w=[("ug", i)])

            def s1b_ln(t):
                i = t % 2
                vg, vn, st = VG[i], VN[i], st1[i]
                S.op("dve", lambda e: e.bn_stats(st[:, 0:6], vg), r=[("vg", i)], w=[("st1a", i)])
                S.op("dve", lambda e: e.bn_aggr(st[:, 6:8], st[:, 0:6]), r=[("st1a", i)], w=[("st1b", i)])
                S.op("dve", lambda e: e.tensor_scalar(st[:, 8:9], st[:, 7:8], EPS, None, op0=ALU.add),
                     r=[("st1b", i)], w=[("st1c", i)])
                S.op("pool", lambda e: e.tensor_tensor(st[:, 8:9], st[:, 8:9], neghalf[:, 0:1], op=ALU.pow),
                     r=[("st1c", i), "neghalf"], w=[("st1c", i)])
                S.op("dve", lambda e: e.scalar_tensor_tensor(out=vg, in0=vg, scalar=st[:, 6:7], in1=sgug[:],
                                                             op0=ALU.subtract, op1=ALU.mult),
                     r=[("vg", i), ("st1b", i), "sgug"], w=[("vg", i)])
                S.op("dve", lambda e: e.scalar_tensor_tensor(out=vn, in0=vg, scalar=st[:, 8:9], in1=sgub[:],
                                                             op0=ALU.mult, op1=ALU.add),
                     r=[("vg", i), ("st1c", i), "sgub"], w=[("vn", i)])

            def s2a(t):
                i = t % 2
                ug, vn = UG[i], VN[i]
                bsv, psv, ksv = next_bank()
                S.op("pe", lambda e: e.matmul(psv, bspb[:], indb[:], start=True, stop=False, skip_group_check=True),
                     r=["bspb", "indb"], w=[ksv])
                for h in range(8):
                    S.op("pe", lambda e, h=h: e.matmul(psv[:, h * 64:(h + 1) * 64], wsp[:, h, :],
                                                      vn[:, h * 64:(h + 1) * 64], start=False, stop=(h == 7),
                                                      skip_group_check=True),
                         r=[("wsp", h), ("vn", i)], w=[ksv])
                S.op("dve", lambda e: e.scalar_tensor_tensor(out=ob, in0=psv, scalar=0.5, in1=ug, op0=ALU.mult, op1=ALU.mult),
                     r=[ksv, ("ug", i)], w=["ob"])

            def s2b(t):
                pt = pst_banks[0]
                for c in range(4):
                    S.op("pe", lambda e, c=c: e.transpose(pt[:, c * 128:(c + 1) * 128],
                                                         ob[:, c * 128:(c + 1) * 128], identb[:]),
                         r=["ob", "identb"], w=[("pst", 0)])
                S.op("act", lambda e: e.copy(obT, pt), r=[("pst", 0)], w=["obT"])

            def s2c(t):
                i = t % 2
                lt = slice((t - 16 * p) * 128, (t - 16 * p + 1) * 128)
                outs = []
                for nh in range(2):
                    bo, po, ko = next_bank()
                    outs.append((po, ko))
                    for c in range(8):
                        if c < 4:
                            lhsT = MG[:, c, lt]
                            rr = []
                        else:
                            lhsT = obT[:, (c - 4) * 128:(c - 3) * 128]
                            rr = ["obT"]
                        S.op("pe", lambda e, po=po, lhsT=lhsT, c=c, nh=nh: e.matmul(
                            po, lhsT, wco[:, c, nh * 512:(nh + 1) * 512], start=(c == 0), stop=(c == 7)),
                            r=rr + ["WCO"], w=[ko])
                for nh in range(2):
                    po, ko = outs[nh]
                    S.op("dve", lambda e, po=po, nh=nh: e.scalar_tensor_tensor(
                        out=hh[:, nh * 512:(nh + 1) * 512], in0=XR[i][:, nh * 512:(nh + 1) * 512], scalar=ALPHA,
                        in1=po, op0=ALU.mult, op1=ALU.add),
                        r=[ko, ("XR", i)], w=[("hh", nh)])
                    S.op("dve", lambda e, nh=nh: e.bn_stats(st2[:, 6 * nh:6 * nh + 6], hh[:, nh * 512:(nh + 1) * 512]),
                         r=[("hh", nh)], w=[("st2a", nh)])
                S.op("dve", lambda e: e.bn_aggr(st2[:, 12:14], st2[:, 0:12]), r=[("st2a", 0), ("st2a", 1)], w=["st2b"])
                S.op("dve", lambda e: e.tensor_scalar(st2[:, 14:15], st2[:, 13:14], EPS, None, op0=ALU.add),
                     r=["st2b"], w=["st2c"])
                S.op("pool", lambda e: e.tensor_tensor(st2[:, 14:15], st2[:, 14:15], neghalf[:, 0:1], op=ALU.pow),
                     r=["st2c", "neghalf"], w=["st2c"])
                S.op("dve", lambda e: e.scalar_tensor_tensor(out=hh, in0=hh, scalar=st2[:, 12:13], in1=lng[:],
                                                             op0=ALU.subtract, op1=ALU.mult),
                     r=[("hh", 0), ("hh", 1), "st2b", "lng"], w=[("hh", 0), ("hh", 1)])
                yo = YO[i]
                S.op("dve", lambda e: e.scalar_tensor_tensor(out=yo, in0=hh, scalar=st2[:, 14:15], in1=lnb[:],
                                                             op0=ALU.mult, op1=ALU.add),
                     r=[("hh", 0), ("hh", 1), "st2c", "lnb"], w=[("YO", i)])
                S.dma("sp", lambda e: e.dma_start(out=y[t * 128:(t + 1) * 128, :], in_=yo),
                      dsY[i], r=[("YO", i)])

            def load_xb(t):
                i = t % 2
                gslot = 4 * (t // 2) + (t % 2)
                rows = slice(gslot * 128, (gslot + 1) * 128)
                S.dma("pool", lambda e: e.dma_start(out=XB[i], in_=xs[rows, :]), dsXB[i], w=[("XB", i)])

            def load_xr(t):
                i = t % 2
                gslot = 4 * (t // 2) + (t % 2)
                rows = slice(gslot * 128, (gslot + 1) * 128)
                S.dma("sp", lambda e: e.dma_start(out=XR[i], in_=xs[rows, :]), dsXR[i], w=[("XR", i)])

            blocks = list(range(16 * p, 16 * p + 16))
            nb = len(blocks)
            load_xb(blocks[0])
            load_xb(blocks[1])
            load_xr(blocks[0])
            load_xr(blocks[1])
            s1a(blocks[0])
            s1b_mm(blocks[0])
            s1b_ln(blocks[0])
            s1a(blocks[1])
            load_xb(blocks[2])
            for bi, t in enumerate(blocks):
                s2a(t)
                if bi + 1 < nb:
                    s1b_mm(blocks[bi + 1])
                if bi + 2 < nb:
                    s1a(blocks[bi + 2])
                if bi + 3 < nb:
                    load_xb(blocks[bi + 3])
                s2b(t)
                if bi + 1 < nb:
                    s1b_ln(blocks[bi + 1])
                s2c(t)
                if bi + 2 < nb:
                    load_xr(blocks[bi + 2])

        for p in passes:
            if "A" in phases:
                phase_A(p)
                S.barrier()
            if "B" in phases:
                if "C" in phases and WC_EARLY:
                    load_WC()
                phase_B(p)
                S.barrier()
            if "C" in phases:
                if "B" not in phases or not WC_EARLY:
                    load_WC()
                phase_C(p)
                S.barrier()
        S.emit(nc, block, esems, dsY)
    return nc


def _slot_perm(par):
    order = []
    for m in range(16):
        if par == 0:
            order += [4 * m, 4 * m + 3, 4 * m + 1, 4 * m + 2]
        else:
            order += [4 * m + 1, 4 * m + 2, 4 * m, 4 * m + 3]
    return order


def make_in_maps(x, positions, w_in, q_norm_g, w_uq, kv_norm_g, w_ukv, sgu_norm_g, sgu_norm_b,
                 w_spatial, b_spatial, w_out, ln_g, ln_b):
    f = np.float32
    w_in = np.ascontiguousarray(w_in, dtype=f)
    swap = np.concatenate([np.arange(16, 32), np.arange(0, 16)])
    w_krs = np.ascontiguousarray(w_in[:, 384:416][:, swap])
    w_uq = np.ascontiguousarray(w_uq, dtype=f)
    w_uqs = np.zeros_like(w_uq)
    for h in range(8):
        w_uqs[:, h * 96 + 64:h * 96 + 96] = w_uq[:, h * 96 + 64:h * 96 + 96][:, swap]
    qg = np.ascontiguousarray(np.asarray(q_norm_g, dtype=f).reshape(2, 128).T)
    kvg = np.ascontiguousarray(np.asarray(kv_norm_g, dtype=f).reshape(128, 1))
    wspT = np.ascontiguousarray(np.transpose(np.asarray(w_spatial, dtype=f), (2, 0, 1)))
    si = np.arange(128)
    tril = (si[:, None] <= si[None, :]).astype(f)
    bspT = np.ascontiguousarray(np.asarray(b_spatial, dtype=f).T)
    ident = np.eye(128, dtype=f)
    invf = (1.0 / (10000.0 ** (np.arange(16, dtype=np.float64) / 16.0)))
    ropec = np.zeros((128, 6), f)
    ropec[:, 1] = -TWO_PI
    ropec[:, 2] = -TWO_PI
    ropec[:, 3] = 0.5 * math.pi
    for pp in range(64, 96):
        i = (pp - 64) % 16
        ropec[pp, 0] = invf[i] / TWO_PI
        ropec[pp, 1] = TWO_PI if pp < 80 else -TWO_PI
    tri = np.where(si[:, None] > si[None, :], NEG, 0.0).astype(f)
    allm = np.full((128, 128), NEG, f)
    zero = np.zeros((128, 128), f)
    ind_h = np.zeros((8, 512), f)
    for h in range(8):
        ind_h[h, h * 64:(h + 1) * 64] = 1.0
    common = dict(bsp_ht=np.ascontiguousarray(np.asarray(b_spatial, dtype=f)), ind_h=ind_h, w_in=w_in, w_krs=w_krs, w_uq=w_uq, w_uqs=w_uqs, qg=qg, w_ukv=np.ascontiguousarray(w_ukv, dtype=f),
                  kvg=kvg, wspT=wspT, tril=tril, bspT=bspT,
                  sgu_g=np.asarray(sgu_norm_g, dtype=f).reshape(1, 512), sgu_b=np.asarray(sgu_norm_b, dtype=f).reshape(1, 512),
                  ln_g=np.asarray(ln_g, dtype=f).reshape(1, D), ln_b=np.asarray(ln_b, dtype=f).reshape(1, D),
                  w_out=np.ascontiguousarray(w_out, dtype=f), ident=ident, ropec=ropec)
    in_maps = []
    perms = []
    for c in range(8):
        b, par = c // 2, c % 2
        order = _slot_perm(par)
        perms.append(order)
        xb = np.asarray(x[b], dtype=f).reshape(64, 128, D)[order].reshape(SEQ, D)
        pb = np.asarray(positions[b], dtype=np.int32).reshape(64, 128)[order].reshape(1, SEQ)
        mA = allm if par == 0 else zero
        mB = zero if par == 0 else allm
        mk = np.concatenate([tri, allm, tri, mA, allm, mB], axis=1)
        d = dict(common)
        d.update(xs=np.ascontiguousarray(xb), pos=np.ascontiguousarray(pb), masks=np.ascontiguousarray(mk))
        in_maps.append(d)
    return in_maps, perms


_NC_CACHE = {}


def kernel(x, positions, w_in, q_norm_g, w_uq, kv_norm_g, w_ukv, sgu_norm_g, sgu_norm_b,
           w_spatial, b_spatial, w_out, ln_g, ln_b):
    in_maps, perms = make_in_maps(x, positions, w_in, q_norm_g, w_uq, kv_norm_g, w_ukv, sgu_norm_g,
                                  sgu_norm_b, w_spatial, b_spatial, w_out, ln_g, ln_b)
    if "nc" not in _NC_CACHE:
        _NC_CACHE["nc"] = build_program()
    nc = _NC_CACHE["nc"]
    res = run_bass_kernel_spmd(nc, in_maps, core_ids=list(range(8)))
    out = np.empty((4, SEQ, D), np.float32)
    for c in range(8):
        b = c // 2
        order = perms[c]
        yc = np.asarray(res.results[c]["y"], dtype=np.float32).reshape(32, 128, D)
        ov = out[b].reshape(64, 128, D)
        for t in range(32):
            slot = 4 * (t // 2) + (t % 2)
            ov[order[slot]] = yc[t]
    return out
```

```python
import math
from contextlib import ExitStack

import numpy as np
import concourse.bass as bass
import concourse.mybir as mybir
from concourse.bass_utils import run_bass_kernel_spmd

F32 = mybir.dt.float32
BF16 = mybir.dt.bfloat16
I32 = mybir.dt.int32
AF = mybir.ActivationFunctionType
ALU = mybir.AluOpType

D = 1024
SEQ = 8192
NSLOT = 64
DIN = 2464
NA = 928
NEG = -30000.0
EPS = 1e-5
ALPHA = 2.0 ** 0.25
SCALE = 1.0 / math.sqrt(96.0)
TWO_PI = 2.0 * math.pi

ENGS = ("pe", "act", "dve", "pool", "sp")
import os
STOP = int(os.environ.get("KSTOP", "0"))
NGROUPS = int(os.environ.get("KGROUPS", "8"))
KB_STEPS = int(os.environ.get("KB_STEPS", "-1"))
WC_EARLY = int(os.environ.get("WC_EARLY", "1"))
TAB_ENG = os.environ.get("TAB_ENG", "dve")
KB_NOEPI = int(os.environ.get("KB_NOEPI", "0"))
KB_NOPV = int(os.environ.get("KB_NOPV", "0"))
KB_NOMASK = int(os.environ.get("KB_NOMASK", "0"))


class DmaSem:
    def __init__(self, sem):
        self.sem = sem
        self.count = 0


class Op:
    __slots__ = ("eng", "fn", "deps", "dma", "dsem", "dval", "signal", "sigval", "barrier")

    def __init__(self, eng, fn, deps, dma=False, dsem=None, dval=0):
        self.eng = eng
        self.fn = fn
        self.deps = deps
        self.dma = dma
        self.dsem = dsem
        self.dval = dval
        self.signal = False
        self.sigval = 0
        self.barrier = False


class Sched:
    def __init__(self):
        self.ops = []
        self.state = {}
        self.last_on = {e: None for e in ENGS}
        self.open_dma = []

    def _add(self, op):
        self.ops.append(op)
        idx = len(self.ops) - 1
        self.last_on[op.eng] = idx
        return idx

    def _mk(self, idxs):
        d = {}
        for i in idxs:
            p = self.ops[i]
            d[i] = p.dsem.count if p.dma else None
        return d

    def _deps(self, reads, writes):
        deps = set()
        for k in reads:
            st = self.state.get(k)
            if st is not None and st[0] is not None:
                deps.add(st[0])
        for k in writes:
            st = self.state.get(k)
            if st is not None:
                if st[0] is not None:
                    deps.add(st[0])
                deps.update(st[1])
        return self._mk(deps)

    def _commit(self, idx, reads, writes):
        for k in reads:
            st = self.state.setdefault(k, [None, []])
            st[1].append(idx)
        for k in writes:
            self.state[k] = [idx, []]

    def op(self, eng, fn, r=(), w=()):
        deps = self._deps(r, w)
        idx = self._add(Op(eng, fn, deps))
        self._commit(idx, r, w)
        return idx

    def dma(self, queue, fn, dsem, r=(), w=()):
        deps = self._deps(r, w)
        dsem.count += 16
        idx = self._add(Op(queue, fn, deps, dma=True, dsem=dsem, dval=dsem.count))
        self._commit(idx, r, w)
        self.open_dma.append(idx)
        return idx

    def barrier(self):
        targets = set(i for i in self.last_on.values() if i is not None)
        targets.update(self.open_dma)
        for e in ENGS:
            o = Op(e, None, self._mk(targets))
            o.barrier = True
            self.ops.append(o)
        self.open_dma = []
        self.state = {}

    def emit(self, nc, block, esems, final_waits):
        ops = self.ops
        for i, o in enumerate(ops):
            for d in o.deps:
                p = ops[d]
                if p.dma:
                    continue
                if p.eng == o.eng and not o.barrier:
                    if p.eng == "pe" or p.eng == "sp":
                        continue
                p.signal = True
        cnt = {e: 0 for e in ENGS}
        for o in ops:
            if o.signal:
                cnt[o.eng] += 1
                o.sigval = cnt[o.eng]

        def stream(name):
            def body(e):
                seen = {}
                for o in ops:
                    if o.eng != name:
                        continue
                    for d in sorted(o.deps):
                        p = ops[d]
                        if p.dma:
                            sem, val = p.dsem.sem, o.deps[d]
                        else:
                            if p.eng == name and (name == "pe" or name == "sp") :
                                continue
                            if p.eng == name and o.barrier and name == "pe":
                                continue
                            sem, val = esems[p.eng], p.sigval
                        key = id(sem)
                        if seen.get(key, 0) >= val:
                            continue
                        seen[key] = val
                        e.wait_ge(sem, val)
                    if o.fn is None:
                        continue
                    ins = o.fn(e)
                    if o.dma:
                        ins.then_inc(o.dsem.sem, 16)
                    elif o.signal:
                        ins.then_inc(esems[name], 1)
                if name == "sp":
                    for ds in final_waits:
                        e.wait_ge(ds.sem, ds.count)
            return body

        block.sync(stream("sp"))
        block.tensor(stream("pe"))
        block.scalar(stream("act"))
        block.vector(stream("dve"))
        block.gpsimd(stream("pool"))


def build_program(passes=(0, 1), phases="ABC", debug=False):
    nc = bass.Bass("TRN2", target_bir_lowering=False)

    def din(name, shape, dt=F32):
        return nc.dram_tensor(name, shape, dt, kind="ExternalInput").ap()

    xs = din("xs", [SEQ, D])
    pos = din("pos", [1, SEQ], I32)
    w_in = din("w_in", [D, DIN])
    w_krs = din("w_krs", [D, 32])
    w_uq = din("w_uq", [256, 768])
    w_uqs = din("w_uqs", [256, 768])
    qg = din("qg", [128, 2])
    w_ukv = din("w_ukv", [128, 1024])
    kvg = din("kvg", [128, 1])
    wspT = din("wspT", [128, 8, 128])
    tril = din("tril", [128, 128])
    bspT = din("bspT", [128, 8])
    bsp_ht = din("bsp_ht", [8, 128])
    ind_h = din("ind_h", [8, 512])
    sgu_g = din("sgu_g", [1, 512])
    sgu_b = din("sgu_b", [1, 512])
    ln_g = din("ln_g", [1, D])
    ln_b = din("ln_b", [1, D])
    w_out = din("w_out", [D, D])
    ident = din("ident", [128, 128])
    ropec = din("ropec", [128, 6])
    masks = din("masks", [128, 768])
    y = nc.dram_tensor("y", [SEQ // 2, D], F32, kind="ExternalOutput").ap()
    w_in_bf = nc.dram_tensor("w_in_bf", [D, DIN], BF16, kind="Internal").ap()
    w_krs_bf = nc.dram_tensor("w_krs_bf", [D, 32], BF16, kind="Internal").ap()
    w_out_bf = nc.dram_tensor("w_out_bf", [D, D], BF16, kind="Internal").ap()
    dbg = {}
    if debug:
        def dout(name, shape):
            dbg[name] = nc.dram_tensor(name, shape, F32, kind="ExternalOutput").ap()
        dout("d_ckvn", [128, 512])
        dout("d_krope", [32, 512])
        dout("d_q0", [96, 512])
        dout("d_g", [128, 512])
        dout("d_kb", [96, 512])
        dout("d_vb", [128, 512])
        dout("d_merged", [128, 512])

    S = Sched()
    with ExitStack() as es:
        def sb(name, shape, dt):
            return es.enter_context(nc.sbuf_tensor(name, shape, dt))

        def ps(name, shape, dt):
            return es.enter_context(nc.psum_tensor(name, shape, dt))

        def newsem(name):
            return es.enter_context(nc.semaphore(name))

        dsem_n = [0]

        def dsem():
            dsem_n[0] += 1
            return DmaSem(newsem("dq%d" % dsem_n[0]))

        identb = sb("identb", [128, 128], BF16)
        onesf = sb("onesf", [128, 128], F32)
        maskb = sb("maskb", [128, 768], BF16)
        ropecs = sb("ropecs", [128, 6], F32)
        epst = sb("epst", [128, 1], F32)
        neghalf = sb("neghalf", [128, 1], F32)
        identf = sb("identf", [128, 128], F32)
        rt = sb("rt", [128, 16], F32)
        wuq_g = sb("wuq_g", [128, 2, 768], BF16)
        wuqs_g = sb("wuqs_g", [128, 2, 768], BF16)
        wukv_g = sb("wukv_g", [128, 1024], BF16)
        wsp = sb("wsp", [128, 8, 128], BF16)
        bsp = sb("bsp", [128, 8], F32)
        bspb = sb("bspb", [8, 128], BF16)
        indb = sb("indb", [8, 512], BF16)
        sgug = sb("sgug", [128, 512], F32)
        sgub = sb("sgub", [128, 512], F32)
        lng = sb("lng", [128, D], F32)
        lnb = sb("lnb", [128, D], F32)
        ckvn = sb("ckvn", [128, SEQ], BF16)
        KB = [sb("kb%d" % i, [96, SEQ], BF16) for i in range(2)]
        QT = sb("qt", [96, 8, 2048], BF16)
        MG = sb("mg", [128, 4, 2048], BF16)
        ARENA = 86528
        arena = sb("arena", [128, ARENA // 2], BF16)

        class Carver:
            def __init__(self):
                self.off = 0

            def take(self, cols, dt, parts=128):
                size = 4 if dt in (F32, I32) else 2
                self.off = (self.off + 3) // 4 * 4
                nbytes = cols * size
                assert self.off + nbytes <= ARENA, ("arena overflow", self.off, nbytes)
                v = arena[0:parts, self.off // 2:(self.off + nbytes) // 2]
                self.off += nbytes
                if dt != BF16:
                    v = v.bitcast(dt)
                return v

        PT4 = [ps("pt%d" % i, [128, 2, 512], F32) for i in range(4)]
        PSW = [PT4[0], PT4[1], PT4[2]]
        PSO1 = PT4[3][:, 0, :]
        PSG = PT4[3][:, 1, :]
        pst_banks = [PT4[3][:, 0, :].bitcast(BF16)[:, 0:512],
                     PT4[3][:, 1, :].bitcast(BF16)[:, 0:512]]

        def bank(i):
            return PT4[i // 2][:, i % 2, :]

        bank_rr = [0]

        def next_bank():
            b = bank_rr[0] % 6
            bank_rr[0] += 1
            return b, bank(b), ("ps", b)

        esems = {e: newsem("e_" + e) for e in ENGS}
        block = es.enter_context(nc.Block())

        cv = Carver()
        st_a = cv.take(2 * 768, F32)
        st_b = cv.take(2 * 768, F32)
        st_c = cv.take(1024, F32)
        st_d = cv.take(8 * 128, F32)
        st_e = cv.take(128, F32)
        qgs = cv.take(2, F32)
        kvgs = cv.take(1, F32)
        ds0 = dsem()
        ds0p = dsem()
        S.dma("pool", lambda e: e.dma_start(out=identb[:], in_=ident), ds0p, w=["identb"])
        S.dma("pool", lambda e: e.dma_start(out=maskb[:], in_=masks), ds0p, w=["maskb"])
        S.dma("pool", lambda e: e.dma_start(out=bspb[:], in_=bsp_ht), ds0p, w=["bspb"])
        S.dma("pool", lambda e: e.dma_start(out=indb[:], in_=ind_h), ds0p, w=["indb"])
        S.dma("sp", lambda e: e.dma_start(out=ropecs[:], in_=ropec), ds0, w=["ropecs"])
        S.dma("sp", lambda e: e.dma_start(out=identf[:], in_=ident), ds0, w=["identf"])
        S.dma("sp", lambda e: e.dma_start(out=st_a.rearrange("p (c n) -> p c n", c=2),
                                          in_=w_uq.rearrange("(c p) n -> p c n", p=128)), ds0, w=["st_a"])
        S.dma("sp", lambda e: e.dma_start(out=st_b.rearrange("p (c n) -> p c n", c=2),
                                          in_=w_uqs.rearrange("(c p) n -> p c n", p=128)), ds0, w=["st_b"])
        S.dma("sp", lambda e: e.dma_start(out=st_c, in_=w_ukv), ds0, w=["st_c"])
        S.dma("sp", lambda e: e.dma_start(out=st_d.rearrange("p (h t) -> p h t", h=8), in_=wspT), ds0, w=["st_d"])
        S.dma("sp", lambda e: e.dma_start(out=st_e, in_=tril), ds0, w=["st_e"])
        S.dma("sp", lambda e: e.dma_start(out=qgs, in_=qg), ds0, w=["qgs"])
        S.dma("sp", lambda e: e.dma_start(out=kvgs, in_=kvg), ds0, w=["kvgs"])
        S.dma("sp", lambda e: e.dma_start(out=bsp[:], in_=bspT), ds0, w=["bsp"])
        S.dma("sp", lambda e: e.dma_start(out=sgug[:], in_=sgu_g.partition_broadcast(128)), ds0, w=["sgug"])
        S.dma("sp", lambda e: e.dma_start(out=sgub[:], in_=sgu_b.partition_broadcast(128)), ds0, w=["sgub"])
        S.dma("sp", lambda e: e.dma_start(out=lng[:], in_=ln_g.partition_broadcast(128)), ds0, w=["lng"])
        S.dma("sp", lambda e: e.dma_start(out=lnb[:], in_=ln_b.partition_broadcast(128)), ds0, w=["lnb"])
        dscv = dsem()
        S.op("dve", lambda e: e.memset(onesf[:], 1.0), w=["onesf"])
        S.op("dve", lambda e: e.memset(epst[:], EPS), w=["epst"])
        S.op("dve", lambda e: e.memset(neghalf[:], -0.5), w=["neghalf"])
        for c in range(2):
            S.op("dve", lambda e, c=c: e.tensor_scalar(wuq_g[:, c, :], st_a[:, c * 768:(c + 1) * 768],
                                                       qgs[:, c:c + 1], None, op0=ALU.mult),
                 r=["st_a", "qgs"], w=[("wuq", c)])
            S.op("dve", lambda e, c=c: e.tensor_scalar(wuqs_g[:, c, :], st_b[:, c * 768:(c + 1) * 768],
                                                       qgs[:, c:c + 1], None, op0=ALU.mult),
                 r=["st_b", "qgs"], w=[("wuqs", c)])
        S.op("dve", lambda e: e.tensor_scalar(wukv_g[:], st_c, kvgs[:, 0:1], None, op0=ALU.mult),
             r=["st_c", "kvgs"], w=["wukv"])
        for h in range(8):
            S.op("dve", lambda e, h=h: e.tensor_tensor(wsp[:, h, :], st_d[:, h * 128:(h + 1) * 128], st_e,
                                                       op=ALU.mult), r=["st_d", "st_e"], w=[("wsp", h)])
        S.barrier()

        def phase_A(p):
            cv = Carver()
            WA = cv.take(8 * NA, BF16).rearrange("p (c n) -> p c n", c=8)
            WK = cv.take(8 * 32, BF16).rearrange("p (c n) -> p c n", c=8)
            XS = [cv.take(4 * D, BF16).rearrange("p (s d) -> p s d", s=4) for _ in range(2)]
            XT = cv.take(8 * 512, BF16).rearrange("p (c t) -> p c t", c=8)
            POSI = [cv.take(512, I32) for _ in range(2)]
            posf = cv.take(512, F32)
            tk = posf.bitcast(I32)
            TABS = [[cv.take(512, F32) for _ in range(2)] for _ in range(2)]
            tt = cv.take(512, F32)
            tkf = cv.take(512, F32)
            sq = [cv.take(512, F32) for _ in range(2)]
            rstd = cv.take(512, F32)
            dg = cv.take(512, F32)
            dg2 = cv.take(256, F32)
            rstdq = cv.take(256, F32)
            KR = [[cv.take(512, F32) for _ in range(2)] for _ in range(2)]
            CQB = [cv.take(512, BF16) for _ in range(2)]
            crsr = cv.take(512, F32)
            qtmp = [cv.take(512, F32) for _ in range(2)]
            dsW = dsem()
            dsX = [dsem(), dsem()]
            dsP = [dsem(), dsem()]
            dsD = dsem()
            groups = list(range(8 * p, 8 * p + NGROUPS))

            def load_group(G):
                i = G % 2
                S.dma("pool", lambda e: e.dma_start(
                    out=XS[i], in_=xs[G * 512:(G + 1) * 512, :].rearrange("(s p) d -> p s d", p=128)),
                    dsX[i], w=[("XS", i)])
                S.dma("sp", lambda e: e.dma_start(
                    out=POSI[i], in_=pos[0:1, G * 512:(G + 1) * 512].partition_broadcast(128)),
                    dsP[i], w=[("POSI", i)])

            load_group(groups[0])
            if p == 0:
                dsW0 = dsem()
                S.dma("pool", lambda e: e.dma_start(out=WA, in_=w_in.rearrange("(c p) n -> p c n", p=128)[:, :, 0:NA]),
                      dsW0, w=["WA"])
                S.dma("pool", lambda e: e.dma_start(out=WK, in_=w_krs.rearrange("(c p) n -> p c n", p=128)),
                      dsW0, w=["WK"])
            else:
                S.dma("sp", lambda e: e.dma_start(out=WA, in_=w_in_bf.rearrange("(c p) n -> p c n", p=128)[:, :, 0:NA]),
                      dsW, w=["WA"])
                S.dma("sp", lambda e: e.dma_start(out=WK, in_=w_krs_bf.rearrange("(c p) n -> p c n", p=128)),
                      dsW, w=["WK"])
            if len(groups) > 1:
                load_group(groups[1])
            if p == 0:
                S.dma("pool", lambda e: e.dma_start(out=w_in_bf, in_=w_in), dscv, w=["w_in_bf"])
                S.dma("pool", lambda e: e.dma_start(out=w_krs_bf, in_=w_krs), dscv, w=["w_krs_bf"])
                S.dma("pool", lambda e: e.dma_start(out=w_out_bf, in_=w_out), dscv, w=["w_out_bf"])

            def make_tables(G):
                i = G % 2
                S.op("act", lambda e: e.activation(tt, POSI[i], AF.Identity, scale=ropecs[:, 0:1]),
                     r=[("POSI", i), "ropecs"], w=["tt"])
                S.op("dve", lambda e: e.tensor_copy(tk, tt), r=["tt"], w=["posf"])
                S.op("dve", lambda e: e.tensor_copy(tkf, tk), r=["posf"], w=["tkf"])
                S.op("dve", lambda e: e.tensor_tensor(tt, tt, tkf, op=ALU.subtract), r=["tt", "tkf"], w=["tt"])
                S.op("dve", lambda e: e.scalar_tensor_tensor(out=tt, in0=tt, scalar=0.5, in1=tt,
                                                             op0=ALU.is_gt, op1=ALU.subtract),
                     r=["tt"], w=["tt"])
                S.op("act", lambda e: e.activation(TABS[i][1], tt, AF.Sin, scale=ropecs[:, 1:2]),
                     r=["tt", "ropecs"], w=[("tab", i, 1)])
                S.op("act", lambda e: e.activation(tkf, tt, AF.Abs), r=["tt"], w=["tkf"])
                S.op("act", lambda e: e.activation(TABS[i][0], tkf, AF.Sin, bias=ropecs[:, 3:4], scale=ropecs[:, 2:3]),
                     r=["tkf", "ropecs"], w=[("tab", i, 0)])

            def stage_x1(G):
                i = G % 2
                xt = XT
                for k in range(8):
                    half = k % 2
                    pt = pst_banks[half]
                    for s_ in range(4):
                        S.op("pe", lambda e, pt=pt, s_=s_, k=k: e.transpose(
                            pt[:, s_ * 128:(s_ + 1) * 128], XS[i][:, s_, k * 128:(k + 1) * 128], identb[:]),
                            r=[("XS", i), "identb"], w=[("pst", half)])
                    if k % 2 == 0:
                        S.op("act", lambda e, pt=pt, k=k: e.copy(xt[:, k, :], pt),
                             r=[("pst", half)], w=[("XT", k)])
                    else:
                        S.op("dve", lambda e, pt=pt, k=k: e.tensor_copy(xt[:, k, :], pt),
                             r=[("pst", half)], w=[("XT", k)])

            def stage_x2(G):
                i = G % 2
                tok = slice(G * 512, (G + 1) * 512)
                lt = slice((G - 8 * p) * 256, (G - 8 * p) * 256 + 256)
                xt = XT
                bc, pc, kc = next_bank()
                for k in range(8):
                    S.op("pe", lambda e, pc=pc, k=k: e.matmul(pc, WA[:, k, 256:384], xt[:, k, :],
                                                              start=(k == 0), stop=(k == 7)),
                         r=["WA", ("XT", k)], w=[kc])
                br, pr, kr = next_bank()
                for k in range(8):
                    S.op("pe", lambda e, pr=pr, k=k: e.matmul(pr[64:96, :], WA[:, k, 384:416], xt[:, k, :],
                                                              start=(k == 0), stop=(k == 7), tile_position=(0, 64)),
                         r=["WA", ("XT", k)], w=[kr])
                bs, psw, ks = next_bank()
                for k in range(8):
                    S.op("pe", lambda e, psw=psw, k=k: e.matmul(psw[64:96, :], WK[:, k, :], xt[:, k, :],
                                                                start=(k == 0), stop=(k == 7), tile_position=(0, 64)),
                         r=["WK", ("XT", k)], w=[ks])
                S.op("act", lambda e, pc=pc: e.copy(ckvn[:, tok], pc), r=[kc], w=[("ckvn", G)])
                S.op("act", lambda e, pc=pc: e.activation(sq[0], pc, AF.Square), r=[kc], w=[("sq", 0)])
                S.op("act", lambda e, pr=pr: e.copy(KR[i][0][64:96, :], pr[64:96, :]), r=[kr], w=[("KR", i, 0)])
                S.op("act", lambda e, psw=psw: e.copy(KR[i][1][64:96, :], psw[64:96, :]), r=[ks], w=[("KR", i, 1)])
                bcq, pcq, kcq = next_bank()
                for c in range(2):
                    for k in range(8):
                        S.op("pe", lambda e, pcq=pcq, c=c, k=k: e.matmul(
                            pcq[:, c * 256:(c + 1) * 256], WA[:, k, c * 128:(c + 1) * 128], xt[:, k, 0:256],
                            start=(k == 0), stop=(k == 7)),
                            r=["WA", ("XT", k)], w=[kcq])
                S.op("act", lambda e, pcq=pcq: e.copy(CQB[i], pcq), r=[kcq], w=[("cqb", i)])
                S.op("act", lambda e, pcq=pcq: e.activation(sq[1], pcq, AF.Square), r=[kcq], w=[("sq", 1)])
                for cc in range(2):
                    bz, pz, kz = next_bank()
                    for c2 in range(2):
                        c = cc * 2 + c2
                        for k in range(8):
                            S.op("pe", lambda e, pz=pz, c=c, c2=c2, k=k: e.matmul(
                                pz[:, c2 * 256:(c2 + 1) * 256], WA[:, k, 416 + c * 128:416 + (c + 1) * 128],
                                xt[:, k, 0:256], start=(k == 0), stop=(k == 7)),
                                r=["WA", ("XT", k)], w=[kz])
                    S.op("act", lambda e, pz=pz, cc=cc: e.activation(
                        MG[:, 2 * cc:2 * cc + 2, lt], pz.rearrange("p (c t) -> p c t", c=2), AF.Silu),
                        r=[kz], w=[("MG", cc, G)])
                bq, pq, kq = next_bank()
                for j in range(4):
                    S.op("pe", lambda e, pq=pq, j=j: e.matmul(pq[:, j:j + 1], sq[0][:, j * 128:(j + 1) * 128], onesf[:, 0:1],
                                                              start=True, stop=True, skip_group_check=True),
                         r=[("sq", 0), "onesf"], w=[kq])
                for j in range(2):
                    for c in range(2):
                        S.op("pe", lambda e, pq=pq, c=c, j=j: e.matmul(
                            pq[:, 4 + j:5 + j], sq[1][:, c * 256 + j * 128:c * 256 + (j + 1) * 128], onesf[:, 0:1],
                            start=(c == 0), stop=(c == 1), skip_group_check=True),
                            r=[("sq", 1), "onesf"], w=[kq])
                ro = 8 * i
                S.op("act", lambda e, pq=pq: e.activation(rt[:, ro:ro + 4], pq[:, 0:4], AF.Identity, bias=epst[:, 0:1],
                                                          scale=1.0 / 128.0), r=[kq, "epst"], w=[("rt", i)])
                S.op("act", lambda e, pq=pq: e.activation(rt[:, ro + 4:ro + 6], pq[:, 4:6], AF.Identity, bias=epst[:, 0:1],
                                                          scale=1.0 / 256.0), r=[kq, "epst"], w=[("rt", i)])
                S.op("pool", lambda e: e.tensor_tensor(rt[:, ro:ro + 6], rt[:, ro:ro + 6],
                                                       neghalf[:, 0:1].to_broadcast([128, 6]), op=ALU.pow),
                     r=[("rt", i), "neghalf"], w=[("rt", i)])

            def stage_yn(G):
                i = G % 2
                tok = slice(G * 512, (G + 1) * 512)
                lt = slice((G - 8 * p) * 256, (G - 8 * p) * 256 + 256)
                tabs = TABS[i]
                tabk = [("tab", i, 0), ("tab", i, 1)]
                ro = 8 * i
                cqb = CQB[i]
                for j in range(4):
                    S.op("dve", lambda e, j=j: e.tensor_scalar(dg[:, j * 128:(j + 1) * 128], identf[:], rt[:, ro + j:ro + j + 1],
                                                               None, op0=ALU.mult), r=[("rt", i), "identf"], w=["dg"])
                for j in range(2):
                    S.op("dve", lambda e, j=j: e.tensor_scalar(dg2[:, j * 128:(j + 1) * 128], identf[:],
                                                               rt[:, ro + 4 + j:ro + 5 + j], None, op0=ALU.mult),
                         r=[("rt", i), "identf"], w=["dg2"])
                bb, pbc, kbc = next_bank()
                for j in range(4):
                    S.op("pe", lambda e, pbc=pbc, j=j: e.matmul(pbc[:, j * 128:(j + 1) * 128], onesf[:],
                                                                dg[:, j * 128:(j + 1) * 128], start=True, stop=True,
                                                                skip_group_check=True),
                         r=["dg", "onesf"], w=[kbc])
                bb2, pbc2, kbc2 = next_bank()
                for j in range(2):
                    S.op("pe", lambda e, pbc2=pbc2, j=j: e.matmul(pbc2[:, j * 128:(j + 1) * 128], onesf[:],
                                                                  dg2[:, j * 128:(j + 1) * 128], start=True, stop=True,
                                                                  skip_group_check=True),
                         r=["dg2", "onesf"], w=[kbc2])
                S.op("act", lambda e, pbc=pbc: e.copy(rstd, pbc), r=[kbc], w=["rstd"])
                S.op("act", lambda e, pbc2=pbc2: e.copy(rstdq[:, 0:256], pbc2[:, 0:256]), r=[kbc2], w=["rstdq"])
                S.op("pool", lambda e: e.tensor_tensor(ckvn[:, tok], ckvn[:, tok], rstd, op=ALU.mult),
                     r=[("ckvn", G), "rstd"], w=[("ckvn", G)])
                kr0, kr1 = KR[i]
                S.op("dve", lambda e: e.tensor_tensor(kr0[64:96, :], kr0[64:96, :], tabs[0][64:96, :], op=ALU.mult),
                     r=[("KR", i, 0), tabk[0]], w=[("KR", i, 0)])
                S.op("dve", lambda e: e.tensor_tensor(kr1[64:96, :], kr1[64:96, :], tabs[1][64:96, :], op=ALU.mult),
                     r=[("KR", i, 1), tabk[1]], w=[("KR", i, 1)])
                S.op("pool", lambda e: e.tensor_tensor(KB[0][64:96, tok], kr0[64:96, :], kr1[64:96, :], op=ALU.add),
                     r=[("KR", i, 0), ("KR", i, 1)], w=[("kbr", 0, G)])
                S.op("pool", lambda e: e.tensor_copy(KB[1][64:96, tok], KB[0][64:96, tok]),
                     r=[("kbr", 0, G)], w=[("kbr", 1, G)])
                for ti in range(2):
                    S.op("dve", lambda e, ti=ti: e.tensor_tensor(crsr[0:96, ti * 256:(ti + 1) * 256],
                                                                 tabs[ti][0:96, 0:256], rstdq[0:96, 0:256], op=ALU.mult),
                         r=[tabk[ti], "rstdq"], w=["crsr"])

            def stage_yq(G):
                i = G % 2
                lt = slice((G - 8 * p) * 256, (G - 8 * p) * 256 + 256)
                tok = slice(G * 512, (G + 1) * 512)
                cqb = CQB[i]
                for h in range(8):
                    bh, ph_, kh = next_bank()
                    for half, wsrc, wk in ((0, wuq_g, "wuq"), (1, wuqs_g, "wuqs")):
                        for c in range(2):
                            S.op("pe", lambda e, ph_=ph_, half=half, wsrc=wsrc, c=c, h=h: e.matmul(
                                ph_[0:96, half * 256:(half + 1) * 256], wsrc[:, c, h * 96:(h + 1) * 96],
                                cqb[:, c * 256:(c + 1) * 256], start=(c == 0), stop=(c == 1)),
                                r=[(wk, c), ("cqb", i)], w=[kh])
                    qt_ = qtmp[h % 2]
                    S.op("dve", lambda e, ph_=ph_, qt_=qt_: e.tensor_tensor(qt_[0:96, :], ph_[0:96, :], crsr[0:96, :],
                                                                            op=ALU.mult),
                         r=[kh, "crsr"], w=[("qtmp", h % 2)])
                    S.op("pool", lambda e, qt_=qt_, h=h: e.tensor_tensor(
                        QT[:, h, lt], qt_[0:96, 0:256], qt_[0:96, 256:512], op=ALU.add),
                        r=[("qtmp", h % 2)], w=[("QT", h, G)])
                if debug and G == 8 * p + 1 and p == 0:
                    dd = qtmp[0]
                    kk = ("qtmp", 0)
                    S.op("dve", lambda e: e.tensor_copy(dd, ckvn[:, tok]), r=[("ckvn", G), kk], w=[kk])
                    S.dma("sp", lambda e: e.dma_start(out=dbg["d_ckvn"], in_=dd), dsD, r=[kk])
                    S.op("dve", lambda e: e.tensor_copy(dd[64:96, :], KB[1][64:96, tok]),
                         r=[("kbr", 1, G), kk], w=[kk])
                    S.dma("sp", lambda e: e.dma_start(out=dbg["d_krope"], in_=dd[64:96, :]), dsD, r=[kk])
                    S.op("dve", lambda e: e.tensor_copy(dd[0:96, 0:256], QT[:, 0, lt]), r=[("QT", 0, G), kk], w=[kk])
                    S.op("dve", lambda e: e.tensor_copy(dd[0:96, 256:512], QT[:, 5, lt]), r=[("QT", 5, G), kk], w=[kk])
                    S.dma("sp", lambda e: e.dma_start(out=dbg["d_q0"], in_=dd[0:96, :]), dsD, r=[kk])
                    S.op("dve", lambda e: e.tensor_copy(dd[:, 0:256], MG[:, 0, lt]), r=[("MG", 0, G), kk], w=[kk])
                    S.op("dve", lambda e: e.tensor_copy(dd[:, 256:512], MG[:, 3, lt]), r=[("MG", 1, G), kk], w=[kk])
                    S.dma("sp", lambda e: e.dma_start(out=dbg["d_g"], in_=dd), dsD, r=[kk])

            ng = len(groups)
            stage_x1(groups[0])
            make_tables(groups[0])
            stage_x2(groups[0])
            if ng > 2:
                load_group(groups[2])
            for gi, G in enumerate(groups):
                nxt = groups[gi + 1] if gi + 1 < ng else None
                if nxt is not None:
                    stage_x1(nxt)
                stage_yn(G)
                if nxt is not None:
                    make_tables(nxt)
                    stage_x2(nxt)
                stage_yq(G)
                if gi + 3 < ng:
                    load_group(groups[gi + 3])

        def phase_B(p):
            cv = Carver()
            nslots = 32 * (p + 1)
            VB = [cv.take(64 * 65, BF16).rearrange("p (s d) -> p s d", s=64),
                  cv.take(64 * 128, BF16).rearrange("p (s d) -> p s d", s=64)]
            PT = [cv.take(1024, BF16).rearrange("p (j n) -> p j n", j=2) for _ in range(3)]
            osb = [cv.take(512, F32) for _ in range(2)]
            rec = [cv.take(512, F32) for _ in range(2)]
            otmp = [cv.take(512, F32) for _ in range(2)]
            dsD = dsem()
            S.op("pool", lambda e: e.memset(VB[0][:, :, 64:65], 1.0), w=[("vbc", 0)])
            S.op("pool", lambda e: e.memset(VB[1][:, :, 0:64], 0.0), w=[("vbc", 1)])
            S.op("pool", lambda e: e.memset(VB[1][:, :, 0:1], 1.0), w=[("vbc", 1)])

            def vaug(b, h, slot):
                return VB[b][:, slot, :]

            def vdst(b, c):
                if b == 0:
                    return VB[0][:, c * 8:(c + 1) * 8, 0:64]
                return VB[1][:, c * 8:(c + 1) * 8, 64:128]

            def gen_chunks(h, upfront=False):
                b = h % 2
                out = []
                cnt = [0]

                def target():
                    n = cnt[0]
                    cnt[0] += 1
                    if not upfront:
                        return PSG, "psg", "dve"
                    t = n % 3
                    return PSW[t][:, (n // 3) % 2, :], ("psw", t), ("dve" if n % 2 == 0 else "act")

                for c in range(nslots // 4):
                    def kchunk(c=c):
                        pg, kg, eng = target()
                        tok = slice(c * 512, (c + 1) * 512)
                        S.op("pe", lambda e: e.matmul(pg[0:64, :], wukv_g[:, h * 128:h * 128 + 64], ckvn[:, tok],
                                                      start=True, stop=True), r=["wukv"], w=[kg])
                        if eng == "dve":
                            S.op("dve", lambda e: e.tensor_copy(KB[b][0:64, tok], pg[0:64, :]),
                                 r=[kg], w=[("kb", b, c)])
                        else:
                            S.op("act", lambda e: e.copy(KB[b][0:64, tok], pg[0:64, :]),
                                 r=[kg], w=[("kb", b, c)])
                    out.append(kchunk)
                for c in range(nslots // 8):
                    def vchunk(c=c):
                        pg, kg, eng = target()
                        for s in range(8):
                            slot = c * 8 + s
                            S.op("pe", lambda e, s=s, slot=slot: e.matmul(
                                pg[:, s * 64:(s + 1) * 64], ckvn[:, slot * 128:(slot + 1) * 128],
                                wukv_g[:, h * 128 + 64:h * 128 + 128], start=True, stop=True),
                                r=["wukv"], w=[kg])
                        if eng == "dve":
                            S.op("dve", lambda e: e.tensor_copy(vdst(b, c), pg.rearrange("p (s d) -> p s d", s=8)),
                                 r=[kg], w=[("vb", b, c)])
                        else:
                            S.op("act", lambda e: e.copy(vdst(b, c), pg.rearrange("p (s d) -> p s d", s=8)),
                                 r=[kg], w=[("vb", b, c)])
                    out.append(vchunk)
                return out

            steps = []
            for h in range(8):
                for g in range(4):
                    gg = 4 * p + g
                    nfull = 4 * gg
                    for i in range(nfull):
                        steps.append(dict(h=h, g=g, slots=(2 * i, 2 * i + 1), c0=0, masks=None))
                    base = 8 * gg
                    tails = [
                        ((base + 0, base + 1), 0, ((0, 128, 0), (256, 128, 128))),
                        ((base + 2, base + 3), 0, ((384, 128, 0), (640, 128, 128))),
                        ((base + 4, base + 5), 256, ((0, 128, 256), (256, 128, 384))),
                        ((base + 6, base + 7), 256, ((384, 128, 256), (640, 128, 384))),
                    ]
                    for slots, c0, mk in tails:
                        steps.append(dict(h=h, g=g, slots=slots, c0=c0, masks=mk))
                    steps[-1]["last"] = True
                    steps[-(nfull + 4)]["first"] = True
            for fn in gen_chunks(0, upfront=True):
                fn()
            pending_gen = []
            pending_epi = []
            pending_rcp = []
            n_steps = len(steps)
            if KB_STEPS >= 0:
                n_steps = KB_STEPS
            ocount = [0]

            def emit_qk(si):
                stp = steps[si]
                h, g, c0 = stp["h"], stp["g"], stp["c0"]
                b = h % 2
                wtile = PSW[si % 3]
                for j, slot in enumerate(stp["slots"]):
                    has_mask = stp["masks"] is not None
                    cj = stp["masks"][j][2] if has_mask else c0
                    S.op("pe", lambda e, j=j, slot=slot, has_mask=has_mask, cj=cj: e.matmul(
                        wtile[:, j, cj:512], KB[b][:, slot * 128:(slot + 1) * 128],
                        QT[:, h, g * 512 + cj:(g + 1) * 512], start=True, stop=(not has_mask or bool(KB_NOMASK)),
                        skip_group_check=True),
                        r=[("kb", b, slot // 4)], w=[("psw", si % 3)])
                    if has_mask and not KB_NOMASK:
                        mo, mw, mc = stp["masks"][j]
                        S.op("pe", lambda e, j=j, mo=mo, mw=mw, mc=mc: e.matmul(
                            wtile[:, j, mc:mc + mw], identb[:], maskb[:, mo:mo + mw], start=False, stop=True,
                            skip_group_check=True),
                            r=[], w=[("psw", si % 3)])

            def emit_exp(si):
                stp = steps[si]
                c0 = stp["c0"]
                S.op("act", lambda e: e.activation(PT[si % 3][:, :, c0:512], PSW[si % 3][:, :, c0:512], AF.Exp,
                                                   scale=SCALE),
                     r=[("psw", si % 3)], w=[("pt", si % 3)])

            def emit_pv(si):
                stp = steps[si]
                h, g, c0 = stp["h"], stp["g"], stp["c0"]
                b = h % 2
                if stp.get("first"):
                    ocount[0] += 1
                o = ocount[0] % 2
                for j, slot in enumerate(stp["slots"]):
                    first = bool(stp.get("first")) and j == 0
                    last = bool(stp.get("last")) and j == 1
                    mrows = 65 if b == 0 else 128
                    cj = stp["masks"][j][2] if stp["masks"] is not None else c0
                    S.op("pe", lambda e, j=j, slot=slot, first=first, last=last, mrows=mrows, cj=cj: e.matmul(
                        PSO1[0:mrows, cj:512], vaug(b, h, slot), PT[si % 3][:, j, cj:512], start=first, stop=last,
                        skip_group_check=True),
                        r=[("pt", si % 3), ("vb", b, slot // 8), ("vbc", b)], w=["pso"])
                if stp.get("last"):
                    mrows = 65 if b == 0 else 128
                    S.op("dve", lambda e, mrows=mrows: e.tensor_copy(osb[o][0:mrows, :], PSO1[0:mrows, :]),
                         r=["pso"], w=[("osb", o)])
                    dr = 64 if h % 2 == 0 else 0
                    for q4 in range(4):
                        pending_rcp.append((si + q4, o, dr, q4))
                    pending_epi.append((si + 6, h, g, o))

            def emit_epilogue(h, g, o):
                if KB_NOEPI:
                    return
                dr = 64 if h % 2 == 0 else 0
                rows = slice(0, 64) if h % 2 == 0 else slice(64, 128)
                r_ = rec[o]
                ot = otmp[o]
                ob_ = osb[o]
                S.op("pe", lambda e: e.matmul(PSG, onesf[dr:dr + 1, :], r_[dr:dr + 1, :], start=True, stop=True),
                     r=[("rec", o)], w=["psg"])
                S.op("dve", lambda e: e.tensor_tensor(ot[rows, :], ob_[rows, :], PSG[rows, :], op=ALU.mult),
                     r=[("osb", o), "psg"], w=[("otmp", o)])
                mg = MG[rows, h // 2, g * 512:(g + 1) * 512]
                S.op("pool", lambda e: e.tensor_tensor(mg, ot[rows, :], mg, op=ALU.mult),
                     r=[("otmp", o)], w=[("mgo", h, g)])

            if n_steps > 0:
                emit_qk(0)
            if n_steps > 1:
                emit_qk(1)
            cur_h = 0
            gen_every = 3 if p == 0 else 4
            since_gen = 0
            for si in range(n_steps):
                emit_exp(si)
                if si + 2 < n_steps:
                    emit_qk(si + 2)
                if not KB_NOPV:
                    emit_pv(si)
                while pending_rcp and pending_rcp[0][0] <= si:
                    _, ro_, rdr, q4 = pending_rcp.pop(0)
                    cs = slice(q4 * 128, (q4 + 1) * 128)
                    S.op("dve", lambda e, ro_=ro_, rdr=rdr, cs=cs: e.reciprocal(rec[ro_][rdr:rdr + 1, cs],
                                                                               osb[ro_][rdr:rdr + 1, cs]),
                         r=[("osb", ro_)], w=[("rec", ro_)])
                while pending_epi and pending_epi[0][0] <= si:
                    _, eh, eg, eo = pending_epi.pop(0)
                    emit_epilogue(eh, eg, eo)
                h = steps[si]["h"]
                if steps[si].get("first") and steps[si]["g"] == 0 and h + 1 < 8:
                    pending_gen = gen_chunks(h + 1)
                    since_gen = 0
                since_gen += 1
                if pending_gen and since_gen >= gen_every:
                    pending_gen.pop(0)()
                    since_gen = 0
                if steps[si].get("last") and steps[si]["g"] == 3:
                    while pending_gen:
                        pending_gen.pop(0)()
            while pending_rcp:
                _, ro_, rdr, q4 = pending_rcp.pop(0)
                cs = slice(q4 * 128, (q4 + 1) * 128)
                S.op("dve", lambda e, ro_=ro_, rdr=rdr, cs=cs: e.reciprocal(rec[ro_][rdr:rdr + 1, cs],
                                                                           osb[ro_][rdr:rdr + 1, cs]),
                     r=[("osb", ro_)], w=[("rec", ro_)])
            while pending_epi:
                _, eh, eg, eo = pending_epi.pop(0)
                emit_epilogue(eh, eg, eo)
            if debug and p == 0:
                dd = cv.take(512, F32)
                S.op("dve", lambda e: e.tensor_copy(dd[0:96, :], KB[1][:, 512:1024]), r=[("kb", 1, 1)], w=["dd"])
                S.dma("sp", lambda e: e.dma_start(out=dbg["d_kb"], in_=dd[0:96, :]), dsD, r=["dd"])
                S.op("dve", lambda e: e.tensor_copy(dd.rearrange("p (s d) -> p s d", s=8), VB[1][:, 8:16, 64:128]),
                     r=[("vb", 1, 1), "dd"], w=["dd"])
                S.dma("sp", lambda e: e.dma_start(out=dbg["d_vb"], in_=dd), dsD, r=["dd"])
                S.op("dve", lambda e: e.tensor_copy(dd[:, 0:256], MG[:, 0, 256:512]),
                     r=[("mgo", 0, 0), ("mgo", 1, 0), "dd"], w=["dd"])
                S.op("dve", lambda e: e.tensor_copy(dd[:, 256:512], MG[:, 3, 1024 + 256:1024 + 512]),
                     r=[("mgo", 6, 2), ("mgo", 7, 2), "dd"], w=["dd"])
                S.dma("sp", lambda e: e.dma_start(out=dbg["d_merged"], in_=dd), dsD, r=["dd"])

        WC_bytes = 8 * 1536 * 2 + 8 * 1024 * 2

        def wc_views():
            base = (ARENA - WC_bytes) // 2
            wci = arena[:, base:base + 8 * 1536].rearrange("p (c n) -> p c n", c=8)
            wco = arena[:, base + 8 * 1536:base + 8 * 1536 + 8 * 1024].rearrange("p (c n) -> p c n", c=8)
            return wci, wco

        dsWC = dsem()

        def load_WC():
            wci, wco = wc_views()
            S.dma("sp", lambda e: e.dma_start(out=wci, in_=w_in_bf.rearrange("(c p) n -> p c n", p=128)[:, :, NA:DIN]),
                  dsWC, w=["WCI"])
            S.dma("sp", lambda e: e.dma_start(out=wco, in_=w_out_bf.rearrange("(c p) n -> p c n", p=128)),
                  dsWC, w=["WCO"])

        dsY = [dsem(), dsem()]

        def phase_C(p):
            cv = Carver()
            wci, wco = wc_views()
            XR = [cv.take(D, F32) for _ in range(2)]
            XB = [cv.take(D, BF16) for _ in range(2)]
            XTC = [cv.take(8 * 128, BF16).rearrange("p (c t) -> p c t", c=8) for _ in range(2)]
            UG = [cv.take(512, F32) for _ in range(2)]
            VG = [cv.take(512, F32) for _ in range(2)]
            ZB = [cv.take(512, F32) for _ in range(2)]
            VN = [cv.take(512, BF16) for _ in range(2)]
            ob = cv.take(512, BF16)
            obT = cv.take(512, BF16)
            hh = cv.take(D, F32)
            YO = [cv.take(D, F32) for _ in range(2)]
            st1 = [cv.take(16, F32) for _ in range(2)]
            st2 = cv.take(24, F32)
            assert cv.off <= ARENA - WC_bytes, ("phase C arena", cv.off)
            dsXR = [dsem(), dsem()]
            dsXB = [dsem(), dsem()]

            def load_block(t):
                i = t % 2
                gslot = 4 * (t // 2) + (t % 2)
                rows = slice(gslot * 128, (gslot + 1) * 128)
                S.dma("sp", lambda e: e.dma_start(out=XR[i], in_=xs[rows, :]), dsXR[i], w=[("XR", i)])
                S.dma("pool", lambda e: e.dma_start(out=XB[i], in_=xs[rows, :]), dsXB[i], w=[("XB", i)])

            def s1a(t):
                i = t % 2
                xtc = XTC[i]
                for hf in range(2):
                    pt = pst_banks[hf]
                    for kk in range(4):
                        k = hf * 4 + kk
                        S.op("pe", lambda e, pt=pt, kk=kk, k=k: e.transpose(
                            pt[:, kk * 128:(kk + 1) * 128], XB[i][:, k * 128:(k + 1) * 128], identb[:]),
                            r=[("XB", i), "identb"], w=[("pst", hf)])
                    dst = xtc[:, hf * 4:hf * 4 + 4, :]
                    if hf == 0:
                        S.op("act", lambda e, pt=pt, dst=dst: e.copy(dst, pt.rearrange("p (c t) -> p c t", c=4)),
                             r=[("pst", hf)], w=[("XTC", i, hf)])
                    else:
                        S.op("dve", lambda e, pt=pt, dst=dst: e.tensor_copy(dst, pt.rearrange("p (c t) -> p c t", c=4)),
                             r=[("pst", hf)], w=[("XTC", i, hf)])

            def s1b_mm(t):
                i = t % 2
                xtc = XTC[i]
                xk = [("XTC", i, 0), ("XTC", i, 1)]
                ug, vg, zb = UG[i], VG[i], ZB[i]
                th = hh[:, 0:512]
                pss = {}
                for j in (1, 0, 2):
                    bb, pb, kb_ = next_bank()
                    pss[j] = (pb, kb_)
                    for k in range(8):
                        S.op("pe", lambda e, pb=pb, j=j, k=k: e.matmul(
                            pb, xtc[:, k, :], wci[:, k, j * 512:(j + 1) * 512], start=(k == 0), stop=(k == 7)),
                            r=xk + ["WCI"], w=[kb_])
                S.op("act", lambda e: e.activation(vg, pss[1][0], AF.Gelu), r=[pss[1][1]], w=[("vg", i)])
                S.op("act", lambda e: e.activation(ug, pss[0][0], AF.Gelu), r=[pss[0][1]], w=[("ug", i)])
                S.op("act", lambda e: e.activation(th, pss[2][0], AF.Tanh, scale=0.5), r=[pss[2][1]], w=[("hh", 0)])
                S.op("act", lambda e: e.copy(zb, pss[2][0]), r=[pss[2][1]], w=[("zb", i)])
                S.op("pool", lambda e: e.tensor_tensor(th, th, zb, op=ALU.mult), r=[("hh", 0), ("zb", i)], w=[("hh", 0)])
                S.op("pool", lambda e: e.tensor_tensor(zb, th, zb, op=ALU.add), r=[("hh", 0), ("zb", i)], w=[("zb", i)])
                S.op("pool", lambda e: e.tensor_tensor(ug, ug, zb, op=ALU.mult), r=[("ug", i), ("zb", i)], w=[("ug", i)])

            def s1b_ln(t):
                i = t % 2
                vg, vn, st = VG[i], VN[i], st1[i]
                S.op("dve", lambda e: e.bn_stats(st[:, 0:6], vg), r=[("vg", i)], w=[("st1a", i)])
                S.op("dve", lambda e: e.bn_aggr(st[:, 6:8], st[:, 0:6]), r=[("st1a", i)], w=[("st1b", i)])
                S.op("dve", lambda e: e.tensor_scalar(st[:, 8:9], st[:, 7:8], EPS, None, op0=ALU.add),
                     r=[("st1b", i)], w=[("st1c", i)])
                S.op("pool", lambda e: e.tensor_tensor(st[:, 8:9], st[:, 8:9], neghalf[:, 0:1], op=ALU.pow),
                     r=[("st1c", i), "neghalf"], w=[("st1c", i)])
                S.op("dve", lambda e: e.scalar_tensor_tensor(out=vg, in0=vg, scalar=st[:, 6:7], in1=sgug[:],
                                                             op0=ALU.subtract, op1=ALU.mult),
                     r=[("vg", i), ("st1b", i), "sgug"], w=[("vg", i)])
                S.op("dve", lambda e: e.scalar_tensor_tensor(out=vn, in0=vg, scalar=st[:, 8:9], in1=sgub[:],
                                                             op0=ALU.mult, op1=ALU.add),
                     r=[("vg", i), ("st1c", i), "sgub"], w=[("vn", i)])

            def s2a(t):
                i = t % 2
                ug, vn = UG[i], VN[i]
                bsv, psv, ksv = next_bank()
                S.op("pe", lambda e: e.matmul(psv, bspb[:], indb[:], start=True, stop=False, skip_group_check=True),
                     r=["bspb", "indb"], w=[ksv])
                for h in range(8):
                    S.op("pe", lambda e, h=h: e.matmul(psv[:, h * 64:(h + 1) * 64], wsp[:, h, :],
                                                      vn[:, h * 64:(h + 1) * 64], start=False, stop=(h == 7),
                                                      skip_group_check=True),
                         r=[("wsp", h), ("vn", i)], w=[ksv])
                S.op("dve", lambda e: e.scalar_tensor_tensor(out=ob, in0=psv, scalar=0.5, in1=ug, op0=ALU.mult, op1=ALU.mult),
                     r=[ksv, ("ug", i)], w=["ob"])

            def s2b(t):
                pt = pst_banks[0]
                for c in range(4):
                    S.op("pe", lambda e, c=c: e.transpose(pt[:, c * 128:(c + 1) * 128],
                                                         ob[:, c * 128:(c + 1) * 128], identb[:]),
                         r=["ob", "identb"], w=[("pst", 0)])
                S.op("act", lambda e: e.copy(obT, pt), r=[("pst", 0)], w=["obT"])

            def s2c(t):
                i = t % 2
                lt = slice((t - 16 * p) * 128, (t - 16 * p + 1) * 128)
                outs = []
                for nh in range(2):
                    bo, po, ko = next_bank()
                    outs.append((po, ko))
                    for c in range(8):
                        if c < 4:
                            lhsT = MG[:, c, lt]
                            rr = []
                        else:
                            lhsT = obT[:, (c - 4) * 128:(c - 3) * 128]
                            rr = ["obT"]
                        S.op("pe", lambda e, po=po, lhsT=lhsT, c=c, nh=nh: e.matmul(
                            po, lhsT, wco[:, c, nh * 512:(nh + 1) * 512], start=(c == 0), stop=(c == 7)),
                            r=rr + ["WCO"], w=[ko])
                for nh in range(2):
                    po, ko = outs[nh]
                    S.op("dve", lambda e, po=po, nh=nh: e.scalar_tensor_tensor(
                        out=hh[:, nh * 512:(nh + 1) * 512], in0=XR[i][:, nh * 512:(nh + 1) * 512], scalar=ALPHA,
                        in1=po, op0=ALU.mult, op1=ALU.add),
                        r=[ko, ("XR", i)], w=[("hh", nh)])
                    S.op("dve", lambda e, nh=nh: e.bn_stats(st2[:, 6 * nh:6 * nh + 6], hh[:, nh * 512:(nh + 1) * 512]),
                         r=[("hh", nh)], w=[("st2a", nh)])
                S.op("dve", lambda e: e.bn_aggr(st2[:, 12:14], st2[:, 0:12]), r=[("st2a", 0), ("st2a", 1)], w=["st2b"])
                S.op("dve", lambda e: e.tensor_scalar(st2[:, 14:15], st2[:, 13:14], EPS, None, op0=ALU.add),
                     r=["st2b"], w=["st2c"])
                S.op("pool", lambda e: e.tensor_tensor(st2[:, 14:15], st2[:, 14:15], neghalf[:, 0:1], op=ALU.pow),
                     r=["st2c", "neghalf"], w=["st2c"])
                S.op("dve", lambda e: e.scalar_tensor_tensor(out=hh, in0=hh, scalar=st2[:, 12:13], in1=lng[:],
                                                             op0=ALU.subtract, op1=ALU.mult),
                     r=[("hh", 0), ("hh", 1), "st2b", "lng"], w=[("hh", 0), ("hh", 1)])
                yo = YO[i]
                S.op("dve", lambda e: e.scalar_tensor_tensor(out=yo, in0=hh, scalar=st2[:, 14:15], in1=lnb[:],
                                                             op0=ALU.mult, op1=ALU.add),
                     r=[("hh", 0), ("hh", 1), "st2c", "lnb"], w=[("YO", i)])
                S.dma("sp", lambda e: e.dma_start(out=y[t * 128:(t + 1) * 128, :], in_=yo),
                      dsY[i], r=[("YO", i)])

            def load_xb(t):
                i = t % 2
                gslot = 4 * (t // 2) + (t % 2)
                rows = slice(gslot * 128, (gslot + 1) * 128)
                S.dma("pool", lambda e: e.dma_start(out=XB[i], in_=xs[rows, :]), dsXB[i], w=[("XB", i)])

            def load_xr(t):
                i = t % 2
                gslot = 4 * (t // 2) + (t % 2)
                rows = slice(gslot * 128, (gslot + 1) * 128)
                S.dma("sp", lambda e: e.dma_start(out=XR[i], in_=xs[rows, :]), dsXR[i], w=[("XR", i)])

            blocks = list(range(16 * p, 16 * p + 16))
            nb = len(blocks)
            load_xb(blocks[0])
            load_xb(blocks[1])
            load_xr(blocks[0])
            load_xr(blocks[1])
            s1a(blocks[0])
            s1b_mm(blocks[0])
            s1b_ln(blocks[0])
            s1a(blocks[1])
            load_xb(blocks[2])
            for bi, t in enumerate(blocks):
                s2a(t)
                if bi + 1 < nb:
                    s1b_mm(blocks[bi + 1])
                if bi + 2 < nb:
                    s1a(blocks[bi + 2])
                if bi + 3 < nb:
                    load_xb(blocks[bi + 3])
                s2b(t)
                if bi + 1 < nb:
                    s1b_ln(blocks[bi + 1])
                s2c(t)
                if bi + 2 < nb:
                    load_xr(blocks[bi + 2])

        for p in passes:
            if "A" in phases:
                phase_A(p)
                S.barrier()
            if "B" in phases:
                if "C" in phases and WC_EARLY:
                    load_WC()
                phase_B(p)
                S.barrier()
            if "C" in phases:
                if "B" not in phases or not WC_EARLY:
                    load_WC()
                phase_C(p)
                S.barrier()
        S.emit(nc, block, esems, dsY)
    return nc


def _slot_perm(par):
    order = []
    for m in range(16):
        if par == 0:
            order += [4 * m, 4 * m + 3, 4 * m + 1, 4 * m + 2]
        else:
            order += [4 * m + 1, 4 * m + 2, 4 * m, 4 * m + 3]
    return order


def make_in_maps(x, positions, w_in, q_norm_g, w_uq, kv_norm_g, w_ukv, sgu_norm_g, sgu_norm_b,
                 w_spatial, b_spatial, w_out, ln_g, ln_b):
    f = np.float32
    w_in = np.ascontiguousarray(w_in, dtype=f)
    swap = np.concatenate([np.arange(16, 32), np.arange(0, 16)])
    w_krs = np.ascontiguousarray(w_in[:, 384:416][:, swap])
    w_uq = np.ascontiguousarray(w_uq, dtype=f)
    w_uqs = np.zeros_like(w_uq)
    for h in range(8):
        w_uqs[:, h * 96 + 64:h * 96 + 96] = w_uq[:, h * 96 + 64:h * 96 + 96][:, swap]
    qg = np.ascontiguousarray(np.asarray(q_norm_g, dtype=f).reshape(2, 128).T)
    kvg = np.ascontiguousarray(np.asarray(kv_norm_g, dtype=f).reshape(128, 1))
    wspT = np.ascontiguousarray(np.transpose(np.asarray(w_spatial, dtype=f), (2, 0, 1)))
    si = np.arange(128)
    tril = (si[:, None] <= si[None, :]).astype(f)
    bspT = np.ascontiguousarray(np.asarray(b_spatial, dtype=f).T)
    ident = np.eye(128, dtype=f)
    invf = (1.0 / (10000.0 ** (np.arange(16, dtype=np.float64) / 16.0)))
    ropec = np.zeros((128, 6), f)
    ropec[:, 1] = -TWO_PI
    ropec[:, 2] = -TWO_PI
    ropec[:, 3] = 0.5 * math.pi
    for pp in range(64, 96):
        i = (pp - 64) % 16
        ropec[pp, 0] = invf[i] / TWO_PI
        ropec[pp, 1] = TWO_PI if pp < 80 else -TWO_PI
    tri = np.where(si[:, None] > si[None, :], NEG, 0.0).astype(f)
    allm = np.full((128, 128), NEG, f)
    zero = np.zeros((128, 128), f)
    ind_h = np.zeros((8, 512), f)
    for h in range(8):
        ind_h[h, h * 64:(h + 1) * 64] = 1.0
    common = dict(bsp_ht=np.ascontiguousarray(np.asarray(b_spatial, dtype=f)), ind_h=ind_h, w_in=w_in, w_krs=w_krs, w_uq=w_uq, w_uqs=w_uqs, qg=qg, w_ukv=np.ascontiguousarray(w_ukv, dtype=f),
                  kvg=kvg, wspT=wspT, tril=tril, bspT=bspT,
                  sgu_g=np.asarray(sgu_norm_g, dtype=f).reshape(1, 512), sgu_b=np.asarray(sgu_norm_b, dtype=f).reshape(1, 512),
                  ln_g=np.asarray(ln_g, dtype=f).reshape(1, D), ln_b=np.asarray(ln_b, dtype=f).reshape(1, D),
                  w_out=np.ascontiguousarray(w_out, dtype=f), ident=ident, ropec=ropec)
    in_maps = []
    perms = []
    for c in range(8):
        b, par = c // 2, c % 2
        order = _slot_perm(par)
        perms.append(order)
        xb = np.asarray(x[b], dtype=f).reshape(64, 128, D)[order].reshape(SEQ, D)
        pb = np.asarray(positions[b], dtype=np.int32).reshape(64, 128)[order].reshape(1, SEQ)
        mA = allm if par == 0 else zero
        mB = zero if par == 0 else allm
        mk = np.concatenate([tri, allm, tri, mA, allm, mB], axis=1)
        d = dict(common)
        d.update(xs=np.ascontiguousarray(xb), pos=np.ascontiguousarray(pb), masks=np.ascontiguousarray(mk))
        in_maps.append(d)
    return in_maps, perms


_NC_CACHE = {}


def kernel(x, positions, w_in, q_norm_g, w_uq, kv_norm_g, w_ukv, sgu_norm_g, sgu_norm_b,
           w_spatial, b_spatial, w_out, ln_g, ln_b):
    in_maps, perms = make_in_maps(x, positions, w_in, q_norm_g, w_uq, kv_norm_g, w_ukv, sgu_norm_g,
                                  sgu_norm_b, w_spatial, b_spatial, w_out, ln_g, ln_b)
    if "nc" not in _NC_CACHE:
        _NC_CACHE["nc"] = build_program()
    nc = _NC_CACHE["nc"]
    res = run_bass_kernel_spmd(nc, in_maps, core_ids=list(range(8)))
    out = np.empty((4, SEQ, D), np.float32)
    for c in range(8):
        b = c // 2
        order = perms[c]
        yc = np.asarray(res.results[c]["y"], dtype=np.float32).reshape(32, 128, D)
        ov = out[b].reshape(64, 128, D)
        for t in range(32):
            slot = 4 * (t // 2) + (t % 2)
            ov[order[slot]] = yc[t]
    return out
```

```python
import math
from contextlib import ExitStack

import numpy as np
import concourse.bass as bass
import concourse.mybir as mybir
from concourse.bass_utils import run_bass_kernel_spmd

F32 = mybir.dt.float32
BF16 = mybir.dt.bfloat16
I32 = mybir.dt.int32
AF = mybir.ActivationFunctionType
ALU = mybir.AluOpType

D = 1024
SEQ = 8192
NSLOT = 64
DIN = 2464
NA = 928
NEG = -30000.0
EPS = 1e-5
ALPHA = 2.0 ** 0.25
SCALE = 1.0 / math.sqrt(96.0)
TWO_PI = 2.0 * math.pi

ENGS = ("pe", "act", "dve", "pool", "sp")
import os
STOP = int(os.environ.get("KSTOP", "0"))
NGROUPS = int(os.environ.get("KGROUPS", "8"))
KB_STEPS = int(os.environ.get("KB_STEPS", "-1"))
WC_EARLY = int(os.environ.get("WC_EARLY", "1"))
TAB_ENG = os.environ.get("TAB_ENG", "dve")
KB_NOEPI = int(os.environ.get("KB_NOEPI", "0"))
KB_NOPV = int(os.environ.get("KB_NOPV", "0"))
KB_NOMASK = int(os.environ.get("KB_NOMASK", "0"))


class DmaSem:
    def __init__(self, sem):
        self.sem = sem
        self.count = 0


class Op:
    __slots__ = ("eng", "fn", "deps", "dma", "dsem", "dval", "signal", "sigval", "barrier")

    def __init__(self, eng, fn, deps, dma=False, dsem=None, dval=0):
        self.eng = eng
        self.fn = fn
        self.deps = deps
        self.dma = dma
        self.dsem = dsem
        self.dval = dval
        self.signal = False
        self.sigval = 0
        self.barrier = False


class Sched:
    def __init__(self):
        self.ops = []
        self.state = {}
        self.last_on = {e: None for e in ENGS}
        self.open_dma = []

    def _add(self, op):
        self.ops.append(op)
        idx = len(self.ops) - 1
        self.last_on[op.eng] = idx
        return idx

    def _mk(self, idxs):
        d = {}
        for i in idxs:
            p = self.ops[i]
            d[i] = p.dsem.count if p.dma else None
        return d

    def _deps(self, reads, writes):
        deps = set()
        for k in reads:
            st = self.state.get(k)
            if st is not None and st[0] is not None:
                deps.add(st[0])
        for k in writes:
            st = self.state.get(k)
            if st is not None:
                if st[0] is not None:
                    deps.add(st[0])
                deps.update(st[1])
        return self._mk(deps)

    def _commit(self, idx, reads, writes):
        for k in reads:
            st = self.state.setdefault(k, [None, []])
            st[1].append(idx)
        for k in writes:
            self.state[k] = [idx, []]

    def op(self, eng, fn, r=(), w=()):
        deps = self._deps(r, w)
        idx = self._add(Op(eng, fn, deps))
        self._commit(idx, r, w)
        return idx

    def dma(self, queue, fn, dsem, r=(), w=()):
        deps = self._deps(r, w)
        dsem.count += 16
        idx = self._add(Op(queue, fn, deps, dma=True, dsem=dsem, dval=dsem.count))
        self._commit(idx, r, w)
        self.open_dma.append(idx)
        return idx

    def barrier(self):
        targets = set(i for i in self.last_on.values() if i is not None)
        targets.update(self.open_dma)
        for e in ENGS:
            o = Op(e, None, self._mk(targets))
            o.barrier = True
            self.ops.append(o)
        self.open_dma = []
        self.state = {}

    def emit(self, nc, block, esems, final_waits):
        ops = self.ops
        for i, o in enumerate(ops):
            for d in o.deps:
                p = ops[d]
                if p.dma:
                    continue
                if p.eng == o.eng and not o.barrier:
                    if p.eng == "pe" or p.eng == "sp":
                        continue
                p.signal = True
        cnt = {e: 0 for e in ENGS}
        for o in ops:
            if o.signal:
                cnt[o.eng] += 1
                o.sigval = cnt[o.eng]

        def stream(name):
            def body(e):
                seen = {}
                for o in ops:
                    if o.eng != name:
                        continue
                    need = {}
                    for d in sorted(o.deps):
                        p = ops[d]
                        if p.dma:
                            sem, val = p.dsem.sem, o.deps[d]
                        else:
                            if p.eng == name and (name == "pe" or name == "sp") :
                                continue
                            if p.eng == name and o.barrier and name == "pe":
                                continue
                            sem, val = esems[p.eng], p.sigval
                        key = id(sem)
                        if key not in need or need[key][1] < val:
                            need[key] = (sem, val)
                    for key, (sem, val) in need.items():
                        if seen.get(key, 0) >= val:
                            continue
                        seen[key] = val
                        e.wait_ge(sem, val)
                    if o.fn is None:
                        continue
                    ins = o.fn(e)
                    if o.dma:
                        ins.then_inc(o.dsem.sem, 16)
                    elif o.signal:
                        ins.then_inc(esems[name], 1)
                if name == "sp":
                    for ds in final_waits:
                        e.wait_ge(ds.sem, ds.count)
            return body

        block.sync(stream("sp"))
        block.tensor(stream("pe"))
        block.scalar(stream("act"))
        block.vector(stream("dve"))
        block.gpsimd(stream("pool"))


def build_program(passes=(0, 1), phases="ABC", debug=False):
    nc = bass.Bass("TRN2", target_bir_lowering=False)

    def din(name, shape, dt=F32):
        return nc.dram_tensor(name, shape, dt, kind="ExternalInput").ap()

    xs = din("xs", [SEQ, D])
    pos = din("pos", [1, SEQ], I32)
    w_in = din("w_in", [D, DIN])
    w_krs = din("w_krs", [D, 32])
    w_uq = din("w_uq", [256, 768])
    w_uqs = din("w_uqs", [256, 768])
    qg = din("qg", [128, 2])
    w_ukv = din("w_ukv", [128, 1024])
    kvg = din("kvg", [128, 1])
    wspT = din("wspT", [128, 8, 128])
    tril = din("tril", [128, 128])
    bspT = din("bspT", [128, 8])
    bsp_ht = din("bsp_ht", [8, 128])
    ind_h = din("ind_h", [8, 512])
    sgu_g = din("sgu_g", [1, 512])
    sgu_b = din("sgu_b", [1, 512])
    ln_g = din("ln_g", [1, D])
    ln_b = din("ln_b", [1, D])
    w_out = din("w_out", [D, D])
    ident = din("ident", [128, 128])
    ropec = din("ropec", [128, 6])
    masks = din("masks", [128, 768])
    y = nc.dram_tensor("y", [SEQ // 2, D], F32, kind="ExternalOutput").ap()
    w_in_bf = nc.dram_tensor("w_in_bf", [D, DIN], BF16, kind="Internal").ap()
    w_krs_bf = nc.dram_tensor("w_krs_bf", [D, 32], BF16, kind="Internal").ap()
    w_out_bf = nc.dram_tensor("w_out_bf", [D, D], BF16, kind="Internal").ap()
    dbg = {}
    if debug:
        def dout(name, shape):
            dbg[name] = nc.dram_tensor(name, shape, F32, kind="ExternalOutput").ap()
        dout("d_ckvn", [128, 512])
        dout("d_krope", [32, 512])
        dout("d_q0", [96, 512])
        dout("d_g", [128, 512])
        dout("d_kb", [96, 512])
        dout("d_vb", [128, 512])
        dout("d_merged", [128, 512])

    S = Sched()
    with ExitStack() as es:
        def sb(name, shape, dt):
            return es.enter_context(nc.sbuf_tensor(name, shape, dt))

        def ps(name, shape, dt):
            return es.enter_context(nc.psum_tensor(name, shape, dt))

        def newsem(name):
            return es.enter_context(nc.semaphore(name))

        dsem_n = [0]

        def dsem():
            dsem_n[0] += 1
            return DmaSem(newsem("dq%d" % dsem_n[0]))

        identb = sb("identb", [128, 128], BF16)
        onesf = sb("onesf", [128, 128], F32)
        maskb = sb("maskb", [128, 768], BF16)
        ropecs = sb("ropecs", [128, 6], F32)
        epst = sb("epst", [128, 1], F32)
        neghalf = sb("neghalf", [128, 1], F32)
        identf = sb("identf", [128, 128], F32)
        rt = sb("rt", [128, 16], F32)
        wuq_g = sb("wuq_g", [128, 2, 768], BF16)
        wuqs_g = sb("wuqs_g", [128, 2, 768], BF16)
        wukv_g = sb("wukv_g", [128, 1024], BF16)
        wsp = sb("wsp", [128, 8, 128], BF16)
        bsp = sb("bsp", [128, 8], F32)
        bspb = sb("bspb", [8, 128], BF16)
        indb = sb("indb", [8, 512], BF16)
        sgug = sb("sgug", [128, 512], F32)
        sgub = sb("sgub", [128, 512], F32)
        lng = sb("lng", [128, D], F32)
        lnb = sb("lnb", [128, D], F32)
        ckvn = sb("ckvn", [128, SEQ], BF16)
        KB = [sb("kb%d" % i, [96, SEQ], BF16) for i in range(2)]
        QT = sb("qt", [96, 8, 2048], BF16)
        MG = sb("mg", [128, 4, 2048], BF16)
        ARENA = 86528
        arena = sb("arena", [128, ARENA // 2], BF16)

        class Carver:
            def __init__(self):
                self.off = 0

            def take(self, cols, dt, parts=128):
                size = 4 if dt in (F32, I32) else 2
                self.off = (self.off + 3) // 4 * 4
                nbytes = cols * size
                assert self.off + nbytes <= ARENA, ("arena overflow", self.off, nbytes)
                v = arena[0:parts, self.off // 2:(self.off + nbytes) // 2]
                self.off += nbytes
                if dt != BF16:
                    v = v.bitcast(dt)
                return v

        PT4 = [ps("pt%d" % i, [128, 2, 512], F32) for i in range(4)]
        PSW = [PT4[0], PT4[1], PT4[2]]
        PSO1 = PT4[3][:, 0, :]
        PSG = PT4[3][:, 1, :]
        pst_banks = [PT4[3][:, 0, :].bitcast(BF16)[:, 0:512],
                     PT4[3][:, 1, :].bitcast(BF16)[:, 0:512]]

        def bank(i):
            return PT4[i // 2][:, i % 2, :]

        bank_rr = [0]

        def next_bank():
            b = bank_rr[0] % 6
            bank_rr[0] += 1
            return b, bank(b), ("ps", b)

        esems = {e: newsem("e_" + e) for e in ENGS}
        block = es.enter_context(nc.Block())

        cv = Carver()
        st_a = cv.take(2 * 768, F32)
        st_b = cv.take(2 * 768, F32)
        st_c = cv.take(1024, F32)
        st_d = cv.take(8 * 128, F32)
        st_e = cv.take(128, F32)
        qgs = cv.take(2, F32)
        kvgs = cv.take(1, F32)
        ds0 = dsem()
        ds0p = dsem()
        S.dma("pool", lambda e: e.dma_start(out=identb[:], in_=ident), ds0p, w=["identb"])
        S.dma("pool", lambda e: e.dma_start(out=maskb[:], in_=masks), ds0p, w=["maskb"])
        S.dma("pool", lambda e: e.dma_start(out=bspb[:], in_=bsp_ht), ds0p, w=["bspb"])
        S.dma("pool", lambda e: e.dma_start(out=indb[:], in_=ind_h), ds0p, w=["indb"])
        S.dma("sp", lambda e: e.dma_start(out=ropecs[:], in_=ropec), ds0, w=["ropecs"])
        S.dma("sp", lambda e: e.dma_start(out=identf[:], in_=ident), ds0, w=["identf"])
        S.dma("sp", lambda e: e.dma_start(out=st_a.rearrange("p (c n) -> p c n", c=2),
                                          in_=w_uq.rearrange("(c p) n -> p c n", p=128)), ds0, w=["st_a"])
        S.dma("sp", lambda e: e.dma_start(out=st_b.rearrange("p (c n) -> p c n", c=2),
                                          in_=w_uqs.rearrange("(c p) n -> p c n", p=128)), ds0, w=["st_b"])
        S.dma("sp", lambda e: e.dma_start(out=st_c, in_=w_ukv), ds0, w=["st_c"])
        S.dma("sp", lambda e: e.dma_start(out=st_d.rearrange("p (h t) -> p h t", h=8), in_=wspT), ds0, w=["st_d"])
        S.dma("sp", lambda e: e.dma_start(out=st_e, in_=tril), ds0, w=["st_e"])
        S.dma("sp", lambda e: e.dma_start(out=qgs, in_=qg), ds0, w=["qgs"])
        S.dma("sp", lambda e: e.dma_start(out=kvgs, in_=kvg), ds0, w=["kvgs"])
        S.dma("sp", lambda e: e.dma_start(out=bsp[:], in_=bspT), ds0, w=["bsp"])
        S.dma("sp", lambda e: e.dma_start(out=sgug[:], in_=sgu_g.partition_broadcast(128)), ds0, w=["sgug"])
        S.dma("sp", lambda e: e.dma_start(out=sgub[:], in_=sgu_b.partition_broadcast(128)), ds0, w=["sgub"])
        S.dma("sp", lambda e: e.dma_start(out=lng[:], in_=ln_g.partition_broadcast(128)), ds0, w=["lng"])
        S.dma("sp", lambda e: e.dma_start(out=lnb[:], in_=ln_b.partition_broadcast(128)), ds0, w=["lnb"])
        dscv = dsem()
        S.op("dve", lambda e: e.memset(onesf[:], 1.0), w=["onesf"])
        S.op("dve", lambda e: e.memset(epst[:], EPS), w=["epst"])
        S.op("dve", lambda e: e.memset(neghalf[:], -0.5), w=["neghalf"])
        for c in range(2):
            S.op("dve", lambda e, c=c: e.tensor_scalar(wuq_g[:, c, :], st_a[:, c * 768:(c + 1) * 768],
                                                       qgs[:, c:c + 1], None, op0=ALU.mult),
                 r=["st_a", "qgs"], w=[("wuq", c)])
            S.op("dve", lambda e, c=c: e.tensor_scalar(wuqs_g[:, c, :], st_b[:, c * 768:(c + 1) * 768],
                                                       qgs[:, c:c + 1], None, op0=ALU.mult),
                 r=["st_b", "qgs"], w=[("wuqs", c)])
        S.op("dve", lambda e: e.tensor_scalar(wukv_g[:], st_c, kvgs[:, 0:1], None, op0=ALU.mult),
             r=["st_c", "kvgs"], w=["wukv"])
        for h in range(8):
            S.op("dve", lambda e, h=h: e.tensor_tensor(wsp[:, h, :], st_d[:, h * 128:(h + 1) * 128], st_e,
                                                       op=ALU.mult), r=["st_d", "st_e"], w=[("wsp", h)])
        S.barrier()

        def phase_A(p):
            cv = Carver()
            WA = cv.take(8 * NA, BF16).rearrange("p (c n) -> p c n", c=8)
            WK = cv.take(8 * 32, BF16).rearrange("p (c n) -> p c n", c=8)
            XS = [cv.take(4 * D, BF16).rearrange("p (s d) -> p s d", s=4) for _ in range(2)]
            XT = cv.take(8 * 512, BF16).rearrange("p (c t) -> p c t", c=8)
            POSI = [cv.take(512, I32) for _ in range(2)]
            posf = cv.take(512, F32)
            tk = posf.bitcast(I32)
            TABS = [[cv.take(512, F32) for _ in range(2)] for _ in range(2)]
            tt = cv.take(512, F32)
            tkf = cv.take(512, F32)
            sq = [cv.take(512, F32) for _ in range(2)]
            rstd = cv.take(512, F32)
            dg = cv.take(512, F32)
            dg2 = cv.take(256, F32)
            rstdq = cv.take(256, F32)
            KR = [[cv.take(512, F32) for _ in range(2)] for _ in range(2)]
            CQB = [cv.take(512, BF16) for _ in range(2)]
            crsr = cv.take(512, F32)
            qtmp = [cv.take(512, F32) for _ in range(2)]
            dsW = dsem()
            dsX = [dsem(), dsem()]
            dsP = [dsem(), dsem()]
            dsD = dsem()
            groups = list(range(8 * p, 8 * p + NGROUPS))

            def load_group(G):
                i = G % 2
                S.dma("pool", lambda e: e.dma_start(
                    out=XS[i], in_=xs[G * 512:(G + 1) * 512, :].rearrange("(s p) d -> p s d", p=128)),
                    dsX[i], w=[("XS", i)])
                S.dma("sp", lambda e: e.dma_start(
                    out=POSI[i], in_=pos[0:1, G * 512:(G + 1) * 512].partition_broadcast(128)),
                    dsP[i], w=[("POSI", i)])

            load_group(groups[0])
            if p == 0:
                dsW0 = dsem()
                S.dma("pool", lambda e: e.dma_start(out=WA, in_=w_in.rearrange("(c p) n -> p c n", p=128)[:, :, 0:NA]),
                      dsW0, w=["WA"])
                S.dma("pool", lambda e: e.dma_start(out=WK, in_=w_krs.rearrange("(c p) n -> p c n", p=128)),
                      dsW0, w=["WK"])
            else:
                S.dma("sp", lambda e: e.dma_start(out=WA, in_=w_in_bf.rearrange("(c p) n -> p c n", p=128)[:, :, 0:NA]),
                      dsW, w=["WA"])
                S.dma("sp", lambda e: e.dma_start(out=WK, in_=w_krs_bf.rearrange("(c p) n -> p c n", p=128)),
                      dsW, w=["WK"])
            if len(groups) > 1:
                load_group(groups[1])
            if p == 0:
                S.dma("pool", lambda e: e.dma_start(out=w_in_bf, in_=w_in), dscv, w=["w_in_bf"])
                S.dma("pool", lambda e: e.dma_start(out=w_krs_bf, in_=w_krs), dscv, w=["w_krs_bf"])
                S.dma("pool", lambda e: e.dma_start(out=w_out_bf, in_=w_out), dscv, w=["w_out_bf"])

            def make_tables(G):
                i = G % 2
                S.op("act", lambda e: e.activation(tt, POSI[i], AF.Identity, scale=ropecs[:, 0:1]),
                     r=[("POSI", i), "ropecs"], w=["tt"])
                S.op("dve", lambda e: e.tensor_copy(tk, tt), r=["tt"], w=["posf"])
                S.op("dve", lambda e: e.tensor_copy(tkf, tk), r=["posf"], w=["tkf"])
                S.op("dve", lambda e: e.tensor_tensor(tt, tt, tkf, op=ALU.subtract), r=["tt", "tkf"], w=["tt"])
                S.op("dve", lambda e: e.scalar_tensor_tensor(out=tt, in0=tt, scalar=0.5, in1=tt,
                                                             op0=ALU.is_gt, op1=ALU.subtract),
                     r=["tt"], w=["tt"])
                S.op("act", lambda e: e.activation(TABS[i][1], tt, AF.Sin, scale=ropecs[:, 1:2]),
                     r=["tt", "ropecs"], w=[("tab", i, 1)])
                S.op("act", lambda e: e.activation(tkf, tt, AF.Abs), r=["tt"], w=["tkf"])
                S.op("act", lambda e: e.activation(TABS[i][0], tkf, AF.Sin, bias=ropecs[:, 3:4], scale=ropecs[:, 2:3]),
                     r=["tkf", "ropecs"], w=[("tab", i, 0)])

            def stage_x1(G):
                i = G % 2
                xt = XT
                for k in range(8):
                    half = k % 2
                    pt = pst_banks[half]
                    for s_ in range(4):
                        S.op("pe", lambda e, pt=pt, s_=s_, k=k: e.transpose(
                            pt[:, s_ * 128:(s_ + 1) * 128], XS[i][:, s_, k * 128:(k + 1) * 128], identb[:]),
                            r=[("XS", i), "identb"], w=[("pst", half)])
                    if k % 2 == 0:
                        S.op("act", lambda e, pt=pt, k=k: e.copy(xt[:, k, :], pt),
                             r=[("pst", half)], w=[("XT", k)])
                    else:
                        S.op("dve", lambda e, pt=pt, k=k: e.tensor_copy(xt[:, k, :], pt),
                             r=[("pst", half)], w=[("XT", k)])

            def stage_x2(G):
                i = G % 2
                tok = slice(G * 512, (G + 1) * 512)
                lt = slice((G - 8 * p) * 256, (G - 8 * p) * 256 + 256)
                xt = XT
                bc, pc, kc = next_bank()
                for k in range(8):
                    S.op("pe", lambda e, pc=pc, k=k: e.matmul(pc, WA[:, k, 256:384], xt[:, k, :],
                                                              start=(k == 0), stop=(k == 7)),
                         r=["WA", ("XT", k)], w=[kc])
                br, pr, kr = next_bank()
                for k in range(8):
                    S.op("pe", lambda e, pr=pr, k=k: e.matmul(pr[64:96, :], WA[:, k, 384:416], xt[:, k, :],
                                                              start=(k == 0), stop=(k == 7), tile_position=(0, 64)),
                         r=["WA", ("XT", k)], w=[kr])
                bs, psw, ks = next_bank()
                for k in range(8):
                    S.op("pe", lambda e, psw=psw, k=k: e.matmul(psw[64:96, :], WK[:, k, :], xt[:, k, :],
                                                                start=(k == 0), stop=(k == 7), tile_position=(0, 64)),
                         r=["WK", ("XT", k)], w=[ks])
                S.op("act", lambda e, pc=pc: e.copy(ckvn[:, tok], pc), r=[kc], w=[("ckvn", G)])
                S.op("act", lambda e, pc=pc: e.activation(sq[0], pc, AF.Square), r=[kc], w=[("sq", 0)])
                S.op("act", lambda e, pr=pr: e.copy(KR[i][0][64:96, :], pr[64:96, :]), r=[kr], w=[("KR", i, 0)])
                S.op("act", lambda e, psw=psw: e.copy(KR[i][1][64:96, :], psw[64:96, :]), r=[ks], w=[("KR", i, 1)])
                bcq, pcq, kcq = next_bank()
                for c in range(2):
                    for k in range(8):
                        S.op("pe", lambda e, pcq=pcq, c=c, k=k: e.matmul(
                            pcq[:, c * 256:(c + 1) * 256], WA[:, k, c * 128:(c + 1) * 128], xt[:, k, 0:256],
                            start=(k == 0), stop=(k == 7)),
                            r=["WA", ("XT", k)], w=[kcq])
                S.op("act", lambda e, pcq=pcq: e.copy(CQB[i], pcq), r=[kcq], w=[("cqb", i)])
                S.op("act", lambda e, pcq=pcq: e.activation(sq[1], pcq, AF.Square), r=[kcq], w=[("sq", 1)])
                for cc in range(2):
                    bz, pz, kz = next_bank()
                    for c2 in range(2):
                        c = cc * 2 + c2
                        for k in range(8):
                            S.op("pe", lambda e, pz=pz, c=c, c2=c2, k=k: e.matmul(
                                pz[:, c2 * 256:(c2 + 1) * 256], WA[:, k, 416 + c * 128:416 + (c + 1) * 128],
                                xt[:, k, 0:256], start=(k == 0), stop=(k == 7)),
                                r=["WA", ("XT", k)], w=[kz])
                    S.op("act", lambda e, pz=pz, cc=cc: e.activation(
                        MG[:, 2 * cc:2 * cc + 2, lt], pz.rearrange("p (c t) -> p c t", c=2), AF.Silu),
                        r=[kz], w=[("MG", cc, G)])
                bq, pq, kq = next_bank()
                for j in range(4):
                    S.op("pe", lambda e, pq=pq, j=j: e.matmul(pq[:, j:j + 1], sq[0][:, j * 128:(j + 1) * 128], onesf[:, 0:1],
                                                              start=True, stop=True, skip_group_check=True),
                         r=[("sq", 0), "onesf"], w=[kq])
                for j in range(2):
                    for c in range(2):
                        S.op("pe", lambda e, pq=pq, c=c, j=j: e.matmul(
                            pq[:, 4 + j:5 + j], sq[1][:, c * 256 + j * 128:c * 256 + (j + 1) * 128], onesf[:, 0:1],
                            start=(c == 0), stop=(c == 1), skip_group_check=True),
                            r=[("sq", 1), "onesf"], w=[kq])
                ro = 8 * i
                S.op("act", lambda e, pq=pq: e.activation(rt[:, ro:ro + 4], pq[:, 0:4], AF.Identity, bias=epst[:, 0:1],
                                                          scale=1.0 / 128.0), r=[kq, "epst"], w=[("rt", i)])
                S.op("act", lambda e, pq=pq: e.activation(rt[:, ro + 4:ro + 6], pq[:, 4:6], AF.Identity, bias=epst[:, 0:1],
                                                          scale=1.0 / 256.0), r=[kq, "epst"], w=[("rt", i)])
                S.op("pool", lambda e: e.tensor_tensor(rt[:, ro:ro + 6], rt[:, ro:ro + 6],
                                                       neghalf[:, 0:1].to_broadcast([128, 6]), op=ALU.pow),
                     r=[("rt", i), "neghalf"], w=[("rt", i)])

            def stage_yn(G):
                i = G % 2
                tok = slice(G * 512, (G + 1) * 512)
                lt = slice((G - 8 * p) * 256, (G - 8 * p) * 256 + 256)
                tabs = TABS[i]
                tabk = [("tab", i, 0), ("tab", i, 1)]
                ro = 8 * i
                cqb = CQB[i]
                for j in range(4):
                    S.op("dve", lambda e, j=j: e.tensor_scalar(dg[:, j * 128:(j + 1) * 128], identf[:], rt[:, ro + j:ro + j + 1],
                                                               None, op0=ALU.mult), r=[("rt", i), "identf"], w=["dg"])
                for j in range(2):
                    S.op("dve", lambda e, j=j: e.tensor_scalar(dg2[:, j * 128:(j + 1) * 128], identf[:],
                                                               rt[:, ro + 4 + j:ro + 5 + j], None, op0=ALU.mult),
                         r=[("rt", i), "identf"], w=["dg2"])
                bb, pbc, kbc = next_bank()
                for j in range(4):
                    S.op("pe", lambda e, pbc=pbc, j=j: e.matmul(pbc[:, j * 128:(j + 1) * 128], onesf[:],
                                                                dg[:, j * 128:(j + 1) * 128], start=True, stop=True,
                                                                skip_group_check=True),
                         r=["dg", "onesf"], w=[kbc])
                bb2, pbc2, kbc2 = next_bank()
                for j in range(2):
                    S.op("pe", lambda e, pbc2=pbc2, j=j: e.matmul(pbc2[:, j * 128:(j + 1) * 128], onesf[:],
                                                                  dg2[:, j * 128:(j + 1) * 128], start=True, stop=True,
                                                                  skip_group_check=True),
                         r=["dg2", "onesf"], w=[kbc2])
                S.op("act", lambda e, pbc=pbc: e.copy(rstd, pbc), r=[kbc], w=["rstd"])
                S.op("act", lambda e, pbc2=pbc2: e.copy(rstdq[:, 0:256], pbc2[:, 0:256]), r=[kbc2], w=["rstdq"])
                S.op("pool", lambda e: e.tensor_tensor(ckvn[:, tok], ckvn[:, tok], rstd, op=ALU.mult),
                     r=[("ckvn", G), "rstd"], w=[("ckvn", G)])
                kr0, kr1 = KR[i]
                S.op("dve", lambda e: e.tensor_tensor(kr0[64:96, :], kr0[64:96, :], tabs[0][64:96, :], op=ALU.mult),
                     r=[("KR", i, 0), tabk[0]], w=[("KR", i, 0)])
                S.op("dve", lambda e: e.tensor_tensor(kr1[64:96, :], kr1[64:96, :], tabs[1][64:96, :], op=ALU.mult),
                     r=[("KR", i, 1), tabk[1]], w=[("KR", i, 1)])
                S.op("pool", lambda e: e.tensor_tensor(KB[0][64:96, tok], kr0[64:96, :], kr1[64:96, :], op=ALU.add),
                     r=[("KR", i, 0), ("KR", i, 1)], w=[("kbr", 0, G)])
                S.op("pool", lambda e: e.tensor_copy(KB[1][64:96, tok], KB[0][64:96, tok]),
                     r=[("kbr", 0, G)], w=[("kbr", 1, G)])
                for ti in range(2):
                    S.op("dve", lambda e, ti=ti: e.tensor_tensor(crsr[0:96, ti * 256:(ti + 1) * 256],
                                                                 tabs[ti][0:96, 0:256], rstdq[0:96, 0:256], op=ALU.mult),
                         r=[tabk[ti], "rstdq"], w=["crsr"])

            def stage_yq(G):
                i = G % 2
                lt = slice((G - 8 * p) * 256, (G - 8 * p) * 256 + 256)
                tok = slice(G * 512, (G + 1) * 512)
                cqb = CQB[i]
                for h in range(8):
                    bh, ph_, kh = next_bank()
                    for half, wsrc, wk in ((0, wuq_g, "wuq"), (1, wuqs_g, "wuqs")):
                        for c in range(2):
                            S.op("pe", lambda e, ph_=ph_, half=half, wsrc=wsrc, c=c, h=h: e.matmul(
                                ph_[0:96, half * 256:(half + 1) * 256], wsrc[:, c, h * 96:(h + 1) * 96],
                                cqb[:, c * 256:(c + 1) * 256], start=(c == 0), stop=(c == 1)),
                                r=[(wk, c), ("cqb", i)], w=[kh])
                    qt_ = qtmp[h % 2]
                    S.op("dve", lambda e, ph_=ph_, qt_=qt_: e.tensor_tensor(qt_[0:96, :], ph_[0:96, :], crsr[0:96, :],
                                                                            op=ALU.mult),
                         r=[kh, "crsr"], w=[("qtmp", h % 2)])
                    S.op("pool", lambda e, qt_=qt_, h=h: e.tensor_tensor(
                        QT[:, h, lt], qt_[0:96, 0:256], qt_[0:96, 256:512], op=ALU.add),
                        r=[("qtmp", h % 2)], w=[("QT", h, G)])
                if debug and G == 8 * p + 1 and p == 0:
                    dd = qtmp[0]
                    kk = ("qtmp", 0)
                    S.op("dve", lambda e: e.tensor_copy(dd, ckvn[:, tok]), r=[("ckvn", G), kk], w=[kk])
                    S.dma("sp", lambda e: e.dma_start(out=dbg["d_ckvn"], in_=dd), dsD, r=[kk])
                    S.op("dve", lambda e: e.tensor_copy(dd[64:96, :], KB[1][64:96, tok]),
                         r=[("kbr", 1, G), kk], w=[kk])
                    S.dma("sp", lambda e: e.dma_start(out=dbg["d_krope"], in_=dd[64:96, :]), dsD, r=[kk])
                    S.op("dve", lambda e: e.tensor_copy(dd[0:96, 0:256], QT[:, 0, lt]), r=[("QT", 0, G), kk], w=[kk])
                    S.op("dve", lambda e: e.tensor_copy(dd[0:96, 256:512], QT[:, 5, lt]), r=[("QT", 5, G), kk], w=[kk])
                    S.dma("sp", lambda e: e.dma_start(out=dbg["d_q0"], in_=dd[0:96, :]), dsD, r=[kk])
                    S.op("dve", lambda e: e.tensor_copy(dd[:, 0:256], MG[:, 0, lt]), r=[("MG", 0, G), kk], w=[kk])
                    S.op("dve", lambda e: e.tensor_copy(dd[:, 256:512], MG[:, 3, lt]), r=[("MG", 1, G), kk], w=[kk])
                    S.dma("sp", lambda e: e.dma_start(out=dbg["d_g"], in_=dd), dsD, r=[kk])

            ng = len(groups)
            stage_x1(groups[0])
            make_tables(groups[0])
            stage_x2(groups[0])
            if ng > 2:
                load_group(groups[2])
            for gi, G in enumerate(groups):
                nxt = groups[gi + 1] if gi + 1 < ng else None
                if nxt is not None:
                    stage_x1(nxt)
                stage_yn(G)
                if nxt is not None:
                    make_tables(nxt)
                    stage_x2(nxt)
                stage_yq(G)
                if gi + 3 < ng:
                    load_group(groups[gi + 3])

        def phase_B(p):
            cv = Carver()
            nslots = 32 * (p + 1)
            VB = [cv.take(64 * 65, BF16).rearrange("p (s d) -> p s d", s=64),
                  cv.take(64 * 128, BF16).rearrange("p (s d) -> p s d", s=64)]
            PT = [cv.take(1024, BF16).rearrange("p (j n) -> p j n", j=2) for _ in range(3)]
            osb = [cv.take(512, F32) for _ in range(2)]
            rec = [cv.take(512, F32) for _ in range(2)]
            otmp = [cv.take(512, F32) for _ in range(2)]
            dsD = dsem()
            S.op("pool", lambda e: e.memset(VB[0][:, :, 64:65], 1.0), w=[("vbc", 0)])
            S.op("pool", lambda e: e.memset(VB[1][:, :, 0:64], 0.0), w=[("vbc", 1)])
            S.op("pool", lambda e: e.memset(VB[1][:, :, 0:1], 1.0), w=[("vbc", 1)])

            def vaug(b, h, slot):
                return VB[b][:, slot, :]

            def vdst(b, c):
                if b == 0:
                    return VB[0][:, c * 8:(c + 1) * 8, 0:64]
                return VB[1][:, c * 8:(c + 1) * 8, 64:128]

            def gen_chunks(h, upfront=False):
                b = h % 2
                out = []
                cnt = [0]

                def target():
                    n = cnt[0]
                    cnt[0] += 1
                    if not upfront:
                        return PSG, "psg", "dve"
                    t = n % 3
                    return PSW[t][:, (n // 3) % 2, :], ("psw", t), ("dve" if n % 2 == 0 else "act")

                for c in range(nslots // 4):
                    def kchunk(c=c):
                        pg, kg, eng = target()
                        tok = slice(c * 512, (c + 1) * 512)
                        S.op("pe", lambda e: e.matmul(pg[0:64, :], wukv_g[:, h * 128:h * 128 + 64], ckvn[:, tok],
                                                      start=True, stop=True), r=["wukv"], w=[kg])
                        if eng == "dve":
                            S.op("dve", lambda e: e.tensor_copy(KB[b][0:64, tok], pg[0:64, :]),
                                 r=[kg], w=[("kb", b, c)])
                        else:
                            S.op("act", lambda e: e.copy(KB[b][0:64, tok], pg[0:64, :]),
                                 r=[kg], w=[("kb", b, c)])
                    out.append(kchunk)
                for c in range(nslots // 8):
                    def vchunk(c=c):
                        pg, kg, eng = target()
                        for s in range(8):
                            slot = c * 8 + s
                            S.op("pe", lambda e, s=s, slot=slot: e.matmul(
                                pg[:, s * 64:(s + 1) * 64], ckvn[:, slot * 128:(slot + 1) * 128],
                                wukv_g[:, h * 128 + 64:h * 128 + 128], start=True, stop=True),
                                r=["wukv"], w=[kg])
                        if eng == "dve":
                            S.op("dve", lambda e: e.tensor_copy(vdst(b, c), pg.rearrange("p (s d) -> p s d", s=8)),
                                 r=[kg], w=[("vb", b, c)])
                        else:
                            S.op("act", lambda e: e.copy(vdst(b, c), pg.rearrange("p (s d) -> p s d", s=8)),
                                 r=[kg], w=[("vb", b, c)])
                    out.append(vchunk)
                return out

            steps = []
            for h in range(8):
                for g in range(4):
                    gg = 4 * p + g
                    nfull = 4 * gg
                    for i in range(nfull):
                        steps.append(dict(h=h, g=g, slots=(2 * i, 2 * i + 1), c0=0, masks=None))
                    base = 8 * gg
                    tails = [
                        ((base + 0, base + 1), 0, ((0, 128, 0), (256, 128, 128))),
                        ((base + 2, base + 3), 0, ((384, 128, 0), (640, 128, 128))),
                        ((base + 4, base + 5), 256, ((0, 128, 256), (256, 128, 384))),
                        ((base + 6, base + 7), 256, ((384, 128, 256), (640, 128, 384))),
                    ]
                    for slots, c0, mk in tails:
                        steps.append(dict(h=h, g=g, slots=slots, c0=c0, masks=mk))
                    steps[-1]["last"] = True
                    steps[-(nfull + 4)]["first"] = True
            for fn in gen_chunks(0, upfront=True):
                fn()
            pending_gen = []
            pending_epi = []
            pending_rcp = []
            n_steps = len(steps)
            if KB_STEPS >= 0:
                n_steps = KB_STEPS
            ocount = [0]

            def emit_qk(si):
                stp = steps[si]
                h, g, c0 = stp["h"], stp["g"], stp["c0"]
                b = h % 2
                wtile = PSW[si % 3]
                for j, slot in enumerate(stp["slots"]):
                    has_mask = stp["masks"] is not None
                    cj = stp["masks"][j][2] if has_mask else c0
                    S.op("pe", lambda e, j=j, slot=slot, has_mask=has_mask, cj=cj: e.matmul(
                        wtile[:, j, cj:512], KB[b][:, slot * 128:(slot + 1) * 128],
                        QT[:, h, g * 512 + cj:(g + 1) * 512], start=True, stop=(not has_mask or bool(KB_NOMASK)),
                        skip_group_check=True),
                        r=[("kb", b, slot // 4)], w=[("psw", si % 3)])
                    if has_mask and not KB_NOMASK:
                        mo, mw, mc = stp["masks"][j]
                        S.op("pe", lambda e, j=j, mo=mo, mw=mw, mc=mc: e.matmul(
                            wtile[:, j, mc:mc + mw], identb[:], maskb[:, mo:mo + mw], start=False, stop=True,
                            skip_group_check=True),
                            r=[], w=[("psw", si % 3)])

            def emit_exp(si):
                stp = steps[si]
                c0 = stp["c0"]
                S.op("act", lambda e: e.activation(PT[si % 3][:, :, c0:512], PSW[si % 3][:, :, c0:512], AF.Exp,
                                                   scale=SCALE),
                     r=[("psw", si % 3)], w=[("pt", si % 3)])

            def emit_pv(si):
                stp = steps[si]
                h, g, c0 = stp["h"], stp["g"], stp["c0"]
                b = h % 2
                if stp.get("first"):
                    ocount[0] += 1
                o = ocount[0] % 2
                for j, slot in enumerate(stp["slots"]):
                    first = bool(stp.get("first")) and j == 0
                    last = bool(stp.get("last")) and j == 1
                    mrows = 65 if b == 0 else 128
                    cj = stp["masks"][j][2] if stp["masks"] is not None else c0
                    S.op("pe", lambda e, j=j, slot=slot, first=first, last=last, mrows=mrows, cj=cj: e.matmul(
                        PSO1[0:mrows, cj:512], vaug(b, h, slot), PT[si % 3][:, j, cj:512], start=first, stop=last,
                        skip_group_check=True),
                        r=[("pt", si % 3), ("vb", b, slot // 8), ("vbc", b)], w=["pso"])
                if stp.get("last"):
                    mrows = 65 if b == 0 else 128
                    S.op("dve", lambda e, mrows=mrows: e.tensor_copy(osb[o][0:mrows, :], PSO1[0:mrows, :]),
                         r=["pso"], w=[("osb", o)])
                    dr = 64 if h % 2 == 0 else 0
                    for q4 in range(4):
                        pending_rcp.append((si + q4, o, dr, q4))
                    pending_epi.append((si + 6, h, g, o))

            def emit_epilogue(h, g, o):
                if KB_NOEPI:
                    return
                dr = 64 if h % 2 == 0 else 0
                rows = slice(0, 64) if h % 2 == 0 else slice(64, 128)
                r_ = rec[o]
                ot = otmp[o]
                ob_ = osb[o]
                S.op("pe", lambda e: e.matmul(PSG, onesf[dr:dr + 1, :], r_[dr:dr + 1, :], start=True, stop=True),
                     r=[("rec", o)], w=["psg"])
                S.op("dve", lambda e: e.tensor_tensor(ot[rows, :], ob_[rows, :], PSG[rows, :], op=ALU.mult),
                     r=[("osb", o), "psg"], w=[("otmp", o)])
                mg = MG[rows, h // 2, g * 512:(g + 1) * 512]
                S.op("pool", lambda e: e.tensor_tensor(mg, ot[rows, :], mg, op=ALU.mult),
                     r=[("otmp", o)], w=[("mgo", h, g)])

            if n_steps > 0:
                emit_qk(0)
            if n_steps > 1:
                emit_qk(1)
            cur_h = 0
            gen_every = 3 if p == 0 else 4
            since_gen = 0
            for si in range(n_steps):
                emit_exp(si)
                if si + 2 < n_steps:
                    emit_qk(si + 2)
                if not KB_NOPV:
                    emit_pv(si)
                while pending_rcp and pending_rcp[0][0] <= si:
                    _, ro_, rdr, q4 = pending_rcp.pop(0)
                    cs = slice(q4 * 128, (q4 + 1) * 128)
                    S.op("dve", lambda e, ro_=ro_, rdr=rdr, cs=cs: e.reciprocal(rec[ro_][rdr:rdr + 1, cs],
                                                                               osb[ro_][rdr:rdr + 1, cs]),
                         r=[("osb", ro_)], w=[("rec", ro_)])
                while pending_epi and pending_epi[0][0] <= si:
                    _, eh, eg, eo = pending_epi.pop(0)
                    emit_epilogue(eh, eg, eo)
                h = steps[si]["h"]
                if steps[si].get("first") and steps[si]["g"] == 0 and h + 1 < 8:
                    pending_gen = gen_chunks(h + 1)
                    since_gen = 0
                since_gen += 1
                if pending_gen and since_gen >= gen_every:
                    pending_gen.pop(0)()
                    since_gen = 0
                if steps[si].get("last") and steps[si]["g"] == 3:
                    while pending_gen:
                        pending_gen.pop(0)()
            while pending_rcp:
                _, ro_, rdr, q4 = pending_rcp.pop(0)
                cs = slice(q4 * 128, (q4 + 1) * 128)
                S.op("dve", lambda e, ro_=ro_, rdr=rdr, cs=cs: e.reciprocal(rec[ro_][rdr:rdr + 1, cs],
                                                                           osb[ro_][rdr:rdr + 1, cs]),
                     r=[("osb", ro_)], w=[("rec", ro_)])
            while pending_epi:
                _, eh, eg, eo = pending_epi.pop(0)
                emit_epilogue(eh, eg, eo)
            if debug and p == 0:
                dd = cv.take(512, F32)
                S.op("dve", lambda e: e.tensor_copy(dd[0:96, :], KB[1][:, 512:1024]), r=[("kb", 1, 1)], w=["dd"])
                S.dma("sp", lambda e: e.dma_start(out=dbg["d_kb"], in_=dd[0:96, :]), dsD, r=["dd"])
                S.op("dve", lambda e: e.tensor_copy(dd.rearrange("p (s d) -> p s d", s=8), VB[1][:, 8:16, 64:128]),
                     r=[("vb", 1, 1), "dd"], w=["dd"])
                S.dma("sp", lambda e: e.dma_start(out=dbg["d_vb"], in_=dd), dsD, r=["dd"])
                S.op("dve", lambda e: e.tensor_copy(dd[:, 0:256], MG[:, 0, 256:512]),
                     r=[("mgo", 0, 0), ("mgo", 1, 0), "dd"], w=["dd"])
                S.op("dve", lambda e: e.tensor_copy(dd[:, 256:512], MG[:, 3, 1024 + 256:1024 + 512]),
                     r=[("mgo", 6, 2), ("mgo", 7, 2), "dd"], w=["dd"])
                S.dma("sp", lambda e: e.dma_start(out=dbg["d_merged"], in_=dd), dsD, r=["dd"])

        WC_bytes = 8 * 1536 * 2 + 8 * 1024 * 2

        def wc_views():
            base = (ARENA - WC_bytes) // 2
            wci = arena[:, base:base + 8 * 1536].rearrange("p (c n) -> p c n", c=8)
            wco = arena[:, base + 8 * 1536:base + 8 * 1536 + 8 * 1024].rearrange("p (c n) -> p c n", c=8)
            return wci, wco

        dsWC = dsem()

        def load_WC():
            wci, wco = wc_views()
            S.dma("sp", lambda e: e.dma_start(out=wci, in_=w_in_bf.rearrange("(c p) n -> p c n", p=128)[:, :, NA:DIN]),
                  dsWC, w=["WCI"])
            S.dma("sp", lambda e: e.dma_start(out=wco, in_=w_out_bf.rearrange("(c p) n -> p c n", p=128)),
                  dsWC, w=["WCO"])

        dsY = [dsem(), dsem()]

        def phase_C(p):
            cv = Carver()
            wci, wco = wc_views()
            XR = [cv.take(D, F32) for _ in range(2)]
            XB = [cv.take(D, BF16) for _ in range(2)]
            XTC = [cv.take(8 * 128, BF16).rearrange("p (c t) -> p c t", c=8) for _ in range(2)]
            UG = [cv.take(512, F32) for _ in range(2)]
            VG = [cv.take(512, F32) for _ in range(2)]
            ZB = [cv.take(512, F32) for _ in range(2)]
            VN = [cv.take(512, BF16) for _ in range(2)]
            ob = cv.take(512, BF16)
            obT = cv.take(512, BF16)
            hh = cv.take(D, F32)
            YO = [cv.take(D, F32) for _ in range(2)]
            st1 = [cv.take(16, F32) for _ in range(2)]
            st2 = cv.take(24, F32)
            assert cv.off <= ARENA - WC_bytes, ("phase C arena", cv.off)
            dsXR = [dsem(), dsem()]
            dsXB = [dsem(), dsem()]

            def load_block(t):
                i = t % 2
                gslot = 4 * (t // 2) + (t % 2)
                rows = slice(gslot * 128, (gslot + 1) * 128)
                S.dma("sp", lambda e: e.dma_start(out=XR[i], in_=xs[rows, :]), dsXR[i], w=[("XR", i)])
                S.dma("pool", lambda e: e.dma_start(out=XB[i], in_=xs[rows, :]), dsXB[i], w=[("XB", i)])

            def s1a(t):
                i = t % 2
                xtc = XTC[i]
                for hf in range(2):
                    pt = pst_banks[hf]
                    for kk in range(4):
                        k = hf * 4 + kk
                        S.op("pe", lambda e, pt=pt, kk=kk, k=k: e.transpose(
                            pt[:, kk * 128:(kk + 1) * 128], XB[i][:, k * 128:(k + 1) * 128], identb[:]),
                            r=[("XB", i), "identb"], w=[("pst", hf)])
                    dst = xtc[:, hf * 4:hf * 4 + 4, :]
                    if hf == 0:
                        S.op("act", lambda e, pt=pt, dst=dst: e.copy(dst, pt.rearrange("p (c t) -> p c t", c=4)),
                             r=[("pst", hf)], w=[("XTC", i, hf)])
                    else:
                        S.op("dve", lambda e, pt=pt, dst=dst: e.tensor_copy(dst, pt.rearrange("p (c t) -> p c t", c=4)),
                             r=[("pst", hf)], w=[("XTC", i, hf)])

            def s1b_mm(t):
                i = t % 2
                xtc = XTC[i]
                xk = [("XTC", i, 0), ("XTC", i, 1)]
                ug, vg, zb = UG[i], VG[i], ZB[i]
                th = hh[:, 0:512]
                pss = {}
                for j in (1, 0, 2):
                    bb, pb, kb_ = next_bank()
                    pss[j] = (pb, kb_)
                    for k in range(8):
                        S.op("pe", lambda e, pb=pb, j=j, k=k: e.matmul(
                            pb, xtc[:, k, :], wci[:, k, j * 512:(j + 1) * 512], start=(k == 0), stop=(k == 7)),
                            r=xk + ["WCI"], w=[kb_])
                S.op("act", lambda e: e.activation(vg, pss[1][0], AF.Gelu), r=[pss[1][1]], w=[("vg", i)])
                S.op("act", lambda e: e.activation(ug, pss[0][0], AF.Gelu), r=[pss[0][1]], w=[("ug", i)])
                S.op("act", lambda e: e.activation(th, pss[2][0], AF.Tanh, scale=0.5), r=[pss[2][1]], w=[("hh", 0)])
                S.op("act", lambda e: e.copy(zb, pss[2][0]), r=[pss[2][1]], w=[("zb", i)])
                S.op("pool", lambda e: e.tensor_tensor(th, th, zb, op=ALU.mult), r=[("hh", 0), ("zb", i)], w=[("hh", 0)])
                S.op("pool", lambda e: e.tensor_tensor(zb, th, zb, op=ALU.add), r=[("hh", 0), ("zb", i)], w=[("zb", i)])
                S.op("pool", lambda e: e.tensor_tensor(ug, ug, zb, op=ALU.mult), r=[("ug", i), ("zb", i)], w=[("ug", i)])

            def s1b_ln(t):
                i = t % 2
                vg, vn, st = VG[i], VN[i], st1[i]
                S.op("dve", lambda e: e.bn_stats(st[:, 0:6], vg), r=[("vg", i)], w=[("st1a", i)])
                S.op("dve", lambda e: e.bn_aggr(st[:, 6:8], st[:, 0:6]), r=[("st1a", i)], w=[("st1b", i)])
                S.op("dve", lambda e: e.tensor_scalar(st[:, 8:9], st[:, 7:8], EPS, None, op0=ALU.add),
                     r=[("st1b", i)], w=[("st1c", i)])
                S.op("pool", lambda e: e.tensor_tensor(st[:, 8:9], st[:, 8:9], neghalf[:, 0:1], op=ALU.pow),
                     r=[("st1c", i), "neghalf"], w=[("st1c", i)])
                S.op("dve", lambda e: e.scalar_tensor_tensor(out=vg, in0=vg, scalar=st[:, 6:7], in1=sgug[:],
                                                             op0=ALU.subtract, op1=ALU.mult),
                     r=[("vg", i), ("st1b", i), "sgug"], w=[("vg", i)])
                S.op("dve", lambda e: e.scalar_tensor_tensor(out=vn, in0=vg, scalar=st[:, 8:9], in1=sgub[:],
                                                             op0=ALU.mult, op1=ALU.add),
                     r=[("vg", i), ("st1c", i), "sgub"], w=[("vn", i)])

            def s2a(t):
                i = t % 2
                ug, vn = UG[i], VN[i]
                bsv, psv, ksv = next_bank()
                S.op("pe", lambda e: e.matmul(psv, bspb[:], indb[:], start=True, stop=False, skip_group_check=True),
                     r=["bspb", "indb"], w=[ksv])
                for h in range(8):
                    S.op("pe", lambda e, h=h: e.matmul(psv[:, h * 64:(h + 1) * 64], wsp[:, h, :],
                                                      vn[:, h * 64:(h + 1) * 64], start=False, stop=(h == 7),
                                                      skip_group_check=True),
                         r=[("wsp", h), ("vn", i)], w=[ksv])
                S.op("dve", lambda e: e.scalar_tensor_tensor(out=ob, in0=psv, scalar=0.5, in1=ug, op0=ALU.mult, op1=ALU.mult),
                     r=[ksv, ("ug", i)], w=["ob"])

            def s2b(t):
                pt = pst_banks[0]
                for c in range(4):
                    S.op("pe", lambda e, c=c: e.transpose(pt[:, c * 128:(c + 1) * 128],
                                                         ob[:, c * 128:(c + 1) * 128], identb[:]),
                         r=["ob", "identb"], w=[("pst", 0)])
                S.op("act", lambda e: e.copy(obT, pt), r=[("pst", 0)], w=["obT"])

            def s2c(t):
                i = t % 2
                lt = slice((t - 16 * p) * 128, (t - 16 * p + 1) * 128)
                outs = []
                for nh in range(2):
                    bo, po, ko = next_bank()
                    outs.append((po, ko))
                    for c in range(8):
                        if c < 4:
                            lhsT = MG[:, c, lt]
                            rr = []
                        else:
                            lhsT = obT[:, (c - 4) * 128:(c - 3) * 128]
                            rr = ["obT"]
                        S.op("pe", lambda e, po=po, lhsT=lhsT, c=c, nh=nh: e.matmul(
                            po, lhsT, wco[:, c, nh * 512:(nh + 1) * 512], start=(c == 0), stop=(c == 7)),
                            r=rr + ["WCO"], w=[ko])
                for nh in range(2):
                    po, ko = outs[nh]
                    S.op("dve", lambda e, po=po, nh=nh: e.scalar_tensor_tensor(
                        out=hh[:, nh * 512:(nh + 1) * 512], in0=XR[i][:, nh * 512:(nh + 1) * 512], scalar=ALPHA,
                        in1=po, op0=ALU.mult, op1=ALU.add),
                        r=[ko, ("XR", i)], w=[("hh", nh)])
                    S.op("dve", lambda e, nh=nh: e.bn_stats(st2[:, 6 * nh:6 * nh + 6], hh[:, nh * 512:(nh + 1) * 512]),
                         r=[("hh", nh)], w=[("st2a", nh)])
                S.op("dve", lambda e: e.bn_aggr(st2[:, 12:14], st2[:, 0:12]), r=[("st2a", 0), ("st2a", 1)], w=["st2b"])
                S.op("dve", lambda e: e.tensor_scalar(st2[:, 14:15], st2[:, 13:14], EPS, None, op0=ALU.add),
                     r=["st2b"], w=["st2c"])
                S.op("pool", lambda e: e.tensor_tensor(st2[:, 14:15], st2[:, 14:15], neghalf[:, 0:1], op=ALU.pow),
                     r=["st2c", "neghalf"], w=["st2c"])
                S.op("dve", lambda e: e.scalar_tensor_tensor(out=hh, in0=hh, scalar=st2[:, 12:13], in1=lng[:],
                                                             op0=ALU.subtract, op1=ALU.mult),
                     r=[("hh", 0), ("hh", 1), "st2b", "lng"], w=[("hh", 0), ("hh", 1)])
                yo = YO[i]
                S.op("dve", lambda e: e.scalar_tensor_tensor(out=yo, in0=hh, scalar=st2[:, 14:15], in1=lnb[:],
                                                             op0=ALU.mult, op1=ALU.add),
                     r=[("hh", 0), ("hh", 1), "st2c", "lnb"], w=[("YO", i)])
                S.dma("sp", lambda e: e.dma_start(out=y[t * 128:(t + 1) * 128, :], in_=yo),
                      dsY[i], r=[("YO", i)])

            def load_xb(t):
                i = t % 2
                gslot = 4 * (t // 2) + (t % 2)
                rows = slice(gslot * 128, (gslot + 1) * 128)
                S.dma("pool", lambda e: e.dma_start(out=XB[i], in_=xs[rows, :]), dsXB[i], w=[("XB", i)])

            def load_xr(t):
                i = t % 2
                gslot = 4 * (t // 2) + (t % 2)
                rows = slice(gslot * 128, (gslot + 1) * 128)
                S.dma("sp", lambda e: e.dma_start(out=XR[i], in_=xs[rows, :]), dsXR[i], w=[("XR", i)])

            blocks = list(range(16 * p, 16 * p + 16))
            nb = len(blocks)
            load_xb(blocks[0])
            load_xb(blocks[1])
            load_xr(blocks[0])
            load_xr(blocks[1])
            s1a(blocks[0])
            s1b_mm(blocks[0])
            s1b_ln(blocks[0])
            s1a(blocks[1])
            load_xb(blocks[2])
            for bi, t in enumerate(blocks):
                s2a(t)
                if bi + 1 < nb:
                    s1b_mm(blocks[bi + 1])
                if bi + 2 < nb:
                    s1a(blocks[bi + 2])
                if bi + 3 < nb:
                    load_xb(blocks[bi + 3])
                s2b(t)
                if bi + 1 < nb:
                    s1b_ln(blocks[bi + 1])
                s2c(t)
                if bi + 2 < nb:
                    load_xr(blocks[bi + 2])

        for p in passes:
            if "A" in phases:
                phase_A(p)
                S.barrier()
            if "B" in phases:
                if "C" in phases and WC_EARLY:
                    load_WC()
                phase_B(p)
                S.barrier()
            if "C" in phases:
                if "B" not in phases or not WC_EARLY:
                    load_WC()
                phase_C(p)
                S.barrier()
        S.emit(nc, block, esems, dsY)
    return nc


def _slot_perm(par):
    order = []
    for m in range(16):
        if par == 0:
            order += [4 * m, 4 * m + 3, 4 * m + 1, 4 * m + 2]
        else:
            order += [4 * m + 1, 4 * m + 2, 4 * m, 4 * m + 3]
    return order


def make_in_maps(x, positions, w_in, q_norm_g, w_uq, kv_norm_g, w_ukv, sgu_norm_g, sgu_norm_b,
                 w_spatial, b_spatial, w_out, ln_g, ln_b):
    f = np.float32
    w_in = np.ascontiguousarray(w_in, dtype=f)
    swap = np.concatenate([np.arange(16, 32), np.arange(0, 16)])
    w_krs = np.ascontiguousarray(w_in[:, 384:416][:, swap])
    w_uq = np.ascontiguousarray(w_uq, dtype=f)
    w_uqs = np.zeros_like(w_uq)
    for h in range(8):
        w_uqs[:, h * 96 + 64:h * 96 + 96] = w_uq[:, h * 96 + 64:h * 96 + 96][:, swap]
    qg = np.ascontiguousarray(np.asarray(q_norm_g, dtype=f).reshape(2, 128).T)
    kvg = np.ascontiguousarray(np.asarray(kv_norm_g, dtype=f).reshape(128, 1))
    wspT = np.ascontiguousarray(np.transpose(np.asarray(w_spatial, dtype=f), (2, 0, 1)))
    si = np.arange(128)
    tril = (si[:, None] <= si[None, :]).astype(f)
    bspT = np.ascontiguousarray(np.asarray(b_spatial, dtype=f).T)
    ident = np.eye(128, dtype=f)
    invf = (1.0 / (10000.0 ** (np.arange(16, dtype=np.float64) / 16.0)))
    ropec = np.zeros((128, 6), f)
    ropec[:, 1] = -TWO_PI
    ropec[:, 2] = -TWO_PI
    ropec[:, 3] = 0.5 * math.pi
    for pp in range(64, 96):
        i = (pp - 64) % 16
        ropec[pp, 0] = invf[i] / TWO_PI
        ropec[pp, 1] = TWO_PI if pp < 80 else -TWO_PI
    tri = np.where(si[:, None] > si[None, :], NEG, 0.0).astype(f)
    allm = np.full((128, 128), NEG, f)
    zero = np.zeros((128, 128), f)
    ind_h = np.zeros((8, 512), f)
    for h in range(8):
        ind_h[h, h * 64:(h + 1) * 64] = 1.0
    common = dict(bsp_ht=np.ascontiguousarray(np.asarray(b_spatial, dtype=f)), ind_h=ind_h, w_in=w_in, w_krs=w_krs, w_uq=w_uq, w_uqs=w_uqs, qg=qg, w_ukv=np.ascontiguousarray(w_ukv, dtype=f),
                  kvg=kvg, wspT=wspT, tril=tril, bspT=bspT,
                  sgu_g=np.asarray(sgu_norm_g, dtype=f).reshape(1, 512), sgu_b=np.asarray(sgu_norm_b, dtype=f).reshape(1, 512),
                  ln_g=np.asarray(ln_g, dtype=f).reshape(1, D), ln_b=np.asarray(ln_b, dtype=f).reshape(1, D),
                  w_out=np.ascontiguousarray(w_out, dtype=f), ident=ident, ropec=ropec)
    in_maps = []
    perms = []
    for c in range(8):
        b, par = c // 2, c % 2
        order = _slot_perm(par)
        perms.append(order)
        xb = np.asarray(x[b], dtype=f).reshape(64, 128, D)[order].reshape(SEQ, D)
        pb = np.asarray(positions[b], dtype=np.int32).reshape(64, 128)[order].reshape(1, SEQ)
        mA = allm if par == 0 else zero
        mB = zero if par == 0 else allm
        mk = np.concatenate([tri, allm, tri, mA, allm, mB], axis=1)
        d = dict(common)
        d.update(xs=np.ascontiguousarray(xb), pos=np.ascontiguousarray(pb), masks=np.ascontiguousarray(mk))
        in_maps.append(d)
    return in_maps, perms


_NC_CACHE = {}


def kernel(x, positions, w_in, q_norm_g, w_uq, kv_norm_g, w_ukv, sgu_norm_g, sgu_norm_b,
           w_spatial, b_spatial, w_out, ln_g, ln_b):
    in_maps, perms = make_in_maps(x, positions, w_in, q_norm_g, w_uq, kv_norm_g, w_ukv, sgu_norm_g,
                                  sgu_norm_b, w_spatial, b_spatial, w_out, ln_g, ln_b)
    if "nc" not in _NC_CACHE:
        _NC_CACHE["nc"] = build_program()
    nc = _NC_CACHE["nc"]
    res = run_bass_kernel_spmd(nc, in_maps, core_ids=list(range(8)))
    out = np.empty((4, SEQ, D), np.float32)
    for c in range(8):
        b = c // 2
        order = perms[c]
        yc = np.asarray(res.results[c]["y"], dtype=np.float32).reshape(32, 128, D)
        ov = out[b].reshape(64, 128, D)
        for t in range(32):
            slot = 4 * (t // 2) + (t % 2)
            ov[order[slot]] = yc[t]
    return out
```

```python
import math
from contextlib import ExitStack

import numpy as np
import concourse.bass as bass
import concourse.mybir as mybir
from concourse.bass_utils import run_bass_kernel_spmd

F32 = mybir.dt.float32
BF16 = mybir.dt.bfloat16
I32 = mybir.dt.int32
AF = mybir.ActivationFunctionType
ALU = mybir.AluOpType

D = 1024
SEQ = 8192
NSLOT = 64
DIN = 2464
NA = 928
NEG = -30000.0
EPS = 1e-5
ALPHA = 2.0 ** 0.25
SCALE = 1.0 / math.sqrt(96.0)
TWO_PI = 2.0 * math.pi

ENGS = ("pe", "act", "dve", "pool", "sp")
import os
STOP = int(os.environ.get("KSTOP", "0"))
NGROUPS = int(os.environ.get("KGROUPS", "8"))
KB_STEPS = int(os.environ.get("KB_STEPS", "-1"))
WC_EARLY = int(os.environ.get("WC_EARLY", "1"))
TAB_ENG = os.environ.get("TAB_ENG", "dve")
KB_NOEPI = int(os.environ.get("KB_NOEPI", "0"))
KB_NOPV = int(os.environ.get("KB_NOPV", "0"))
KB_NOMASK = int(os.environ.get("KB_NOMASK", "0"))


class DmaSem:
    def __init__(self, sem):
        self.sem = sem
        self.count = 0


class Op:
    __slots__ = ("eng", "fn", "deps", "dma", "dsem", "dval", "signal", "sigval", "barrier")

    def __init__(self, eng, fn, deps, dma=False, dsem=None, dval=0):
        self.eng = eng
        self.fn = fn
        self.deps = deps
        self.dma = dma
        self.dsem = dsem
        self.dval = dval
        self.signal = False
        self.sigval = 0
        self.barrier = False


class Sched:
    def __init__(self):
        self.ops = []
        self.state = {}
        self.last_on = {e: None for e in ENGS}
        self.open_dma = []

    def _add(self, op):
        self.ops.append(op)
        idx = len(self.ops) - 1
        self.last_on[op.eng] = idx
        return idx

    def _mk(self, idxs):
        d = {}
        for i in idxs:
            p = self.ops[i]
            d[i] = p.dsem.count if p.dma else None
        return d

    def _deps(self, reads, writes):
        deps = set()
        for k in reads:
            st = self.state.get(k)
            if st is not None and st[0] is not None:
                deps.add(st[0])
        for k in writes:
            st = self.state.get(k)
            if st is not None:
                if st[0] is not None:
                    deps.add(st[0])
                deps.update(st[1])
        return self._mk(deps)

    def _commit(self, idx, reads, writes):
        me = self.ops[idx]
        for k in reads:
            st = self.state.setdefault(k, [None, []])
            if not me.dma:
                st[1] = [r for r in st[1] if self.ops[r].dma or self.ops[r].eng != me.eng]
            st[1].append(idx)
        for k in writes:
            self.state[k] = [idx, []]

    def op(self, eng, fn, r=(), w=()):
        deps = self._deps(r, w)
        idx = self._add(Op(eng, fn, deps))
        self._commit(idx, r, w)
        return idx

    def dma(self, queue, fn, dsem, r=(), w=()):
        deps = self._deps(r, w)
        dsem.count += 16
        idx = self._add(Op(queue, fn, deps, dma=True, dsem=dsem, dval=dsem.count))
        self._commit(idx, r, w)
        self.open_dma.append(idx)
        return idx

    def barrier(self):
        targets = set(i for i in self.last_on.values() if i is not None)
        targets.update(self.open_dma)
        for e in ENGS:
            o = Op(e, None, self._mk(targets))
            o.barrier = True
            self.ops.append(o)
        self.open_dma = []
        self.state = {}

    def emit(self, nc, block, esems, final_waits):
        ops = self.ops
        for i, o in enumerate(ops):
            for d in o.deps:
                p = ops[d]
                if p.dma:
                    continue
                if p.eng == o.eng and not o.barrier:
                    if p.eng == "pe" or p.eng == "sp":
                        continue
                p.signal = True
        cnt = {e: 0 for e in ENGS}
        for o in ops:
            if o.signal:
                cnt[o.eng] += 1
                o.sigval = cnt[o.eng]

        def stream(name):
            def body(e):
                seen = {}
                for o in ops:
                    if o.eng != name:
                        continue
                    need = {}
                    for d in sorted(o.deps):
                        p = ops[d]
                        if p.dma:
                            sem, val = p.dsem.sem, o.deps[d]
                        else:
                            if p.eng == name and (name == "pe" or name == "sp") :
                                continue
                            if p.eng == name and o.barrier and name == "pe":
                                continue
                            sem, val = esems[p.eng], p.sigval
                        key = id(sem)
                        if key not in need or need[key][1] < val:
                            need[key] = (sem, val)
                    for key, (sem, val) in need.items():
                        if seen.get(key, 0) >= val:
                            continue
                        seen[key] = val
                        e.wait_ge(sem, val)
                    if o.fn is None:
                        continue
                    ins = o.fn(e)
                    if o.dma:
                        ins.then_inc(o.dsem.sem, 16)
                    elif o.signal:
                        ins.then_inc(esems[name], 1)
                if name == "sp":
                    for ds in final_waits:
                        e.wait_ge(ds.sem, ds.count)
            return body

        block.sync(stream("sp"))
        block.tensor(stream("pe"))
        block.scalar(stream("act"))
        block.vector(stream("dve"))
        block.gpsimd(stream("pool"))


def build_program(passes=(0, 1), phases="ABC", debug=False):
    nc = bass.Bass("TRN2", target_bir_lowering=False)

    def din(name, shape, dt=F32):
        return nc.dram_tensor(name, shape, dt, kind="ExternalInput").ap()

    xs = din("xs", [SEQ, D])
    pos = din("pos", [1, SEQ], I32)
    w_in = din("w_in", [D, DIN])
    w_krs = din("w_krs", [D, 32])
    w_uq = din("w_uq", [256, 768])
    w_uqs = din("w_uqs", [256, 768])
    qg = din("qg", [128, 2])
    w_ukv = din("w_ukv", [128, 1024])
    kvg = din("kvg", [128, 1])
    wspT = din("wspT", [128, 8, 128])
    tril = din("tril", [128, 128])
    bspT = din("bspT", [128, 8])
    bsp_ht = din("bsp_ht", [8, 128])
    ind_h = din("ind_h", [8, 512])
    sgu_g = din("sgu_g", [1, 512])
    sgu_b = din("sgu_b", [1, 512])
    ln_g = din("ln_g", [1, D])
    ln_b = din("ln_b", [1, D])
    w_out = din("w_out", [D, D])
    ident = din("ident", [128, 128])
    ropec = din("ropec", [128, 6])
    masks = din("masks", [128, 768])
    y = nc.dram_tensor("y", [SEQ // 2, D], F32, kind="ExternalOutput").ap()
    w_in_bf = nc.dram_tensor("w_in_bf", [D, DIN], BF16, kind="Internal").ap()
    w_krs_bf = nc.dram_tensor("w_krs_bf", [D, 32], BF16, kind="Internal").ap()
    w_out_bf = nc.dram_tensor("w_out_bf", [D, D], BF16, kind="Internal").ap()
    dbg = {}
    if debug:
        def dout(name, shape):
            dbg[name] = nc.dram_tensor(name, shape, F32, kind="ExternalOutput").ap()
        dout("d_ckvn", [128, 512])
        dout("d_krope", [32, 512])
        dout("d_q0", [96, 512])
        dout("d_g", [128, 512])
        dout("d_kb", [96, 512])
        dout("d_vb", [128, 512])
        dout("d_merged", [128, 512])

    S = Sched()
    with ExitStack() as es:
        def sb(name, shape, dt):
            return es.enter_context(nc.sbuf_tensor(name, shape, dt))

        def ps(name, shape, dt):
            return es.enter_context(nc.psum_tensor(name, shape, dt))

        def newsem(name):
            return es.enter_context(nc.semaphore(name))

        dsem_n = [0]

        def dsem():
            dsem_n[0] += 1
            return DmaSem(newsem("dq%d" % dsem_n[0]))

        identb = sb("identb", [128, 128], BF16)
        onesf = sb("onesf", [128, 128], F32)
        maskb = sb("maskb", [128, 768], BF16)
        ropecs = sb("ropecs", [128, 6], F32)
        epst = sb("epst", [128, 1], F32)
        neghalf = sb("neghalf", [128, 1], F32)
        identf = sb("identf", [128, 128], F32)
        rt = sb("rt", [128, 16], F32)
        wuq_g = sb("wuq_g", [128, 2, 768], BF16)
        wuqs_g = sb("wuqs_g", [128, 2, 768], BF16)
        wukv_g = sb("wukv_g", [128, 1024], BF16)
        wsp = sb("wsp", [128, 8, 128], BF16)
        bsp = sb("bsp", [128, 8], F32)
        bspb = sb("bspb", [8, 128], BF16)
        indb = sb("indb", [8, 512], BF16)
        sgug = sb("sgug", [128, 512], F32)
        sgub = sb("sgub", [128, 512], F32)
        lng = sb("lng", [128, D], F32)
        lnb = sb("lnb", [128, D], F32)
        ckvn = sb("ckvn", [128, SEQ], BF16)
        KB = [sb("kb%d" % i, [96, SEQ], BF16) for i in range(2)]
        QT = sb("qt", [96, 8, 2048], BF16)
        MG = sb("mg", [128, 4, 2048], BF16)
        ARENA = 86528
        arena = sb("arena", [128, ARENA // 2], BF16)

        class Carver:
            def __init__(self):
                self.off = 0

            def take(self, cols, dt, parts=128):
                size = 4 if dt in (F32, I32) else 2
                self.off = (self.off + 3) // 4 * 4
                nbytes = cols * size
                assert self.off + nbytes <= ARENA, ("arena overflow", self.off, nbytes)
                v = arena[0:parts, self.off // 2:(self.off + nbytes) // 2]
                self.off += nbytes
                if dt != BF16:
                    v = v.bitcast(dt)
                return v

        PT4 = [ps("pt%d" % i, [128, 2, 512], F32) for i in range(4)]
        PSW = [PT4[0], PT4[1], PT4[2]]
        PSO1 = PT4[3][:, 0, :]
        PSG = PT4[3][:, 1, :]
        pst_banks = [PT4[3][:, 0, :].bitcast(BF16)[:, 0:512],
                     PT4[3][:, 1, :].bitcast(BF16)[:, 0:512]]

        def bank(i):
            return PT4[i // 2][:, i % 2, :]

        bank_rr = [0]

        def next_bank():
            b = bank_rr[0] % 6
            bank_rr[0] += 1
            return b, bank(b), ("ps", b)

        esems = {e: newsem("e_" + e) for e in ENGS}
        block = es.enter_context(nc.Block())

        cv = Carver()
        st_a = cv.take(2 * 768, F32)
        st_b = cv.take(2 * 768, F32)
        st_c = cv.take(1024, F32)
        st_d = cv.take(8 * 128, F32)
        st_e = cv.take(128, F32)
        qgs = cv.take(2, F32)
        kvgs = cv.take(1, F32)
        ds0 = dsem()
        ds0p = dsem()
        S.dma("pool", lambda e: e.dma_start(out=identb[:], in_=ident), ds0p, w=["identb"])
        S.dma("pool", lambda e: e.dma_start(out=maskb[:], in_=masks), ds0p, w=["maskb"])
        S.dma("pool", lambda e: e.dma_start(out=bspb[:], in_=bsp_ht), ds0p, w=["bspb"])
        S.dma("pool", lambda e: e.dma_start(out=indb[:], in_=ind_h), ds0p, w=["indb"])
        S.dma("sp", lambda e: e.dma_start(out=ropecs[:], in_=ropec), ds0, w=["ropecs"])
        S.dma("sp", lambda e: e.dma_start(out=identf[:], in_=ident), ds0, w=["identf"])
        S.dma("sp", lambda e: e.dma_start(out=st_a.rearrange("p (c n) -> p c n", c=2),
                                          in_=w_uq.rearrange("(c p) n -> p c n", p=128)), ds0, w=["st_a"])
        S.dma("sp", lambda e: e.dma_start(out=st_b.rearrange("p (c n) -> p c n", c=2),
                                          in_=w_uqs.rearrange("(c p) n -> p c n", p=128)), ds0, w=["st_b"])
        S.dma("sp", lambda e: e.dma_start(out=st_c, in_=w_ukv), ds0, w=["st_c"])
        S.dma("sp", lambda e: e.dma_start(out=st_d.rearrange("p (h t) -> p h t", h=8), in_=wspT), ds0, w=["st_d"])
        S.dma("sp", lambda e: e.dma_start(out=st_e, in_=tril), ds0, w=["st_e"])
        S.dma("sp", lambda e: e.dma_start(out=qgs, in_=qg), ds0, w=["qgs"])
        S.dma("sp", lambda e: e.dma_start(out=kvgs, in_=kvg), ds0, w=["kvgs"])
        S.dma("sp", lambda e: e.dma_start(out=bsp[:], in_=bspT), ds0, w=["bsp"])
        S.dma("sp", lambda e: e.dma_start(out=sgug[:], in_=sgu_g.partition_broadcast(128)), ds0, w=["sgug"])
        S.dma("sp", lambda e: e.dma_start(out=sgub[:], in_=sgu_b.partition_broadcast(128)), ds0, w=["sgub"])
        S.dma("sp", lambda e: e.dma_start(out=lng[:], in_=ln_g.partition_broadcast(128)), ds0, w=["lng"])
        S.dma("sp", lambda e: e.dma_start(out=lnb[:], in_=ln_b.partition_broadcast(128)), ds0, w=["lnb"])
        dscv = dsem()
        S.op("dve", lambda e: e.memset(onesf[:], 1.0), w=["onesf"])
        S.op("dve", lambda e: e.memset(epst[:], EPS), w=["epst"])
        S.op("dve", lambda e: e.memset(neghalf[:], -0.5), w=["neghalf"])
        for c in range(2):
            S.op("dve", lambda e, c=c: e.tensor_scalar(wuq_g[:, c, :], st_a[:, c * 768:(c + 1) * 768],
                                                       qgs[:, c:c + 1], None, op0=ALU.mult),
                 r=["st_a", "qgs"], w=[("wuq", c)])
            S.op("dve", lambda e, c=c: e.tensor_scalar(wuqs_g[:, c, :], st_b[:, c * 768:(c + 1) * 768],
                                                       qgs[:, c:c + 1], None, op0=ALU.mult),
                 r=["st_b", "qgs"], w=[("wuqs", c)])
        S.op("dve", lambda e: e.tensor_scalar(wukv_g[:], st_c, kvgs[:, 0:1], None, op0=ALU.mult),
             r=["st_c", "kvgs"], w=["wukv"])
        for h in range(8):
            S.op("dve", lambda e, h=h: e.tensor_tensor(wsp[:, h, :], st_d[:, h * 128:(h + 1) * 128], st_e,
                                                       op=ALU.mult), r=["st_d", "st_e"], w=[("wsp", h)])
        S.barrier()

        def phase_A(p):
            cv = Carver()
            WA = cv.take(8 * NA, BF16).rearrange("p (c n) -> p c n", c=8)
            WK = cv.take(8 * 32, BF16).rearrange("p (c n) -> p c n", c=8)
            XS = [cv.take(4 * D, BF16).rearrange("p (s d) -> p s d", s=4) for _ in range(2)]
            XT = cv.take(8 * 512, BF16).rearrange("p (c t) -> p c t", c=8)
            POSI = [cv.take(512, I32) for _ in range(2)]
            posf = cv.take(512, F32)
            tk = posf.bitcast(I32)
            TABS = [[cv.take(512, F32) for _ in range(2)] for _ in range(2)]
            tt = cv.take(512, F32)
            tkf = cv.take(512, F32)
            sq = [cv.take(512, F32) for _ in range(2)]
            rstd = cv.take(512, F32)
            dg = cv.take(512, F32)
            dg2 = cv.take(256, F32)
            rstdq = cv.take(256, F32)
            KR = [[cv.take(512, F32) for _ in range(2)] for _ in range(2)]
            CQB = [cv.take(512, BF16) for _ in range(2)]
            crsr = cv.take(512, F32)
            qtmp = [cv.take(512, F32) for _ in range(2)]
            dsW = dsem()
            dsX = [dsem(), dsem()]
            dsP = [dsem(), dsem()]
            dsD = dsem()
            groups = list(range(8 * p, 8 * p + NGROUPS))

            def load_group(G):
                i = G % 2
                S.dma("pool", lambda e: e.dma_start(
                    out=XS[i], in_=xs[G * 512:(G + 1) * 512, :].rearrange("(s p) d -> p s d", p=128)),
                    dsX[i], w=[("XS", i)])
                S.dma("sp", lambda e: e.dma_start(
                    out=POSI[i], in_=pos[0:1, G * 512:(G + 1) * 512].partition_broadcast(128)),
                    dsP[i], w=[("POSI", i)])

            load_group(groups[0])
            if p == 0:
                dsW0 = dsem()
                S.dma("pool", lambda e: e.dma_start(out=WA, in_=w_in.rearrange("(c p) n -> p c n", p=128)[:, :, 0:NA]),
                      dsW0, w=["WA"])
                S.dma("pool", lambda e: e.dma_start(out=WK, in_=w_krs.rearrange("(c p) n -> p c n", p=128)),
                      dsW0, w=["WK"])
            else:
                S.dma("sp", lambda e: e.dma_start(out=WA, in_=w_in_bf.rearrange("(c p) n -> p c n", p=128)[:, :, 0:NA]),
                      dsW, w=["WA"])
                S.dma("sp", lambda e: e.dma_start(out=WK, in_=w_krs_bf.rearrange("(c p) n -> p c n", p=128)),
                      dsW, w=["WK"])
            if len(groups) > 1:
                load_group(groups[1])
            if p == 0:
                S.dma("pool", lambda e: e.dma_start(out=w_in_bf, in_=w_in), dscv, w=["w_in_bf"])
                S.dma("pool", lambda e: e.dma_start(out=w_krs_bf, in_=w_krs), dscv, w=["w_krs_bf"])
                S.dma("pool", lambda e: e.dma_start(out=w_out_bf, in_=w_out), dscv, w=["w_out_bf"])

            def make_tables(G):
                i = G % 2
                S.op("act", lambda e: e.activation(tt, POSI[i], AF.Identity, scale=ropecs[:, 0:1]),
                     r=[("POSI", i), "ropecs"], w=["tt"])
                S.op("dve", lambda e: e.tensor_copy(tk, tt), r=["tt"], w=["posf"])
                S.op("dve", lambda e: e.tensor_copy(tkf, tk), r=["posf"], w=["tkf"])
                S.op("dve", lambda e: e.tensor_tensor(tt, tt, tkf, op=ALU.subtract), r=["tt", "tkf"], w=["tt"])
                S.op("dve", lambda e: e.scalar_tensor_tensor(out=tt, in0=tt, scalar=0.5, in1=tt,
                                                             op0=ALU.is_gt, op1=ALU.subtract),
                     r=["tt"], w=["tt"])
                S.op("act", lambda e: e.activation(TABS[i][1], tt, AF.Sin, scale=ropecs[:, 1:2]),
                     r=["tt", "ropecs"], w=[("tab", i, 1)])
                S.op("act", lambda e: e.activation(tkf, tt, AF.Abs), r=["tt"], w=["tkf"])
                S.op("act", lambda e: e.activation(TABS[i][0], tkf, AF.Sin, bias=ropecs[:, 3:4], scale=ropecs[:, 2:3]),
                     r=["tkf", "ropecs"], w=[("tab", i, 0)])

            def stage_x1(G):
                i = G % 2
                xt = XT
                for k in range(8):
                    half = k % 2
                    pt = pst_banks[half]
                    for s_ in range(4):
                        S.op("pe", lambda e, pt=pt, s_=s_, k=k: e.transpose(
                            pt[:, s_ * 128:(s_ + 1) * 128], XS[i][:, s_, k * 128:(k + 1) * 128], identb[:]),
                            r=[("XS", i), "identb"], w=[("pst", half)])
                    if k % 2 == 0:
                        S.op("act", lambda e, pt=pt, k=k: e.copy(xt[:, k, :], pt),
                             r=[("pst", half)], w=[("XT", k)])
                    else:
                        S.op("dve", lambda e, pt=pt, k=k: e.tensor_copy(xt[:, k, :], pt),
                             r=[("pst", half)], w=[("XT", k)])

            def stage_x2(G):
                i = G % 2
                tok = slice(G * 512, (G + 1) * 512)
                lt = slice((G - 8 * p) * 256, (G - 8 * p) * 256 + 256)
                xt = XT
                bc, pc, kc = next_bank()
                for k in range(8):
                    S.op("pe", lambda e, pc=pc, k=k: e.matmul(pc, WA[:, k, 256:384], xt[:, k, :],
                                                              start=(k == 0), stop=(k == 7)),
                         r=["WA", ("XT", k)], w=[kc])
                br, pr, kr = next_bank()
                for k in range(8):
                    S.op("pe", lambda e, pr=pr, k=k: e.matmul(pr[64:96, :], WA[:, k, 384:416], xt[:, k, :],
                                                              start=(k == 0), stop=(k == 7), tile_position=(0, 64)),
                         r=["WA", ("XT", k)], w=[kr])
                bs, psw, ks = next_bank()
                for k in range(8):
                    S.op("pe", lambda e, psw=psw, k=k: e.matmul(psw[64:96, :], WK[:, k, :], xt[:, k, :],
                                                                start=(k == 0), stop=(k == 7), tile_position=(0, 64)),
                         r=["WK", ("XT", k)], w=[ks])
                S.op("act", lambda e, pc=pc: e.copy(ckvn[:, tok], pc), r=[kc], w=[("ckvn", G)])
                S.op("act", lambda e, pc=pc: e.activation(sq[0], pc, AF.Square), r=[kc], w=[("sq", 0)])
                S.op("act", lambda e, pr=pr: e.copy(KR[i][0][64:96, :], pr[64:96, :]), r=[kr], w=[("KR", i, 0)])
                S.op("act", lambda e, psw=psw: e.copy(KR[i][1][64:96, :], psw[64:96, :]), r=[ks], w=[("KR", i, 1)])
                bcq, pcq, kcq = next_bank()
                for c in range(2):
                    for k in range(8):
                        S.op("pe", lambda e, pcq=pcq, c=c, k=k: e.matmul(
                            pcq[:, c * 256:(c + 1) * 256], WA[:, k, c * 128:(c + 1) * 128], xt[:, k, 0:256],
                            start=(k == 0), stop=(k == 7)),
                            r=["WA", ("XT", k)], w=[kcq])
                S.op("act", lambda e, pcq=pcq: e.copy(CQB[i], pcq), r=[kcq], w=[("cqb", i)])
                S.op("act", lambda e, pcq=pcq: e.activation(sq[1], pcq, AF.Square), r=[kcq], w=[("sq", 1)])
                for cc in range(2):
                    bz, pz, kz = next_bank()
                    for c2 in range(2):
                        c = cc * 2 + c2
                        for k in range(8):
                            S.op("pe", lambda e, pz=pz, c=c, c2=c2, k=k: e.matmul(
                                pz[:, c2 * 256:(c2 + 1) * 256], WA[:, k, 416 + c * 128:416 + (c + 1) * 128],
                                xt[:, k, 0:256], start=(k == 0), stop=(k == 7)),
                                r=["WA", ("XT", k)], w=[kz])
                    S.op("act", lambda e, pz=pz, cc=cc: e.activation(
                        MG[:, 2 * cc:2 * cc + 2, lt], pz.rearrange("p (c t) -> p c t", c=2), AF.Silu),
                        r=[kz], w=[("MG", cc, G)])
                bq, pq, kq = next_bank()
                for j in range(4):
                    S.op("pe", lambda e, pq=pq, j=j: e.matmul(pq[:, j:j + 1], sq[0][:, j * 128:(j + 1) * 128], onesf[:, 0:1],
                                                              start=True, stop=True, skip_group_check=True),
                         r=[("sq", 0), "onesf"], w=[kq])
                for j in range(2):
                    for c in range(2):
                        S.op("pe", lambda e, pq=pq, c=c, j=j: e.matmul(
                            pq[:, 4 + j:5 + j], sq[1][:, c * 256 + j * 128:c * 256 + (j + 1) * 128], onesf[:, 0:1],
                            start=(c == 0), stop=(c == 1), skip_group_check=True),
                            r=[("sq", 1), "onesf"], w=[kq])
                ro = 8 * i
                S.op("act", lambda e, pq=pq: e.activation(rt[:, ro:ro + 4], pq[:, 0:4], AF.Identity, bias=epst[:, 0:1],
                                                          scale=1.0 / 128.0), r=[kq, "epst"], w=[("rt", i)])
                S.op("act", lambda e, pq=pq: e.activation(rt[:, ro + 4:ro + 6], pq[:, 4:6], AF.Identity, bias=epst[:, 0:1],
                                                          scale=1.0 / 256.0), r=[kq, "epst"], w=[("rt", i)])
                S.op("pool", lambda e: e.tensor_tensor(rt[:, ro:ro + 6], rt[:, ro:ro + 6],
                                                       neghalf[:, 0:1].to_broadcast([128, 6]), op=ALU.pow),
                     r=[("rt", i), "neghalf"], w=[("rt", i)])

            def stage_yn(G):
                i = G % 2
                tok = slice(G * 512, (G + 1) * 512)
                lt = slice((G - 8 * p) * 256, (G - 8 * p) * 256 + 256)
                tabs = TABS[i]
                tabk = [("tab", i, 0), ("tab", i, 1)]
                ro = 8 * i
                cqb = CQB[i]
                for j in range(4):
                    S.op("dve", lambda e, j=j: e.tensor_scalar(dg[:, j * 128:(j + 1) * 128], identf[:], rt[:, ro + j:ro + j + 1],
                                                               None, op0=ALU.mult), r=[("rt", i), "identf"], w=["dg"])
                for j in range(2):
                    S.op("dve", lambda e, j=j: e.tensor_scalar(dg2[:, j * 128:(j + 1) * 128], identf[:],
                                                               rt[:, ro + 4 + j:ro + 5 + j], None, op0=ALU.mult),
                         r=[("rt", i), "identf"], w=["dg2"])
                bb, pbc, kbc = next_bank()
                for j in range(4):
                    S.op("pe", lambda e, pbc=pbc, j=j: e.matmul(pbc[:, j * 128:(j + 1) * 128], onesf[:],
                                                                dg[:, j * 128:(j + 1) * 128], start=True, stop=True,
                                                                skip_group_check=True),
                         r=["dg", "onesf"], w=[kbc])
                bb2, pbc2, kbc2 = next_bank()
                for j in range(2):
                    S.op("pe", lambda e, pbc2=pbc2, j=j: e.matmul(pbc2[:, j * 128:(j + 1) * 128], onesf[:],
                                                                  dg2[:, j * 128:(j + 1) * 128], start=True, stop=True,
                                                                  skip_group_check=True),
                         r=["dg2", "onesf"], w=[kbc2])
                S.op("act", lambda e, pbc=pbc: e.copy(rstd, pbc), r=[kbc], w=["rstd"])
                S.op("act", lambda e, pbc2=pbc2: e.copy(rstdq[:, 0:256], pbc2[:, 0:256]), r=[kbc2], w=["rstdq"])
                S.op("pool", lambda e: e.tensor_tensor(ckvn[:, tok], ckvn[:, tok], rstd, op=ALU.mult),
                     r=[("ckvn", G), "rstd"], w=[("ckvn", G)])
                kr0, kr1 = KR[i]
                S.op("dve", lambda e: e.tensor_tensor(kr0[64:96, :], kr0[64:96, :], tabs[0][64:96, :], op=ALU.mult),
                     r=[("KR", i, 0), tabk[0]], w=[("KR", i, 0)])
                S.op("dve", lambda e: e.tensor_tensor(kr1[64:96, :], kr1[64:96, :], tabs[1][64:96, :], op=ALU.mult),
                     r=[("KR", i, 1), tabk[1]], w=[("KR", i, 1)])
                S.op("pool", lambda e: e.tensor_tensor(KB[0][64:96, tok], kr0[64:96, :], kr1[64:96, :], op=ALU.add),
                     r=[("KR", i, 0), ("KR", i, 1)], w=[("kbr", 0, G)])
                S.op("pool", lambda e: e.tensor_copy(KB[1][64:96, tok], KB[0][64:96, tok]),
                     r=[("kbr", 0, G)], w=[("kbr", 1, G)])
                for ti in range(2):
                    S.op("dve", lambda e, ti=ti: e.tensor_tensor(crsr[0:96, ti * 256:(ti + 1) * 256],
                                                                 tabs[ti][0:96, 0:256], rstdq[0:96, 0:256], op=ALU.mult),
                         r=[tabk[ti], "rstdq"], w=["crsr"])

            def stage_yq(G):
                i = G % 2
                lt = slice((G - 8 * p) * 256, (G - 8 * p) * 256 + 256)
                tok = slice(G * 512, (G + 1) * 512)
                cqb = CQB[i]
                for h in range(8):
                    bh, ph_, kh = next_bank()
                    for half, wsrc, wk in ((0, wuq_g, "wuq"), (1, wuqs_g, "wuqs")):
                        for c in range(2):
                            S.op("pe", lambda e, ph_=ph_, half=half, wsrc=wsrc, c=c, h=h: e.matmul(
                                ph_[0:96, half * 256:(half + 1) * 256], wsrc[:, c, h * 96:(h + 1) * 96],
                                cqb[:, c * 256:(c + 1) * 256], start=(c == 0), stop=(c == 1)),
                                r=[(wk, c), ("cqb", i)], w=[kh])
                    qt_ = qtmp[h % 2]
                    S.op("dve", lambda e, ph_=ph_, qt_=qt_: e.tensor_tensor(qt_[0:96, :], ph_[0:96, :], crsr[0:96, :],
                                                                            op=ALU.mult),
                         r=[kh, "crsr"], w=[("qtmp", h % 2)])
                    S.op("pool", lambda e, qt_=qt_, h=h: e.tensor_tensor(
                        QT[:, h, lt], qt_[0:96, 0:256], qt_[0:96, 256:512], op=ALU.add),
                        r=[("qtmp", h % 2)], w=[("QT", h, G)])
                if debug and G == 8 * p + 1 and p == 0:
                    dd = qtmp[0]
                    kk = ("qtmp", 0)
                    S.op("dve", lambda e: e.tensor_copy(dd, ckvn[:, tok]), r=[("ckvn", G), kk], w=[kk])
                    S.dma("sp", lambda e: e.dma_start(out=dbg["d_ckvn"], in_=dd), dsD, r=[kk])
                    S.op("dve", lambda e: e.tensor_copy(dd[64:96, :], KB[1][64:96, tok]),
                         r=[("kbr", 1, G), kk], w=[kk])
                    S.dma("sp", lambda e: e.dma_start(out=dbg["d_krope"], in_=dd[64:96, :]), dsD, r=[kk])
                    S.op("dve", lambda e: e.tensor_copy(dd[0:96, 0:256], QT[:, 0, lt]), r=[("QT", 0, G), kk], w=[kk])
                    S.op("dve", lambda e: e.tensor_copy(dd[0:96, 256:512], QT[:, 5, lt]), r=[("QT", 5, G), kk], w=[kk])
                    S.dma("sp", lambda e: e.dma_start(out=dbg["d_q0"], in_=dd[0:96, :]), dsD, r=[kk])
                    S.op("dve", lambda e: e.tensor_copy(dd[:, 0:256], MG[:, 0, lt]), r=[("MG", 0, G), kk], w=[kk])
                    S.op("dve", lambda e: e.tensor_copy(dd[:, 256:512], MG[:, 3, lt]), r=[("MG", 1, G), kk], w=[kk])
                    S.dma("sp", lambda e: e.dma_start(out=dbg["d_g"], in_=dd), dsD, r=[kk])

            ng = len(groups)
            stage_x1(groups[0])
            make_tables(groups[0])
            stage_x2(groups[0])
            if ng > 2:
                load_group(groups[2])
            for gi, G in enumerate(groups):
                nxt = groups[gi + 1] if gi + 1 < ng else None
                if nxt is not None:
                    stage_x1(nxt)
                stage_yn(G)
                if nxt is not None:
                    make_tables(nxt)
                    stage_x2(nxt)
                stage_yq(G)
                if gi + 3 < ng:
                    load_group(groups[gi + 3])

        def phase_B(p):
            cv = Carver()
            nslots = 32 * (p + 1)
            VB = [cv.take(64 * 65, BF16).rearrange("p (s d) -> p s d", s=64),
                  cv.take(64 * 128, BF16).rearrange("p (s d) -> p s d", s=64)]
            PT = [cv.take(1024, BF16).rearrange("p (j n) -> p j n", j=2) for _ in range(3)]
            osb = [cv.take(512, F32) for _ in range(2)]
            rec = [cv.take(512, F32) for _ in range(2)]
            otmp = [cv.take(512, F32) for _ in range(2)]
            dsD = dsem()
            S.op("pool", lambda e: e.memset(VB[0][:, :, 64:65], 1.0), w=[("vbc", 0)])
            S.op("pool", lambda e: e.memset(VB[1][:, :, 0:64], 0.0), w=[("vbc", 1)])
            S.op("pool", lambda e: e.memset(VB[1][:, :, 0:1], 1.0), w=[("vbc", 1)])

            def vaug(b, h, slot):
                return VB[b][:, slot, :]

            def vdst(b, c):
                if b == 0:
                    return VB[0][:, c * 8:(c + 1) * 8, 0:64]
                return VB[1][:, c * 8:(c + 1) * 8, 64:128]

            def gen_chunks(h, upfront=False):
                b = h % 2
                out = []
                cnt = [0]

                def target():
                    n = cnt[0]
                    cnt[0] += 1
                    if not upfront:
                        return PSG, "psg", "dve"
                    t = n % 3
                    return PSW[t][:, (n // 3) % 2, :], ("psw", t), ("dve" if n % 2 == 0 else "act")

                for c in range(nslots // 4):
                    def kchunk(c=c):
                        pg, kg, eng = target()
                        tok = slice(c * 512, (c + 1) * 512)
                        S.op("pe", lambda e: e.matmul(pg[0:64, :], wukv_g[:, h * 128:h * 128 + 64], ckvn[:, tok],
                                                      start=True, stop=True), r=["wukv"], w=[kg])
                        if eng == "dve":
                            S.op("dve", lambda e: e.tensor_copy(KB[b][0:64, tok], pg[0:64, :]),
                                 r=[kg], w=[("kb", b, c)])
                        else:
                            S.op("act", lambda e: e.copy(KB[b][0:64, tok], pg[0:64, :]),
                                 r=[kg], w=[("kb", b, c)])
                    out.append(kchunk)
                for c in range(nslots // 8):
                    def vchunk(c=c):
                        pg, kg, eng = target()
                        for s in range(8):
                            slot = c * 8 + s
                            S.op("pe", lambda e, s=s, slot=slot: e.matmul(
                                pg[:, s * 64:(s + 1) * 64], ckvn[:, slot * 128:(slot + 1) * 128],
                                wukv_g[:, h * 128 + 64:h * 128 + 128], start=True, stop=True),
                                r=["wukv"], w=[kg])
                        if eng == "dve":
                            S.op("dve", lambda e: e.tensor_copy(vdst(b, c), pg.rearrange("p (s d) -> p s d", s=8)),
                                 r=[kg], w=[("vb", b, c)])
                        else:
                            S.op("act", lambda e: e.copy(vdst(b, c), pg.rearrange("p (s d) -> p s d", s=8)),
                                 r=[kg], w=[("vb", b, c)])
                    out.append(vchunk)
                return out

            steps = []
            for h in range(8):
                for g in range(4):
                    gg = 4 * p + g
                    nfull = 4 * gg
                    for i in range(nfull):
                        steps.append(dict(h=h, g=g, slots=(2 * i, 2 * i + 1), c0=0, masks=None))
                    base = 8 * gg
                    tails = [
                        ((base + 0, base + 1), 0, ((0, 128, 0), (256, 128, 128))),
                        ((base + 2, base + 3), 0, ((384, 128, 0), (640, 128, 128))),
                        ((base + 4, base + 5), 256, ((0, 128, 256), (256, 128, 384))),
                        ((base + 6, base + 7), 256, ((384, 128, 256), (640, 128, 384))),
                    ]
                    for slots, c0, mk in tails:
                        steps.append(dict(h=h, g=g, slots=slots, c0=c0, masks=mk))
                    steps[-1]["last"] = True
                    steps[-(nfull + 4)]["first"] = True
            for fn in gen_chunks(0, upfront=True):
                fn()
            pending_gen = []
            pending_epi = []
            pending_rcp = []
            n_steps = len(steps)
            if KB_STEPS >= 0:
                n_steps = KB_STEPS
            ocount = [0]

            def emit_qk(si):
                stp = steps[si]
                h, g, c0 = stp["h"], stp["g"], stp["c0"]
                b = h % 2
                wtile = PSW[si % 3]
                for j, slot in enumerate(stp["slots"]):
                    has_mask = stp["masks"] is not None
                    cj = stp["masks"][j][2] if has_mask else c0
                    S.op("pe", lambda e, j=j, slot=slot, has_mask=has_mask, cj=cj: e.matmul(
                        wtile[:, j, cj:512], KB[b][:, slot * 128:(slot + 1) * 128],
                        QT[:, h, g * 512 + cj:(g + 1) * 512], start=True, stop=(not has_mask or bool(KB_NOMASK)),
                        skip_group_check=True),
                        r=[("kb", b, slot // 4)], w=[("psw", si % 3)])
                    if has_mask and not KB_NOMASK:
                        mo, mw, mc = stp["masks"][j]
                        S.op("pe", lambda e, j=j, mo=mo, mw=mw, mc=mc: e.matmul(
                            wtile[:, j, mc:mc + mw], identb[:], maskb[:, mo:mo + mw], start=False, stop=True,
                            skip_group_check=True),
                            r=[], w=[("psw", si % 3)])

            def emit_exp(si):
                stp = steps[si]
                c0 = stp["c0"]
                S.op("act", lambda e: e.activation(PT[si % 3][:, :, c0:512], PSW[si % 3][:, :, c0:512], AF.Exp,
                                                   scale=SCALE),
                     r=[("psw", si % 3)], w=[("pt", si % 3)])

            def emit_pv(si):
                stp = steps[si]
                h, g, c0 = stp["h"], stp["g"], stp["c0"]
                b = h % 2
                if stp.get("first"):
                    ocount[0] += 1
                o = ocount[0] % 2
                for j, slot in enumerate(stp["slots"]):
                    first = bool(stp.get("first")) and j == 0
                    last = bool(stp.get("last")) and j == 1
                    mrows = 65 if b == 0 else 128
                    cj = stp["masks"][j][2] if stp["masks"] is not None else c0
                    S.op("pe", lambda e, j=j, slot=slot, first=first, last=last, mrows=mrows, cj=cj: e.matmul(
                        PSO1[0:mrows, cj:512], vaug(b, h, slot), PT[si % 3][:, j, cj:512], start=first, stop=last,
                        skip_group_check=True),
                        r=[("pt", si % 3), ("vb", b, slot // 8), ("vbc", b)], w=["pso"])
                if stp.get("last"):
                    mrows = 65 if b == 0 else 128
                    S.op("dve", lambda e, mrows=mrows: e.tensor_copy(osb[o][0:mrows, :], PSO1[0:mrows, :]),
                         r=["pso"], w=[("osb", o)])
                    dr = 64 if h % 2 == 0 else 0
                    for q4 in range(4):
                        pending_rcp.append((si + q4, o, dr, q4))
                    pending_epi.append((si + 6, h, g, o))

            def emit_epilogue(h, g, o):
                if KB_NOEPI:
                    return
                dr = 64 if h % 2 == 0 else 0
                rows = slice(0, 64) if h % 2 == 0 else slice(64, 128)
                r_ = rec[o]
                ot = otmp[o]
                ob_ = osb[o]
                S.op("pe", lambda e: e.matmul(PSG, onesf[dr:dr + 1, :], r_[dr:dr + 1, :], start=True, stop=True),
                     r=[("rec", o)], w=["psg"])
                S.op("dve", lambda e: e.tensor_tensor(ot[rows, :], ob_[rows, :], PSG[rows, :], op=ALU.mult),
                     r=[("osb", o), "psg"], w=[("otmp", o)])
                mg = MG[rows, h // 2, g * 512:(g + 1) * 512]
                S.op("pool", lambda e: e.tensor_tensor(mg, ot[rows, :], mg, op=ALU.mult),
                     r=[("otmp", o)], w=[("mgo", h, g)])

            if n_steps > 0:
                emit_qk(0)
            if n_steps > 1:
                emit_qk(1)
            cur_h = 0
            gen_every = 3 if p == 0 else 4
            since_gen = 0
            for si in range(n_steps):
                emit_exp(si)
                if si + 2 < n_steps:
                    emit_qk(si + 2)
                if not KB_NOPV:
                    emit_pv(si)
                while pending_rcp and pending_rcp[0][0] <= si:
                    _, ro_, rdr, q4 = pending_rcp.pop(0)
                    cs = slice(q4 * 128, (q4 + 1) * 128)
                    S.op("dve", lambda e, ro_=ro_, rdr=rdr, cs=cs: e.reciprocal(rec[ro_][rdr:rdr + 1, cs],
                                                                               osb[ro_][rdr:rdr + 1, cs]),
                         r=[("osb", ro_)], w=[("rec", ro_)])
                while pending_epi and pending_epi[0][0] <= si:
                    _, eh, eg, eo = pending_epi.pop(0)
                    emit_epilogue(eh, eg, eo)
                h = steps[si]["h"]
                if steps[si].get("first") and steps[si]["g"] == 0 and h + 1 < 8:
                    pending_gen = gen_chunks(h + 1)
                    since_gen = 0
                since_gen += 1
                if pending_gen and since_gen >= gen_every:
                    pending_gen.pop(0)()
                    since_gen = 0
                if steps[si].get("last") and steps[si]["g"] == 3:
                    while pending_gen:
                        pending_gen.pop(0)()
            while pending_rcp:
                _, ro_, rdr, q4 = pending_rcp.pop(0)
                cs = slice(q4 * 128, (q4 + 1) * 128)
                S.op("dve", lambda e, ro_=ro_, rdr=rdr, cs=cs: e.reciprocal(rec[ro_][rdr:rdr + 1, cs],
                                                                           osb[ro_][rdr:rdr + 1, cs]),
                     r=[("osb", ro_)], w=[("rec", ro_)])
            while pending_epi:
                _, eh, eg, eo = pending_epi.pop(0)
                emit_epilogue(eh, eg, eo)
            if debug and p == 0:
                dd = cv.take(512, F32)
                S.op("dve", lambda e: e.tensor_copy(dd[0:96, :], KB[1][:, 512:1024]), r=[("kb", 1, 1)], w=["dd"])
                S.dma("sp", lambda e: e.dma_start(out=dbg["d_kb"], in_=dd[0:96, :]), dsD, r=["dd"])
                S.op("dve", lambda e: e.tensor_copy(dd.rearrange("p (s d) -> p s d", s=8), VB[1][:, 8:16, 64:128]),
                     r=[("vb", 1, 1), "dd"], w=["dd"])
                S.dma("sp", lambda e: e.dma_start(out=dbg["d_vb"], in_=dd), dsD, r=["dd"])
                S.op("dve", lambda e: e.tensor_copy(dd[:, 0:256], MG[:, 0, 256:512]),
                     r=[("mgo", 0, 0), ("mgo", 1, 0), "dd"], w=["dd"])
                S.op("dve", lambda e: e.tensor_copy(dd[:, 256:512], MG[:, 3, 1024 + 256:1024 + 512]),
                     r=[("mgo", 6, 2), ("mgo", 7, 2), "dd"], w=["dd"])
                S.dma("sp", lambda e: e.dma_start(out=dbg["d_merged"], in_=dd), dsD, r=["dd"])

        WC_bytes = 8 * 1536 * 2 + 8 * 1024 * 2

        def wc_views():
            base = (ARENA - WC_bytes) // 2
            wci = arena[:, base:base + 8 * 1536].rearrange("p (c n) -> p c n", c=8)
            wco = arena[:, base + 8 * 1536:base + 8 * 1536 + 8 * 1024].rearrange("p (c n) -> p c n", c=8)
            return wci, wco

        dsWC = dsem()

        def load_WC():
            wci, wco = wc_views()
            S.dma("sp", lambda e: e.dma_start(out=wci, in_=w_in_bf.rearrange("(c p) n -> p c n", p=128)[:, :, NA:DIN]),
                  dsWC, w=["WCI"])
            S.dma("sp", lambda e: e.dma_start(out=wco, in_=w_out_bf.rearrange("(c p) n -> p c n", p=128)),
                  dsWC, w=["WCO"])

        dsY = [dsem(), dsem()]

        def phase_C(p):
            cv = Carver()
            wci, wco = wc_views()
            XR = [cv.take(D, F32) for _ in range(2)]
            XB = [cv.take(D, BF16) for _ in range(2)]
            XTC = [cv.take(8 * 128, BF16).rearrange("p (c t) -> p c t", c=8) for _ in range(2)]
            UG = [cv.take(512, F32) for _ in range(2)]
            VG = [cv.take(512, F32) for _ in range(2)]
            ZB = [cv.take(512, F32) for _ in range(2)]
            VN = [cv.take(512, BF16) for _ in range(2)]
            ob = cv.take(512, BF16)
            obT = cv.take(512, BF16)
            hh = cv.take(D, F32)
            YO = [cv.take(D, F32) for _ in range(2)]
            st1 = [cv.take(16, F32) for _ in range(2)]
            st2 = cv.take(24, F32)
            assert cv.off <= ARENA - WC_bytes, ("phase C arena", cv.off)
            dsXR = [dsem(), dsem()]
            dsXB = [dsem(), dsem()]

            def load_block(t):
                i = t % 2
                gslot = 4 * (t // 2) + (t % 2)
                rows = slice(gslot * 128, (gslot + 1) * 128)
                S.dma("sp", lambda e: e.dma_start(out=XR[i], in_=xs[rows, :]), dsXR[i], w=[("XR", i)])
                S.dma("pool", lambda e: e.dma_start(out=XB[i], in_=xs[rows, :]), dsXB[i], w=[("XB", i)])

            def s1a(t):
                i = t % 2
                xtc = XTC[i]
                for hf in range(2):
                    pt = pst_banks[hf]
                    for kk in range(4):
                        k = hf * 4 + kk
                        S.op("pe", lambda e, pt=pt, kk=kk, k=k: e.transpose(
                            pt[:, kk * 128:(kk + 1) * 128], XB[i][:, k * 128:(k + 1) * 128], identb[:]),
                            r=[("XB", i), "identb"], w=[("pst", hf)])
                    dst = xtc[:, hf * 4:hf * 4 + 4, :]
                    if hf == 0:
                        S.op("act", lambda e, pt=pt, dst=dst: e.copy(dst, pt.rearrange("p (c t) -> p c t", c=4)),
                             r=[("pst", hf)], w=[("XTC", i, hf)])
                    else:
                        S.op("dve", lambda e, pt=pt, dst=dst: e.tensor_copy(dst, pt.rearrange("p (c t) -> p c t", c=4)),
                             r=[("pst", hf)], w=[("XTC", i, hf)])

            def s1b_mm(t):
                i = t % 2
                xtc = XTC[i]
                xk = [("XTC", i, 0), ("XTC", i, 1)]
                ug, vg, zb = UG[i], VG[i], ZB[i]
                th = hh[:, 0:512]
                pss = {}
                for j in (1, 0, 2):
                    bb, pb, kb_ = next_bank()
                    pss[j] = (pb, kb_)
                    for k in range(8):
                        S.op("pe", lambda e, pb=pb, j=j, k=k: e.matmul(
                            pb, xtc[:, k, :], wci[:, k, j * 512:(j + 1) * 512], start=(k == 0), stop=(k == 7)),
                            r=xk + ["WCI"], w=[kb_])
                S.op("act", lambda e: e.activation(vg, pss[1][0], AF.Gelu), r=[pss[1][1]], w=[("vg", i)])
                S.op("act", lambda e: e.activation(ug, pss[0][0], AF.Gelu), r=[pss[0][1]], w=[("ug", i)])
                S.op("act", lambda e: e.activation(th, pss[2][0], AF.Tanh, scale=0.5), r=[pss[2][1]], w=[("hh", 0)])
                S.op("act", lambda e: e.copy(zb, pss[2][0]), r=[pss[2][1]], w=[("zb", i)])
                S.op("pool", lambda e: e.tensor_tensor(th, th, zb, op=ALU.mult), r=[("hh", 0), ("zb", i)], w=[("hh", 0)])
                S.op("pool", lambda e: e.tensor_tensor(zb, th, zb, op=ALU.add), r=[("hh", 0), ("zb", i)], w=[("zb", i)])
                S.op("pool", lambda e: e.tensor_tensor(ug, ug, zb, op=ALU.mult), r=[("ug", i), ("zb", i)], w=[("ug", i)])

            def s1b_ln(t):
                i = t % 2
                vg, vn, st = VG[i], VN[i], st1[i]
                S.op("dve", lambda e: e.bn_stats(st[:, 0:6], vg), r=[("vg", i)], w=[("st1a", i)])
                S.op("dve", lambda e: e.bn_aggr(st[:, 6:8], st[:, 0:6]), r=[("st1a", i)], w=[("st1b", i)])
                S.op("dve", lambda e: e.tensor_scalar(st[:, 8:9], st[:, 7:8], EPS, None, op0=ALU.add),
                     r=[("st1b", i)], w=[("st1c", i)])
                S.op("pool", lambda e: e.tensor_tensor(st[:, 8:9], st[:, 8:9], neghalf[:, 0:1], op=ALU.pow),
                     r=[("st1c", i), "neghalf"], w=[("st1c", i)])
                S.op("dve", lambda e: e.scalar_tensor_tensor(out=vg, in0=vg, scalar=st[:, 6:7], in1=sgug[:],
                                                             op0=ALU.subtract, op1=ALU.mult),
                     r=[("vg", i), ("st1b", i), "sgug"], w=[("vg", i)])
                S.op("dve", lambda e: e.scalar_tensor_tensor(out=vn, in0=vg, scalar=st[:, 8:9], in1=sgub[:],
                                                             op0=ALU.mult, op1=ALU.add),
                     r=[("vg", i), ("st1c", i), "sgub"], w=[("vn", i)])

            def s2a(t):
                i = t % 2
                ug, vn = UG[i], VN[i]
                bsv, psv, ksv = next_bank()
                S.op("pe", lambda e: e.matmul(psv, bspb[:], indb[:], start=True, stop=False, skip_group_check=True),
                     r=["bspb", "indb"], w=[ksv])
                for h in range(8):
                    S.op("pe", lambda e, h=h: e.matmul(psv[:, h * 64:(h + 1) * 64], wsp[:, h, :],
                                                      vn[:, h * 64:(h + 1) * 64], start=False, stop=(h == 7),
                                                      skip_group_check=True),
                         r=[("wsp", h), ("vn", i)], w=[ksv])
                S.op("dve", lambda e: e.scalar_tensor_tensor(out=ob, in0=psv, scalar=0.5, in1=ug, op0=ALU.mult, op1=ALU.mult),
                     r=[ksv, ("ug", i)], w=["ob"])

            def s2b(t):
                pt = pst_banks[0]
                for c in range(4):
                    S.op("pe", lambda e, c=c: e.transpose(pt[:, c * 128:(c + 1) * 128],
                                                         ob[:, c * 128:(c + 1) * 128], identb[:]),
                         r=["ob", "identb"], w=[("pst", 0)])
                S.op("act", lambda e: e.copy(obT, pt), r=[("pst", 0)], w=["obT"])

            def s2c(t):
                i = t % 2
                lt = slice((t - 16 * p) * 128, (t - 16 * p + 1) * 128)
                outs = []
                for nh in range(2):
                    bo, po, ko = next_bank()
                    outs.append((po, ko))
                    for c in range(8):
                        if c < 4:
                            lhsT = MG[:, c, lt]
                            rr = []
                        else:
                            lhsT = obT[:, (c - 4) * 128:(c - 3) * 128]
                            rr = ["obT"]
                        S.op("pe", lambda e, po=po, lhsT=lhsT, c=c, nh=nh: e.matmul(
                            po, lhsT, wco[:, c, nh * 512:(nh + 1) * 512], start=(c == 0), stop=(c == 7)),
                            r=rr + ["WCO"], w=[ko])
                for nh in range(2):
                    po, ko = outs[nh]
                    S.op("dve", lambda e, po=po, nh=nh: e.scalar_tensor_tensor(
                        out=hh[:, nh * 512:(nh + 1) * 512], in0=XR[i][:, nh * 512:(nh + 1) * 512], scalar=ALPHA,
                        in1=po, op0=ALU.mult, op1=ALU.add),
                        r=[ko, ("XR", i)], w=[("hh", nh)])
                    S.op("dve", lambda e, nh=nh: e.bn_stats(st2[:, 6 * nh:6 * nh + 6], hh[:, nh * 512:(nh + 1) * 512]),
                         r=[("hh", nh)], w=[("st2a", nh)])
                S.op("dve", lambda e: e.bn_aggr(st2[:, 12:14], st2[:, 0:12]), r=[("st2a", 0), ("st2a", 1)], w=["st2b"])
                S.op("dve", lambda e: e.tensor_scalar(st2[:, 14:15], st2[:, 13:14], EPS, None, op0=ALU.add),
                     r=["st2b"], w=["st2c"])
                S.op("pool", lambda e: e.tensor_tensor(st2[:, 14:15], st2[:, 14:15], neghalf[:, 0:1], op=ALU.pow),
                     r=["st2c", "neghalf"], w=["st2c"])
                S.op("dve", lambda e: e.scalar_tensor_tensor(out=hh, in0=hh, scalar=st2[:, 12:13], in1=lng[:],
                                                             op0=ALU.subtract, op1=ALU.mult),
                     r=[("hh", 0), ("hh", 1), "st2b", "lng"], w=[("hh", 0), ("hh", 1)])
                yo = YO[i]
                S.op("dve", lambda e: e.scalar_tensor_tensor(out=yo, in0=hh, scalar=st2[:, 14:15], in1=lnb[:],
                                                             op0=ALU.mult, op1=ALU.add),
                     r=[("hh", 0), ("hh", 1), "st2c", "lnb"], w=[("YO", i)])
                S.dma("sp", lambda e: e.dma_start(out=y[t * 128:(t + 1) * 128, :], in_=yo),
                      dsY[i], r=[("YO", i)])

            def load_xb(t):
                i = t % 2
                gslot = 4 * (t // 2) + (t % 2)
                rows = slice(gslot * 128, (gslot + 1) * 128)
                S.dma("pool", lambda e: e.dma_start(out=XB[i], in_=xs[rows, :]), dsXB[i], w=[("XB", i)])

            def load_xr(t):
                i = t % 2
                gslot = 4 * (t // 2) + (t % 2)
                rows = slice(gslot * 128, (gslot + 1) * 128)
                S.dma("sp", lambda e: e.dma_start(out=XR[i], in_=xs[rows, :]), dsXR[i], w=[("XR", i)])

            blocks = list(range(16 * p, 16 * p + 16))
            nb = len(blocks)
            load_xb(blocks[0])
            load_xb(blocks[1])
            load_xr(blocks[0])
            load_xr(blocks[1])
            s1a(blocks[0])
            s1b_mm(blocks[0])
            s1b_ln(blocks[0])
            s1a(blocks[1])
            load_xb(blocks[2])
            for bi, t in enumerate(blocks):
                s2a(t)
                if bi + 1 < nb:
                    s1b_mm(blocks[bi + 1])
                if bi + 2 < nb:
                    s1a(blocks[bi + 2])
                if bi + 3 < nb:
                    load_xb(blocks[bi + 3])
                s2b(t)
                if bi + 1 < nb:
                    s1b_ln(blocks[bi + 1])
                s2c(t)
                if bi + 2 < nb:
                    load_xr(blocks[bi + 2])

        for p in passes:
            if "A" in phases:
                phase_A(p)
                S.barrier()
            if "B" in phases:
                if "C" in phases and WC_EARLY:
                    load_WC()
                phase_B(p)
                S.barrier()
            if "C" in phases:
                if "B" not in phases or not WC_EARLY:
                    load_WC()
                phase_C(p)
                S.barrier()
        S.emit(nc, block, esems, dsY)
    return nc


def _slot_perm(par):
    order = []
    for m in range(16):
        if par == 0:
            order += [4 * m, 4 * m + 3, 4 * m + 1, 4 * m + 2]
        else:
            order += [4 * m + 1, 4 * m + 2, 4 * m, 4 * m + 3]
    return order


def make_in_maps(x, positions, w_in, q_norm_g, w_uq, kv_norm_g, w_ukv, sgu_norm_g, sgu_norm_b,
                 w_spatial, b_spatial, w_out, ln_g, ln_b):
    f = np.float32
    w_in = np.ascontiguousarray(w_in, dtype=f)
    swap = np.concatenate([np.arange(16, 32), np.arange(0, 16)])
    w_krs = np.ascontiguousarray(w_in[:, 384:416][:, swap])
    w_uq = np.ascontiguousarray(w_uq, dtype=f)
    w_uqs = np.zeros_like(w_uq)
    for h in range(8):
        w_uqs[:, h * 96 + 64:h * 96 + 96] = w_uq[:, h * 96 + 64:h * 96 + 96][:, swap]
    qg = np.ascontiguousarray(np.asarray(q_norm_g, dtype=f).reshape(2, 128).T)
    kvg = np.ascontiguousarray(np.asarray(kv_norm_g, dtype=f).reshape(128, 1))
    wspT = np.ascontiguousarray(np.transpose(np.asarray(w_spatial, dtype=f), (2, 0, 1)))
    si = np.arange(128)
    tril = (si[:, None] <= si[None, :]).astype(f)
    bspT = np.ascontiguousarray(np.asarray(b_spatial, dtype=f).T)
    ident = np.eye(128, dtype=f)
    invf = (1.0 / (10000.0 ** (np.arange(16, dtype=np.float64) / 16.0)))
    ropec = np.zeros((128, 6), f)
    ropec[:, 1] = -TWO_PI
    ropec[:, 2] = -TWO_PI
    ropec[:, 3] = 0.5 * math.pi
    for pp in range(64, 96):
        i = (pp - 64) % 16
        ropec[pp, 0] = invf[i] / TWO_PI
        ropec[pp, 1] = TWO_PI if pp < 80 else -TWO_PI
    tri = np.where(si[:, None] > si[None, :], NEG, 0.0).astype(f)
    allm = np.full((128, 128), NEG, f)
    zero = np.zeros((128, 128), f)
    ind_h = np.zeros((8, 512), f)
    for h in range(8):
        ind_h[h, h * 64:(h + 1) * 64] = 1.0
    common = dict(bsp_ht=np.ascontiguousarray(np.asarray(b_spatial, dtype=f)), ind_h=ind_h, w_in=w_in, w_krs=w_krs, w_uq=w_uq, w_uqs=w_uqs, qg=qg, w_ukv=np.ascontiguousarray(w_ukv, dtype=f),
                  kvg=kvg, wspT=wspT, tril=tril, bspT=bspT,
                  sgu_g=np.asarray(sgu_norm_g, dtype=f).reshape(1, 512), sgu_b=np.asarray(sgu_norm_b, dtype=f).reshape(1, 512),
                  ln_g=np.asarray(ln_g, dtype=f).reshape(1, D), ln_b=np.asarray(ln_b, dtype=f).reshape(1, D),
                  w_out=np.ascontiguousarray(w_out, dtype=f), ident=ident, ropec=ropec)
    in_maps = []
    perms = []
    for c in range(8):
        b, par = c // 2, c % 2
        order = _slot_perm(par)
        perms.append(order)
        xb = np.asarray(x[b], dtype=f).reshape(64, 128, D)[order].reshape(SEQ, D)
        pb = np.asarray(positions[b], dtype=np.int32).reshape(64, 128)[order].reshape(1, SEQ)
        mA = allm if par == 0 else zero
        mB = zero if par == 0 else allm
        mk = np.concatenate([tri, allm, tri, mA, allm, mB], axis=1)
        d = dict(common)
        d.update(xs=np.ascontiguousarray(xb), pos=np.ascontiguousarray(pb), masks=np.ascontiguousarray(mk))
        in_maps.append(d)
    return in_maps, perms


_NC_CACHE = {}


def kernel(x, positions, w_in, q_norm_g, w_uq, kv_norm_g, w_ukv, sgu_norm_g, sgu_norm_b,
           w_spatial, b_spatial, w_out, ln_g, ln_b):
    in_maps, perms = make_in_maps(x, positions, w_in, q_norm_g, w_uq, kv_norm_g, w_ukv, sgu_norm_g,
                                  sgu_norm_b, w_spatial, b_spatial, w_out, ln_g, ln_b)
    if "nc" not in _NC_CACHE:
        _NC_CACHE["nc"] = build_program()
    nc = _NC_CACHE["nc"]
    res = run_bass_kernel_spmd(nc, in_maps, core_ids=list(range(8)))
    out = np.empty((4, SEQ, D), np.float32)
    for c in range(8):
        b = c // 2
        order = perms[c]
        yc = np.asarray(res.results[c]["y"], dtype=np.float32).reshape(32, 128, D)
        ov = out[b].reshape(64, 128, D)
        for t in range(32):
            slot = 4 * (t // 2) + (t % 2)
            ov[order[slot]] = yc[t]
    return out
```

```python
import math
from contextlib import ExitStack

import numpy as np
import concourse.bass as bass
import concourse.mybir as mybir
from concourse.bass_utils import run_bass_kernel_spmd

F32 = mybir.dt.float32
BF16 = mybir.dt.bfloat16
I32 = mybir.dt.int32
AF = mybir.ActivationFunctionType
ALU = mybir.AluOpType

D = 1024
SEQ = 8192
NSLOT = 64
DIN = 2464
NA = 928
NEG = -30000.0
EPS = 1e-5
ALPHA = 2.0 ** 0.25
SCALE = 1.0 / math.sqrt(96.0)
TWO_PI = 2.0 * math.pi

ENGS = ("pe", "act", "dve", "pool", "sp")
import os
STOP = int(os.environ.get("KSTOP", "0"))
NGROUPS = int(os.environ.get("KGROUPS", "8"))
KB_STEPS = int(os.environ.get("KB_STEPS", "-1"))
WC_EARLY = int(os.environ.get("WC_EARLY", "1"))
TAB_ENG = os.environ.get("TAB_ENG", "dve")
KB_NOEPI = int(os.environ.get("KB_NOEPI", "0"))
KB_NOPV = int(os.environ.get("KB_NOPV", "0"))
KB_NOMASK = int(os.environ.get("KB_NOMASK", "0"))


class DmaSem:
    def __init__(self, sem):
        self.sem = sem
        self.count = 0


class Op:
    __slots__ = ("eng", "fn", "deps", "dma", "dsem", "dval", "signal", "sigval", "barrier")

    def __init__(self, eng, fn, deps, dma=False, dsem=None, dval=0):
        self.eng = eng
        self.fn = fn
        self.deps = deps
        self.dma = dma
        self.dsem = dsem
        self.dval = dval
        self.signal = False
        self.sigval = 0
        self.barrier = False


class Sched:
    def __init__(self):
        self.ops = []
        self.state = {}
        self.last_on = {e: None for e in ENGS}
        self.open_dma = []

    def _add(self, op):
        self.ops.append(op)
        idx = len(self.ops) - 1
        self.last_on[op.eng] = idx
        return idx

    def _mk(self, idxs):
        d = {}
        for i in idxs:
            p = self.ops[i]
            d[i] = p.dsem.count if p.dma else None
        return d

    def _deps(self, reads, writes):
        deps = set()
        for k in reads:
            st = self.state.get(k)
            if st is not None and st[0] is not None:
                deps.add(st[0])
        for k in writes:
            st = self.state.get(k)
            if st is not None:
                if st[0] is not None:
                    deps.add(st[0])
                deps.update(st[1])
        return self._mk(deps)

    def _commit(self, idx, reads, writes):
        for k in reads:
            st = self.state.setdefault(k, [None, []])
            st[1].append(idx)
        for k in writes:
            self.state[k] = [idx, []]

    def op(self, eng, fn, r=(), w=()):
        deps = self._deps(r, w)
        idx = self._add(Op(eng, fn, deps))
        self._commit(idx, r, w)
        return idx

    def dma(self, queue, fn, dsem, r=(), w=()):
        deps = self._deps(r, w)
        dsem.count += 16
        idx = self._add(Op(queue, fn, deps, dma=True, dsem=dsem, dval=dsem.count))
        self._commit(idx, r, w)
        self.open_dma.append(idx)
        return idx

    def barrier(self):
        targets = set(i for i in self.last_on.values() if i is not None)
        targets.update(self.open_dma)
        for e in ENGS:
            o = Op(e, None, self._mk(targets))
            o.barrier = True
            self.ops.append(o)
        self.open_dma = []
        self.state = {}

    def emit(self, nc, block, esems, final_waits):
        ops = self.ops
        for i, o in enumerate(ops):
            for d in o.deps:
                p = ops[d]
                if p.dma:
                    continue
                if p.eng == o.eng and not o.barrier:
                    if p.eng == "pe" or p.eng == "sp":
                        continue
                p.signal = True
        cnt = {e: 0 for e in ENGS}
        for o in ops:
            if o.signal:
                cnt[o.eng] += 1
                o.sigval = cnt[o.eng]

        def stream(name):
            def body(e):
                seen = {}
                for o in ops:
                    if o.eng != name:
                        continue
                    need = {}
                    for d in sorted(o.deps):
                        p = ops[d]
                        if p.dma:
                            sem, val = p.dsem.sem, o.deps[d]
                        else:
                            if p.eng == name and (name == "pe" or name == "sp") :
                                continue
                            if p.eng == name and o.barrier and name == "pe":
                                continue
                            sem, val = esems[p.eng], p.sigval
                        key = id(sem)
                        if key not in need or need[key][1] < val:
                            need[key] = (sem, val)
                    for key, (sem, val) in need.items():
                        if seen.get(key, 0) >= val:
                            continue
                        seen[key] = val
                        e.wait_ge(sem, val)
                    if o.fn is None:
                        continue
                    ins = o.fn(e)
                    if o.dma:
                        ins.then_inc(o.dsem.sem, 16)
                    elif o.signal:
                        ins.then_inc(esems[name], 1)
                if name == "sp":
                    for ds in final_waits:
                        e.wait_ge(ds.sem, ds.count)
            return body

        block.sync(stream("sp"))
        block.tensor(stream("pe"))
        block.scalar(stream("act"))
        block.vector(stream("dve"))
        block.gpsimd(stream("pool"))


def build_program(passes=(0, 1), phases="ABC", debug=False):
    nc = bass.Bass("TRN2", target_bir_lowering=False)

    def din(name, shape, dt=F32):
        return nc.dram_tensor(name, shape, dt, kind="ExternalInput").ap()

    xs = din("xs", [SEQ, D])
    pos = din("pos", [1, SEQ], I32)
    w_in = din("w_in", [D, DIN])
    w_krs = din("w_krs", [D, 32])
    w_uq = din("w_uq", [256, 768])
    w_uqs = din("w_uqs", [256, 768])
    qg = din("qg", [128, 2])
    w_ukv = din("w_ukv", [128, 1024])
    kvg = din("kvg", [128, 1])
    wspT = din("wspT", [128, 8, 128])
    tril = din("tril", [128, 128])
    bspT = din("bspT", [128, 8])
    bsp_ht = din("bsp_ht", [8, 128])
    ind_h = din("ind_h", [8, 512])
    sgu_g = din("sgu_g", [1, 512])
    sgu_b = din("sgu_b", [1, 512])
    ln_g = din("ln_g", [1, D])
    ln_b = din("ln_b", [1, D])
    w_out = din("w_out", [D, D])
    ident = din("ident", [128, 128])
    ropec = din("ropec", [128, 6])
    masks = din("masks", [128, 768])
    y = nc.dram_tensor("y", [SEQ // 2, D], F32, kind="ExternalOutput").ap()
    w_in_bf = nc.dram_tensor("w_in_bf", [D, DIN], BF16, kind="Internal").ap()
    w_krs_bf = nc.dram_tensor("w_krs_bf", [D, 32], BF16, kind="Internal").ap()
    w_out_bf = nc.dram_tensor("w_out_bf", [D, D], BF16, kind="Internal").ap()
    dbg = {}
    if debug:
        def dout(name, shape):
            dbg[name] = nc.dram_tensor(name, shape, F32, kind="ExternalOutput").ap()
        dout("d_ckvn", [128, 512])
        dout("d_krope", [32, 512])
        dout("d_q0", [96, 512])
        dout("d_g", [128, 512])
        dout("d_kb", [96, 512])
        dout("d_vb", [128, 512])
        dout("d_merged", [128, 512])

    S = Sched()
    with ExitStack() as es:
        def sb(name, shape, dt):
            return es.enter_context(nc.sbuf_tensor(name, shape, dt))

        def ps(name, shape, dt):
            return es.enter_context(nc.psum_tensor(name, shape, dt))

        def newsem(name):
            return es.enter_context(nc.semaphore(name))

        dsem_n = [0]

        def dsem():
            dsem_n[0] += 1
            return DmaSem(newsem("dq%d" % dsem_n[0]))

        identb = sb("identb", [128, 128], BF16)
        onesf = sb("onesf", [128, 128], F32)
        maskb = sb("maskb", [128, 768], BF16)
        ropecs = sb("ropecs", [128, 6], F32)
        epst = sb("epst", [128, 1], F32)
        neghalf = sb("neghalf", [128, 1], F32)
        identf = sb("identf", [128, 128], F32)
        rt = sb("rt", [128, 16], F32)
        wuq_g = sb("wuq_g", [128, 2, 768], BF16)
        wuqs_g = sb("wuqs_g", [128, 2, 768], BF16)
        wukv_g = sb("wukv_g", [128, 1024], BF16)
        wsp = sb("wsp", [128, 8, 128], BF16)
        bsp = sb("bsp", [128, 8], F32)
        bspb = sb("bspb", [8, 128], BF16)
        indb = sb("indb", [8, 512], BF16)
        sgug = sb("sgug", [128, 512], F32)
        sgub = sb("sgub", [128, 512], F32)
        lng = sb("lng", [128, D], F32)
        lnb = sb("lnb", [128, D], F32)
        ckvn = sb("ckvn", [128, SEQ], BF16)
        KB = [sb("kb%d" % i, [96, SEQ], BF16) for i in range(2)]
        QT = sb("qt", [96, 8, 2048], BF16)
        MG = sb("mg", [128, 4, 2048], BF16)
        ARENA = 86528
        arena = sb("arena", [128, ARENA // 2], BF16)

        class Carver:
            def __init__(self):
                self.off = 0

            def take(self, cols, dt, parts=128):
                size = 4 if dt in (F32, I32) else 2
                self.off = (self.off + 3) // 4 * 4
                nbytes = cols * size
                assert self.off + nbytes <= ARENA, ("arena overflow", self.off, nbytes)
                v = arena[0:parts, self.off // 2:(self.off + nbytes) // 2]
                self.off += nbytes
                if dt != BF16:
                    v = v.bitcast(dt)
                return v

        PT4 = [ps("pt%d" % i, [128, 2, 512], F32) for i in range(4)]
        PSW = [PT4[0], PT4[1], PT4[2]]
        PSO1 = PT4[3][:, 0, :]
        PSG = PT4[3][:, 1, :]
        pst_banks = [PT4[3][:, 0, :].bitcast(BF16)[:, 0:512],
                     PT4[3][:, 1, :].bitcast(BF16)[:, 0:512]]

        def bank(i):
            return PT4[i // 2][:, i % 2, :]

        bank_rr = [0]

        def next_bank():
            b = bank_rr[0] % 6
            bank_rr[0] += 1
            return b, bank(b), ("ps", b)

        esems = {e: newsem("e_" + e) for e in ENGS}
        block = es.enter_context(nc.Block())

        cv = Carver()
        st_a = cv.take(2 * 768, F32)
        st_b = cv.take(2 * 768, F32)
        st_c = cv.take(1024, F32)
        st_d = cv.take(8 * 128, F32)
        st_e = cv.take(128, F32)
        qgs = cv.take(2, F32)
        kvgs = cv.take(1, F32)
        ds0 = dsem()
        ds0p = dsem()
        S.dma("pool", lambda e: e.dma_start(out=identb[:], in_=ident), ds0p, w=["identb"])
        S.dma("pool", lambda e: e.dma_start(out=maskb[:], in_=masks), ds0p, w=["maskb"])
        S.dma("pool", lambda e: e.dma_start(out=bspb[:], in_=bsp_ht), ds0p, w=["bspb"])
        S.dma("pool", lambda e: e.dma_start(out=indb[:], in_=ind_h), ds0p, w=["indb"])
        S.dma("sp", lambda e: e.dma_start(out=ropecs[:], in_=ropec), ds0, w=["ropecs"])
        S.dma("sp", lambda e: e.dma_start(out=identf[:], in_=ident), ds0, w=["identf"])
        S.dma("sp", lambda e: e.dma_start(out=st_a.rearrange("p (c n) -> p c n", c=2),
                                          in_=w_uq.rearrange("(c p) n -> p c n", p=128)), ds0, w=["st_a"])
        S.dma("sp", lambda e: e.dma_start(out=st_b.rearrange("p (c n) -> p c n", c=2),
                                          in_=w_uqs.rearrange("(c p) n -> p c n", p=128)), ds0, w=["st_b"])
        S.dma("sp", lambda e: e.dma_start(out=st_c, in_=w_ukv), ds0, w=["st_c"])
        S.dma("sp", lambda e: e.dma_start(out=st_d.rearrange("p (h t) -> p h t", h=8), in_=wspT), ds0, w=["st_d"])
        S.dma("sp", lambda e: e.dma_start(out=st_e, in_=tril), ds0, w=["st_e"])
        S.dma("sp", lambda e: e.dma_start(out=qgs, in_=qg), ds0, w=["qgs"])
        S.dma("sp", lambda e: e.dma_start(out=kvgs, in_=kvg), ds0, w=["kvgs"])
        S.dma("sp", lambda e: e.dma_start(out=bsp[:], in_=bspT), ds0, w=["bsp"])
        S.dma("sp", lambda e: e.dma_start(out=sgug[:], in_=sgu_g.partition_broadcast(128)), ds0, w=["sgug"])
        S.dma("sp", lambda e: e.dma_start(out=sgub[:], in_=sgu_b.partition_broadcast(128)), ds0, w=["sgub"])
        S.dma("sp", lambda e: e.dma_start(out=lng[:], in_=ln_g.partition_broadcast(128)), ds0, w=["lng"])
        S.dma("sp", lambda e: e.dma_start(out=lnb[:], in_=ln_b.partition_broadcast(128)), ds0, w=["lnb"])
        dscv = dsem()
        S.op("dve", lambda e: e.memset(onesf[:], 1.0), w=["onesf"])
        S.op("dve", lambda e: e.memset(epst[:], EPS), w=["epst"])
        S.op("dve", lambda e: e.memset(neghalf[:], -0.5), w=["neghalf"])
        for c in range(2):
            S.op("dve", lambda e, c=c: e.tensor_scalar(wuq_g[:, c, :], st_a[:, c * 768:(c + 1) * 768],
                                                       qgs[:, c:c + 1], None, op0=ALU.mult),
                 r=["st_a", "qgs"], w=[("wuq", c)])
            S.op("dve", lambda e, c=c: e.tensor_scalar(wuqs_g[:, c, :], st_b[:, c * 768:(c + 1) * 768],
                                                       qgs[:, c:c + 1], None, op0=ALU.mult),
                 r=["st_b", "qgs"], w=[("wuqs", c)])
        S.op("dve", lambda e: e.tensor_scalar(wukv_g[:], st_c, kvgs[:, 0:1], None, op0=ALU.mult),
             r=["st_c", "kvgs"], w=["wukv"])
        for h in range(8):
            S.op("dve", lambda e, h=h: e.tensor_tensor(wsp[:, h, :], st_d[:, h * 128:(h + 1) * 128], st_e,
                                                       op=ALU.mult), r=["st_d", "st_e"], w=[("wsp", h)])
        S.barrier()

        def phase_A(p):
            cv = Carver()
            WA = cv.take(8 * NA, BF16).rearrange("p (c n) -> p c n", c=8)
            WK = cv.take(8 * 32, BF16).rearrange("p (c n) -> p c n", c=8)
            XS = [cv.take(4 * D, BF16).rearrange("p (s d) -> p s d", s=4) for _ in range(2)]
            XT = cv.take(8 * 512, BF16).rearrange("p (c t) -> p c t", c=8)
            POSI = [cv.take(512, I32) for _ in range(2)]
            posf = cv.take(512, F32)
            tk = posf.bitcast(I32)
            TABS = [[cv.take(512, F32) for _ in range(2)] for _ in range(2)]
            tt = cv.take(512, F32)
            tkf = cv.take(512, F32)
            sq = [cv.take(512, F32) for _ in range(2)]
            rstd = cv.take(512, F32)
            dg = cv.take(512, F32)
            dg2 = cv.take(256, F32)
            rstdq = cv.take(256, F32)
            KR = [[cv.take(512, F32) for _ in range(2)] for _ in range(2)]
            CQB = [cv.take(512, BF16) for _ in range(2)]
            crsr = cv.take(512, F32)
            qtmp = [cv.take(512, F32) for _ in range(2)]
            dsW = dsem()
            dsX = [dsem(), dsem()]
            dsP = [dsem(), dsem()]
            dsD = dsem()
            groups = list(range(8 * p, 8 * p + NGROUPS))

            def load_group(G):
                i = G % 2
                S.dma("pool", lambda e: e.dma_start(
                    out=XS[i], in_=xs[G * 512:(G + 1) * 512, :].rearrange("(s p) d -> p s d", p=128)),
                    dsX[i], w=[("XS", i)])
                S.dma("sp", lambda e: e.dma_start(
                    out=POSI[i], in_=pos[0:1, G * 512:(G + 1) * 512].partition_broadcast(128)),
                    dsP[i], w=[("POSI", i)])

            load_group(groups[0])
            if p == 0:
                dsW0 = dsem()
                S.dma("pool", lambda e: e.dma_start(out=WA, in_=w_in.rearrange("(c p) n -> p c n", p=128)[:, :, 0:NA]),
                      dsW0, w=["WA"])
                S.dma("pool", lambda e: e.dma_start(out=WK, in_=w_krs.rearrange("(c p) n -> p c n", p=128)),
                      dsW0, w=["WK"])
            else:
                S.dma("sp", lambda e: e.dma_start(out=WA, in_=w_in_bf.rearrange("(c p) n -> p c n", p=128)[:, :, 0:NA]),
                      dsW, w=["WA"])
                S.dma("sp", lambda e: e.dma_start(out=WK, in_=w_krs_bf.rearrange("(c p) n -> p c n", p=128)),
                      dsW, w=["WK"])
            if len(groups) > 1:
                load_group(groups[1])
            if p == 0:
                S.dma("pool", lambda e: e.dma_start(out=w_in_bf, in_=w_in), dscv, w=["w_in_bf"])
                S.dma("pool", lambda e: e.dma_start(out=w_krs_bf, in_=w_krs), dscv, w=["w_krs_bf"])
                S.dma("pool", lambda e: e.dma_start(out=w_out_bf, in_=w_out), dscv, w=["w_out_bf"])

            def make_tables(G):
                i = G % 2
                S.op("act", lambda e: e.activation(tt, POSI[i], AF.Identity, scale=ropecs[:, 0:1]),
                     r=[("POSI", i), "ropecs"], w=["tt"])
                S.op("dve", lambda e: e.tensor_copy(tk, tt), r=["tt"], w=["posf"])
                S.op("dve", lambda e: e.tensor_copy(tkf, tk), r=["posf"], w=["tkf"])
                S.op("dve", lambda e: e.tensor_tensor(tt, tt, tkf, op=ALU.subtract), r=["tt", "tkf"], w=["tt"])
                S.op("dve", lambda e: e.scalar_tensor_tensor(out=tt, in0=tt, scalar=0.5, in1=tt,
                                                             op0=ALU.is_gt, op1=ALU.subtract),
                     r=["tt"], w=["tt"])
                S.op("act", lambda e: e.activation(TABS[i][1], tt, AF.Sin, scale=ropecs[:, 1:2]),
                     r=["tt", "ropecs"], w=[("tab", i, 1)])
                S.op("act", lambda e: e.activation(tkf, tt, AF.Abs), r=["tt"], w=["tkf"])
                S.op("act", lambda e: e.activation(TABS[i][0], tkf, AF.Sin, bias=ropecs[:, 3:4], scale=ropecs[:, 2:3]),
                     r=["tkf", "ropecs"], w=[("tab", i, 0)])

            def stage_x1(G):
                i = G % 2
                xt = XT
                for k in range(8):
                    half = k % 2
                    pt = pst_banks[half]
                    for s_ in range(4):
                        S.op("pe", lambda e, pt=pt, s_=s_, k=k: e.transpose(
                            pt[:, s_ * 128:(s_ + 1) * 128], XS[i][:, s_, k * 128:(k + 1) * 128], identb[:]),
                            r=[("XS", i), "identb"], w=[("pst", half)])
                    if k % 4 != 3:
                        S.op("act", lambda e, pt=pt, k=k: e.copy(xt[:, k, :], pt),
                             r=[("pst", half)], w=[("XT", k)])
                    else:
                        S.op("dve", lambda e, pt=pt, k=k: e.tensor_copy(xt[:, k, :], pt),
                             r=[("pst", half)], w=[("XT", k)])

            def stage_x2(G):
                i = G % 2
                tok = slice(G * 512, (G + 1) * 512)
                lt = slice((G - 8 * p) * 256, (G - 8 * p) * 256 + 256)
                xt = XT
                bc, pc, kc = next_bank()
                for k in range(8):
                    S.op("pe", lambda e, pc=pc, k=k: e.matmul(pc, WA[:, k, 256:384], xt[:, k, :],
                                                              start=(k == 0), stop=(k == 7)),
                         r=["WA", ("XT", k)], w=[kc])
                br, pr, kr = next_bank()
                for k in range(8):
                    S.op("pe", lambda e, pr=pr, k=k: e.matmul(pr[64:96, :], WA[:, k, 384:416], xt[:, k, :],
                                                              start=(k == 0), stop=(k == 7), tile_position=(0, 64)),
                         r=["WA", ("XT", k)], w=[kr])
                bs, psw, ks = next_bank()
                for k in range(8):
                    S.op("pe", lambda e, psw=psw, k=k: e.matmul(psw[64:96, :], WK[:, k, :], xt[:, k, :],
                                                                start=(k == 0), stop=(k == 7), tile_position=(0, 64)),
                         r=["WK", ("XT", k)], w=[ks])
                S.op("act", lambda e, pc=pc: e.copy(ckvn[:, tok], pc), r=[kc], w=[("ckvn", G)])
                S.op("act", lambda e, pc=pc: e.activation(sq[0], pc, AF.Square), r=[kc], w=[("sq", 0)])
                S.op("act", lambda e, pr=pr: e.copy(KR[i][0][64:96, :], pr[64:96, :]), r=[kr], w=[("KR", i, 0)])
                S.op("act", lambda e, psw=psw: e.copy(KR[i][1][64:96, :], psw[64:96, :]), r=[ks], w=[("KR", i, 1)])
                bcq, pcq, kcq = next_bank()
                for c in range(2):
                    for k in range(8):
                        S.op("pe", lambda e, pcq=pcq, c=c, k=k: e.matmul(
                            pcq[:, c * 256:(c + 1) * 256], WA[:, k, c * 128:(c + 1) * 128], xt[:, k, 0:256],
                            start=(k == 0), stop=(k == 7)),
                            r=["WA", ("XT", k)], w=[kcq])
                S.op("act", lambda e, pcq=pcq: e.copy(CQB[i], pcq), r=[kcq], w=[("cqb", i)])
                S.op("act", lambda e, pcq=pcq: e.activation(sq[1], pcq, AF.Square), r=[kcq], w=[("sq", 1)])
                for cc in range(2):
                    bz, pz, kz = next_bank()
                    for c2 in range(2):
                        c = cc * 2 + c2
                        for k in range(8):
                            S.op("pe", lambda e, pz=pz, c=c, c2=c2, k=k: e.matmul(
                                pz[:, c2 * 256:(c2 + 1) * 256], WA[:, k, 416 + c * 128:416 + (c + 1) * 128],
                                xt[:, k, 0:256], start=(k == 0), stop=(k == 7)),
                                r=["WA", ("XT", k)], w=[kz])
                    S.op("act", lambda e, pz=pz, cc=cc: e.activation(
                        MG[:, 2 * cc:2 * cc + 2, lt], pz.rearrange("p (c t) -> p c t", c=2), AF.Silu),
                        r=[kz], w=[("MG", cc, G)])
                bq, pq, kq = next_bank()
                for j in range(4):
                    S.op("pe", lambda e, pq=pq, j=j: e.matmul(pq[:, j:j + 1], sq[0][:, j * 128:(j + 1) * 128], onesf[:, 0:1],
                                                              start=True, stop=True, skip_group_check=True),
                         r=[("sq", 0), "onesf"], w=[kq])
                for j in range(2):
                    for c in range(2):
                        S.op("pe", lambda e, pq=pq, c=c, j=j: e.matmul(
                            pq[:, 4 + j:5 + j], sq[1][:, c * 256 + j * 128:c * 256 + (j + 1) * 128], onesf[:, 0:1],
                            start=(c == 0), stop=(c == 1), skip_group_check=True),
                            r=[("sq", 1), "onesf"], w=[kq])
                ro = 8 * i
                S.op("act", lambda e, pq=pq: e.activation(rt[:, ro:ro + 4], pq[:, 0:4], AF.Identity, bias=epst[:, 0:1],
                                                          scale=1.0 / 128.0), r=[kq, "epst"], w=[("rt", i)])
                S.op("act", lambda e, pq=pq: e.activation(rt[:, ro + 4:ro + 6], pq[:, 4:6], AF.Identity, bias=epst[:, 0:1],
                                                          scale=1.0 / 256.0), r=[kq, "epst"], w=[("rt", i)])
                S.op("pool", lambda e: e.tensor_tensor(rt[:, ro:ro + 6], rt[:, ro:ro + 6],
                                                       neghalf[:, 0:1].to_broadcast([128, 6]), op=ALU.pow),
                     r=[("rt", i), "neghalf"], w=[("rt", i)])

            def stage_yn(G):
                i = G % 2
                tok = slice(G * 512, (G + 1) * 512)
                lt = slice((G - 8 * p) * 256, (G - 8 * p) * 256 + 256)
                tabs = TABS[i]
                tabk = [("tab", i, 0), ("tab", i, 1)]
                ro = 8 * i
                cqb = CQB[i]
                for j in range(4):
                    S.op("dve", lambda e, j=j: e.tensor_scalar(dg[:, j * 128:(j + 1) * 128], identf[:], rt[:, ro + j:ro + j + 1],
                                                               None, op0=ALU.mult), r=[("rt", i), "identf"], w=["dg"])
                for j in range(2):
                    S.op("dve", lambda e, j=j: e.tensor_scalar(dg2[:, j * 128:(j + 1) * 128], identf[:],
                                                               rt[:, ro + 4 + j:ro + 5 + j], None, op0=ALU.mult),
                         r=[("rt", i), "identf"], w=["dg2"])
                bb, pbc, kbc = next_bank()
                for j in range(4):
                    S.op("pe", lambda e, pbc=pbc, j=j: e.matmul(pbc[:, j * 128:(j + 1) * 128], onesf[:],
                                                                dg[:, j * 128:(j + 1) * 128], start=True, stop=True,
                                                                skip_group_check=True),
                         r=["dg", "onesf"], w=[kbc])
                bb2, pbc2, kbc2 = next_bank()
                for j in range(2):
                    S.op("pe", lambda e, pbc2=pbc2, j=j: e.matmul(pbc2[:, j * 128:(j + 1) * 128], onesf[:],
                                                                  dg2[:, j * 128:(j + 1) * 128], start=True, stop=True,
                                                                  skip_group_check=True),
                         r=["dg2", "onesf"], w=[kbc2])
                S.op("act", lambda e, pbc=pbc: e.copy(rstd, pbc), r=[kbc], w=["rstd"])
                S.op("act", lambda e, pbc2=pbc2: e.copy(rstdq[:, 0:256], pbc2[:, 0:256]), r=[kbc2], w=["rstdq"])
                S.op("pool", lambda e: e.tensor_tensor(ckvn[:, tok], ckvn[:, tok], rstd, op=ALU.mult),
                     r=[("ckvn", G), "rstd"], w=[("ckvn", G)])
                kr0, kr1 = KR[i]
                S.op("dve", lambda e: e.tensor_tensor(kr0[64:96, :], kr0[64:96, :], tabs[0][64:96, :], op=ALU.mult),
                     r=[("KR", i, 0), tabk[0]], w=[("KR", i, 0)])
                S.op("dve", lambda e: e.tensor_tensor(kr1[64:96, :], kr1[64:96, :], tabs[1][64:96, :], op=ALU.mult),
                     r=[("KR", i, 1), tabk[1]], w=[("KR", i, 1)])
                S.op("pool", lambda e: e.tensor_tensor(KB[0][64:96, tok], kr0[64:96, :], kr1[64:96, :], op=ALU.add),
                     r=[("KR", i, 0), ("KR", i, 1)], w=[("kbr", 0, G)])
                S.op("pool", lambda e: e.tensor_copy(KB[1][64:96, tok], KB[0][64:96, tok]),
                     r=[("kbr", 0, G)], w=[("kbr", 1, G)])
                for ti in range(2):
                    S.op("dve", lambda e, ti=ti: e.tensor_tensor(crsr[0:96, ti * 256:(ti + 1) * 256],
                                                                 tabs[ti][0:96, 0:256], rstdq[0:96, 0:256], op=ALU.mult),
                         r=[tabk[ti], "rstdq"], w=["crsr"])

            def stage_yq(G):
                i = G % 2
                lt = slice((G - 8 * p) * 256, (G - 8 * p) * 256 + 256)
                tok = slice(G * 512, (G + 1) * 512)
                cqb = CQB[i]
                for h in range(8):
                    bh, ph_, kh = next_bank()
                    for half, wsrc, wk in ((0, wuq_g, "wuq"), (1, wuqs_g, "wuqs")):
                        for c in range(2):
                            S.op("pe", lambda e, ph_=ph_, half=half, wsrc=wsrc, c=c, h=h: e.matmul(
                                ph_[0:96, half * 256:(half + 1) * 256], wsrc[:, c, h * 96:(h + 1) * 96],
                                cqb[:, c * 256:(c + 1) * 256], start=(c == 0), stop=(c == 1)),
                                r=[(wk, c), ("cqb", i)], w=[kh])
                    qt_ = qtmp[h % 2]
                    S.op("dve", lambda e, ph_=ph_, qt_=qt_: e.tensor_tensor(qt_[0:96, :], ph_[0:96, :], crsr[0:96, :],
                                                                            op=ALU.mult),
                         r=[kh, "crsr"], w=[("qtmp", h % 2)])
                    S.op("pool", lambda e, qt_=qt_, h=h: e.tensor_tensor(
                        QT[:, h, lt], qt_[0:96, 0:256], qt_[0:96, 256:512], op=ALU.add),
                        r=[("qtmp", h % 2)], w=[("QT", h, G)])
                if debug and G == 8 * p + 1 and p == 0:
                    dd = qtmp[0]
                    kk = ("qtmp", 0)
                    S.op("dve", lambda e: e.tensor_copy(dd, ckvn[:, tok]), r=[("ckvn", G), kk], w=[kk])
                    S.dma("sp", lambda e: e.dma_start(out=dbg["d_ckvn"], in_=dd), dsD, r=[kk])
                    S.op("dve", lambda e: e.tensor_copy(dd[64:96, :], KB[1][64:96, tok]),
                         r=[("kbr", 1, G), kk], w=[kk])
                    S.dma("sp", lambda e: e.dma_start(out=dbg["d_krope"], in_=dd[64:96, :]), dsD, r=[kk])
                    S.op("dve", lambda e: e.tensor_copy(dd[0:96, 0:256], QT[:, 0, lt]), r=[("QT", 0, G), kk], w=[kk])
                    S.op("dve", lambda e: e.tensor_copy(dd[0:96, 256:512], QT[:, 5, lt]), r=[("QT", 5, G), kk], w=[kk])
                    S.dma("sp", lambda e: e.dma_start(out=dbg["d_q0"], in_=dd[0:96, :]), dsD, r=[kk])
                    S.op("dve", lambda e: e.tensor_copy(dd[:, 0:256], MG[:, 0, lt]), r=[("MG", 0, G), kk], w=[kk])
                    S.op("dve", lambda e: e.tensor_copy(dd[:, 256:512], MG[:, 3, lt]), r=[("MG", 1, G), kk], w=[kk])
                    S.dma("sp", lambda e: e.dma_start(out=dbg["d_g"], in_=dd), dsD, r=[kk])

            ng = len(groups)
            stage_x1(groups[0])
            make_tables(groups[0])
            stage_x2(groups[0])
            if ng > 2:
                load_group(groups[2])
            for gi, G in enumerate(groups):
                nxt = groups[gi + 1] if gi + 1 < ng else None
                if nxt is not None:
                    stage_x1(nxt)
                stage_yn(G)
                if nxt is not None:
                    make_tables(nxt)
                    stage_x2(nxt)
                stage_yq(G)
                if gi + 3 < ng:
                    load_group(groups[gi + 3])

        def phase_B(p):
            cv = Carver()
            nslots = 32 * (p + 1)
            VB = [cv.take(64 * 65, BF16).rearrange("p (s d) -> p s d", s=64),
                  cv.take(64 * 128, BF16).rearrange("p (s d) -> p s d", s=64)]
            PT = [cv.take(1024, BF16).rearrange("p (j n) -> p j n", j=2) for _ in range(3)]
            osb = [cv.take(512, F32) for _ in range(2)]
            rec = [cv.take(512, F32) for _ in range(2)]
            otmp = [cv.take(512, F32) for _ in range(2)]
            dsD = dsem()
            S.op("pool", lambda e: e.memset(VB[0][:, :, 64:65], 1.0), w=[("vbc", 0)])
            S.op("pool", lambda e: e.memset(VB[1][:, :, 0:64], 0.0), w=[("vbc", 1)])
            S.op("pool", lambda e: e.memset(VB[1][:, :, 0:1], 1.0), w=[("vbc", 1)])

            def vaug(b, h, slot):
                return VB[b][:, slot, :]

            def vdst(b, c):
                if b == 0:
                    return VB[0][:, c * 8:(c + 1) * 8, 0:64]
                return VB[1][:, c * 8:(c + 1) * 8, 64:128]

            def gen_chunks(h, upfront=False):
                b = h % 2
                out = []
                cnt = [0]

                def target():
                    n = cnt[0]
                    cnt[0] += 1
                    if not upfront:
                        return PSG, "psg", "dve"
                    t = n % 3
                    return PSW[t][:, (n // 3) % 2, :], ("psw", t), ("dve" if n % 2 == 0 else "act")

                for c in range(nslots // 4):
                    def kchunk(c=c):
                        pg, kg, eng = target()
                        tok = slice(c * 512, (c + 1) * 512)
                        S.op("pe", lambda e: e.matmul(pg[0:64, :], wukv_g[:, h * 128:h * 128 + 64], ckvn[:, tok],
                                                      start=True, stop=True), r=["wukv"], w=[kg])
                        if eng == "dve":
                            S.op("dve", lambda e: e.tensor_copy(KB[b][0:64, tok], pg[0:64, :]),
                                 r=[kg], w=[("kb", b, c)])
                        else:
                            S.op("act", lambda e: e.copy(KB[b][0:64, tok], pg[0:64, :]),
                                 r=[kg], w=[("kb", b, c)])
                    out.append(kchunk)
                for c in range(nslots // 8):
                    def vchunk(c=c):
                        pg, kg, eng = target()
                        for s in range(8):
                            slot = c * 8 + s
                            S.op("pe", lambda e, s=s, slot=slot: e.matmul(
                                pg[:, s * 64:(s + 1) * 64], ckvn[:, slot * 128:(slot + 1) * 128],
                                wukv_g[:, h * 128 + 64:h * 128 + 128], start=True, stop=True),
                                r=["wukv"], w=[kg])
                        if eng == "dve":
                            S.op("dve", lambda e: e.tensor_copy(vdst(b, c), pg.rearrange("p (s d) -> p s d", s=8)),
                                 r=[kg], w=[("vb", b, c)])
                        else:
                            S.op("act", lambda e: e.copy(vdst(b, c), pg.rearrange("p (s d) -> p s d", s=8)),
                                 r=[kg], w=[("vb", b, c)])
                    out.append(vchunk)
                return out

            steps = []
            for h in range(8):
                for g in range(4):
                    gg = 4 * p + g
                    nfull = 4 * gg
                    for i in range(nfull):
                        steps.append(dict(h=h, g=g, slots=(2 * i, 2 * i + 1), c0=0, masks=None))
                    base = 8 * gg
                    tails = [
                        ((base + 0, base + 1), 0, ((0, 128, 0), (256, 128, 128))),
                        ((base + 2, base + 3), 0, ((384, 128, 0), (640, 128, 128))),
                        ((base + 4, base + 5), 256, ((0, 128, 256), (256, 128, 384))),
                        ((base + 6, base + 7), 256, ((384, 128, 256), (640, 128, 384))),
                    ]
                    for slots, c0, mk in tails:
                        steps.append(dict(h=h, g=g, slots=slots, c0=c0, masks=mk))
                    steps[-1]["last"] = True
                    steps[-(nfull + 4)]["first"] = True
            for fn in gen_chunks(0, upfront=True):
                fn()
            pending_gen = []
            pending_epi = []
            pending_rcp = []
            n_steps = len(steps)
            if KB_STEPS >= 0:
                n_steps = KB_STEPS
            ocount = [0]

            def emit_qk(si):
                stp = steps[si]
                h, g, c0 = stp["h"], stp["g"], stp["c0"]
                b = h % 2
                wtile = PSW[si % 3]
                for j, slot in enumerate(stp["slots"]):
                    has_mask = stp["masks"] is not None
                    cj = stp["masks"][j][2] if has_mask else c0
                    S.op("pe", lambda e, j=j, slot=slot, has_mask=has_mask, cj=cj: e.matmul(
                        wtile[:, j, cj:512], KB[b][:, slot * 128:(slot + 1) * 128],
                        QT[:, h, g * 512 + cj:(g + 1) * 512], start=True, stop=(not has_mask or bool(KB_NOMASK)),
                        skip_group_check=True),
                        r=[("kb", b, slot // 4)], w=[("psw", si % 3)])
                    if has_mask and not KB_NOMASK:
                        mo, mw, mc = stp["masks"][j]
                        S.op("pe", lambda e, j=j, mo=mo, mw=mw, mc=mc: e.matmul(
                            wtile[:, j, mc:mc + mw], identb[:], maskb[:, mo:mo + mw], start=False, stop=True,
                            skip_group_check=True),
                            r=[], w=[("psw", si % 3)])

            def emit_exp(si):
                stp = steps[si]
                c0 = stp["c0"]
                S.op("act", lambda e: e.activation(PT[si % 3][:, :, c0:512], PSW[si % 3][:, :, c0:512], AF.Exp,
                                                   scale=SCALE),
                     r=[("psw", si % 3)], w=[("pt", si % 3)])

            def emit_pv(si):
                stp = steps[si]
                h, g, c0 = stp["h"], stp["g"], stp["c0"]
                b = h % 2
                if stp.get("first"):
                    ocount[0] += 1
                o = ocount[0] % 2
                for j, slot in enumerate(stp["slots"]):
                    first = bool(stp.get("first")) and j == 0
                    last = bool(stp.get("last")) and j == 1
                    mrows = 65 if b == 0 else 128
                    cj = stp["masks"][j][2] if stp["masks"] is not None else c0
                    S.op("pe", lambda e, j=j, slot=slot, first=first, last=last, mrows=mrows, cj=cj: e.matmul(
                        PSO1[0:mrows, cj:512], vaug(b, h, slot), PT[si % 3][:, j, cj:512], start=first, stop=last,
                        skip_group_check=True),
                        r=[("pt", si % 3), ("vb", b, slot // 8), ("vbc", b)], w=["pso"])
                if stp.get("last"):
                    mrows = 65 if b == 0 else 128
                    S.op("dve", lambda e, mrows=mrows: e.tensor_copy(osb[o][0:mrows, :], PSO1[0:mrows, :]),
                         r=["pso"], w=[("osb", o)])
                    dr = 64 if h % 2 == 0 else 0
                    for q4 in range(4):
                        pending_rcp.append((si + q4, o, dr, q4))
                    pending_epi.append((si + 6, h, g, o))

            def emit_epilogue(h, g, o):
                if KB_NOEPI:
                    return
                dr = 64 if h % 2 == 0 else 0
                rows = slice(0, 64) if h % 2 == 0 else slice(64, 128)
                r_ = rec[o]
                ot = otmp[o]
                ob_ = osb[o]
                S.op("pe", lambda e: e.matmul(PSG, onesf[dr:dr + 1, :], r_[dr:dr + 1, :], start=True, stop=True),
                     r=[("rec", o)], w=["psg"])
                S.op("dve", lambda e: e.tensor_tensor(ot[rows, :], ob_[rows, :], PSG[rows, :], op=ALU.mult),
                     r=[("osb", o), "psg"], w=[("otmp", o)])
                mg = MG[rows, h // 2, g * 512:(g + 1) * 512]
                S.op("pool", lambda e: e.tensor_tensor(mg, ot[rows, :], mg, op=ALU.mult),
                     r=[("otmp", o)], w=[("mgo", h, g)])

            if n_steps > 0:
                emit_qk(0)
            if n_steps > 1:
                emit_qk(1)
            cur_h = 0
            gen_every = 3 if p == 0 else 4
            since_gen = 0
            for si in range(n_steps):
                emit_exp(si)
                if si + 2 < n_steps:
                    emit_qk(si + 2)
                if not KB_NOPV:
                    emit_pv(si)
                while pending_rcp and pending_rcp[0][0] <= si:
                    _, ro_, rdr, q4 = pending_rcp.pop(0)
                    cs = slice(q4 * 128, (q4 + 1) * 128)
                    S.op("dve", lambda e, ro_=ro_, rdr=rdr, cs=cs: e.reciprocal(rec[ro_][rdr:rdr + 1, cs],
                                                                               osb[ro_][rdr:rdr + 1, cs]),
                         r=[("osb", ro_)], w=[("rec", ro_)])
                while pending_epi and pending_epi[0][0] <= si:
                    _, eh, eg, eo = pending_epi.pop(0)
                    emit_epilogue(eh, eg, eo)
                h = steps[si]["h"]
                if steps[si].get("first") and steps[si]["g"] == 0 and h + 1 < 8:
                    pending_gen = gen_chunks(h + 1)
                    since_gen = 0
                since_gen += 1
                if pending_gen and since_gen >= gen_every:
                    pending_gen.pop(0)()
                    since_gen = 0
                if steps[si].get("last") and steps[si]["g"] == 3:
                    while pending_gen:
                        pending_gen.pop(0)()
            while pending_rcp:
                _, ro_, rdr, q4 = pending_rcp.pop(0)
                cs = slice(q4 * 128, (q4 + 1) * 128)
                S.op("dve", lambda e, ro_=ro_, rdr=rdr, cs=cs: e.reciprocal(rec[ro_][rdr:rdr + 1, cs],
                                                                           osb[ro_][rdr:rdr + 1, cs]),
                     r=[("osb", ro_)], w=[("rec", ro_)])
            while pending_epi:
                _, eh, eg, eo = pending_epi.pop(0)
                emit_epilogue(eh, eg, eo)
            if debug and p == 0:
                dd = cv.take(512, F32)
                S.op("dve", lambda e: e.tensor_copy(dd[0:96, :], KB[1][:, 512:1024]), r=[("kb", 1, 1)], w=["dd"])
                S.dma("sp", lambda e: e.dma_start(out=dbg["d_kb"], in_=dd[0:96, :]), dsD, r=["dd"])
                S.op("dve", lambda e: e.tensor_copy(dd.rearrange("p (s d) -> p s d", s=8), VB[1][:, 8:16, 64:128]),
                     r=[("vb", 1, 1), "dd"], w=["dd"])
                S.dma("sp", lambda e: e.dma_start(out=dbg["d_vb"], in_=dd), dsD, r=["dd"])
                S.op("dve", lambda e: e.tensor_copy(dd[:, 0:256], MG[:, 0, 256:512]),
                     r=[("mgo", 0, 0), ("mgo", 1, 0), "dd"], w=["dd"])
                S.op("dve", lambda e: e.tensor_copy(dd[:, 256:512], MG[:, 3, 1024 + 256:1024 + 512]),
                     r=[("mgo", 6, 2), ("mgo", 7, 2), "dd"], w=["dd"])
                S.dma("sp", lambda e: e.dma_start(out=dbg["d_merged"], in_=dd), dsD, r=["dd"])

        WC_bytes = 8 * 1536 * 2 + 8 * 1024 * 2

        def wc_views():
            base = (ARENA - WC_bytes) // 2
            wci = arena[:, base:base + 8 * 1536].rearrange("p (c n) -> p c n", c=8)
            wco = arena[:, base + 8 * 1536:base + 8 * 1536 + 8 * 1024].rearrange("p (c n) -> p c n", c=8)
            return wci, wco

        dsWC = dsem()

        def load_WC():
            wci, wco = wc_views()
            S.dma("sp", lambda e: e.dma_start(out=wci, in_=w_in_bf.rearrange("(c p) n -> p c n", p=128)[:, :, NA:DIN]),
                  dsWC, w=["WCI"])
            S.dma("sp", lambda e: e.dma_start(out=wco, in_=w_out_bf.rearrange("(c p) n -> p c n", p=128)),
                  dsWC, w=["WCO"])

        dsY = [dsem(), dsem()]

        def phase_C(p):
            cv = Carver()
            wci, wco = wc_views()
            XR = [cv.take(D, F32) for _ in range(2)]
            XB = [cv.take(D, BF16) for _ in range(2)]
            XTC = [cv.take(8 * 128, BF16).rearrange("p (c t) -> p c t", c=8) for _ in range(2)]
            UG = [cv.take(512, F32) for _ in range(2)]
            VG = [cv.take(512, F32) for _ in range(2)]
            ZB = [cv.take(512, F32) for _ in range(2)]
            VN = [cv.take(512, BF16) for _ in range(2)]
            ob = cv.take(512, BF16)
            obT = cv.take(512, BF16)
            hh = cv.take(D, F32)
            YO = [cv.take(D, F32) for _ in range(2)]
            st1 = [cv.take(16, F32) for _ in range(2)]
            st2 = cv.take(24, F32)
            assert cv.off <= ARENA - WC_bytes, ("phase C arena", cv.off)
            dsXR = [dsem(), dsem()]
            dsXB = [dsem(), dsem()]

            def load_block(t):
                i = t % 2
                gslot = 4 * (t // 2) + (t % 2)
                rows = slice(gslot * 128, (gslot + 1) * 128)
                S.dma("sp", lambda e: e.dma_start(out=XR[i], in_=xs[rows, :]), dsXR[i], w=[("XR", i)])
                S.dma("pool", lambda e: e.dma_start(out=XB[i], in_=xs[rows, :]), dsXB[i], w=[("XB", i)])

            def s1a(t):
                i = t % 2
                xtc = XTC[i]
                for hf in range(2):
                    pt = pst_banks[hf]
                    for kk in range(4):
                        k = hf * 4 + kk
                        S.op("pe", lambda e, pt=pt, kk=kk, k=k: e.transpose(
                            pt[:, kk * 128:(kk + 1) * 128], XB[i][:, k * 128:(k + 1) * 128], identb[:]),
                            r=[("XB", i), "identb"], w=[("pst", hf)])
                    dst = xtc[:, hf * 4:hf * 4 + 4, :]
                    if True:
                        S.op("act", lambda e, pt=pt, dst=dst: e.copy(dst, pt.rearrange("p (c t) -> p c t", c=4)),
                             r=[("pst", hf)], w=[("XTC", i, hf)])
                    else:
                        S.op("dve", lambda e, pt=pt, dst=dst: e.tensor_copy(dst, pt.rearrange("p (c t) -> p c t", c=4)),
                             r=[("pst", hf)], w=[("XTC", i, hf)])

            def s1b_mm(t):
                i = t % 2
                xtc = XTC[i]
                xk = [("XTC", i, 0), ("XTC", i, 1)]
                ug, vg, zb = UG[i], VG[i], ZB[i]
                th = hh[:, 0:512]
                pss = {}
                for j in (1, 0, 2):
                    bb, pb, kb_ = next_bank()
                    pss[j] = (pb, kb_)
                    for k in range(8):
                        S.op("pe", lambda e, pb=pb, j=j, k=k: e.matmul(
                            pb, xtc[:, k, :], wci[:, k, j * 512:(j + 1) * 512], start=(k == 0), stop=(k == 7)),
                            r=xk + ["WCI"], w=[kb_])
                S.op("act", lambda e: e.activation(vg, pss[1][0], AF.Gelu), r=[pss[1][1]], w=[("vg", i)])
                S.op("act", lambda e: e.activation(ug, pss[0][0], AF.Gelu), r=[pss[0][1]], w=[("ug", i)])
                S.op("act", lambda e: e.activation(th, pss[2][0], AF.Tanh, scale=0.5), r=[pss[2][1]], w=[("hh", 0)])
                S.op("act", lambda e: e.copy(zb, pss[2][0]), r=[pss[2][1]], w=[("zb", i)])
                S.op("pool", lambda e: e.tensor_tensor(th, th, zb, op=ALU.mult), r=[("hh", 0), ("zb", i)], w=[("hh", 0)])
                S.op("pool", lambda e: e.tensor_tensor(zb, th, zb, op=ALU.add), r=[("hh", 0), ("zb", i)], w=[("zb", i)])
                S.op("pool", lambda e: e.tensor_tensor(ug, ug, zb, op=ALU.mult), r=[("ug", i), ("zb", i)], w=[("ug", i)])

            def s1b_ln(t):
                i = t % 2
                vg, vn, st = VG[i], VN[i], st1[i]
                S.op("dve", lambda e: e.bn_stats(st[:, 0:6], vg), r=[("vg", i)], w=[("st1a", i)])
                S.op("dve", lambda e: e.bn_aggr(st[:, 6:8], st[:, 0:6]), r=[("st1a", i)], w=[("st1b", i)])
                S.op("dve", lambda e: e.tensor_scalar(st[:, 8:9], st[:, 7:8], EPS, None, op0=ALU.add),
                     r=[("st1b", i)], w=[("st1c", i)])
                S.op("pool", lambda e: e.tensor_tensor(st[:, 8:9], st[:, 8:9], neghalf[:, 0:1], op=ALU.pow),
                     r=[("st1c", i), "neghalf"], w=[("st1c", i)])
                S.op("dve", lambda e: e.scalar_tensor_tensor(out=vg, in0=vg, scalar=st[:, 6:7], in1=sgug[:],
                                                             op0=ALU.subtract, op1=ALU.mult),
                     r=[("vg", i), ("st1b", i), "sgug"], w=[("vg", i)])
                S.op("dve", lambda e: e.scalar_tensor_tensor(out=vn, in0=vg, scalar=st[:, 8:9], in1=sgub[:],
                                                             op0=ALU.mult, op1=ALU.add),
                     r=[("vg", i), ("st1c", i), "sgub"], w=[("vn", i)])

            def s2a(t):
                i = t % 2
                ug, vn = UG[i], VN[i]
                bsv, psv, ksv = next_bank()
                S.op("pe", lambda e: e.matmul(psv, bspb[:], indb[:], start=True, stop=False, skip_group_check=True),
                     r=["bspb", "indb"], w=[ksv])
                for h in range(8):
                    S.op("pe", lambda e, h=h: e.matmul(psv[:, h * 64:(h + 1) * 64], wsp[:, h, :],
                                                      vn[:, h * 64:(h + 1) * 64], start=False, stop=(h == 7),
                                                      skip_group_check=True),
                         r=[("wsp", h), ("vn", i)], w=[ksv])
                S.op("dve", lambda e: e.scalar_tensor_tensor(out=ob, in0=psv, scalar=0.5, in1=ug, op0=ALU.mult, op1=ALU.mult),
                     r=[ksv, ("ug", i)], w=["ob"])

            def s2b(t):
                pt = pst_banks[0]
                for c in range(4):
                    S.op("pe", lambda e, c=c: e.transpose(pt[:, c * 128:(c + 1) * 128],
                                                         ob[:, c * 128:(c + 1) * 128], identb[:]),
                         r=["ob", "identb"], w=[("pst", 0)])
                S.op("act", lambda e: e.copy(obT, pt), r=[("pst", 0)], w=["obT"])

            def s2c(t):
                i = t % 2
                lt = slice((t - 16 * p) * 128, (t - 16 * p + 1) * 128)
                outs = []
                for nh in range(2):
                    bo, po, ko = next_bank()
                    outs.append((po, ko))
                    for c in range(8):
                        if c < 4:
                            lhsT = MG[:, c, lt]
                            rr = []
                        else:
                            lhsT = obT[:, (c - 4) * 128:(c - 3) * 128]
                            rr = ["obT"]
                        S.op("pe", lambda e, po=po, lhsT=lhsT, c=c, nh=nh: e.matmul(
                            po, lhsT, wco[:, c, nh * 512:(nh + 1) * 512], start=(c == 0), stop=(c == 7)),
                            r=rr + ["WCO"], w=[ko])
                for nh in range(2):
                    po, ko = outs[nh]
                    S.op("dve", lambda e, po=po, nh=nh: e.scalar_tensor_tensor(
                        out=hh[:, nh * 512:(nh + 1) * 512], in0=XR[i][:, nh * 512:(nh + 1) * 512], scalar=ALPHA,
                        in1=po, op0=ALU.mult, op1=ALU.add),
                        r=[ko, ("XR", i)], w=[("hh", nh)])
                    S.op("dve", lambda e, nh=nh: e.bn_stats(st2[:, 6 * nh:6 * nh + 6], hh[:, nh * 512:(nh + 1) * 512]),
                         r=[("hh", nh)], w=[("st2a", nh)])
                S.op("dve", lambda e: e.bn_aggr(st2[:, 12:14], st2[:, 0:12]), r=[("st2a", 0), ("st2a", 1)], w=["st2b"])
                S.op("dve", lambda e: e.tensor_scalar(st2[:, 14:15], st2[:, 13:14], EPS, None, op0=ALU.add),
                     r=["st2b"], w=["st2c"])
                S.op("pool", lambda e: e.tensor_tensor(st2[:, 14:15], st2[:, 14:15], neghalf[:, 0:1], op=ALU.pow),
                     r=["st2c", "neghalf"], w=["st2c"])
                S.op("dve", lambda e: e.scalar_tensor_tensor(out=hh, in0=hh, scalar=st2[:, 12:13], in1=lng[:],
                                                             op0=ALU.subtract, op1=ALU.mult),
                     r=[("hh", 0), ("hh", 1), "st2b", "lng"], w=[("hh", 0), ("hh", 1)])
                yo = YO[i]
                S.op("dve", lambda e: e.scalar_tensor_tensor(out=yo, in0=hh, scalar=st2[:, 14:15], in1=lnb[:],
                                                             op0=ALU.mult, op1=ALU.add),
                     r=[("hh", 0), ("hh", 1), "st2c", "lnb"], w=[("YO", i)])
                S.dma("sp", lambda e: e.dma_start(out=y[t * 128:(t + 1) * 128, :], in_=yo),
                      dsY[i], r=[("YO", i)])

            def load_xb(t):
                i = t % 2
                gslot = 4 * (t // 2) + (t % 2)
                rows = slice(gslot * 128, (gslot + 1) * 128)
                S.dma("pool", lambda e: e.dma_start(out=XB[i], in_=xs[rows, :]), dsXB[i], w=[("XB", i)])

            def load_xr(t):
                i = t % 2
                gslot = 4 * (t // 2) + (t % 2)
                rows = slice(gslot * 128, (gslot + 1) * 128)
                S.dma("sp", lambda e: e.dma_start(out=XR[i], in_=xs[rows, :]), dsXR[i], w=[("XR", i)])

            blocks = list(range(16 * p, 16 * p + 16))
            nb = len(blocks)
            load_xb(blocks[0])
            load_xb(blocks[1])
            load_xr(blocks[0])
            load_xr(blocks[1])
            s1a(blocks[0])
            s1b_mm(blocks[0])
            s1b_ln(blocks[0])
            s1a(blocks[1])
            load_xb(blocks[2])
            for bi, t in enumerate(blocks):
                s2a(t)
                if bi + 1 < nb:
                    s1b_mm(blocks[bi + 1])
                if bi + 2 < nb:
                    s1a(blocks[bi + 2])
                if bi + 3 < nb:
                    load_xb(blocks[bi + 3])
                s2b(t)
                if bi + 1 < nb:
                    s1b_ln(blocks[bi + 1])
                s2c(t)
                if bi + 2 < nb:
                    load_xr(blocks[bi + 2])

        for p in passes:
            if "A" in phases:
                phase_A(p)
                S.barrier()
            if "B" in phases:
                if "C" in phases and WC_EARLY:
                    load_WC()
                phase_B(p)
                S.barrier()
            if "C" in phases:
                if "B" not in phases or not WC_EARLY:
                    load_WC()
                phase_C(p)
                S.barrier()
        S.emit(nc, block, esems, dsY)
    return nc


def _slot_perm(par):
    order = []
    for m in range(16):
        if par == 0:
            order += [4 * m, 4 * m + 3, 4 * m + 1, 4 * m + 2]
        else:
            order += [4 * m + 1, 4 * m + 2, 4 * m, 4 * m + 3]
    return order


def make_in_maps(x, positions, w_in, q_norm_g, w_uq, kv_norm_g, w_ukv, sgu_norm_g, sgu_norm_b,
                 w_spatial, b_spatial, w_out, ln_g, ln_b):
    f = np.float32
    w_in = np.ascontiguousarray(w_in, dtype=f)
    swap = np.concatenate([np.arange(16, 32), np.arange(0, 16)])
    w_krs = np.ascontiguousarray(w_in[:, 384:416][:, swap])
    w_uq = np.ascontiguousarray(w_uq, dtype=f)
    w_uqs = np.zeros_like(w_uq)
    for h in range(8):
        w_uqs[:, h * 96 + 64:h * 96 + 96] = w_uq[:, h * 96 + 64:h * 96 + 96][:, swap]
    qg = np.ascontiguousarray(np.asarray(q_norm_g, dtype=f).reshape(2, 128).T)
    kvg = np.ascontiguousarray(np.asarray(kv_norm_g, dtype=f).reshape(128, 1))
    wspT = np.ascontiguousarray(np.transpose(np.asarray(w_spatial, dtype=f), (2, 0, 1)))
    si = np.arange(128)
    tril = (si[:, None] <= si[None, :]).astype(f)
    bspT = np.ascontiguousarray(np.asarray(b_spatial, dtype=f).T)
    ident = np.eye(128, dtype=f)
    invf = (1.0 / (10000.0 ** (np.arange(16, dtype=np.float64) / 16.0)))
    ropec = np.zeros((128, 6), f)
    ropec[:, 1] = -TWO_PI
    ropec[:, 2] = -TWO_PI
    ropec[:, 3] = 0.5 * math.pi
    for pp in range(64, 96):
        i = (pp - 64) % 16
        ropec[pp, 0] = invf[i] / TWO_PI
        ropec[pp, 1] = TWO_PI if pp < 80 else -TWO_PI
    tri = np.where(si[:, None] > si[None, :], NEG, 0.0).astype(f)
    allm = np.full((128, 128), NEG, f)
    zero = np.zeros((128, 128), f)
    ind_h = np.zeros((8, 512), f)
    for h in range(8):
        ind_h[h, h * 64:(h + 1) * 64] = 1.0
    common = dict(bsp_ht=np.ascontiguousarray(np.asarray(b_spatial, dtype=f)), ind_h=ind_h, w_in=w_in, w_krs=w_krs, w_uq=w_uq, w_uqs=w_uqs, qg=qg, w_ukv=np.ascontiguousarray(w_ukv, dtype=f),
                  kvg=kvg, wspT=wspT, tril=tril, bspT=bspT,
                  sgu_g=np.asarray(sgu_norm_g, dtype=f).reshape(1, 512), sgu_b=np.asarray(sgu_norm_b, dtype=f).reshape(1, 512),
                  ln_g=np.asarray(ln_g, dtype=f).reshape(1, D), ln_b=np.asarray(ln_b, dtype=f).reshape(1, D),
                  w_out=np.ascontiguousarray(w_out, dtype=f), ident=ident, ropec=ropec)
    in_maps = []
    perms = []
    for c in range(8):
        b, par = c // 2, c % 2
        order = _slot_perm(par)
        perms.append(order)
        xb = np.asarray(x[b], dtype=f).reshape(64, 128, D)[order].reshape(SEQ, D)
        pb = np.asarray(positions[b], dtype=np.int32).reshape(64, 128)[order].reshape(1, SEQ)
        mA = allm if par == 0 else zero
        mB = zero if par == 0 else allm
        mk = np.concatenate([tri, allm, tri, mA, allm, mB], axis=1)
        d = dict(common)
        d.update(xs=np.ascontiguousarray(xb), pos=np.ascontiguousarray(pb), masks=np.ascontiguousarray(mk))
        in_maps.append(d)
    return in_maps, perms


_NC_CACHE = {}


def kernel(x, positions, w_in, q_norm_g, w_uq, kv_norm_g, w_ukv, sgu_norm_g, sgu_norm_b,
           w_spatial, b_spatial, w_out, ln_g, ln_b):
    in_maps, perms = make_in_maps(x, positions, w_in, q_norm_g, w_uq, kv_norm_g, w_ukv, sgu_norm_g,
                                  sgu_norm_b, w_spatial, b_spatial, w_out, ln_g, ln_b)
    if "nc" not in _NC_CACHE:
        _NC_CACHE["nc"] = build_program()
    nc = _NC_CACHE["nc"]
    res = run_bass_kernel_spmd(nc, in_maps, core_ids=list(range(8)))
    out = np.empty((4, SEQ, D), np.float32)
    for c in range(8):
        b = c // 2
        order = perms[c]
        yc = np.asarray(res.results[c]["y"], dtype=np.float32).reshape(32, 128, D)
        ov = out[b].reshape(64, 128, D)
        for t in range(32):
            slot = 4 * (t // 2) + (t % 2)
            ov[order[slot]] = yc[t]
    return out
```
